# Optimizing a Trainium2 kernel written in Bass

```python
import math
import jax, jax.numpy as jnp
from jax import lax
import numpy as np

D_MODEL = 1024
BATCH = 4
SEQ = 8192
DEPTH = 4

N_MIXERS = 3
N_POOL_LAYERS = (DEPTH + 2) // 3
N_SSD_LAYERS = (DEPTH + 1) // 3
N_SB_LAYERS = DEPTH // 3
NORM_EPS = 1e-6

POOL_GROUPS = 4
POOL_WINDOWS = (2, 4, 8, 16)
POOL_GROUP_DIM = D_MODEL // POOL_GROUPS

SSD_D_INNER = 2 * D_MODEL
SSD_HEAD_DIM = 64
SSD_HEADS = SSD_D_INNER // SSD_HEAD_DIM
SSD_GROUPS = 8
SSD_HEADS_PER_GROUP = SSD_HEADS // SSD_GROUPS
SSD_STATE = 128
SSD_CONV = 4
SSD_CHUNK = 256
SSD_GN = SSD_GROUPS * SSD_STATE
SSD_CONV_CH = SSD_D_INNER + 2 * SSD_GN
SSD_IN_DIM = SSD_D_INNER + SSD_CONV_CH + SSD_HEADS
SSD_NORM_GROUP = SSD_D_INNER // SSD_GROUPS

SB_HEADS = 16
SB_HEAD_DIM = D_MODEL // SB_HEADS
SB_BLOCK = 128

FFN_HIDDEN = ((8 * D_MODEL + 3 * 256 - 1) // (3 * 256)) * 256

kernel_name = "hybrid_pool_ssd_stickbreak_block"


def rms_norm(x, gain):
    xf = x.astype(jnp.float32)
    y = xf * lax.rsqrt(jnp.mean(xf * xf, axis=-1, keepdims=True) + NORM_EPS)
    return (y * gain.astype(jnp.float32)).astype(x.dtype)


def pool_mixer(h, w_in, w_group, scale):
    b, s, _ = h.shape
    u = (h @ w_in).reshape(b, s, POOL_GROUPS, POOL_GROUP_DIM).astype(jnp.float32)
    cs = jnp.cumsum(u, axis=1)
    pos = jnp.arange(s)
    outs = []
    for g, w in enumerate(POOL_WINDOWS):
        csg = cs[:, :, g]
        lagged = jnp.pad(csg[:, : s - w], ((0, 0), (w, 0), (0, 0)))
        count = jnp.minimum(pos + 1, w).astype(jnp.float32)[None, :, None]
        outs.append((csg - lagged) / count - u[:, :, g])
    p = jnp.stack(outs, axis=2)
    y = jnp.einsum('bsgc,gcd->bsgd', p, w_group.astype(jnp.float32))
    y = y.reshape(b, s, D_MODEL) * scale.astype(jnp.float32)
    return y.astype(h.dtype)


def ssd_chunked_scan(xdt, da, bmat, cmat):
    b, s = da.shape[:2]
    pad = (-s) % SSD_CHUNK

    def chunks(t):
        t = jnp.pad(t, [(0, 0), (0, pad)] + [(0, 0)] * (t.ndim - 2))
        return jnp.swapaxes(t.reshape(b, -1, SSD_CHUNK, *t.shape[2:]), 0, 1)

    causal = jnp.tril(jnp.ones((SSD_CHUNK, SSD_CHUNK), bool))[None, :, :, None, None]

    def step(state, inp):
        xc, ac, bc, cc = inp
        acum = jnp.cumsum(ac, axis=1)
        diff = acum[:, :, None] - acum[:, None, :]
        decay = jnp.exp(jnp.where(causal, diff, -jnp.inf))
        cb = jnp.einsum('btgn,bsgn->btsg', cc, bc)
        y = jnp.einsum('btsg,btsgh,bsghp->btghp', cb, decay, xc)
        y = y + jnp.einsum('btgn,bghpn,btgh->btghp', cc, state, jnp.exp(acum))
        a_last = acum[:, -1]
        w = jnp.exp(a_last[:, None] - acum)
        state = state * jnp.exp(a_last)[..., None, None] + jnp.einsum('bsgn,bsgh,bsghp->bghpn', bc, w, xc)
        return state, y

    state0 = jnp.zeros((b, SSD_GROUPS, SSD_HEADS_PER_GROUP, SSD_HEAD_DIM, SSD_STATE), jnp.float32)
    _, ys = lax.scan(step, state0, (chunks(xdt), chunks(da), chunks(bmat), chunks(cmat)))
    ys = jnp.swapaxes(ys, 0, 1).reshape(b, -1, *ys.shape[3:])
    return ys[:, :s]


def ssd_mixer(h, w_in, conv_w, conv_b, dt_bias, a_log, d_skip, out_norm, w_out):
    b, s, _ = h.shape
    f32 = jnp.float32
    proj = h @ w_in
    z = proj[..., :SSD_D_INNER]
    xbc = proj[..., SSD_D_INNER:SSD_D_INNER + SSD_CONV_CH]
    dt = proj[..., SSD_D_INNER + SSD_CONV_CH:]
    xbc = lax.conv_general_dilated(
        xbc, conv_w[:, None, :].astype(xbc.dtype), window_strides=(1,),
        padding=[(SSD_CONV - 1, 0)], dimension_numbers=('NWC', 'WIO', 'NWC'),
        feature_group_count=SSD_CONV_CH)
    xbc = jax.nn.silu(xbc.astype(f32) + conv_b.astype(f32))
    xs = xbc[..., :SSD_D_INNER].reshape(b, s, SSD_GROUPS, SSD_HEADS_PER_GROUP, SSD_HEAD_DIM)
    bm = xbc[..., SSD_D_INNER:SSD_D_INNER + SSD_GN].reshape(b, s, SSD_GROUPS, SSD_STATE)
    cm = xbc[..., SSD_D_INNER + SSD_GN:].reshape(b, s, SSD_GROUPS, SSD_STATE)
    dt = jax.nn.softplus(dt.astype(f32) + dt_bias.astype(f32)).reshape(b, s, SSD_GROUPS, SSD_HEADS_PER_GROUP)
    a = -jnp.exp(a_log.astype(f32)).reshape(SSD_GROUPS, SSD_HEADS_PER_GROUP)
    y = ssd_chunked_scan(xs * dt[..., None], dt * a, bm, cm)
    y = y + d_skip.astype(f32).reshape(SSD_GROUPS, SSD_HEADS_PER_GROUP, 1) * xs
    g = (y.reshape(b, s, SSD_D_INNER) * jax.nn.silu(z.astype(f32))).reshape(b, s, SSD_GROUPS, SSD_NORM_GROUP)
    g = g * lax.rsqrt(jnp.mean(g * g, axis=-1, keepdims=True) + NORM_EPS)
    g = g.reshape(b, s, SSD_D_INNER) * out_norm.astype(f32)
    return g.astype(h.dtype) @ w_out


def stick_breaking_mixer(h, w_qkv, q_norm, k_norm, w_out):
    b, s, _ = h.shape
    f32 = jnp.float32
    qkv = (h @ w_qkv).reshape(b, s, 3, SB_HEADS, SB_HEAD_DIM)
    q = rms_norm(qkv[:, :, 0], q_norm).astype(f32).transpose(0, 2, 1, 3)
    k = rms_norm(qkv[:, :, 1], k_norm).astype(f32).transpose(0, 2, 1, 3)
    v = qkv[:, :, 2].astype(f32).transpose(0, 2, 1, 3)
    n_blocks = s // SB_BLOCK
    qb = q.reshape(b, SB_HEADS, n_blocks, SB_BLOCK, SB_HEAD_DIM).transpose(2, 0, 1, 3, 4)
    inv_sqrt_d = 1.0 / math.sqrt(SB_HEAD_DIM)
    key_pos = jnp.arange(s)

    def block(args):
        q_blk, blk = args
        z = jnp.einsum('bhqd,bhkd->bhqk', q_blk, k) * inv_sqrt_d
        t = blk * SB_BLOCK + jnp.arange(SB_BLOCK)
        mask = key_pos[None, :] < t[:, None]
        log_1m = jnp.where(mask, jax.nn.log_sigmoid(-z), 0.0)
        after = lax.cumsum(log_1m, axis=3, reverse=True) - log_1m
        a = jnp.where(mask, jnp.exp(jax.nn.log_sigmoid(z) + after), 0.0)
        return jnp.einsum('bhqk,bhkd->bhqd', a, v)

    o = lax.map(block, (qb, jnp.arange(n_blocks)))
    o = o.transpose(1, 0, 3, 2, 4).reshape(b, s, D_MODEL)
    return o.astype(h.dtype) @ w_out


def swiglu(h, w_gate, w_up, w_down):
    return (jax.nn.silu(h @ w_gate) * (h @ w_up)) @ w_down


def setup_inputs(seed: int = 0) -> dict:
    key = jax.random.key(seed)
    ks = jax.random.split(key, 24)
    f32 = jnp.float32

    def nrm(k, shape, scale):
        return jax.random.normal(k, shape, f32) * scale

    def gain(k, shape):
        return 1.0 + 0.02 * jax.random.normal(k, shape, f32)

    dt0 = jnp.exp(jax.random.uniform(ks[10], (N_SSD_LAYERS, SSD_HEADS), f32, math.log(1e-3), math.log(1e-1)))
    return {
        "x": nrm(ks[0], (BATCH, SEQ, D_MODEL), 1.0),
        "mix_norm": gain(ks[1], (DEPTH, D_MODEL)),
        "pool_in": nrm(ks[2], (N_POOL_LAYERS, D_MODEL, D_MODEL), D_MODEL ** -0.5),
        "pool_group": nrm(ks[3], (N_POOL_LAYERS, POOL_GROUPS, POOL_GROUP_DIM, POOL_GROUP_DIM), POOL_GROUP_DIM ** -0.5),
        "pool_scale": gain(ks[4], (N_POOL_LAYERS, D_MODEL)),
        "ssd_in": nrm(ks[5], (N_SSD_LAYERS, D_MODEL, SSD_IN_DIM), D_MODEL ** -0.5),
        "ssd_conv_w": nrm(ks[6], (N_SSD_LAYERS, SSD_CONV, SSD_CONV_CH), SSD_CONV ** -0.5),
        "ssd_conv_b": nrm(ks[7], (N_SSD_LAYERS, SSD_CONV_CH), 0.01),
        "ssd_dt_bias": dt0 + jnp.log(-jnp.expm1(-dt0)),
        "ssd_a_log": jnp.log(jax.random.uniform(ks[8], (N_SSD_LAYERS, SSD_HEADS), f32, 1.0, 16.0)),
        "ssd_d": gain(ks[9], (N_SSD_LAYERS, SSD_HEADS)),
        "ssd_out_norm": gain(ks[11], (N_SSD_LAYERS, SSD_D_INNER)),
        "ssd_out": nrm(ks[12], (N_SSD_LAYERS, SSD_D_INNER, D_MODEL), SSD_D_INNER ** -0.5),
        "sb_qkv": nrm(ks[13], (N_SB_LAYERS, D_MODEL, 3 * D_MODEL), D_MODEL ** -0.5),
        "sb_q_norm": gain(ks[14], (N_SB_LAYERS, SB_HEAD_DIM)),
        "sb_k_norm": gain(ks[15], (N_SB_LAYERS, SB_HEAD_DIM)),
        "sb_out": nrm(ks[16], (N_SB_LAYERS, D_MODEL, D_MODEL), D_MODEL ** -0.5),
        "ffn_norm": gain(ks[17], (DEPTH, D_MODEL)),
        "ffn_gate": nrm(ks[18], (DEPTH, D_MODEL, FFN_HIDDEN), D_MODEL ** -0.5),
        "ffn_up": nrm(ks[19], (DEPTH, D_MODEL, FFN_HIDDEN), D_MODEL ** -0.5),
        "ffn_down": nrm(ks[20], (DEPTH, FFN_HIDDEN, D_MODEL), FFN_HIDDEN ** -0.5),
    }


def reference(x, mix_norm, pool_in, pool_group, pool_scale, ssd_in, ssd_conv_w, ssd_conv_b,
              ssd_dt_bias, ssd_a_log, ssd_d, ssd_out_norm, ssd_out, sb_qkv, sb_q_norm, sb_k_norm,
              sb_out, ffn_norm, ffn_gate, ffn_up, ffn_down):
    for i in range(DEPTH):
        kind, j = i % N_MIXERS, i // N_MIXERS
        h = rms_norm(x, mix_norm[i])
        if kind == 0:
            m = pool_mixer(h, pool_in[j], pool_group[j], pool_scale[j])
        elif kind == 1:
            m = ssd_mixer(h, ssd_in[j], ssd_conv_w[j], ssd_conv_b[j], ssd_dt_bias[j], ssd_a_log[j],
                          ssd_d[j], ssd_out_norm[j], ssd_out[j])
        else:
            m = stick_breaking_mixer(h, sb_qkv[j], sb_q_norm[j], sb_k_norm[j], sb_out[j])
        x = x + m
        h = rms_norm(x, ffn_norm[i])
        x = x + swiglu(h, ffn_gate[i], ffn_up[i], ffn_down[i])
    return x
```

```python
import numpy as np
from contextlib import ExitStack
import concourse.bass as bass
import concourse.mybir as mybir
from concourse.bass_utils import run_bass_kernel_spmd

F32 = mybir.dt.float32
BF16 = mybir.dt.bfloat16
AF = mybir.ActivationFunctionType
ALU = mybir.AluOpType

D = 1024
NCH = 8
T = 512
EPS = 1e-6
FFN_H = 2816
FFN_HC = 22
SSD_DI = 2048
SSD_IN = 6176
SEQ = 8192
DEPTH = 4
FH = 11
PCH = 4
NG = 4
NHD = 16
SBP = 4
PAIRS = [[0, 1], [2, 3], [4, 5], [6, 7]]
SSD_PIPE = True


class Buf:
    __slots__ = ("name", "w", "r", "dsem", "dcnt", "q")

    def __init__(self, name):
        self.name = name
        self.w = {}
        self.r = {}
        self.dsem = None
        self.dcnt = 0
        self.q = None


class EngW:
    def __init__(self, nc, eng, name):
        self.eng = eng
        self.name = name
        self.sem = nc.alloc_semaphore("tl_" + name)
        self.cnt = 0
        self.seen = {}


class K:
    def __init__(self, nc):
        self.nc = nc
        self.pe = EngW(nc, nc.tensor, "pe")
        self.act = EngW(nc, nc.scalar, "act")
        self.dve = EngW(nc, nc.vector, "dve")
        self.pool = EngW(nc, nc.gpsimd, "pool")
        self.sp = EngW(nc, nc.sync, "sp")
        self.out_events = []
        self.nbuf = 0
        self.nsem = 0
        self.cc_sem = None
        self.cc_cnt = 0
        self.free_sems = {}
        self.stage_bufs = []

    def barrier(self):
        engs = [self.pe, self.act, self.dve, self.pool, self.sp]
        evs = [(e.sem, e.cnt) for e in engs if e.cnt > 0]
        for b in self.stage_bufs:
            if b.dsem is not None and b.dcnt > 0:
                evs.append((b.dsem, b.dcnt))
        for e in engs:
            self._wait(e, evs)

    def release_stage(self):
        self.barrier()
        for b in self.stage_bufs:
            if b.dsem is not None:
                self.free_sems.setdefault(b.q.name, []).append((b.dsem, b.dcnt))
                b.dsem = None
        self.stage_bufs = []

    def buf(self, name=None):
        self.nbuf += 1
        return Buf(name or f"b{self.nbuf}")

    def _wait(self, E, events, own=False):
        for sem, cnt in events:
            if sem is E.sem and (not own or E is self.pe):
                continue
            key = id(sem)
            if E.seen.get(key, 0) < cnt:
                E.eng.wait_ge(sem, cnt)
                E.seen[key] = cnt

    def op(self, E, fn, reads=(), writes=(), inc=True):
        for b in reads:
            self._wait(E, list(b.w.values()), own=True)
        for b in writes:
            self._wait(E, list(b.w.values()), own=True)
            self._wait(E, list(b.r.values()))
        ins = fn()
        if inc:
            E.cnt += 1
            ins.then_inc(E.sem, 1)
            seq = E.cnt
        else:
            seq = E.cnt + 1
        ev = (E.sem, seq)
        k = id(E.sem)
        for b in reads:
            b.r[k] = ev
        for b in writes:
            b.w = {k: ev}
            b.r = {}
        return ins

    def dma(self, Q, out, in_, reads, writes, sb, partial=False, is_out=False, **kw):
        for b in reads:
            self._wait(Q, list(b.w.values()))
        for b in writes:
            if not partial:
                self._wait(Q, list(b.w.values()))
            self._wait(Q, list(b.r.values()))
        if sb.dsem is None:
            fl = self.free_sems.get(Q.name, [])
            if fl:
                sem, cnt = fl.pop()
                if Q.seen.get(id(sem), 0) < cnt:
                    Q.eng.wait_ge(sem, cnt)
                    Q.seen[id(sem)] = cnt
                sb.dsem, sb.dcnt = sem, cnt
            else:
                self.nsem += 1
                sb.dsem = self.nc.alloc_semaphore(f"dq{self.nsem}")
            sb.q = Q
            self.stage_bufs.append(sb)
        assert sb.q is Q
        sb.dcnt += 16
        Q.eng.dma_start(out=out, in_=in_, **kw).then_inc(sb.dsem, 16)
        ev = (sb.dsem, sb.dcnt)
        k = id(sb.dsem)
        for b in reads:
            b.r[k] = ev
        for b in writes:
            if partial:
                b.w[k] = ev
            else:
                b.w = {k: ev}
            b.r = {}
        if is_out:
            self.out_events.append(ev)

    def allreduce(self, src, dst, reads, writes):
        Q = self.pool
        for b in reads:
            self._wait(Q, list(b.w.values()))
        for b in writes:
            self._wait(Q, list(b.w.values()))
            self._wait(Q, list(b.r.values()))
        if self.cc_sem is None:
            self.cc_sem = self.nc.alloc_semaphore("cc_sem")
            self.cc_cnt = 0
        self.cc_cnt += 1
        self.nc.gpsimd.collective_compute("AllReduce", ALU.add, replica_groups=PAIRS, ins=[src],
                                          outs=[dst]).then_inc(self.cc_sem, 1)
        ev = (self.cc_sem, self.cc_cnt)
        kk = id(self.cc_sem)
        for b in reads:
            b.r[kk] = ev
        for b in writes:
            b.w = {kk: ev}
            b.r = {}

    def finish(self):
        last = {}
        for sem, cnt in self.out_events:
            k = id(sem)
            if k not in last or last[k][1] < cnt:
                last[k] = (sem, cnt)
        self._wait(self.sp, list(last.values()))


class Ctx:
    def __init__(self, nc, S):
        self.nc = nc
        self.S = S
        self.NT = S // T
        self.k = K(nc)
        self.dbufs = {}
        self.ps = []
        for i in range(8):
            t = nc.alloc_psum_tensor(f"ps{i}", [128, 512], F32)
            self.ps.append((t, self.k.buf(f"ps{i}")))
        self._psi = 0

    def dbuf(self, name, i):
        key = (name, i)
        if key not in self.dbufs:
            self.dbufs[key] = self.k.buf(f"{name}_{i}")
        return self.dbufs[key]

    def psum(self):
        p = self.ps[self._psi % 8]
        self._psi += 1
        return p


def xtile(X, i):
    return X[i].rearrange("(c p) t -> p c t", p=128)


def emit_norm(cx, xt, xt_b, gain, ht, ht_b, sq, sq_b, rstd, rstd_b, consts):
    nc, k = cx.nc, cx.k
    gain, gain_b = gain
    ones_bf, ones_b = consts["ones"], consts["ones_b"]
    k.op(k.act, lambda: nc.scalar.activation(out=sq[:], in_=xt[:], func=AF.Square),
         reads=[xt_b], writes=[sq_b])
    ps, ps_b = cx.psum()
    for c in range(NCH):
        k.op(k.pe, lambda c=c: nc.tensor.matmul(ps[:], ones_bf[:], sq[:, c, :],
                                                 start=(c == 0), stop=(c == NCH - 1)),
             reads=[sq_b, ones_b], writes=[ps_b], inc=(c == NCH - 1))
    k.op(k.act, lambda: nc.scalar.activation(out=rstd[:], in_=ps[:], func=AF.Sqrt, bias=consts["eps"][:],
                                             scale=1.0 / D),
         reads=[ps_b, consts["eps_b"]], writes=[rstd_b])
    k.op(k.dve, lambda: nc.vector.reciprocal(out=rstd[:], in_=rstd[:]), reads=[rstd_b], writes=[rstd_b])
    for c in range(NCH):
        k.op(k.dve, lambda c=c: nc.vector.scalar_tensor_tensor(
            out=ht[:, c, :], in0=xt[:, c, :], scalar=gain[:, c:c + 1], in1=rstd[:],
            op0=ALU.mult, op1=ALU.mult),
            reads=[xt_b, rstd_b, gain_b], writes=[ht_b])


def load_vec(cx, Q, dst, dst_b, src_1d, n, partial=False):
    cx.k.dma(Q, dst, src_1d.rearrange("(c p) -> p c", p=128), reads=[], writes=[dst_b], sb=dst_b,
             partial=partial, allow_slow_non_contiguous=True)


def store_reduce(cx, xt, xt_b, i, Xo, oname):
    k = cx.k
    k.dma(k.sp, xtile(cx.Q, i), xt[:], [xt_b], [cx.dbuf("Q", i)], xt_b)
    k.allreduce(cx.Q[i], Xo[i], [cx.dbuf("Q", i)], [cx.dbuf(oname, i)])


def stage_ffn(cx, li, Xi, iname, Xo, oname, W, consts):
    nc, k = cx.nc, cx.k
    nhc = FH
    H = nhc * 128
    with (
        nc.sbuf_tensor(f"wg{li}", [128, NCH, H], BF16) as wg,
        nc.sbuf_tensor(f"wu{li}", [128, NCH, H], BF16) as wu,
        nc.sbuf_tensor(f"wd{li}", [128, nhc, D], BF16) as wd,
        nc.sbuf_tensor(f"fg{li}", [128, NCH], F32) as gain,
        nc.sbuf_tensor(f"fx0{li}", [128, NCH, T], F32) as xt0,
        nc.sbuf_tensor(f"fx1{li}", [128, NCH, T], F32) as xt1,
        nc.sbuf_tensor(f"fh{li}", [128, NCH, T], BF16) as ht,
        nc.sbuf_tensor(f"fs{li}", [128, NCH, T], BF16) as sq,
        nc.sbuf_tensor(f"fr{li}", [128, T], F32) as rstd,
        nc.sbuf_tensor(f"fa{li}", [128, nhc, T], BF16) as act,
        nc.sbuf_tensor(f"fsg0{li}", [128, T], F32) as sg0,
        nc.sbuf_tensor(f"fsg1{li}", [128, T], F32) as sg1,
    ):
        wg_b, wu_b, wd_b, gain_b = k.buf("wg"), k.buf("wu"), k.buf("wd"), k.buf("fgain")
        xts = [(xt0, k.buf("fx0")), (xt1, k.buf("fx1"))]
        ht_b, sq_b, rstd_b, act_b = k.buf("fh"), k.buf("fs"), k.buf("fr"), k.buf("fa")
        sgs = [(sg0, k.buf("fsg0")), (sg1, k.buf("fsg1"))]
        for c in range(NCH):
            k.dma(k.pool, wg[:, c, :], W["gate"][c * 128:(c + 1) * 128, :], [], [wg_b], wg_b, partial=(c > 0))
            k.dma(k.pool, wu[:, c, :], W["up"][c * 128:(c + 1) * 128, :], [], [wu_b], wu_b, partial=(c > 0))
        for j in range(nhc):
            k.dma(k.pool, wd[:, j, :], W["down"][j * 128:(j + 1) * 128, :], [], [wd_b], wd_b, partial=(j > 0))
        load_vec(cx, k.sp, gain[:], gain_b, W["norm"], NCH)

        def load(i):
            xt, xt_b = xts[i % 2]
            k.dma(k.sp, xt[:], xtile(Xi, i), [cx.dbuf(iname, i)], [xt_b], xt_b)

        load(0)
        for i in range(cx.NT):
            xt, xt_b = xts[i % 2]
            if i + 1 < cx.NT:
                load(i + 1)
            emit_norm(cx, xt, xt_b, (gain, gain_b), ht, ht_b, sq, sq_b, rstd, rstd_b, consts)
            for j in range(nhc):
                pg, pg_b = cx.psum()
                pu, pu_b = cx.psum()
                for c in range(NCH):
                    k.op(k.pe, lambda c=c: nc.tensor.matmul(pg[:], wg[:, c, j * 128:(j + 1) * 128], ht[:, c, :],
                                                             start=(c == 0), stop=(c == NCH - 1)),
                         reads=[wg_b, ht_b], writes=[pg_b], inc=(c == NCH - 1))
                for c in range(NCH):
                    k.op(k.pe, lambda c=c: nc.tensor.matmul(pu[:], wu[:, c, j * 128:(j + 1) * 128], ht[:, c, :],
                                                             start=(c == 0), stop=(c == NCH - 1)),
                         reads=[wu_b, ht_b], writes=[pu_b], inc=(c == NCH - 1))
                sg, sg_b = sgs[j % 2]
                k.op(k.act, lambda: nc.scalar.activation(out=sg[:], in_=pg[:], func=AF.Silu),
                     reads=[pg_b], writes=[sg_b])
                k.op(k.dve, lambda: nc.vector.tensor_tensor(out=act[:, j, :], in0=sg[:], in1=pu[:], op=ALU.mult),
                     reads=[sg_b, pu_b], writes=[act_b])
            for m in range(NCH):
                po, po_b = cx.psum()
                for j in range(nhc):
                    k.op(k.pe, lambda j=j: nc.tensor.matmul(po[:], wd[:, j, m * 128:(m + 1) * 128], act[:, j, :],
                                                             start=(j == 0), stop=(j == nhc - 1)),
                         reads=[wd_b, act_b], writes=[po_b], inc=(j == nhc - 1))
                k.op(k.dve, lambda: nc.vector.scalar_tensor_tensor(out=xt[:, m, :], in0=xt[:, m, :], scalar=0.5,
                                                                   in1=po[:], op0=ALU.mult, op1=ALU.add),
                     reads=[po_b, xt_b], writes=[xt_b])
            store_reduce(cx, xt, xt_b, i, Xo, oname)
        k.release_stage()


def stage_pool(cx, li, Xi, iname, Xo, oname, W, consts):
    nc, k = cx.nc, cx.k
    E = 16
    with (
        nc.sbuf_tensor(f"pw{li}", [128, NCH, PCH * 128], BF16) as win,
        nc.sbuf_tensor(f"pg{li}", [128, 4, 256], BF16) as wgr,
        nc.sbuf_tensor(f"pgn{li}", [128, NCH], F32) as gain,
        nc.sbuf_tensor(f"psc{li}", [128, NCH], F32) as psc,
        nc.sbuf_tensor(f"pic{li}", [128, 4, E], F32) as invc,
        nc.sbuf_tensor(f"px0{li}", [128, NCH, T], F32) as xt0,
        nc.sbuf_tensor(f"px1{li}", [128, NCH, T], F32) as xt1,
        nc.sbuf_tensor(f"ph{li}", [128, NCH, T], BF16) as ht,
        nc.sbuf_tensor(f"pq{li}", [128, NCH, T], BF16) as sq,
        nc.sbuf_tensor(f"pr{li}", [128, T], F32) as rstd,
        nc.sbuf_tensor(f"pu{li}", [128, PCH, E + T], F32) as u,
        nc.sbuf_tensor(f"pa{li}", [128, PCH, E + T], F32) as A,
        nc.sbuf_tensor(f"pb{li}", [128, PCH, E + T], F32) as B,
        nc.sbuf_tensor(f"pp{li}", [128, PCH, T], BF16) as p,
        nc.sbuf_tensor(f"ptmp{li}", [128, E], F32) as tmp,
    ):
        win_b, wgr_b, gain_b, psc_b, invc_b = (k.buf("pwin"), k.buf("pwgr"), k.buf("pgain"), k.buf("psc"),
                                               k.buf("invc"))
        xts = [(xt0, k.buf("px0")), (xt1, k.buf("px1"))]
        ht_b, sq_b, rstd_b = k.buf("ph"), k.buf("pq"), k.buf("pr")
        u_b, A_b, B_b, p_b, tmp_b = k.buf("pu"), k.buf("pA"), k.buf("pB"), k.buf("pp"), k.buf("ptmp")
        for c in range(NCH):
            k.dma(k.pool, win[:, c, :], W["in"][c * 128:(c + 1) * 128, :], [], [win_b], win_b, partial=(c > 0))
        for g in range(4):
            k.dma(k.pool, wgr[:, g, :], W["group"][g, :, :], [], [wgr_b], wgr_b, partial=(g > 0))
        load_vec(cx, k.sp, gain[:], gain_b, W["norm"], NCH)
        load_vec(cx, k.sp, psc[:], psc_b, W["scale"], NCH)
        for g in range(4):
            w = 2 ** (g + 1)
            for t in range(E):
                k.op(k.pool, lambda g=g, t=t, w=w: nc.gpsimd.memset(invc[:, g, t:t + 1], 1.0 / min(t + 1, w)),
                     reads=[], writes=[invc_b])
        k.op(k.pool, lambda: nc.gpsimd.memset(u[:, :, 0:E], 0.0), reads=[], writes=[u_b])
        k.op(k.pool, lambda: nc.gpsimd.memset(A[:, :, 0:E], 0.0), reads=[], writes=[A_b])
        k.op(k.pool, lambda: nc.gpsimd.memset(B[:, :, 0:E], 0.0), reads=[], writes=[B_b])

        def load(i):
            xt, xt_b = xts[i % 2]
            k.dma(k.sp, xt[:], xtile(Xi, i), [cx.dbuf(iname, i)], [xt_b], xt_b)

        load(0)
        for i in range(cx.NT):
            xt, xt_b = xts[i % 2]
            if i + 1 < cx.NT:
                load(i + 1)
            emit_norm(cx, xt, xt_b, (gain, gain_b), ht, ht_b, sq, sq_b, rstd, rstd_b, consts)
            if i > 0:
                k.op(k.pool, lambda: nc.gpsimd.tensor_copy(out=u[:, :, 0:E], in_=u[:, :, T:T + E]),
                     reads=[u_b], writes=[u_b])
            for m in range(PCH):
                pu_, pu_b = cx.psum()
                for c in range(NCH):
                    k.op(k.pe, lambda c=c: nc.tensor.matmul(pu_[:], win[:, c, m * 128:(m + 1) * 128], ht[:, c, :],
                                                             start=(c == 0), stop=(c == NCH - 1)),
                         reads=[win_b, ht_b], writes=[pu_b], inc=(c == NCH - 1))
                k.op(k.act, lambda: nc.scalar.copy(out=u[:, m, E:E + T], in_=pu_[:]),
                     reads=[pu_b], writes=[u_b])
            k.op(k.act, lambda: nc.scalar.activation(out=xt[:], in_=xt[:], func=AF.Copy, scale=0.5), reads=[xt_b], writes=[xt_b])
            k.op(k.dve, lambda: nc.vector.tensor_tensor(out=A[:, :, 1:E + T], in0=u[:, :, 1:E + T],
                                                        in1=u[:, :, 0:E + T - 1], op=ALU.add),
                 reads=[u_b], writes=[A_b])
            k.op(k.pool, lambda: nc.gpsimd.tensor_tensor(out=B[:, 1:4, 3:E + T], in0=A[:, 1:4, 3:E + T],
                                                         in1=A[:, 1:4, 1:E + T - 2], op=ALU.add),
                 reads=[A_b], writes=[B_b])
            k.op(k.dve, lambda: nc.vector.tensor_tensor(out=A[:, 2:4, 7:E + T], in0=B[:, 2:4, 7:E + T],
                                                        in1=B[:, 2:4, 3:E + T - 4], op=ALU.add),
                 reads=[B_b], writes=[A_b])
            k.op(k.pool, lambda: nc.gpsimd.tensor_tensor(out=B[:, 3:4, 15:E + T], in0=A[:, 3:4, 15:E + T],
                                                         in1=A[:, 3:4, 7:E + T - 8], op=ALU.add),
                 reads=[A_b], writes=[B_b])
            srcs = [(A, A_b), (B, B_b), (A, A_b), (B, B_b)]
            for g in range(4):
                s_, s_b = srcs[g]
                w = 2 ** (g + 1)
                k.op(k.dve, lambda g=g, s_=s_, w=w: nc.vector.scalar_tensor_tensor(
                    out=p[:, g, :], in0=s_[:, g, E:E + T], scalar=1.0 / w,
                    in1=u[:, g, E:E + T], op0=ALU.mult, op1=ALU.subtract),
                    reads=[s_b, u_b], writes=[p_b])
                if i == 0:
                    k.op(k.dve, lambda g=g, s_=s_: nc.vector.tensor_tensor(
                        out=tmp[:], in0=s_[:, g, E:2 * E], in1=invc[:, g, :], op=ALU.mult),
                        reads=[s_b, invc_b], writes=[tmp_b])
                    k.op(k.dve, lambda g=g: nc.vector.tensor_tensor(
                        out=p[:, g, 0:E], in0=tmp[:], in1=u[:, g, E:2 * E], op=ALU.subtract),
                        reads=[tmp_b, u_b], writes=[p_b])
            for g in range(4):
                for mo in range(2):
                    cc = 2 * g + mo
                    py, py_b = cx.psum()
                    k.op(k.pe, lambda: nc.tensor.matmul(py[:], wgr[:, g, mo * 128:(mo + 1) * 128], p[:, g, :],
                                                        start=True, stop=True),
                         reads=[wgr_b, p_b], writes=[py_b])
                    k.op(k.dve, lambda cc=cc: nc.vector.scalar_tensor_tensor(
                        out=xt[:, cc, :], in0=py[:], scalar=psc[:, cc:cc + 1], in1=xt[:, cc, :],
                        op0=ALU.mult, op1=ALU.add),
                        reads=[py_b, psc_b, xt_b], writes=[xt_b])
            store_reduce(cx, xt, xt_b, i, Xo, oname)
        k.release_stage()


C_ID, C_SM, C_SA, C_SB2, C_TRI, C_OMT, C_BO, C_MB = 0, 128, 640, 768, 784, 912, 1040, 1168
C_F32W = 784
C_TOT = 1168 + 2048


def make_consts():
    c = np.zeros((128, C_TOT), np.float32)
    j = np.arange(128)
    c[:, C_ID:C_ID + 128] = np.eye(128)
    t256 = np.arange(256)
    c[:, C_SM:C_SM + 256] = (j[:, None] <= t256[None, :])
    c[:, C_SM + 256:C_SM + 512] = (128 + j[:, None] <= t256[None, :])
    k32 = np.arange(32)
    c[:32, C_SA:C_SA + 128] = ((k32[:, None] % 2) == (j[None, :] // 64))
    c[:32, C_SB2:C_SB2 + 16] = ((k32[:, None] // 2) == np.arange(16)[None, :])
    c[:, C_TRI:C_TRI + 128] = (j[:, None] >= j[None, :])
    c[:, C_OMT:C_OMT + 128] = (j[:, None] < j[None, :])
    c[:, C_BO:C_BO + 128] = ((j[:, None] // 64) == (j[None, :] // 64))
    t512 = np.arange(512)
    for o in range(4):
        valid = (128 * o + j[:, None]) < t512[None, :]
        c[:, C_MB + 512 * o:C_MB + 512 * (o + 1)] = np.where(valid, 0.0, -30000.0)
    return c


def stage_outproj(cx, tag, Xi, iname, Xo, oname, G, gname, Wd, KC):
    nc, k = cx.nc, cx.k
    with (
        nc.sbuf_tensor(f"ow{tag}", [128, KC, D], BF16) as w,
        nc.sbuf_tensor(f"ox0{tag}", [128, NCH, T], F32) as xt0,
        nc.sbuf_tensor(f"ox1{tag}", [128, NCH, T], F32) as xt1,
        nc.sbuf_tensor(f"og0{tag}", [128, KC, T], BF16) as g0,
        nc.sbuf_tensor(f"og1{tag}", [128, KC, T], BF16) as g1,
    ):
        w_b = k.buf("ow")
        xts = [(xt0, k.buf("ox0")), (xt1, k.buf("ox1"))]
        gs = [(g0, k.buf("og0")), (g1, k.buf("og1"))]
        for c in range(KC):
            k.dma(k.pool, w[:, c, :], Wd[c * 128:(c + 1) * 128, :], [], [w_b], w_b, partial=(c > 0))

        def load(i):
            xt, xt_b = xts[i % 2]
            gt, gt_b = gs[i % 2]
            k.dma(k.sp, xt[:], xtile(Xi, i), [cx.dbuf(iname, i)], [xt_b], xt_b)
            k.dma(k.sp, gt[:], G.rearrange("(c p) t -> p c t", p=128)[:, :, i * T:(i + 1) * T],
                  [cx.dbuf(gname, i)], [gt_b], gt_b)

        load(0)
        for i in range(cx.NT):
            xt, xt_b = xts[i % 2]
            gt, gt_b = gs[i % 2]
            if i + 1 < cx.NT:
                load(i + 1)
            for m in range(NCH):
                po, po_b = cx.psum()
                for c in range(KC):
                    k.op(k.pe, lambda c=c: nc.tensor.matmul(po[:], w[:, c, m * 128:(m + 1) * 128], gt[:, c, :],
                                                             start=(c == 0), stop=(c == KC - 1)),
                         reads=[w_b, gt_b], writes=[po_b], inc=(c == KC - 1))
                k.op(k.dve, lambda: nc.vector.scalar_tensor_tensor(out=xt[:, m, :], in0=xt[:, m, :], scalar=0.5,
                                                                   in1=po[:], op0=ALU.mult, op1=ALU.add),
                     reads=[po_b, xt_b], writes=[xt_b])
            store_reduce(cx, xt, xt_b, i, Xo, oname)
        k.release_stage()


def stage_sb_qkv(cx, li, Xi, iname, W, consts, QT, KT, V):
    nc, k = cx.nc, cx.k
    cb, cb_b = consts["cbf"], consts["cbf_b"]
    with (
        nc.sbuf_tensor(f"sw{li}", [128, NCH, 3 * 512], BF16) as w,
        nc.sbuf_tensor(f"sgn{li}", [128, NCH], F32) as gain,
        nc.sbuf_tensor(f"sgq{li}", [128, 2], F32) as gqk,
        nc.sbuf_tensor(f"sx0{li}", [128, NCH, T], F32) as xt0,
        nc.sbuf_tensor(f"sx1{li}", [128, NCH, T], F32) as xt1,
        nc.sbuf_tensor(f"sh{li}", [128, NCH, T], BF16) as ht,
        nc.sbuf_tensor(f"ssq{li}", [128, NCH, T], BF16) as sq,
        nc.sbuf_tensor(f"sr{li}", [128, T], F32) as rstd,
        nc.sbuf_tensor(f"sqt{li}", [128, SBP, T], BF16) as qt,
        nc.sbuf_tensor(f"skt{li}", [128, SBP, T], BF16) as kt,
        nc.sbuf_tensor(f"svt{li}", [128, 4, 512], BF16) as vt,
        nc.sbuf_tensor(f"ssc0{li}", [128, T], BF16) as sqc0,
        nc.sbuf_tensor(f"ssc1{li}", [128, T], BF16) as sqc1,
        nc.sbuf_tensor(f"srs0{li}", [128, T], F32) as rs0,
        nc.sbuf_tensor(f"srs1{li}", [128, T], F32) as rs1,
    ):
        w_b, gain_b, gqk_b = k.buf("sw"), k.buf("sgn"), k.buf("sgq")
        xts = [(xt0, k.buf("sx0")), (xt1, k.buf("sx1"))]
        ht_b, sq_b, rstd_b = k.buf("sh"), k.buf("ssq"), k.buf("sr")
        qt_b, kt_b, vt_b = k.buf("sqt"), k.buf("skt"), k.buf("svt")
        sqcs = [(sqc0, k.buf("ssc0")), (sqc1, k.buf("ssc1"))]
        rss = [(rs0, k.buf("srs0")), (rs1, k.buf("srs1"))]
        for c in range(NCH):
            k.dma(k.pool, w[:, c, :], W["qkv"][c * 128:(c + 1) * 128, :], [], [w_b], w_b, partial=(c > 0))
        load_vec(cx, k.sp, gain[:], gain_b, W["norm"], NCH)
        first = True
        for col, nm in ((0, "qn"), (1, "kn")):
            for hh in range(2):
                k.dma(k.sp, gqk[hh * 64:(hh + 1) * 64, col:col + 1], W[nm].rearrange("(p o) -> p o", o=1), [],
                      [gqk_b], gqk_b, partial=(not first), allow_slow_non_contiguous=True)
                first = False
        k.op(k.dve, lambda: nc.vector.tensor_scalar(out=gqk[:, 0:1], in0=gqk[:, 0:1], scalar1=0.125, scalar2=None,
                                                    op0=ALU.mult), reads=[gqk_b], writes=[gqk_b])

        def load(i):
            xt, xt_b = xts[i % 2]
            k.dma(k.sp, xt[:], xtile(Xi, i), [cx.dbuf(iname, i)], [xt_b], xt_b)

        load(0)
        n = 0
        for i in range(cx.NT):
            xt, xt_b = xts[i % 2]
            if i + 1 < cx.NT:
                load(i + 1)
            emit_norm(cx, xt, xt_b, (gain, gain_b), ht, ht_b, sq, sq_b, rstd, rstd_b, consts)
            for which, (dst, dst_b) in enumerate(((qt, qt_b), (kt, kt_b))):
                for oc in range(SBP):
                    c0 = which * 512 + oc * 128
                    pq, pq_b = cx.psum()
                    for c in range(NCH):
                        k.op(k.pe, lambda c=c: nc.tensor.matmul(pq[:], w[:, c, c0:c0 + 128], ht[:, c, :],
                                                                 start=(c == 0), stop=(c == NCH - 1)),
                             reads=[w_b, ht_b], writes=[pq_b], inc=(c == NCH - 1))
                    sqc, sqc_b = sqcs[n % 2]
                    rs, rs_b = rss[n % 2]
                    n += 1
                    k.op(k.act, lambda: nc.scalar.activation(out=sqc[:], in_=pq[:], func=AF.Square),
                         reads=[pq_b], writes=[sqc_b])
                    pss, pss_b = cx.psum()
                    k.op(k.pe, lambda: nc.tensor.matmul(pss[:], cb[:, C_BO:C_BO + 128], sqc[:], start=True, stop=True),
                         reads=[sqc_b, cb_b], writes=[pss_b])
                    k.op(k.act, lambda: nc.scalar.activation(out=rs[:], in_=pss[:], func=AF.Sqrt,
                                                             bias=consts["eps"][:], scale=1.0 / 64),
                         reads=[pss_b, consts["eps_b"]], writes=[rs_b])
                    k.op(k.dve, lambda: nc.vector.reciprocal(out=rs[:], in_=rs[:]), reads=[rs_b], writes=[rs_b])
                    k.op(k.dve, lambda: nc.vector.scalar_tensor_tensor(
                        out=dst[:, oc, :], in0=pq[:], scalar=gqk[:, which:which + 1], in1=rs[:],
                        op0=ALU.mult, op1=ALU.mult), reads=[pq_b, gqk_b, rs_b], writes=[dst_b])
            for blk in range(4):
                for half in range(1):
                    pv, pv_b = cx.psum()
                    c0 = 2 * 512
                    for c in range(NCH):
                        k.op(k.pe, lambda c=c: nc.tensor.matmul(pv[:], ht[:, c, blk * 128:(blk + 1) * 128],
                                                                 w[:, c, c0:c0 + 512], start=(c == 0),
                                                                 stop=(c == NCH - 1)),
                             reads=[w_b, ht_b], writes=[pv_b], inc=(c == NCH - 1))
                    k.op(k.act, lambda: nc.scalar.copy(out=vt[:, blk, half * 512:(half + 1) * 512], in_=pv[:]),
                         reads=[pv_b], writes=[vt_b])
            k.dma(k.sp, QT.rearrange("(c p) t -> p c t", p=128)[:, :, i * T:(i + 1) * T], qt[:], [qt_b],
                  [cx.dbuf("QT", i)], qt_b)
            k.dma(k.sp, KT.rearrange("(c p) t -> p c t", p=128)[:, :, i * T:(i + 1) * T], kt[:], [kt_b],
                  [cx.dbuf("KT", i)], kt_b)
            k.dma(k.sp, V[i * T:(i + 1) * T, :].rearrange("(b p) d -> p b d", p=128), vt[:], [vt_b],
                  [cx.dbuf("V", i)], vt_b)
        k.release_stage()


def stage_sb_attn(cx, li, consts, QT, KT, V, OT):
    nc, k = cx.nc, cx.k
    S = cx.S
    NB = S // 128
    cb, cb_b = consts["cbf"], consts["cbf_b"]
    NE, NSP, NG_, NA = 8, 8, 4, 6
    with (
        nc.sbuf_tensor(f"akc0{li}", [128, S], BF16) as kc0,
        nc.sbuf_tensor(f"aqc0{li}", [128, S], BF16) as qc0,
        nc.sbuf_tensor(f"avc0{li}", [128, NB, 128], BF16) as vc0,
        nc.sbuf_tensor(f"akc1{li}", [128, S], BF16) as kc1,
        nc.sbuf_tensor(f"aqc1{li}", [128, S], BF16) as qc1,
        nc.sbuf_tensor(f"avc1{li}", [128, NB, 128], BF16) as vc1,
        nc.sbuf_tensor(f"aoc{li}", [128, S], BF16) as oc,
        nc.sbuf_tensor(f"aE{li}", [128, NE, T], F32) as Et,
        nc.sbuf_tensor(f"aSP{li}", [128, NSP, T], BF16) as SPt,
        nc.sbuf_tensor(f"aG{li}", [128, NG_, T], F32) as Gt,
        nc.sbuf_tensor(f"aA{li}", [128, NA, T], BF16) as At,
    ):
        kqv = [(kc0, qc0, vc0, k.buf("akc0"), k.buf("aqc0"), k.buf("avc0")),
               (kc1, qc1, vc1, k.buf("akc1"), k.buf("aqc1"), k.buf("avc1"))]
        oc_b = k.buf("aoc")
        E_b = [k.buf(f"aE{x}") for x in range(NE)]
        SP_b = [k.buf(f"aSP{x}") for x in range(NSP)]
        G_b = [k.buf(f"aG{x}") for x in range(NG_)]
        A_b = [k.buf(f"aA{x}") for x in range(NA)]
        Zs = [cx.ps[x] for x in range(4)]
        Ts = [cx.ps[4], cx.ps[5]]
        Obanks = [(cx.ps[6][0], [k.buf("aO00"), k.buf("aO01")]), (cx.ps[7][0], [k.buf("aO10"), k.buf("aO11")])]
        one_ap = consts["onef"]
        cnt = {"z": 0, "e": 0, "sp": 0, "g": 0, "a": 0}

        def load(c):
            kc, qc, vc, kc_b, qc_b, vc_b = kqv[c % 2]
            rd = [cx.dbuf("KT", i) for i in range(cx.NT)]
            k.dma(k.sp, kc[:], KT[c * 128:(c + 1) * 128, :], rd, [kc_b], kc_b)
            rd = [cx.dbuf("QT", i) for i in range(cx.NT)]
            k.dma(k.sp, qc[:], QT[c * 128:(c + 1) * 128, :], rd, [qc_b], qc_b)
            rd = [cx.dbuf("V", i) for i in range(cx.NT)]
            k.dma(k.sp, vc[:], V[:, c * 128:(c + 1) * 128].rearrange("(b p) d -> p b d", p=128), rd, [vc_b], vc_b)

        load(0)
        for c in range(SBP):
            kc, qc, vc, kc_b, qc_b, vc_b = kqv[c % 2]
            if c + 1 < SBP:
                load(c + 1)
            items = [(i, J) for i in range(cx.NT) for J in range(4 * i + 3, -1, -1)]
            n = len(items)
            st = [dict() for _ in range(n)]

            def s1(b):
                i, J = items[b]
                diag = J >= 4 * i
                for hh in range(2):
                    pb = 64 * hh
                    Z, Z_b = Zs[cnt["z"] % 4]
                    cnt["z"] += 1
                    st[b][("Z", hh)] = (Z, Z_b)
                    k.op(k.pe, lambda pb=pb, Z=Z: nc.tensor.matmul(
                        Z[:], kc[pb:pb + 64, J * 128:(J + 1) * 128], qc[pb:pb + 64, i * T:(i + 1) * T],
                        start=True, stop=(not diag)), reads=[kc_b, qc_b], writes=[Z_b], inc=(not diag))
                    if diag:
                        o = J - 4 * i
                        k.op(k.pe, lambda Z=Z, o=o: nc.tensor.matmul(
                            Z[:], cb[:, C_ID:C_ID + 128], cb[:, C_MB + 512 * o:C_MB + 512 * (o + 1)],
                            start=False, stop=True), reads=[cb_b], writes=[Z_b])

            def s2(b):
                for hh in range(2):
                    Z, Z_b = st[b][("Z", hh)]
                    e = cnt["e"] % NE
                    cnt["e"] += 1
                    sp = cnt["sp"] % NSP
                    cnt["sp"] += 1
                    st[b][("e", hh)] = e
                    st[b][("sp", hh)] = sp
                    k.op(k.act, lambda Z=Z, e=e: nc.scalar.activation(out=Et[:, e, :], in_=Z[:], func=AF.Exp),
                         reads=[Z_b], writes=[E_b[e]])
                    k.op(k.act, lambda e=e, sp=sp: nc.scalar.activation(out=SPt[:, sp, :], in_=Et[:, e, :], func=AF.Ln,
                                                                        bias=one_ap[:], scale=1.0),
                         reads=[E_b[e], consts["onef_b"]], writes=[SP_b[sp]])

            def s3_tri(b):
                i, J = items[b]
                first = (J == 4 * i + 3)
                for hh in range(2):
                    Tt, T_b = Ts[hh]
                    sp = st[b][("sp", hh)]
                    k.op(k.pe, lambda Tt=Tt, sp=sp: nc.tensor.matmul(
                        Tt[:], cb[:, C_TRI:C_TRI + 128], SPt[:, sp, :], start=first, stop=False,
                        skip_group_check=True), reads=[SP_b[sp], cb_b], writes=[T_b])

            def s3_g(b):
                for hh in range(2):
                    Tt, T_b = Ts[hh]
                    g = cnt["g"] % NG_
                    cnt["g"] += 1
                    st[b][("g", hh)] = g
                    k.op(k.act, lambda Tt=Tt, g=g: nc.scalar.activation(out=Gt[:, g, :], in_=Tt[:], func=AF.Exp,
                                                                        scale=-1.0),
                         reads=[T_b], writes=[G_b[g]])

            def s3_omt(b):
                i, J = items[b]
                if J == 0:
                    return
                for hh in range(2):
                    Tt, T_b = Ts[hh]
                    sp = st[b][("sp", hh)]
                    k.op(k.pe, lambda Tt=Tt, sp=sp: nc.tensor.matmul(
                        Tt[:], cb[:, C_OMT:C_OMT + 128], SPt[:, sp, :], start=False, stop=False,
                        skip_group_check=True), reads=[SP_b[sp], cb_b], writes=[T_b])

            def s4_a(b):
                for hh in range(2):
                    e, g = st[b][("e", hh)], st[b][("g", hh)]
                    a = cnt["a"] % NA
                    cnt["a"] += 1
                    st[b][("a", hh)] = a
                    k.op(k.pool, lambda e=e, g=g, a=a: nc.gpsimd.tensor_tensor(out=At[:, a, :], in0=Et[:, e, :],
                                                                               in1=Gt[:, g, :], op=ALU.mult),
                         reads=[E_b[e], G_b[g]], writes=[A_b[a]])

            def s4_av(b):
                i, J = items[b]
                first = (J == 4 * i + 3)
                Ot, O_b = Obanks[i % 2]
                for hh in range(2):
                    pb = 64 * hh
                    a = st[b][("a", hh)]
                    k.op(k.pe, lambda pb=pb, a=a, Ot=Ot: nc.tensor.matmul(
                        Ot[pb:pb + 64, :], vc[:, J, pb:pb + 64], At[:, a, :], start=first, stop=(J == 0),
                        skip_group_check=True), reads=[vc_b, A_b[a]], writes=[O_b[hh]])
                if J == 0:
                    for hh in range(2):
                        pb = 64 * hh
                        k.op(k.dve, lambda pb=pb, Ot=Ot: nc.vector.tensor_copy(
                            out=oc[pb:pb + 64, i * T:(i + 1) * T], in_=Ot[pb:pb + 64, :]),
                            reads=[O_b[hh]], writes=[oc_b])

            for t in range(n + 3):
                if 0 <= t - 2 < n:
                    s3_tri(t - 2)
                    s3_g(t - 2)
                if t < n:
                    s1(t)
                if 0 <= t - 3 < n:
                    s4_av(t - 3)
                if 0 <= t - 2 < n:
                    s3_omt(t - 2)
                    s4_a(t - 2)
                if 0 <= t - 1 < n:
                    s2(t - 1)
            wr = [cx.dbuf("OT", i) for i in range(cx.NT)]
            k.dma(k.sp, OT[c * 128:(c + 1) * 128, :], oc[:], [oc_b], wr, oc_b, partial=(c > 0))
        k.release_stage()


def stage_ssd_a1(cx, li, Xi, iname, W, consts, ZS, DT, DA):
    nc, k = cx.nc, cx.k
    with (
        nc.sbuf_tensor(f"d1w{li}", [128, NCH, 1024 + NHD], BF16) as w,
        nc.sbuf_tensor(f"d1g{li}", [128, NCH], F32) as gain,
        nc.sbuf_tensor(f"d1p{li}", [NHD, 4], F32) as prm,
        nc.sbuf_tensor(f"d1x0{li}", [128, NCH, T], F32) as xt0,
        nc.sbuf_tensor(f"d1x1{li}", [128, NCH, T], F32) as xt1,
        nc.sbuf_tensor(f"d1h{li}", [128, NCH, T], BF16) as ht,
        nc.sbuf_tensor(f"d1q{li}", [128, NCH, T], BF16) as sq,
        nc.sbuf_tensor(f"d1r{li}", [128, T], F32) as rstd,
        nc.sbuf_tensor(f"d1z{li}", [128, 8, T], BF16) as zs,
        nc.sbuf_tensor(f"d1dt{li}", [NHD, 2, T], F32) as dd,
    ):
        w_b, gain_b, prm_b = k.buf("d1w"), k.buf("d1g"), k.buf("d1p")
        xts = [(xt0, k.buf("d1x0")), (xt1, k.buf("d1x1"))]
        ht_b, sq_b, rstd_b, zs_b, dd_b = k.buf("d1h"), k.buf("d1q"), k.buf("d1r"), k.buf("d1z"), k.buf("d1dt")
        for c in range(NCH):
            k.dma(k.pool, w[:, c, 0:1024], W["in"][c * 128:(c + 1) * 128, 0:1024], [], [w_b], w_b, partial=(c > 0))
            k.dma(k.pool, w[:, c, 1024:1024 + NHD], W["in"][c * 128:(c + 1) * 128, 3072:3072 + NHD], [], [w_b], w_b,
                  partial=True)
        load_vec(cx, k.sp, gain[:], gain_b, W["norm"], NCH)
        k.dma(k.sp, prm[:, 0:1], W["dt_bias"].rearrange("(p o) -> p o", o=1), [], [prm_b], prm_b,
              allow_slow_non_contiguous=True)
        k.dma(k.sp, prm[:, 1:2], W["a_log"].rearrange("(p o) -> p o", o=1), [], [prm_b], prm_b, partial=True,
              allow_slow_non_contiguous=True)
        k.op(k.act, lambda: nc.scalar.activation(out=prm[:, 2:3], in_=prm[:, 1:2], func=AF.Exp),
             reads=[prm_b], writes=[prm_b])
        k.op(k.dve, lambda: nc.vector.tensor_scalar(out=prm[:, 2:3], in0=prm[:, 2:3], scalar1=-1.0, scalar2=None,
                                                    op0=ALU.mult), reads=[prm_b], writes=[prm_b])

        def load(i):
            xt, xt_b = xts[i % 2]
            k.dma(k.sp, xt[:], xtile(Xi, i), [cx.dbuf(iname, i)], [xt_b], xt_b)

        load(0)
        for i in range(cx.NT):
            xt, xt_b = xts[i % 2]
            if i + 1 < cx.NT:
                load(i + 1)
            emit_norm(cx, xt, xt_b, (gain, gain_b), ht, ht_b, sq, sq_b, rstd, rstd_b, consts)
            for oc in range(8):
                pz, pz_b = cx.psum()
                for c in range(NCH):
                    k.op(k.pe, lambda c=c: nc.tensor.matmul(pz[:], w[:, c, oc * 128:(oc + 1) * 128], ht[:, c, :],
                                                             start=(c == 0), stop=(c == NCH - 1)),
                         reads=[w_b, ht_b], writes=[pz_b], inc=(c == NCH - 1))
                k.op(k.act, lambda: nc.scalar.activation(out=zs[:, oc, :], in_=pz[:], func=AF.Silu),
                     reads=[pz_b], writes=[zs_b])
            pd, pd_b = cx.psum()
            for c in range(NCH):
                k.op(k.pe, lambda c=c: nc.tensor.matmul(pd[0:NHD, :], w[:, c, 1024:1024 + NHD], ht[:, c, :],
                                                         start=(c == 0), stop=(c == NCH - 1)),
                     reads=[w_b, ht_b], writes=[pd_b], inc=(c == NCH - 1))
            k.op(k.act, lambda: nc.scalar.activation(out=dd[:, 0, :], in_=pd[0:NHD, :], func=AF.Exp,
                                                     bias=prm[:, 0:1], scale=1.0),
                 reads=[pd_b, prm_b], writes=[dd_b])
            k.op(k.act, lambda: nc.scalar.activation(out=dd[:, 0, :], in_=dd[:, 0, :], func=AF.Ln,
                                                     bias=consts["onef"][0:NHD, :], scale=1.0),
                 reads=[dd_b, consts["onef_b"]], writes=[dd_b])
            k.op(k.dve, lambda: nc.vector.tensor_scalar(out=dd[:, 1, :], in0=dd[:, 0, :], scalar1=prm[:, 2:3],
                                                        scalar2=None, op0=ALU.mult),
                 reads=[dd_b, prm_b], writes=[dd_b])
            k.dma(k.sp, ZS.rearrange("(c p) t -> p c t", p=128)[:, :, i * T:(i + 1) * T], zs[:], [zs_b],
                  [cx.dbuf("ZS", i)], zs_b)
            k.dma(k.sp, DT[:, i * T:(i + 1) * T], dd[:, 0, :], [dd_b], [cx.dbuf("DT", i)], dd_b)
            k.dma(k.sp, DA[:, i * T:(i + 1) * T], dd[:, 1, :], [dd_b], [cx.dbuf("DA", i)], dd_b)
        k.release_stage()


def stage_ssd_a2(cx, li, Xi, iname, W, consts, XS, BT, CT):
    nc, k = cx.nc, cx.k
    cf, cf_b = consts["cf"], consts["cf_b"]
    with (
        nc.sbuf_tensor(f"d2w{li}", [128, NCH, 2048], BF16) as w,
        nc.sbuf_tensor(f"d2g{li}", [128, NCH], F32) as gain,
        nc.sbuf_tensor(f"d2cr{li}", [64, 128], F32) as cwr,
        nc.sbuf_tensor(f"d2br{li}", [16, 128], F32) as cbr,
        nc.sbuf_tensor(f"d2cw{li}", [128, 4, 16], F32) as cw,
        nc.sbuf_tensor(f"d2cb{li}", [128, 16], F32) as cbias,
        nc.sbuf_tensor(f"d2x0{li}", [128, NCH, T], F32) as xt0,
        nc.sbuf_tensor(f"d2x1{li}", [128, NCH, T], F32) as xt1,
        nc.sbuf_tensor(f"d2h{li}", [128, NCH, T], BF16) as ht,
        nc.sbuf_tensor(f"d2q{li}", [128, NCH, T], BF16) as sq,
        nc.sbuf_tensor(f"d2r{li}", [128, T], F32) as rstd,
        nc.sbuf_tensor(f"d2hl{li}", [128, 16, 4], F32) as halo,
        nc.sbuf_tensor(f"d2wk0{li}", [128, 4 + T], F32) as wk0,
        nc.sbuf_tensor(f"d2wk1{li}", [128, 4 + T], F32) as wk1,
        nc.sbuf_tensor(f"d2ac0{li}", [128, T], F32) as ac0,
        nc.sbuf_tensor(f"d2ac1{li}", [128, T], F32) as ac1,
        nc.sbuf_tensor(f"d2xo{li}", [128, 16, T], BF16) as xo,
    ):
        w_b, gain_b, cwr_b, cbr_b, cw_b, cbias_b = (k.buf("d2w"), k.buf("d2g"), k.buf("d2cr"), k.buf("d2br"),
                                                    k.buf("d2cw"), k.buf("d2cb"))
        xts = [(xt0, k.buf("d2x0")), (xt1, k.buf("d2x1"))]
        ht_b, sq_b, rstd_b, halo_b, xo_b = k.buf("d2h"), k.buf("d2q"), k.buf("d2r"), k.buf("d2hl"), k.buf("d2xo")
        wks = [(wk0, k.buf("d2wk0")), (wk1, k.buf("d2wk1"))]
        acs = [(ac0, k.buf("d2ac0")), (ac1, k.buf("d2ac1"))]
        for c in range(NCH):
            k.dma(k.pool, w[:, c, :], W["in"][c * 128:(c + 1) * 128, 1024:3072], [], [w_b], w_b, partial=(c > 0))
        load_vec(cx, k.sp, gain[:], gain_b, W["norm"], NCH)
        k.dma(k.sp, cwr[:], W["conv_w"].rearrange("k (c p) -> (k c) p", p=128), [], [cwr_b], cwr_b)
        k.dma(k.sp, cbr[:], W["conv_b"].rearrange("(c p) -> c p", p=128), [], [cbr_b], cbr_b)
        pt, pt_b = cx.psum()
        k.op(k.pe, lambda: nc.tensor.transpose(pt[:, 0:64], cwr[:], cf[0:64, C_ID:C_ID + 64]),
             reads=[cwr_b, cf_b], writes=[pt_b])
        k.op(k.dve, lambda: nc.vector.tensor_copy(out=cw[:].rearrange("p k c -> p (k c)"), in_=pt[:, 0:64]),
             reads=[pt_b], writes=[cw_b])
        pt2, pt2_b = cx.psum()
        k.op(k.pe, lambda: nc.tensor.transpose(pt2[:, 0:16], cbr[:], cf[0:16, C_ID:C_ID + 16]),
             reads=[cbr_b, cf_b], writes=[pt2_b])
        k.op(k.dve, lambda: nc.vector.tensor_copy(out=cbias[:], in_=pt2[:, 0:16]), reads=[pt2_b], writes=[cbias_b])
        k.op(k.pool, lambda: nc.gpsimd.memset(halo[:], 0.0), reads=[], writes=[halo_b])

        def load(i):
            xt, xt_b = xts[i % 2]
            k.dma(k.sp, xt[:], xtile(Xi, i), [cx.dbuf(iname, i)], [xt_b], xt_b)

        load(0)
        n = 0
        for i in range(cx.NT):
            xt, xt_b = xts[i % 2]
            if i + 1 < cx.NT:
                load(i + 1)
            emit_norm(cx, xt, xt_b, (gain, gain_b), ht, ht_b, sq, sq_b, rstd, rstd_b, consts)
            for ch in range(16):
                pp, pp_b = cx.psum()
                for c in range(NCH):
                    k.op(k.pe, lambda c=c: nc.tensor.matmul(pp[:], w[:, c, ch * 128:(ch + 1) * 128], ht[:, c, :],
                                                             start=(c == 0), stop=(c == NCH - 1)),
                         reads=[w_b, ht_b], writes=[pp_b], inc=(c == NCH - 1))
                wk, wk_b = wks[n % 2]
                ac, ac_b = acs[n % 2]
                n += 1
                k.op(k.pool, lambda wk=wk: nc.gpsimd.tensor_copy(out=wk[:, 0:4], in_=halo[:, ch, :]),
                     reads=[halo_b], writes=[wk_b])
                k.op(k.act, lambda wk=wk: nc.scalar.copy(out=wk[:, 4:4 + T], in_=pp[:]), reads=[pp_b], writes=[wk_b])
                k.op(k.pool, lambda wk=wk: nc.gpsimd.tensor_copy(out=halo[:, ch, :], in_=wk[:, T:T + 4]),
                     reads=[wk_b], writes=[halo_b])
                k.op(k.dve, lambda wk=wk, ac=ac: nc.vector.tensor_scalar(
                    out=ac[:], in0=wk[:, 4:4 + T], scalar1=cw[:, 3, ch:ch + 1], scalar2=cbias[:, ch:ch + 1],
                    op0=ALU.mult, op1=ALU.add), reads=[wk_b, cw_b, cbias_b], writes=[ac_b])
                for tap in (2, 1, 0):
                    sh = 3 - tap
                    k.op(k.dve, lambda wk=wk, ac=ac, tap=tap, sh=sh: nc.vector.scalar_tensor_tensor(
                        out=ac[:], in0=wk[:, 4 - sh:4 - sh + T], scalar=cw[:, tap, ch:ch + 1], in1=ac[:],
                        op0=ALU.mult, op1=ALU.add), reads=[wk_b, cw_b, ac_b], writes=[ac_b])
                k.op(k.act, lambda ac=ac: nc.scalar.activation(out=xo[:, ch, :], in_=ac[:], func=AF.Silu),
                     reads=[ac_b], writes=[xo_b])
            k.dma(k.sp, XS.rearrange("(c p) t -> p c t", p=128)[:, :, i * T:(i + 1) * T], xo[:, 0:8, :], [xo_b],
                  [cx.dbuf("XS", i)], xo_b)
            k.dma(k.sp, BT.rearrange("(c p) t -> p c t", p=128)[:, :, i * T:(i + 1) * T], xo[:, 8:12, :], [xo_b],
                  [cx.dbuf("BT", i)], xo_b)
            k.dma(k.sp, CT.rearrange("(c p) t -> p c t", p=128)[:, :, i * T:(i + 1) * T], xo[:, 12:16, :], [xo_b],
                  [cx.dbuf("CT", i)], xo_b)
        k.release_stage()


def stage_ssd_scan(cx, li, W, consts, ZS, DT, DA, XS, BT, CT, GN):
    nc, k = cx.nc, cx.k
    S = cx.S
    L = 256
    NCK = S // L
    cb, cb_b = consts["cbf"], consts["cbf_b"]
    cf, cf_b = consts["cf"], consts["cf_b"]
    with ExitStack() as es:
        onesf = es.enter_context(nc.sbuf_tensor(f"s3of{li}", [128, 256], F32))
        selH = es.enter_context(nc.sbuf_tensor(f"s3sh{li}", [NHD, NHD, 128], F32))
        prm = es.enter_context(nc.sbuf_tensor(f"s3pr{li}", [NHD, 2], F32))
        rhsD = es.enter_context(nc.sbuf_tensor(f"s3rd{li}", [NHD, 8], F32))
        Dvec = es.enter_context(nc.sbuf_tensor(f"s3dv{li}", [128, 8], F32))
        onorm = es.enter_context(nc.sbuf_tensor(f"s3on{li}", [128, 8], F32))
        stf = es.enter_context(nc.sbuf_tensor(f"s3st{li}", [128, NHD, 64], F32))
        stb = es.enter_context(nc.sbuf_tensor(f"s3sb{li}", [128, NHD, 64], BF16))
        dtc = es.enter_context(nc.sbuf_tensor(f"s3dt{li}", [NHD, L], F32))
        dac = es.enter_context(nc.sbuf_tensor(f"s3da{li}", [NHD, L], F32))
        acum = es.enter_context(nc.sbuf_tensor(f"s3ac{li}", [NHD, L], F32))
        dg = es.enter_context(nc.sbuf_tensor(f"s3dg{li}", [NHD, NHD], F32))
        tm = es.enter_context(nc.sbuf_tensor(f"s3tm{li}", [128, 4, NHD], F32))
        tmpd = es.enter_context(nc.sbuf_tensor(f"s3td{li}", [128, 2, NHD], F32))
        wdt = es.enter_context(nc.sbuf_tensor(f"s3wd{li}", [128, 2, NHD], F32))
        xs = es.enter_context(nc.sbuf_tensor(f"s3xs{li}", [128, 8, L], BF16))
        zs = es.enter_context(nc.sbuf_tensor(f"s3zs{li}", [128, 8, L], BF16))
        Bt = es.enter_context(nc.sbuf_tensor(f"s3bt{li}", [128, NG, L], BF16))
        Ct = es.enter_context(nc.sbuf_tensor(f"s3ct{li}", [128, NG, L], BF16))
        Gm0 = es.enter_context(nc.sbuf_tensor(f"s3gm0{li}", [128, 2, L], F32))
        Gm1 = es.enter_context(nc.sbuf_tensor(f"s3gm1{li}", [128, 2, L], F32))
        xstm0 = es.enter_context(nc.sbuf_tensor(f"s3xt0{li}", [128, 2, 256], BF16))
        xstm1 = es.enter_context(nc.sbuf_tensor(f"s3xt1{li}", [128, 2, 256], BF16))
        Bw0 = es.enter_context(nc.sbuf_tensor(f"s3bw0{li}", [128, 2, 4, 128], BF16))
        Bw1 = es.enter_context(nc.sbuf_tensor(f"s3bw1{li}", [128, 2, 4, 128], BF16))
        Dd0 = es.enter_context(nc.sbuf_tensor(f"s3d0{li}", [128, 4, 2, L], F32))
        Dd1 = es.enter_context(nc.sbuf_tensor(f"s3d1{li}", [128, 4, 2, L], F32))
        Mh0 = es.enter_context(nc.sbuf_tensor(f"s3m0{li}", [128, 4, 2, L], BF16))
        Mh1 = es.enter_context(nc.sbuf_tensor(f"s3m1{li}", [128, 4, 2, L], BF16))
        Ea0 = es.enter_context(nc.sbuf_tensor(f"s3e0{li}", [128, 4, L], F32))
        Ea1 = es.enter_context(nc.sbuf_tensor(f"s3e1{li}", [128, 4, L], F32))
        Ch0 = es.enter_context(nc.sbuf_tensor(f"s3c0{li}", [128, 4, L], BF16))
        Ch1 = es.enter_context(nc.sbuf_tensor(f"s3c1{li}", [128, 4, L], BF16))
        yv = es.enter_context(nc.sbuf_tensor(f"s3yv{li}", [128, 2, L], F32))
        sqg = es.enter_context(nc.sbuf_tensor(f"s3sq{li}", [128, 2, L], BF16))
        rs = es.enter_context(nc.sbuf_tensor(f"s3rs{li}", [128, L], F32))
        gn = es.enter_context(nc.sbuf_tensor(f"s3gn{li}", [128, 8, L], BF16))
        B = k.buf
        onesf_b, selH_b, prm_b, rhsD_b, Dvec_b, onorm_b = B("onesf"), B("selH"), B("s3pr"), B("rhsD"), B("Dvec"), B("onorm")
        stf_b = [B(f"stf{h}") for h in range(NHD)]
        stb_b = [B(f"stb{h}") for h in range(NHD)]
        dtc_b, dac_b, acum_b, dg_b, tm_b, tmpd_b, wdt_b = (B("dtc"), B("dac"), B("acum"), B("dg"), B("tm"), B("tmpd"),
                                                           B("wdt"))
        xs_b, zs_b, Bt_b, Ct_b = B("xs"), B("zs"), B("Bt"), B("Ct")
        Gms = [(Gm0, B("Gm0")), (Gm1, B("Gm1"))]
        xstms = [(xstm0, B("xstm0")), (xstm1, B("xstm1"))]
        Bws = [(Bw0, B("Bw0")), (Bw1, B("Bw1"))]
        gcount = [0]
        Dds = [(Dd0, B("Dd0")), (Dd1, B("Dd1"))]
        Mhs = [(Mh0, B("Mh0")), (Mh1, B("Mh1"))]
        Eas = [(Ea0, B("Ea0")), (Ea1, B("Ea1"))]
        Chs = [(Ch0, B("Ch0")), (Ch1, B("Ch1"))]
        yv_b, sqg_b, rs_b, gn_b = B("yv"), B("sqg"), B("rs"), B("gn")
        rot = [0]

        def ps():
            p = cx.ps[rot[0] % 4]
            rot[0] += 1
            return p

        ACB = [cx.ps[4], cx.ps[5]]

        Ybanks = [cx.ps[6], cx.ps[7]]
        k.op(k.pool, lambda: nc.gpsimd.memset(onesf[:], 1.0), reads=[], writes=[onesf_b])
        k.op(k.pool, lambda: nc.gpsimd.memset(stf[:], 0.0), reads=[], writes=stf_b)
        k.op(k.pool, lambda: nc.gpsimd.memset(stb[:], 0.0), reads=[], writes=stb_b)
        for h in range(NHD):
            k.op(k.pool, lambda h=h: nc.gpsimd.tensor_scalar(out=selH[:, h, :], in0=onesf[0:NHD, 0:128],
                                                             scalar1=cf[0:NHD, C_ID + h:C_ID + h + 1], scalar2=None,
                                                             op0=ALU.mult),
                 reads=[onesf_b, cf_b], writes=[selH_b])
        k.dma(k.sp, prm[:, 0:1], W["d"].rearrange("(p o) -> p o", o=1), [], [prm_b], prm_b,
              allow_slow_non_contiguous=True)
        load_vec(cx, k.sp, onorm[:], onorm_b, W["out_norm"], 8)
        k.op(k.dve, lambda: nc.vector.tensor_scalar(out=rhsD[:], in0=cf[0:NHD, C_SB2:C_SB2 + 8], scalar1=prm[:, 0:1],
                                                    scalar2=None, op0=ALU.mult),
             reads=[cf_b, prm_b], writes=[rhsD_b])
        pD, pD_b = ps()
        k.op(k.pe, lambda: nc.tensor.matmul(pD[:, 0:8], cf[0:NHD, C_SA:C_SA + 128], rhsD[:], start=True, stop=True),
             reads=[cf_b, rhsD_b], writes=[pD_b])
        k.op(k.dve, lambda: nc.vector.tensor_copy(out=Dvec[:], in_=pD[:, 0:8]), reads=[pD_b], writes=[Dvec_b])

        for c in range(NCK):
            ti = c // (T // L)
            cs = slice(c * L, (c + 1) * L)
            k.dma(k.sp, dtc[:], DT[:, cs], [cx.dbuf("DT", ti)], [dtc_b], dtc_b)
            k.dma(k.sp, dac[:], DA[:, cs], [cx.dbuf("DA", ti)], [dac_b], dac_b)
            k.dma(k.sp, xs[:], XS.rearrange("(c p) t -> p c t", p=128)[:, :, cs], [cx.dbuf("XS", ti)], [xs_b], xs_b)
            k.dma(k.sp, zs[:], ZS.rearrange("(c p) t -> p c t", p=128)[:, :, cs], [cx.dbuf("ZS", ti)], [zs_b], zs_b)
            k.dma(k.sp, Bt[:], BT.rearrange("(c p) t -> p c t", p=128)[:, :, cs], [cx.dbuf("BT", ti)], [Bt_b], Bt_b)
            k.dma(k.sp, Ct[:], CT.rearrange("(c p) t -> p c t", p=128)[:, :, cs], [cx.dbuf("CT", ti)], [Ct_b], Ct_b)
            k.op(k.dve, lambda: nc.vector.tensor_tensor_scan(out=acum[:], data0=onesf[0:NHD, 0:L], data1=dac[:],
                                                             initial=0.0, op0=ALU.mult, op1=ALU.add),
                 reads=[onesf_b, dac_b], writes=[acum_b])
            ptm, ptm_b = ps()
            for q in range(4):
                src, src_b = (dtc, dtc_b) if q < 2 else (acum, acum_b)
                sb_ = q % 2
                k.op(k.pe, lambda q=q, src=src, sb_=sb_: nc.tensor.transpose(
                    ptm[:, q * NHD:(q + 1) * NHD], src[:, sb_ * 128:(sb_ + 1) * 128], cf[0:NHD, C_ID:C_ID + NHD]),
                    reads=[src_b, cf_b], writes=[ptm_b])
            k.op(k.dve, lambda: nc.vector.tensor_copy(out=tm[:].rearrange("p q h -> p (q h)"), in_=ptm[:, 0:4 * NHD]),
                 reads=[ptm_b], writes=[tm_b])
            k.op(k.dve, lambda: nc.vector.tensor_scalar(out=dg[:], in0=cf[0:NHD, C_ID:C_ID + NHD],
                                                        scalar1=acum[:, L - 1:L], scalar2=None, op0=ALU.mult),
                 reads=[cf_b, acum_b], writes=[dg_b])
            pal, pal_b = ps()
            k.op(k.pe, lambda: nc.tensor.matmul(pal[:, 0:NHD], onesf[0:NHD, 0:128], dg[:], start=True, stop=True),
                 reads=[onesf_b, dg_b], writes=[pal_b])
            for sb_ in range(2):
                k.op(k.dve, lambda sb_=sb_: nc.vector.tensor_tensor(out=tmpd[:, sb_, :], in0=pal[:, 0:NHD],
                                                                    in1=tm[:, 2 + sb_, :], op=ALU.subtract),
                     reads=[pal_b, tm_b], writes=[tmpd_b])
            k.op(k.act, lambda: nc.scalar.activation(out=tmpd[:], in_=tmpd[:], func=AF.Exp),
                 reads=[tmpd_b], writes=[tmpd_b])
            k.op(k.dve, lambda: nc.vector.tensor_tensor(out=wdt[:], in0=tmpd[:], in1=tm[:, 0:2, :], op=ALU.mult),
                 reads=[tmpd_b, tm_b], writes=[wdt_b])

            def prologue(g):
                gb = gcount[0] % 2
                Gm, Gm_b = Gms[gb]
                xstm, xstm_b = xstms[gb]
                Bw, Bw_b = Bws[gb]
                pacs = []
                pacs_all[g] = pacs

                def acb(hl):
                    pacb, pacb_b = ACB[hl // 2]
                    hi = hl % 2
                    pacs.append((pacb, pacb_b, hi))
                    h = 4 * g + hl
                    k.op(k.pe, lambda pacb=pacb, h=h, hi=hi: nc.tensor.matmul(
                        pacb[:, hi * L:(hi + 1) * L], selH[:, h, :], acum[:], start=True, stop=True,
                        skip_group_check=True), reads=[selH_b, acum_b], writes=[pacb_b])

                for sb_ in range(2):
                    pG, pG_b = ps()
                    k.op(k.pe, lambda sb_=sb_, pG=pG: nc.tensor.matmul(pG[:, 0:L], Bt[:, g, sb_ * 128:(sb_ + 1) * 128],
                                                                         Ct[:, g, :], start=True, stop=True),
                         reads=[Bt_b, Ct_b], writes=[pG_b])
                    k.op(k.dve, lambda sb_=sb_, pG=pG: nc.vector.tensor_tensor(
                        out=Gm[:, sb_, :], in0=pG[:, 0:L], in1=cf[:, C_SM + sb_ * 256:C_SM + (sb_ + 1) * 256],
                        op=ALU.mult), reads=[pG_b, cf_b], writes=[Gm_b])
                    acb(sb_)
                px, px_b = ps()
                pxb = px[:].bitcast(BF16)
                for sb_ in range(2):
                    for ci in range(2):
                        col = sb_ * 256 + ci * 128
                        k.op(k.pe, lambda sb_=sb_, ci=ci, col=col: nc.tensor.transpose(
                            pxb[:, col:col + 128], xs[:, 2 * g + ci, sb_ * 128:(sb_ + 1) * 128], cb[:, C_ID:C_ID + 128]),
                            reads=[xs_b, cb_b], writes=[px_b])
                k.op(k.act, lambda: nc.scalar.copy(out=xstm[:].rearrange("p s c -> p (s c)"), in_=pxb[:, 0:512]),
                     reads=[px_b], writes=[xstm_b])
                acb(2)
                pB, pB_b = ps()
                pBb = pB[:].bitcast(BF16)
                for sb_ in range(2):
                    k.op(k.pe, lambda sb_=sb_: nc.tensor.transpose(
                        pBb[:, sb_ * 128:(sb_ + 1) * 128], Bt[:, g, sb_ * 128:(sb_ + 1) * 128], cb[:, C_ID:C_ID + 128]),
                        reads=[Bt_b, cb_b], writes=[pB_b])
                acb(3)
                for sb_ in range(2):
                    for hl in range(4):
                        h = 4 * g + hl
                        eng = k.dve
                        if eng is k.dve:
                            k.op(k.dve, lambda sb_=sb_, hl=hl, h=h: nc.vector.tensor_scalar(
                                out=Bw[:, sb_, hl, :], in0=pBb[:, sb_ * 128:(sb_ + 1) * 128],
                                scalar1=wdt[:, sb_, h:h + 1], scalar2=None, op0=ALU.mult),
                                reads=[pB_b, wdt_b], writes=[Bw_b])
                        else:
                            k.op(k.act, lambda sb_=sb_, hl=hl, h=h: nc.scalar.activation(
                                out=Bw[:, sb_, hl, :], in_=pBb[:, sb_ * 128:(sb_ + 1) * 128], func=AF.Copy,
                                scale=wdt[:, sb_, h:h + 1]),
                                reads=[pB_b, wdt_b], writes=[Bw_b])
                return gb

            def heads_front(g, gb):
                Gm, Gm_b = Gms[gb]
                Dd, Dd_b = Dds[gb]
                Mh, Mh_b = Mhs[gb]
                Ea, Ea_b = Eas[gb]
                Ch, Ch_b = Chs[gb]
                pacs = pacs_all[g]
                for hl in range(4):
                    h = 4 * g + hl
                    pacb, pacb_b, hi = pacs[hl]
                    for sb_ in range(2):
                        k.op(k.dve, lambda sb_=sb_, pacb=pacb, hl=hl, h=h, hi=hi: nc.vector.tensor_scalar(
                            out=Dd[:, hl, sb_, :], in0=pacb[:, hi * L:(hi + 1) * L], scalar1=tm[:, 2 + sb_, h:h + 1],
                            scalar2=0.0, op0=ALU.subtract, op1=ALU.min), reads=[pacb_b, tm_b], writes=[Dd_b])
                for hl in range(4):
                    pacb, pacb_b, hi = pacs[hl]
                    k.op(k.act, lambda pacb=pacb, hl=hl, hi=hi: nc.scalar.activation(
                        out=Ea[:, hl, :], in_=pacb[:, hi * L:(hi + 1) * L], func=AF.Exp),
                        reads=[pacb_b], writes=[Ea_b])
                for hl in range(4):
                    k.op(k.act, lambda hl=hl: nc.scalar.activation(out=Dd[:, hl, :, :], in_=Dd[:, hl, :, :], func=AF.Exp),
                         reads=[Dd_b], writes=[Dd_b])
                for hl in range(4):
                    k.op(k.pool, lambda hl=hl: nc.gpsimd.tensor_tensor(out=Ch[:, hl, :], in0=Ct[:, g, :],
                                                                       in1=Ea[:, hl, :], op=ALU.mult),
                         reads=[Ct_b, Ea_b], writes=[Ch_b])
                for hl in range(4):
                    h = 4 * g + hl
                    for sb_ in range(2):
                        k.op(k.dve, lambda sb_=sb_, hl=hl, h=h: nc.vector.scalar_tensor_tensor(
                            out=Mh[:, hl, sb_, :], in0=Dd[:, hl, sb_, :], scalar=tm[:, sb_, h:h + 1], in1=Gm[:, sb_, :],
                            op0=ALU.mult, op1=ALU.mult), reads=[Dd_b, tm_b, Gm_b], writes=[Mh_b])

            def heads_back(g, gb):
                xstm, xstm_b = xstms[gb]
                Bw, Bw_b = Bws[gb]
                Mh, Mh_b = Mhs[gb]
                Ea, Ea_b = Eas[gb]
                Ch, Ch_b = Chs[gb]
                Yt, Y_b = Ybanks[gb]
                pS1 = ps()
                pSs = [pS1] * 4
                for hl in range(4):
                    h = 4 * g + hl
                    ci, po = hl // 2, (hl % 2) * 64
                    yo = Yt[po:po + 64, ci * L:(ci + 1) * L]
                    xcol = ci * 128 + po
                    k.op(k.pe, lambda yo=yo, hl=hl, xcol=xcol: nc.tensor.matmul(
                        yo, xstm[:, 0, xcol:xcol + 64], Mh[:, hl, 0, :], start=True, stop=False, skip_group_check=True),
                        reads=[xstm_b, Mh_b], writes=[Y_b], inc=False)
                    k.op(k.pe, lambda yo=yo, hl=hl, xcol=xcol: nc.tensor.matmul(
                        yo, xstm[:, 1, xcol:xcol + 64], Mh[:, hl, 1, :], start=False, stop=False, skip_group_check=True),
                        reads=[xstm_b, Mh_b], writes=[Y_b], inc=False)
                    k.op(k.pe, lambda yo=yo, hl=hl, h=h: nc.tensor.matmul(
                        yo, stb[:, h, :], Ch[:, hl, :], start=False, stop=True, skip_group_check=True),
                        reads=[stb_b[h], Ch_b], writes=[Y_b])
                for hl in range(4):
                    ci, po = hl // 2, (hl % 2) * 64
                    xcol = ci * 128 + po
                    pS, pS_b = pSs[hl]
                    for sb_ in range(2):
                        k.op(k.pe, lambda sb_=sb_, hl=hl, xcol=xcol, pS=pS: nc.tensor.matmul(
                            pS[:, hl * 64:(hl + 1) * 64], Bw[:, sb_, hl, :], xstm[:, sb_, xcol:xcol + 64],
                            start=(sb_ == 0), stop=(sb_ == 1), skip_group_check=True),
                            reads=[Bw_b, xstm_b], writes=[pS_b], inc=(sb_ == 1))
                for hl in range(4):
                    h = 4 * g + hl
                    pS, pS_b = pSs[hl]
                    k.op(k.dve, lambda hl=hl, h=h, pS=pS: nc.vector.scalar_tensor_tensor(
                        out=stf[:, h, :], in0=stf[:, h, :], scalar=Ea[:, hl, L - 1:L], in1=pS[:, hl * 64:(hl + 1) * 64],
                        op0=ALU.mult, op1=ALU.add), reads=[stf_b[h], Ea_b, pS_b], writes=[stf_b[h]])
                    k.op(k.act, lambda h=h: nc.scalar.copy(out=stb[:, h, :], in_=stf[:, h, :]),
                         reads=[stf_b[h]], writes=[stb_b[h]])
                for ci in range(2):
                    cc = 2 * g + ci
                    k.op(k.dve, lambda ci=ci, cc=cc: nc.vector.scalar_tensor_tensor(
                        out=yv[:, ci, :], in0=xs[:, cc, :], scalar=Dvec[:, cc:cc + 1], in1=Yt[:, ci * L:(ci + 1) * L],
                        op0=ALU.mult, op1=ALU.add), reads=[xs_b, Dvec_b, Y_b], writes=[yv_b])
                k.op(k.dve, lambda: nc.vector.tensor_tensor(out=yv[:], in0=yv[:], in1=zs[:, 2 * g:2 * g + 2, :],
                                                            op=ALU.mult), reads=[yv_b, zs_b], writes=[yv_b])
                k.op(k.act, lambda: nc.scalar.activation(out=sqg[:], in_=yv[:], func=AF.Square),
                     reads=[yv_b], writes=[sqg_b])
                pss, pss_b = ps()
                for ci in range(2):
                    k.op(k.pe, lambda ci=ci, pss=pss: nc.tensor.matmul(pss[:, 0:L], consts["ones"][:], sqg[:, ci, :],
                                                                        start=(ci == 0), stop=(ci == 1)),
                         reads=[sqg_b, consts["ones_b"]], writes=[pss_b], inc=(ci == 1))
                k.op(k.act, lambda pss=pss: nc.scalar.activation(out=rs[:], in_=pss[:, 0:L], func=AF.Sqrt,
                                                                 bias=consts["eps"][:], scale=1.0 / 256),
                     reads=[pss_b, consts["eps_b"]], writes=[rs_b])
                k.op(k.dve, lambda: nc.vector.reciprocal(out=rs[:], in_=rs[:]), reads=[rs_b], writes=[rs_b])
                for ci in range(2):
                    cc = 2 * g + ci
                    k.op(k.dve, lambda ci=ci, cc=cc: nc.vector.scalar_tensor_tensor(
                        out=gn[:, cc, :], in0=yv[:, ci, :], scalar=onorm[:, cc:cc + 1], in1=rs[:],
                        op0=ALU.mult, op1=ALU.mult), reads=[yv_b, onorm_b, rs_b], writes=[gn_b])

            gbs = {}
            pacs_all = {}
            if SSD_PIPE:
                gbs[0] = prologue(0)
                gcount[0] += 1
                heads_front(0, gbs[0])
                for g in range(NG):
                    if g + 1 < NG:
                        gbs[g + 1] = prologue(g + 1)
                        gcount[0] += 1
                        heads_front(g + 1, gbs[g + 1])
                    heads_back(g, gbs[g])
            else:
                for g in range(NG):
                    gbs[g] = prologue(g)
                    gcount[0] += 1
                    heads_front(g, gbs[g])
                    heads_back(g, gbs[g])
            k.dma(k.sp, GN.rearrange("(c p) t -> p c t", p=128)[:, :, cs], gn[:], [gn_b], [cx.dbuf("GN", ti)], gn_b,
                  partial=(c % 2 == 1))
        k.release_stage()


WEIGHT_SPECS = [
    ("mix_norm", [4, 1024]), ("ffn_norm", [4, 1024]),
    ("pool_in", [2, 1024, 512]), ("pool_group", [2, 4, 128, 256]), ("pool_scale", [2, 1024]),
    ("ssd_in", [1, 1024, 3088]), ("ssd_conv_w", [1, 4, 2048]), ("ssd_conv_b", [1, 2048]),
    ("ssd_dt_bias", [1, 16]), ("ssd_a_log", [1, 16]), ("ssd_d", [1, 16]),
    ("ssd_out_norm", [1, 1024]), ("ssd_out", [1, 1024, 1024]),
    ("sb_qkv", [1, 1024, 1536]), ("sb_q_norm", [1, 64]), ("sb_k_norm", [1, 64]), ("sb_out", [1, 512, 1024]),
    ("ffn_gate", [4, 1024, 1408]), ("ffn_up", [4, 1024, 1408]), ("ffn_down", [4, 1408, 1024]),
]


def local_weights(inp, r):
    f = lambda a: np.ascontiguousarray(np.asarray(a, dtype=np.float32))
    cat = np.concatenate
    w = {}
    for n in ("mix_norm", "ffn_norm", "pool_scale", "sb_q_norm", "sb_k_norm"):
        w[n] = f(inp[n])
    pin = np.asarray(inp["pool_in"])
    w["pool_in"] = f(cat([pin[:, :, (2 * g + r) * 128:(2 * g + r + 1) * 128] for g in range(4)], axis=2))
    w["pool_group"] = f(np.asarray(inp["pool_group"])[:, :, r * 128:(r + 1) * 128, :])
    si = np.asarray(inp["ssd_in"])
    w["ssd_in"] = f(cat([si[:, :, r * 1024:(r + 1) * 1024], si[:, :, 2048 + r * 1024:2048 + (r + 1) * 1024],
                         si[:, :, 4096 + r * 512:4096 + (r + 1) * 512], si[:, :, 5120 + r * 512:5120 + (r + 1) * 512],
                         si[:, :, 6144 + r * 16:6144 + (r + 1) * 16]], axis=2))
    cwv = np.asarray(inp["ssd_conv_w"])
    w["ssd_conv_w"] = f(cat([cwv[:, :, r * 1024:(r + 1) * 1024], cwv[:, :, 2048 + r * 512:2048 + (r + 1) * 512],
                             cwv[:, :, 3072 + r * 512:3072 + (r + 1) * 512]], axis=2))
    cbv = np.asarray(inp["ssd_conv_b"])
    w["ssd_conv_b"] = f(cat([cbv[:, r * 1024:(r + 1) * 1024], cbv[:, 2048 + r * 512:2048 + (r + 1) * 512],
                             cbv[:, 3072 + r * 512:3072 + (r + 1) * 512]], axis=1))
    for n in ("ssd_dt_bias", "ssd_a_log", "ssd_d"):
        w[n] = f(np.asarray(inp[n])[:, r * 16:(r + 1) * 16])
    w["ssd_out_norm"] = f(np.asarray(inp["ssd_out_norm"])[:, r * 1024:(r + 1) * 1024])
    w["ssd_out"] = f(np.asarray(inp["ssd_out"])[:, r * 1024:(r + 1) * 1024, :])
    q = np.asarray(inp["sb_qkv"])
    w["sb_qkv"] = f(cat([q[:, :, r * 512:(r + 1) * 512], q[:, :, 1024 + r * 512:1024 + (r + 1) * 512],
                         q[:, :, 2048 + r * 512:2048 + (r + 1) * 512]], axis=2))
    w["sb_out"] = f(np.asarray(inp["sb_out"])[:, r * 512:(r + 1) * 512, :])
    w["ffn_gate"] = f(np.asarray(inp["ffn_gate"])[:, :, r * 1408:(r + 1) * 1408])
    w["ffn_up"] = f(np.asarray(inp["ffn_up"])[:, :, r * 1408:(r + 1) * 1408])
    w["ffn_down"] = f(np.asarray(inp["ffn_down"])[:, r * 1408:(r + 1) * 1408, :])
    return w


def build_program(S, layers, do_mixer=True, do_ffn=True):
    nc = bass.Bass("TRN2", target_bir_lowering=False, num_devices=8)
    cx = Ctx(nc, S)
    k = cx.k
    NT = S // T
    xT = nc.dram_tensor("xT", [NT, D, T], F32, kind="ExternalInput").ap()
    yT = nc.dram_tensor("yT", [NT, D, T], F32, kind="ExternalOutput").ap()
    Wt = {n: nc.dram_tensor(n, shp, F32, kind="ExternalInput").ap() for n, shp in WEIGHT_SPECS}
    cx.Q = nc.dram_tensor("resQ", [NT, D, T], F32, kind="Internal").ap()
    R = [nc.dram_tensor(f"resR{i}", [NT, D, T], F32, kind="Internal").ap() for i in range(2)]
    ones = nc.alloc_sbuf_tensor("c_ones", [128, 128], BF16)
    ones_b = k.buf("ones")
    k.op(k.pool, lambda: nc.gpsimd.memset(ones[:], 1.0), reads=[], writes=[ones_b])
    epst = nc.alloc_sbuf_tensor("c_eps", [128, 1], F32)
    eps_b = k.buf("eps")
    k.op(k.pool, lambda: nc.gpsimd.memset(epst[:], EPS), reads=[], writes=[eps_b])
    onef = nc.alloc_sbuf_tensor("c_onef", [128, 1], F32)
    onef_b = k.buf("onef")
    k.op(k.pool, lambda: nc.gpsimd.memset(onef[:], 1.0), reads=[], writes=[onef_b])
    cst = nc.dram_tensor("cst", [128, C_TOT], F32, kind="ExternalInput").ap()
    cbf = nc.alloc_sbuf_tensor("c_cbf", [128, C_TOT], BF16)
    cf = nc.alloc_sbuf_tensor("c_cf", [128, C_F32W], F32)
    cbf_b, cf_b = k.buf("cbf"), k.buf("cf")
    k.dma(k.pool, cbf[:], cst[:, :], [], [cbf_b], cbf_b)
    k.dma(k.sp, cf[:], cst[:, 0:C_F32W], [], [cf_b], cf_b)
    k.stage_bufs = []
    consts = {"ones": ones, "ones_b": ones_b, "eps": epst, "eps_b": eps_b, "onef": onef, "onef_b": onef_b,
              "cbf": cbf, "cbf_b": cbf_b, "cf": cf, "cf_b": cf_b}
    scr = {}

    def scratch(name, shape, dt=BF16):
        if name not in scr:
            scr[name] = nc.dram_tensor("scr_" + name, shape, dt, kind="Internal").ap()
        return scr[name]

    cur, cur_name = xT, "xT"
    nstage = [0]

    def nxt():
        nstage[0] += 1
        return R[nstage[0] % 2], f"R{nstage[0]}"

    for idx, li in enumerate(layers):
        kind, j = li % 3, li // 3
        if do_mixer:
            mo, mo_name = nxt()
            if kind == 0:
                W = {"in": Wt["pool_in"][j], "group": Wt["pool_group"][j], "scale": Wt["pool_scale"][j],
                     "norm": Wt["mix_norm"][li]}
                stage_pool(cx, li, cur, cur_name, mo, mo_name, W, consts)
            elif kind == 2:
                QT, KT, V, OT = (scratch("QT", [512, S]), scratch("KT", [512, S]), scratch("V", [S, 512]),
                                 scratch("OT", [512, S]))
                W = {"qkv": Wt["sb_qkv"][j], "qn": Wt["sb_q_norm"][j], "kn": Wt["sb_k_norm"][j],
                     "norm": Wt["mix_norm"][li]}
                stage_sb_qkv(cx, li, cur, cur_name, W, consts, QT, KT, V)
                stage_sb_attn(cx, li, consts, QT, KT, V, OT)
                stage_outproj(cx, f"sb{li}", cur, cur_name, mo, mo_name, OT, "OT", Wt["sb_out"][j], 4)
            else:
                ZS, XS, GN = scratch("ZS", [1024, S]), scratch("XS", [1024, S]), scratch("GN", [1024, S])
                BT, CT = scratch("BT", [512, S]), scratch("CT", [512, S])
                DT, DA = scratch("DT", [NHD, S], F32), scratch("DA", [NHD, S], F32)
                W = {"in": Wt["ssd_in"][j], "conv_w": Wt["ssd_conv_w"][j], "conv_b": Wt["ssd_conv_b"][j],
                     "dt_bias": Wt["ssd_dt_bias"][j], "a_log": Wt["ssd_a_log"][j], "d": Wt["ssd_d"][j],
                     "out_norm": Wt["ssd_out_norm"][j], "norm": Wt["mix_norm"][li]}
                stage_ssd_a1(cx, li, cur, cur_name, W, consts, ZS, DT, DA)
                stage_ssd_a2(cx, li, cur, cur_name, W, consts, XS, BT, CT)
                stage_ssd_scan(cx, li, W, consts, ZS, DT, DA, XS, BT, CT, GN)
                stage_outproj(cx, f"ssd{li}", cur, cur_name, mo, mo_name, GN, "GN", Wt["ssd_out"][j], 8)
            cur, cur_name = mo, mo_name
        if do_ffn:
            Xo, oname = nxt()
            Wf = {"gate": Wt["ffn_gate"][li], "up": Wt["ffn_up"][li], "down": Wt["ffn_down"][li],
                  "norm": Wt["ffn_norm"][li]}
            stage_ffn(cx, li, cur, cur_name, Xo, oname, Wf, consts)
            cur, cur_name = Xo, oname
    cp_b = k.buf("outcopy")
    for i in range(NT):
        k.dma(k.sp, yT[i], cur[i], [cx.dbuf(cur_name, i)], [cx.dbuf("yT", i)], cp_b, is_out=True)
    k.finish()
    return nc


def make_in_maps(inputs, S):
    x = np.asarray(inputs["x"], dtype=np.float32)[:, :S]
    Bn = x.shape[0]
    NT = S // T
    cst = make_consts()
    lw = [local_weights(inputs, r) for r in range(2)]
    in_maps = []
    for c in range(8):
        b, r = (c // 2) % Bn, c % 2
        xt = np.ascontiguousarray(x[b].T.reshape(D, NT, T).transpose(1, 0, 2))
        m = {"xT": xt, "cst": cst}
        m.update(lw[r])
        in_maps.append(m)
    return in_maps


def untile(yt):
    NT = yt.shape[0]
    return np.ascontiguousarray(yt.transpose(1, 0, 2).reshape(D, NT * T).T)


_PROG_CACHE = {}


def kernel(**inputs):
    x = np.asarray(inputs["x"], dtype=np.float32)
    Bn, S, _ = x.shape
    layers = list(range(DEPTH))
    key = (S, tuple(layers))
    if key not in _PROG_CACHE:
        _PROG_CACHE[key] = build_program(S, layers)
    nc = _PROG_CACHE[key]
    in_maps = make_in_maps(inputs, S)
    res = run_bass_kernel_spmd(nc, in_maps, core_ids=list(range(8)))
    out = np.empty_like(x)
    for b in range(Bn):
        out[b] = untile(res.results[2 * b]["yT"])
    return out
```

```python
import numpy as np
from contextlib import ExitStack
import concourse.bass as bass
import concourse.mybir as mybir
from concourse.bass_utils import run_bass_kernel_spmd

F32 = mybir.dt.float32
BF16 = mybir.dt.bfloat16
AF = mybir.ActivationFunctionType
ALU = mybir.AluOpType

D = 1024
NCH = 8
T = 512
EPS = 1e-6
FFN_H = 2816
FFN_HC = 22
SSD_DI = 2048
SSD_IN = 6176
SEQ = 8192
DEPTH = 4
FH = 11
PCH = 4
NG = 4
NHD = 16
SBP = 4
PAIRS = [[0, 1], [2, 3], [4, 5], [6, 7]]
SSD_PIPE = True


class Buf:
    __slots__ = ("name", "w", "r", "dsem", "dcnt", "q")

    def __init__(self, name):
        self.name = name
        self.w = {}
        self.r = {}
        self.dsem = None
        self.dcnt = 0
        self.q = None


class EngW:
    def __init__(self, nc, eng, name):
        self.eng = eng
        self.name = name
        self.sem = nc.alloc_semaphore("tl_" + name)
        self.cnt = 0
        self.seen = {}


class K:
    def __init__(self, nc):
        self.nc = nc
        self.pe = EngW(nc, nc.tensor, "pe")
        self.act = EngW(nc, nc.scalar, "act")
        self.dve = EngW(nc, nc.vector, "dve")
        self.pool = EngW(nc, nc.gpsimd, "pool")
        self.sp = EngW(nc, nc.sync, "sp")
        self.out_events = []
        self.nbuf = 0
        self.nsem = 0
        self.cc_sem = None
        self.cc_cnt = 0
        self.free_sems = {}
        self.stage_bufs = []

    def barrier(self):
        engs = [self.pe, self.act, self.dve, self.pool, self.sp]
        evs = [(e.sem, e.cnt) for e in engs if e.cnt > 0]
        for b in self.stage_bufs:
            if b.dsem is not None and b.dcnt > 0:
                evs.append((b.dsem, b.dcnt))
        for e in engs:
            self._wait(e, evs)

    def release_stage(self):
        self.barrier()
        for b in self.stage_bufs:
            if b.dsem is not None:
                self.free_sems.setdefault(b.q.name, []).append((b.dsem, b.dcnt))
                b.dsem = None
        self.stage_bufs = []

    def buf(self, name=None):
        self.nbuf += 1
        return Buf(name or f"b{self.nbuf}")

    def _wait(self, E, events, own=False):
        for sem, cnt in events:
            if sem is E.sem and (not own or E is self.pe):
                continue
            key = id(sem)
            if E.seen.get(key, 0) < cnt:
                E.eng.wait_ge(sem, cnt)
                E.seen[key] = cnt

    def op(self, E, fn, reads=(), writes=(), inc=True):
        for b in reads:
            self._wait(E, list(b.w.values()), own=True)
        for b in writes:
            self._wait(E, list(b.w.values()), own=True)
            self._wait(E, list(b.r.values()))
        ins = fn()
        if inc:
            E.cnt += 1
            ins.then_inc(E.sem, 1)
            seq = E.cnt
        else:
            seq = E.cnt + 1
        ev = (E.sem, seq)
        k = id(E.sem)
        for b in reads:
            b.r[k] = ev
        for b in writes:
            b.w = {k: ev}
            b.r = {}
        return ins

    def dma(self, Q, out, in_, reads, writes, sb, partial=False, is_out=False, **kw):
        for b in reads:
            self._wait(Q, list(b.w.values()))
        for b in writes:
            if not partial:
                self._wait(Q, list(b.w.values()))
            self._wait(Q, list(b.r.values()))
        if sb.dsem is None:
            fl = self.free_sems.get(Q.name, [])
            if fl:
                sem, cnt = fl.pop()
                if Q.seen.get(id(sem), 0) < cnt:
                    Q.eng.wait_ge(sem, cnt)
                    Q.seen[id(sem)] = cnt
                sb.dsem, sb.dcnt = sem, cnt
            else:
                self.nsem += 1
                sb.dsem = self.nc.alloc_semaphore(f"dq{self.nsem}")
            sb.q = Q
            self.stage_bufs.append(sb)
        assert sb.q is Q
        sb.dcnt += 16
        Q.eng.dma_start(out=out, in_=in_, **kw).then_inc(sb.dsem, 16)
        ev = (sb.dsem, sb.dcnt)
        k = id(sb.dsem)
        for b in reads:
            b.r[k] = ev
        for b in writes:
            if partial:
                b.w[k] = ev
            else:
                b.w = {k: ev}
            b.r = {}
        if is_out:
            self.out_events.append(ev)

    def allreduce(self, src, dst, reads, writes):
        Q = self.pool
        for b in reads:
            self._wait(Q, list(b.w.values()))
        for b in writes:
            self._wait(Q, list(b.w.values()))
            self._wait(Q, list(b.r.values()))
        if self.cc_sem is None:
            self.cc_sem = self.nc.alloc_semaphore("cc_sem")
            self.cc_cnt = 0
        self.cc_cnt += 1
        self.nc.gpsimd.collective_compute("AllReduce", ALU.add, replica_groups=PAIRS, ins=[src],
                                          outs=[dst]).then_inc(self.cc_sem, 1)
        ev = (self.cc_sem, self.cc_cnt)
        kk = id(self.cc_sem)
        for b in reads:
            b.r[kk] = ev
        for b in writes:
            b.w = {kk: ev}
            b.r = {}

    def finish(self):
        last = {}
        for sem, cnt in self.out_events:
            k = id(sem)
            if k not in last or last[k][1] < cnt:
                last[k] = (sem, cnt)
        self._wait(self.sp, list(last.values()))


class Ctx:
    def __init__(self, nc, S):
        self.nc = nc
        self.S = S
        self.NT = S // T
        self.k = K(nc)
        self.dbufs = {}
        self.ps = []
        for i in range(8):
            t = nc.alloc_psum_tensor(f"ps{i}", [128, 512], F32)
            self.ps.append((t, self.k.buf(f"ps{i}")))
        self._psi = 0

    def dbuf(self, name, i):
        key = (name, i)
        if key not in self.dbufs:
            self.dbufs[key] = self.k.buf(f"{name}_{i}")
        return self.dbufs[key]

    def psum(self):
        p = self.ps[self._psi % 8]
        self._psi += 1
        return p


def xtile(X, i):
    return X[i].rearrange("(c p) t -> p c t", p=128)


def emit_norm(cx, xt, xt_b, gain, ht, ht_b, sq, sq_b, rstd, rstd_b, consts):
    nc, k = cx.nc, cx.k
    gain, gain_b = gain
    ones_bf, ones_b = consts["ones"], consts["ones_b"]
    k.op(k.act, lambda: nc.scalar.activation(out=sq[:], in_=xt[:], func=AF.Square),
         reads=[xt_b], writes=[sq_b])
    ps, ps_b = cx.psum()
    for c in range(NCH):
        k.op(k.pe, lambda c=c: nc.tensor.matmul(ps[:], ones_bf[:], sq[:, c, :],
                                                 start=(c == 0), stop=(c == NCH - 1)),
             reads=[sq_b, ones_b], writes=[ps_b], inc=(c == NCH - 1))
    k.op(k.act, lambda: nc.scalar.activation(out=rstd[:], in_=ps[:], func=AF.Sqrt, bias=consts["eps"][:],
                                             scale=1.0 / D),
         reads=[ps_b, consts["eps_b"]], writes=[rstd_b])
    k.op(k.dve, lambda: nc.vector.reciprocal(out=rstd[:], in_=rstd[:]), reads=[rstd_b], writes=[rstd_b])
    for c in range(NCH):
        k.op(k.dve, lambda c=c: nc.vector.scalar_tensor_tensor(
            out=ht[:, c, :], in0=xt[:, c, :], scalar=gain[:, c:c + 1], in1=rstd[:],
            op0=ALU.mult, op1=ALU.mult),
            reads=[xt_b, rstd_b, gain_b], writes=[ht_b])


def load_vec(cx, Q, dst, dst_b, src_1d, n, partial=False):
    cx.k.dma(Q, dst, src_1d.rearrange("(c p) -> p c", p=128), reads=[], writes=[dst_b], sb=dst_b,
             partial=partial, allow_slow_non_contiguous=True)


def store_reduce(cx, xt, xt_b, i, Xo, oname):
    k = cx.k
    k.dma(k.sp, xtile(cx.Q, i), xt[:], [xt_b], [cx.dbuf("Q", i)], xt_b)
    k.allreduce(cx.Q[i], Xo[i], [cx.dbuf("Q", i)], [cx.dbuf(oname, i)])


def stage_ffn(cx, li, Xi, iname, Xo, oname, W, consts):
    nc, k = cx.nc, cx.k
    nhc = FH
    H = nhc * 128
    with (
        nc.sbuf_tensor(f"wg{li}", [128, NCH, H], BF16) as wg,
        nc.sbuf_tensor(f"wu{li}", [128, NCH, H], BF16) as wu,
        nc.sbuf_tensor(f"wd{li}", [128, nhc, D], BF16) as wd,
        nc.sbuf_tensor(f"fg{li}", [128, NCH], F32) as gain,
        nc.sbuf_tensor(f"fx0{li}", [128, NCH, T], F32) as xt0,
        nc.sbuf_tensor(f"fx1{li}", [128, NCH, T], F32) as xt1,
        nc.sbuf_tensor(f"fh{li}", [128, NCH, T], BF16) as ht,
        nc.sbuf_tensor(f"fs{li}", [128, NCH, T], BF16) as sq,
        nc.sbuf_tensor(f"fr{li}", [128, T], F32) as rstd,
        nc.sbuf_tensor(f"fa{li}", [128, nhc, T], BF16) as act,
        nc.sbuf_tensor(f"fsg0{li}", [128, T], F32) as sg0,
        nc.sbuf_tensor(f"fsg1{li}", [128, T], F32) as sg1,
    ):
        wg_b, wu_b, wd_b, gain_b = k.buf("wg"), k.buf("wu"), k.buf("wd"), k.buf("fgain")
        xts = [(xt0, k.buf("fx0")), (xt1, k.buf("fx1"))]
        ht_b, sq_b, rstd_b, act_b = k.buf("fh"), k.buf("fs"), k.buf("fr"), k.buf("fa")
        sgs = [(sg0, k.buf("fsg0")), (sg1, k.buf("fsg1"))]
        for c in range(NCH):
            k.dma(k.pool, wg[:, c, :], W["gate"][c * 128:(c + 1) * 128, :], [], [wg_b], wg_b, partial=(c > 0))
            k.dma(k.pool, wu[:, c, :], W["up"][c * 128:(c + 1) * 128, :], [], [wu_b], wu_b, partial=(c > 0))
        for j in range(nhc):
            k.dma(k.pool, wd[:, j, :], W["down"][j * 128:(j + 1) * 128, :], [], [wd_b], wd_b, partial=(j > 0))
        load_vec(cx, k.sp, gain[:], gain_b, W["norm"], NCH)

        def load(i):
            xt, xt_b = xts[i % 2]
            k.dma(k.sp, xt[:], xtile(Xi, i), [cx.dbuf(iname, i)], [xt_b], xt_b)

        load(0)
        for i in range(cx.NT):
            xt, xt_b = xts[i % 2]
            if i + 1 < cx.NT:
                load(i + 1)
            emit_norm(cx, xt, xt_b, (gain, gain_b), ht, ht_b, sq, sq_b, rstd, rstd_b, consts)
            for j in range(nhc):
                pg, pg_b = cx.psum()
                pu, pu_b = cx.psum()
                for c in range(NCH):
                    k.op(k.pe, lambda c=c: nc.tensor.matmul(pg[:], wg[:, c, j * 128:(j + 1) * 128], ht[:, c, :],
                                                             start=(c == 0), stop=(c == NCH - 1)),
                         reads=[wg_b, ht_b], writes=[pg_b], inc=(c == NCH - 1))
                for c in range(NCH):
                    k.op(k.pe, lambda c=c: nc.tensor.matmul(pu[:], wu[:, c, j * 128:(j + 1) * 128], ht[:, c, :],
                                                             start=(c == 0), stop=(c == NCH - 1)),
                         reads=[wu_b, ht_b], writes=[pu_b], inc=(c == NCH - 1))
                sg, sg_b = sgs[j % 2]
                k.op(k.act, lambda: nc.scalar.activation(out=sg[:], in_=pg[:], func=AF.Silu),
                     reads=[pg_b], writes=[sg_b])
                k.op(k.dve, lambda: nc.vector.tensor_tensor(out=act[:, j, :], in0=sg[:], in1=pu[:], op=ALU.mult),
                     reads=[sg_b, pu_b], writes=[act_b])
            for m in range(NCH):
                po, po_b = cx.psum()
                for j in range(nhc):
                    k.op(k.pe, lambda j=j: nc.tensor.matmul(po[:], wd[:, j, m * 128:(m + 1) * 128], act[:, j, :],
                                                             start=(j == 0), stop=(j == nhc - 1)),
                         reads=[wd_b, act_b], writes=[po_b], inc=(j == nhc - 1))
                k.op(k.dve, lambda: nc.vector.scalar_tensor_tensor(out=xt[:, m, :], in0=xt[:, m, :], scalar=0.5,
                                                                   in1=po[:], op0=ALU.mult, op1=ALU.add),
                     reads=[po_b, xt_b], writes=[xt_b])
            store_reduce(cx, xt, xt_b, i, Xo, oname)
        k.release_stage()


def stage_pool(cx, li, Xi, iname, Xo, oname, W, consts):
    nc, k = cx.nc, cx.k
    E = 16
    with (
        nc.sbuf_tensor(f"pw{li}", [128, NCH, PCH * 128], BF16) as win,
        nc.sbuf_tensor(f"pg{li}", [128, 4, 256], BF16) as wgr,
        nc.sbuf_tensor(f"pgn{li}", [128, NCH], F32) as gain,
        nc.sbuf_tensor(f"psc{li}", [128, NCH], F32) as psc,
        nc.sbuf_tensor(f"pic{li}", [128, 4, E], F32) as invc,
        nc.sbuf_tensor(f"px0{li}", [128, NCH, T], F32) as xt0,
        nc.sbuf_tensor(f"px1{li}", [128, NCH, T], F32) as xt1,
        nc.sbuf_tensor(f"ph{li}", [128, NCH, T], BF16) as ht,
        nc.sbuf_tensor(f"pq{li}", [128, NCH, T], BF16) as sq,
        nc.sbuf_tensor(f"pr{li}", [128, T], F32) as rstd,
        nc.sbuf_tensor(f"pu{li}", [128, PCH, E + T], F32) as u,
        nc.sbuf_tensor(f"pa{li}", [128, PCH, E + T], F32) as A,
        nc.sbuf_tensor(f"pb{li}", [128, PCH, E + T], F32) as B,
        nc.sbuf_tensor(f"pp{li}", [128, PCH, T], BF16) as p,
        nc.sbuf_tensor(f"ptmp{li}", [128, E], F32) as tmp,
    ):
        win_b, wgr_b, gain_b, psc_b, invc_b = (k.buf("pwin"), k.buf("pwgr"), k.buf("pgain"), k.buf("psc"),
                                               k.buf("invc"))
        xts = [(xt0, k.buf("px0")), (xt1, k.buf("px1"))]
        ht_b, sq_b, rstd_b = k.buf("ph"), k.buf("pq"), k.buf("pr")
        u_b, A_b, B_b, p_b, tmp_b = k.buf("pu"), k.buf("pA"), k.buf("pB"), k.buf("pp"), k.buf("ptmp")
        for c in range(NCH):
            k.dma(k.pool, win[:, c, :], W["in"][c * 128:(c + 1) * 128, :], [], [win_b], win_b, partial=(c > 0))
        for g in range(4):
            k.dma(k.pool, wgr[:, g, :], W["group"][g, :, :], [], [wgr_b], wgr_b, partial=(g > 0))
        load_vec(cx, k.sp, gain[:], gain_b, W["norm"], NCH)
        load_vec(cx, k.sp, psc[:], psc_b, W["scale"], NCH)
        for g in range(4):
            w = 2 ** (g + 1)
            for t in range(E):
                k.op(k.pool, lambda g=g, t=t, w=w: nc.gpsimd.memset(invc[:, g, t:t + 1], 1.0 / min(t + 1, w)),
                     reads=[], writes=[invc_b])
        k.op(k.pool, lambda: nc.gpsimd.memset(u[:, :, 0:E], 0.0), reads=[], writes=[u_b])
        k.op(k.pool, lambda: nc.gpsimd.memset(A[:, :, 0:E], 0.0), reads=[], writes=[A_b])
        k.op(k.pool, lambda: nc.gpsimd.memset(B[:, :, 0:E], 0.0), reads=[], writes=[B_b])

        def load(i):
            xt, xt_b = xts[i % 2]
            k.dma(k.sp, xt[:], xtile(Xi, i), [cx.dbuf(iname, i)], [xt_b], xt_b)

        load(0)
        for i in range(cx.NT):
            xt, xt_b = xts[i % 2]
            if i + 1 < cx.NT:
                load(i + 1)
            emit_norm(cx, xt, xt_b, (gain, gain_b), ht, ht_b, sq, sq_b, rstd, rstd_b, consts)
            if i > 0:
                k.op(k.pool, lambda: nc.gpsimd.tensor_copy(out=u[:, :, 0:E], in_=u[:, :, T:T + E]),
                     reads=[u_b], writes=[u_b])
            for m in range(PCH):
                pu_, pu_b = cx.psum()
                for c in range(NCH):
                    k.op(k.pe, lambda c=c: nc.tensor.matmul(pu_[:], win[:, c, m * 128:(m + 1) * 128], ht[:, c, :],
                                                             start=(c == 0), stop=(c == NCH - 1)),
                         reads=[win_b, ht_b], writes=[pu_b], inc=(c == NCH - 1))
                k.op(k.act, lambda: nc.scalar.copy(out=u[:, m, E:E + T], in_=pu_[:]),
                     reads=[pu_b], writes=[u_b])
            k.op(k.act, lambda: nc.scalar.activation(out=xt[:], in_=xt[:], func=AF.Copy, scale=0.5), reads=[xt_b], writes=[xt_b])
            k.op(k.dve, lambda: nc.vector.tensor_tensor(out=A[:, :, 1:E + T], in0=u[:, :, 1:E + T],
                                                        in1=u[:, :, 0:E + T - 1], op=ALU.add),
                 reads=[u_b], writes=[A_b])
            k.op(k.pool, lambda: nc.gpsimd.tensor_tensor(out=B[:, 1:4, 3:E + T], in0=A[:, 1:4, 3:E + T],
                                                         in1=A[:, 1:4, 1:E + T - 2], op=ALU.add),
                 reads=[A_b], writes=[B_b])
            k.op(k.dve, lambda: nc.vector.tensor_tensor(out=A[:, 2:4, 7:E + T], in0=B[:, 2:4, 7:E + T],
                                                        in1=B[:, 2:4, 3:E + T - 4], op=ALU.add),
                 reads=[B_b], writes=[A_b])
            k.op(k.pool, lambda: nc.gpsimd.tensor_tensor(out=B[:, 3:4, 15:E + T], in0=A[:, 3:4, 15:E + T],
                                                         in1=A[:, 3:4, 7:E + T - 8], op=ALU.add),
                 reads=[A_b], writes=[B_b])
            srcs = [(A, A_b), (B, B_b), (A, A_b), (B, B_b)]
            for g in range(4):
                s_, s_b = srcs[g]
                w = 2 ** (g + 1)
                k.op(k.dve, lambda g=g, s_=s_, w=w: nc.vector.scalar_tensor_tensor(
                    out=p[:, g, :], in0=s_[:, g, E:E + T], scalar=1.0 / w,
                    in1=u[:, g, E:E + T], op0=ALU.mult, op1=ALU.subtract),
                    reads=[s_b, u_b], writes=[p_b])
                if i == 0:
                    k.op(k.dve, lambda g=g, s_=s_: nc.vector.tensor_tensor(
                        out=tmp[:], in0=s_[:, g, E:2 * E], in1=invc[:, g, :], op=ALU.mult),
                        reads=[s_b, invc_b], writes=[tmp_b])
                    k.op(k.dve, lambda g=g: nc.vector.tensor_tensor(
                        out=p[:, g, 0:E], in0=tmp[:], in1=u[:, g, E:2 * E], op=ALU.subtract),
                        reads=[tmp_b, u_b], writes=[p_b])
            for g in range(4):
                for mo in range(2):
                    cc = 2 * g + mo
                    py, py_b = cx.psum()
                    k.op(k.pe, lambda: nc.tensor.matmul(py[:], wgr[:, g, mo * 128:(mo + 1) * 128], p[:, g, :],
                                                        start=True, stop=True),
                         reads=[wgr_b, p_b], writes=[py_b])
                    k.op(k.dve, lambda cc=cc: nc.vector.scalar_tensor_tensor(
                        out=xt[:, cc, :], in0=py[:], scalar=psc[:, cc:cc + 1], in1=xt[:, cc, :],
                        op0=ALU.mult, op1=ALU.add),
                        reads=[py_b, psc_b, xt_b], writes=[xt_b])
            store_reduce(cx, xt, xt_b, i, Xo, oname)
        k.release_stage()


C_ID, C_SM, C_SA, C_SB2, C_TRI, C_OMT, C_BO, C_MB = 0, 128, 640, 768, 784, 912, 1040, 1168
C_F32W = 784
C_TOT = 1168 + 2048


def make_consts():
    c = np.zeros((128, C_TOT), np.float32)
    j = np.arange(128)
    c[:, C_ID:C_ID + 128] = np.eye(128)
    t256 = np.arange(256)
    c[:, C_SM:C_SM + 256] = (j[:, None] <= t256[None, :])
    c[:, C_SM + 256:C_SM + 512] = (128 + j[:, None] <= t256[None, :])
    k32 = np.arange(32)
    c[:32, C_SA:C_SA + 128] = ((k32[:, None] % 2) == (j[None, :] // 64))
    c[:32, C_SB2:C_SB2 + 16] = ((k32[:, None] // 2) == np.arange(16)[None, :])
    c[:, C_TRI:C_TRI + 128] = (j[:, None] >= j[None, :])
    c[:, C_OMT:C_OMT + 128] = (j[:, None] < j[None, :])
    c[:, C_BO:C_BO + 128] = ((j[:, None] // 64) == (j[None, :] // 64))
    t512 = np.arange(512)
    for o in range(4):
        valid = (128 * o + j[:, None]) < t512[None, :]
        c[:, C_MB + 512 * o:C_MB + 512 * (o + 1)] = np.where(valid, 0.0, -30000.0)
    return c


def stage_outproj(cx, tag, Xi, iname, Xo, oname, G, gname, Wd, KC):
    nc, k = cx.nc, cx.k
    with (
        nc.sbuf_tensor(f"ow{tag}", [128, KC, D], BF16) as w,
        nc.sbuf_tensor(f"ox0{tag}", [128, NCH, T], F32) as xt0,
        nc.sbuf_tensor(f"ox1{tag}", [128, NCH, T], F32) as xt1,
        nc.sbuf_tensor(f"og0{tag}", [128, KC, T], BF16) as g0,
        nc.sbuf_tensor(f"og1{tag}", [128, KC, T], BF16) as g1,
    ):
        w_b = k.buf("ow")
        xts = [(xt0, k.buf("ox0")), (xt1, k.buf("ox1"))]
        gs = [(g0, k.buf("og0")), (g1, k.buf("og1"))]
        for c in range(KC):
            k.dma(k.pool, w[:, c, :], Wd[c * 128:(c + 1) * 128, :], [], [w_b], w_b, partial=(c > 0))

        def load(i):
            xt, xt_b = xts[i % 2]
            gt, gt_b = gs[i % 2]
            k.dma(k.sp, xt[:], xtile(Xi, i), [cx.dbuf(iname, i)], [xt_b], xt_b)
            k.dma(k.sp, gt[:], G.rearrange("(c p) t -> p c t", p=128)[:, :, i * T:(i + 1) * T],
                  [cx.dbuf(gname, i)], [gt_b], gt_b)

        load(0)
        for i in range(cx.NT):
            xt, xt_b = xts[i % 2]
            gt, gt_b = gs[i % 2]
            if i + 1 < cx.NT:
                load(i + 1)
            for m in range(NCH):
                po, po_b = cx.psum()
                for c in range(KC):
                    k.op(k.pe, lambda c=c: nc.tensor.matmul(po[:], w[:, c, m * 128:(m + 1) * 128], gt[:, c, :],
                                                             start=(c == 0), stop=(c == KC - 1)),
                         reads=[w_b, gt_b], writes=[po_b], inc=(c == KC - 1))
                k.op(k.dve, lambda: nc.vector.scalar_tensor_tensor(out=xt[:, m, :], in0=xt[:, m, :], scalar=0.5,
                                                                   in1=po[:], op0=ALU.mult, op1=ALU.add),
                     reads=[po_b, xt_b], writes=[xt_b])
            store_reduce(cx, xt, xt_b, i, Xo, oname)
        k.release_stage()


def stage_sb_qkv(cx, li, Xi, iname, W, consts, QT, KT, V):
    nc, k = cx.nc, cx.k
    cb, cb_b = consts["cbf"], consts["cbf_b"]
    with (
        nc.sbuf_tensor(f"sw{li}", [128, NCH, 3 * 512], BF16) as w,
        nc.sbuf_tensor(f"sgn{li}", [128, NCH], F32) as gain,
        nc.sbuf_tensor(f"sgq{li}", [128, 2], F32) as gqk,
        nc.sbuf_tensor(f"sx0{li}", [128, NCH, T], F32) as xt0,
        nc.sbuf_tensor(f"sx1{li}", [128, NCH, T], F32) as xt1,
        nc.sbuf_tensor(f"sh{li}", [128, NCH, T], BF16) as ht,
        nc.sbuf_tensor(f"ssq{li}", [128, NCH, T], BF16) as sq,
        nc.sbuf_tensor(f"sr{li}", [128, T], F32) as rstd,
        nc.sbuf_tensor(f"sqt{li}", [128, SBP, T], BF16) as qt,
        nc.sbuf_tensor(f"skt{li}", [128, SBP, T], BF16) as kt,
        nc.sbuf_tensor(f"svt{li}", [128, 4, 512], BF16) as vt,
        nc.sbuf_tensor(f"ssc0{li}", [128, T], BF16) as sqc0,
        nc.sbuf_tensor(f"ssc1{li}", [128, T], BF16) as sqc1,
        nc.sbuf_tensor(f"srs0{li}", [128, T], F32) as rs0,
        nc.sbuf_tensor(f"srs1{li}", [128, T], F32) as rs1,
    ):
        w_b, gain_b, gqk_b = k.buf("sw"), k.buf("sgn"), k.buf("sgq")
        xts = [(xt0, k.buf("sx0")), (xt1, k.buf("sx1"))]
        ht_b, sq_b, rstd_b = k.buf("sh"), k.buf("ssq"), k.buf("sr")
        qt_b, kt_b, vt_b = k.buf("sqt"), k.buf("skt"), k.buf("svt")
        sqcs = [(sqc0, k.buf("ssc0")), (sqc1, k.buf("ssc1"))]
        rss = [(rs0, k.buf("srs0")), (rs1, k.buf("srs1"))]
        for c in range(NCH):
            k.dma(k.pool, w[:, c, :], W["qkv"][c * 128:(c + 1) * 128, :], [], [w_b], w_b, partial=(c > 0))
        load_vec(cx, k.sp, gain[:], gain_b, W["norm"], NCH)
        first = True
        for col, nm in ((0, "qn"), (1, "kn")):
            for hh in range(2):
                k.dma(k.sp, gqk[hh * 64:(hh + 1) * 64, col:col + 1], W[nm].rearrange("(p o) -> p o", o=1), [],
                      [gqk_b], gqk_b, partial=(not first), allow_slow_non_contiguous=True)
                first = False
        k.op(k.dve, lambda: nc.vector.tensor_scalar(out=gqk[:, 0:1], in0=gqk[:, 0:1], scalar1=0.125, scalar2=None,
                                                    op0=ALU.mult), reads=[gqk_b], writes=[gqk_b])

        def load(i):
            xt, xt_b = xts[i % 2]
            k.dma(k.sp, xt[:], xtile(Xi, i), [cx.dbuf(iname, i)], [xt_b], xt_b)

        load(0)
        n = 0
        for i in range(cx.NT):
            xt, xt_b = xts[i % 2]
            if i + 1 < cx.NT:
                load(i + 1)
            emit_norm(cx, xt, xt_b, (gain, gain_b), ht, ht_b, sq, sq_b, rstd, rstd_b, consts)
            for which, (dst, dst_b) in enumerate(((qt, qt_b), (kt, kt_b))):
                for oc in range(SBP):
                    c0 = which * 512 + oc * 128
                    pq, pq_b = cx.psum()
                    for c in range(NCH):
                        k.op(k.pe, lambda c=c: nc.tensor.matmul(pq[:], w[:, c, c0:c0 + 128], ht[:, c, :],
                                                                 start=(c == 0), stop=(c == NCH - 1)),
                             reads=[w_b, ht_b], writes=[pq_b], inc=(c == NCH - 1))
                    sqc, sqc_b = sqcs[n % 2]
                    rs, rs_b = rss[n % 2]
                    n += 1
                    k.op(k.act, lambda: nc.scalar.activation(out=sqc[:], in_=pq[:], func=AF.Square),
                         reads=[pq_b], writes=[sqc_b])
                    pss, pss_b = cx.psum()
                    k.op(k.pe, lambda: nc.tensor.matmul(pss[:], cb[:, C_BO:C_BO + 128], sqc[:], start=True, stop=True),
                         reads=[sqc_b, cb_b], writes=[pss_b])
                    k.op(k.act, lambda: nc.scalar.activation(out=rs[:], in_=pss[:], func=AF.Sqrt,
                                                             bias=consts["eps"][:], scale=1.0 / 64),
                         reads=[pss_b, consts["eps_b"]], writes=[rs_b])
                    k.op(k.dve, lambda: nc.vector.reciprocal(out=rs[:], in_=rs[:]), reads=[rs_b], writes=[rs_b])
                    k.op(k.dve, lambda: nc.vector.scalar_tensor_tensor(
                        out=dst[:, oc, :], in0=pq[:], scalar=gqk[:, which:which + 1], in1=rs[:],
                        op0=ALU.mult, op1=ALU.mult), reads=[pq_b, gqk_b, rs_b], writes=[dst_b])
            for blk in range(4):
                for half in range(1):
                    pv, pv_b = cx.psum()
                    c0 = 2 * 512
                    for c in range(NCH):
                        k.op(k.pe, lambda c=c: nc.tensor.matmul(pv[:], ht[:, c, blk * 128:(blk + 1) * 128],
                                                                 w[:, c, c0:c0 + 512], start=(c == 0),
                                                                 stop=(c == NCH - 1)),
                             reads=[w_b, ht_b], writes=[pv_b], inc=(c == NCH - 1))
                    k.op(k.act, lambda: nc.scalar.copy(out=vt[:, blk, half * 512:(half + 1) * 512], in_=pv[:]),
                         reads=[pv_b], writes=[vt_b])
            k.dma(k.sp, QT.rearrange("(c p) t -> p c t", p=128)[:, :, i * T:(i + 1) * T], qt[:], [qt_b],
                  [cx.dbuf("QT", i)], qt_b)
            k.dma(k.sp, KT.rearrange("(c p) t -> p c t", p=128)[:, :, i * T:(i + 1) * T], kt[:], [kt_b],
                  [cx.dbuf("KT", i)], kt_b)
            k.dma(k.sp, V[i * T:(i + 1) * T, :].rearrange("(b p) d -> p b d", p=128), vt[:], [vt_b],
                  [cx.dbuf("V", i)], vt_b)
        k.release_stage()


def stage_sb_attn(cx, li, consts, QT, KT, V, OT):
    nc, k = cx.nc, cx.k
    S = cx.S
    NB = S // 128
    cb, cb_b = consts["cbf"], consts["cbf_b"]
    NE, NSP, NG_, NA = 8, 8, 4, 6
    with (
        nc.sbuf_tensor(f"akc0{li}", [128, S], BF16) as kc0,
        nc.sbuf_tensor(f"aqc0{li}", [128, S], BF16) as qc0,
        nc.sbuf_tensor(f"avc0{li}", [128, NB, 128], BF16) as vc0,
        nc.sbuf_tensor(f"akc1{li}", [128, S], BF16) as kc1,
        nc.sbuf_tensor(f"aqc1{li}", [128, S], BF16) as qc1,
        nc.sbuf_tensor(f"avc1{li}", [128, NB, 128], BF16) as vc1,
        nc.sbuf_tensor(f"aoc{li}", [128, S], BF16) as oc,
        nc.sbuf_tensor(f"aE{li}", [128, NE, T], F32) as Et,
        nc.sbuf_tensor(f"aSP{li}", [128, NSP, T], BF16) as SPt,
        nc.sbuf_tensor(f"aG{li}", [128, NG_, T], F32) as Gt,
        nc.sbuf_tensor(f"aA{li}", [128, NA, T], BF16) as At,
    ):
        kqv = [(kc0, qc0, vc0, k.buf("akc0"), k.buf("aqc0"), k.buf("avc0")),
               (kc1, qc1, vc1, k.buf("akc1"), k.buf("aqc1"), k.buf("avc1"))]
        oc_b = k.buf("aoc")
        E_b = [k.buf(f"aE{x}") for x in range(NE)]
        SP_b = [k.buf(f"aSP{x}") for x in range(NSP)]
        G_b = [k.buf(f"aG{x}") for x in range(NG_)]
        A_b = [k.buf(f"aA{x}") for x in range(NA)]
        Zs = [cx.ps[x] for x in range(4)]
        Ts = [cx.ps[4], cx.ps[5]]
        Obanks = [(cx.ps[6][0], [k.buf("aO00"), k.buf("aO01")]), (cx.ps[7][0], [k.buf("aO10"), k.buf("aO11")])]
        one_ap = consts["onef"]
        cnt = {"z": 0, "e": 0, "sp": 0, "g": 0, "a": 0}

        def load(c):
            kc, qc, vc, kc_b, qc_b, vc_b = kqv[c % 2]
            rd = [cx.dbuf("KT", i) for i in range(cx.NT)]
            k.dma(k.sp, kc[:], KT[c * 128:(c + 1) * 128, :], rd, [kc_b], kc_b)
            rd = [cx.dbuf("QT", i) for i in range(cx.NT)]
            k.dma(k.sp, qc[:], QT[c * 128:(c + 1) * 128, :], rd, [qc_b], qc_b)
            rd = [cx.dbuf("V", i) for i in range(cx.NT)]
            k.dma(k.sp, vc[:], V[:, c * 128:(c + 1) * 128].rearrange("(b p) d -> p b d", p=128), rd, [vc_b], vc_b)

        load(0)
        for c in range(SBP):
            kc, qc, vc, kc_b, qc_b, vc_b = kqv[c % 2]
            if c + 1 < SBP:
                load(c + 1)
            items = [(i, J) for i in range(cx.NT) for J in range(4 * i + 3, -1, -1)]
            n = len(items)
            st = [dict() for _ in range(n)]

            def s1(b):
                i, J = items[b]
                diag = J >= 4 * i
                for hh in range(2):
                    pb = 64 * hh
                    Z, Z_b = Zs[cnt["z"] % 4]
                    cnt["z"] += 1
                    st[b][("Z", hh)] = (Z, Z_b)
                    k.op(k.pe, lambda pb=pb, Z=Z: nc.tensor.matmul(
                        Z[:], kc[pb:pb + 64, J * 128:(J + 1) * 128], qc[pb:pb + 64, i * T:(i + 1) * T],
                        start=True, stop=(not diag)), reads=[kc_b, qc_b], writes=[Z_b], inc=(not diag))
                    if diag:
                        o = J - 4 * i
                        k.op(k.pe, lambda Z=Z, o=o: nc.tensor.matmul(
                            Z[:], cb[:, C_ID:C_ID + 128], cb[:, C_MB + 512 * o:C_MB + 512 * (o + 1)],
                            start=False, stop=True), reads=[cb_b], writes=[Z_b])

            def s2(b):
                for hh in range(2):
                    Z, Z_b = st[b][("Z", hh)]
                    e = cnt["e"] % NE
                    cnt["e"] += 1
                    sp = cnt["sp"] % NSP
                    cnt["sp"] += 1
                    st[b][("e", hh)] = e
                    st[b][("sp", hh)] = sp
                    k.op(k.act, lambda Z=Z, e=e: nc.scalar.activation(out=Et[:, e, :], in_=Z[:], func=AF.Exp),
                         reads=[Z_b], writes=[E_b[e]])
                    k.op(k.act, lambda e=e, sp=sp: nc.scalar.activation(out=SPt[:, sp, :], in_=Et[:, e, :], func=AF.Ln,
                                                                        bias=one_ap[:], scale=1.0),
                         reads=[E_b[e], consts["onef_b"]], writes=[SP_b[sp]])

            def s3_tri(b):
                i, J = items[b]
                first = (J == 4 * i + 3)
                for hh in range(2):
                    Tt, T_b = Ts[hh]
                    sp = st[b][("sp", hh)]
                    k.op(k.pe, lambda Tt=Tt, sp=sp: nc.tensor.matmul(
                        Tt[:], cb[:, C_TRI:C_TRI + 128], SPt[:, sp, :], start=first, stop=False,
                        skip_group_check=True), reads=[SP_b[sp], cb_b], writes=[T_b])

            def s3_g(b):
                for hh in range(2):
                    Tt, T_b = Ts[hh]
                    g = cnt["g"] % NG_
                    cnt["g"] += 1
                    st[b][("g", hh)] = g
                    k.op(k.act, lambda Tt=Tt, g=g: nc.scalar.activation(out=Gt[:, g, :], in_=Tt[:], func=AF.Exp,
                                                                        scale=-1.0),
                         reads=[T_b], writes=[G_b[g]])

            def s3_omt(b):
                i, J = items[b]
                if J == 0:
                    return
                for hh in range(2):
                    Tt, T_b = Ts[hh]
                    sp = st[b][("sp", hh)]
                    k.op(k.pe, lambda Tt=Tt, sp=sp: nc.tensor.matmul(
                        Tt[:], cb[:, C_OMT:C_OMT + 128], SPt[:, sp, :], start=False, stop=False,
                        skip_group_check=True), reads=[SP_b[sp], cb_b], writes=[T_b])

            def s4_a(b):
                for hh in range(2):
                    e, g = st[b][("e", hh)], st[b][("g", hh)]
                    a = cnt["a"] % NA
                    cnt["a"] += 1
                    st[b][("a", hh)] = a
                    k.op(k.pool, lambda e=e, g=g, a=a: nc.gpsimd.tensor_tensor(out=At[:, a, :], in0=Et[:, e, :],
                                                                               in1=Gt[:, g, :], op=ALU.mult),
                         reads=[E_b[e], G_b[g]], writes=[A_b[a]])

            def s4_av(b):
                i, J = items[b]
                first = (J == 4 * i + 3)
                Ot, O_b = Obanks[i % 2]
                for hh in range(2):
                    pb = 64 * hh
                    a = st[b][("a", hh)]
                    k.op(k.pe, lambda pb=pb, a=a, Ot=Ot: nc.tensor.matmul(
                        Ot[pb:pb + 64, :], vc[:, J, pb:pb + 64], At[:, a, :], start=first, stop=(J == 0),
                        skip_group_check=True), reads=[vc_b, A_b[a]], writes=[O_b[hh]])
                if J == 0:
                    for hh in range(2):
                        pb = 64 * hh
                        k.op(k.dve, lambda pb=pb, Ot=Ot: nc.vector.tensor_copy(
                            out=oc[pb:pb + 64, i * T:(i + 1) * T], in_=Ot[pb:pb + 64, :]),
                            reads=[O_b[hh]], writes=[oc_b])

            for t in range(n + 3):
                if 0 <= t - 2 < n:
                    s3_tri(t - 2)
                    s3_g(t - 2)
                if t < n:
                    s1(t)
                if 0 <= t - 3 < n:
                    s4_av(t - 3)
                if 0 <= t - 2 < n:
                    s3_omt(t - 2)
                    s4_a(t - 2)
                if 0 <= t - 1 < n:
                    s2(t - 1)
            wr = [cx.dbuf("OT", i) for i in range(cx.NT)]
            k.dma(k.sp, OT[c * 128:(c + 1) * 128, :], oc[:], [oc_b], wr, oc_b, partial=(c > 0))
        k.release_stage()


def stage_ssd_a1(cx, li, Xi, iname, W, consts, ZS, DT, DA):
    nc, k = cx.nc, cx.k
    with (
        nc.sbuf_tensor(f"d1w{li}", [128, NCH, 1024 + NHD], BF16) as w,
        nc.sbuf_tensor(f"d1g{li}", [128, NCH], F32) as gain,
        nc.sbuf_tensor(f"d1p{li}", [NHD, 4], F32) as prm,
        nc.sbuf_tensor(f"d1x0{li}", [128, NCH, T], F32) as xt0,
        nc.sbuf_tensor(f"d1x1{li}", [128, NCH, T], F32) as xt1,
        nc.sbuf_tensor(f"d1h{li}", [128, NCH, T], BF16) as ht,
        nc.sbuf_tensor(f"d1q{li}", [128, NCH, T], BF16) as sq,
        nc.sbuf_tensor(f"d1r{li}", [128, T], F32) as rstd,
        nc.sbuf_tensor(f"d1z{li}", [128, 8, T], BF16) as zs,
        nc.sbuf_tensor(f"d1dt{li}", [NHD, 2, T], F32) as dd,
    ):
        w_b, gain_b, prm_b = k.buf("d1w"), k.buf("d1g"), k.buf("d1p")
        xts = [(xt0, k.buf("d1x0")), (xt1, k.buf("d1x1"))]
        ht_b, sq_b, rstd_b, zs_b, dd_b = k.buf("d1h"), k.buf("d1q"), k.buf("d1r"), k.buf("d1z"), k.buf("d1dt")
        for c in range(NCH):
            k.dma(k.pool, w[:, c, 0:1024], W["in"][c * 128:(c + 1) * 128, 0:1024], [], [w_b], w_b, partial=(c > 0))
            k.dma(k.pool, w[:, c, 1024:1024 + NHD], W["in"][c * 128:(c + 1) * 128, 3072:3072 + NHD], [], [w_b], w_b,
                  partial=True)
        load_vec(cx, k.sp, gain[:], gain_b, W["norm"], NCH)
        k.dma(k.sp, prm[:, 0:1], W["dt_bias"].rearrange("(p o) -> p o", o=1), [], [prm_b], prm_b,
              allow_slow_non_contiguous=True)
        k.dma(k.sp, prm[:, 1:2], W["a_log"].rearrange("(p o) -> p o", o=1), [], [prm_b], prm_b, partial=True,
              allow_slow_non_contiguous=True)
        k.op(k.act, lambda: nc.scalar.activation(out=prm[:, 2:3], in_=prm[:, 1:2], func=AF.Exp),
             reads=[prm_b], writes=[prm_b])
        k.op(k.dve, lambda: nc.vector.tensor_scalar(out=prm[:, 2:3], in0=prm[:, 2:3], scalar1=-1.0, scalar2=None,
                                                    op0=ALU.mult), reads=[prm_b], writes=[prm_b])

        def load(i):
            xt, xt_b = xts[i % 2]
            k.dma(k.sp, xt[:], xtile(Xi, i), [cx.dbuf(iname, i)], [xt_b], xt_b)

        load(0)
        for i in range(cx.NT):
            xt, xt_b = xts[i % 2]
            if i + 1 < cx.NT:
                load(i + 1)
            emit_norm(cx, xt, xt_b, (gain, gain_b), ht, ht_b, sq, sq_b, rstd, rstd_b, consts)
            for oc in range(8):
                pz, pz_b = cx.psum()
                for c in range(NCH):
                    k.op(k.pe, lambda c=c: nc.tensor.matmul(pz[:], w[:, c, oc * 128:(oc + 1) * 128], ht[:, c, :],
                                                             start=(c == 0), stop=(c == NCH - 1)),
                         reads=[w_b, ht_b], writes=[pz_b], inc=(c == NCH - 1))
                k.op(k.act, lambda: nc.scalar.activation(out=zs[:, oc, :], in_=pz[:], func=AF.Silu),
                     reads=[pz_b], writes=[zs_b])
            pd, pd_b = cx.psum()
            for c in range(NCH):
                k.op(k.pe, lambda c=c: nc.tensor.matmul(pd[0:NHD, :], w[:, c, 1024:1024 + NHD], ht[:, c, :],
                                                         start=(c == 0), stop=(c == NCH - 1)),
                     reads=[w_b, ht_b], writes=[pd_b], inc=(c == NCH - 1))
            k.op(k.act, lambda: nc.scalar.activation(out=dd[:, 0, :], in_=pd[0:NHD, :], func=AF.Exp,
                                                     bias=prm[:, 0:1], scale=1.0),
                 reads=[pd_b, prm_b], writes=[dd_b])
            k.op(k.act, lambda: nc.scalar.activation(out=dd[:, 0, :], in_=dd[:, 0, :], func=AF.Ln,
                                                     bias=consts["onef"][0:NHD, :], scale=1.0),
                 reads=[dd_b, consts["onef_b"]], writes=[dd_b])
            k.op(k.dve, lambda: nc.vector.tensor_scalar(out=dd[:, 1, :], in0=dd[:, 0, :], scalar1=prm[:, 2:3],
                                                        scalar2=None, op0=ALU.mult),
                 reads=[dd_b, prm_b], writes=[dd_b])
            k.dma(k.sp, ZS.rearrange("(c p) t -> p c t", p=128)[:, :, i * T:(i + 1) * T], zs[:], [zs_b],
                  [cx.dbuf("ZS", i)], zs_b)
            k.dma(k.sp, DT[:, i * T:(i + 1) * T], dd[:, 0, :], [dd_b], [cx.dbuf("DT", i)], dd_b)
            k.dma(k.sp, DA[:, i * T:(i + 1) * T], dd[:, 1, :], [dd_b], [cx.dbuf("DA", i)], dd_b)
        k.release_stage()


def stage_ssd_a2(cx, li, Xi, iname, W, consts, XS, BT, CT):
    nc, k = cx.nc, cx.k
    cf, cf_b = consts["cf"], consts["cf_b"]
    with (
        nc.sbuf_tensor(f"d2w{li}", [128, NCH, 2048], BF16) as w,
        nc.sbuf_tensor(f"d2g{li}", [128, NCH], F32) as gain,
        nc.sbuf_tensor(f"d2cr{li}", [64, 128], F32) as cwr,
        nc.sbuf_tensor(f"d2br{li}", [16, 128], F32) as cbr,
        nc.sbuf_tensor(f"d2cw{li}", [128, 4, 16], F32) as cw,
        nc.sbuf_tensor(f"d2cb{li}", [128, 16], F32) as cbias,
        nc.sbuf_tensor(f"d2x0{li}", [128, NCH, T], F32) as xt0,
        nc.sbuf_tensor(f"d2x1{li}", [128, NCH, T], F32) as xt1,
        nc.sbuf_tensor(f"d2h{li}", [128, NCH, T], BF16) as ht,
        nc.sbuf_tensor(f"d2q{li}", [128, NCH, T], BF16) as sq,
        nc.sbuf_tensor(f"d2r{li}", [128, T], F32) as rstd,
        nc.sbuf_tensor(f"d2hl{li}", [128, 16, 4], F32) as halo,
        nc.sbuf_tensor(f"d2wk0{li}", [128, 4 + T], F32) as wk0,
        nc.sbuf_tensor(f"d2wk1{li}", [128, 4 + T], F32) as wk1,
        nc.sbuf_tensor(f"d2ac0{li}", [128, T], F32) as ac0,
        nc.sbuf_tensor(f"d2ac1{li}", [128, T], F32) as ac1,
        nc.sbuf_tensor(f"d2xo{li}", [128, 16, T], BF16) as xo,
    ):
        w_b, gain_b, cwr_b, cbr_b, cw_b, cbias_b = (k.buf("d2w"), k.buf("d2g"), k.buf("d2cr"), k.buf("d2br"),
                                                    k.buf("d2cw"), k.buf("d2cb"))
        xts = [(xt0, k.buf("d2x0")), (xt1, k.buf("d2x1"))]
        ht_b, sq_b, rstd_b, halo_b, xo_b = k.buf("d2h"), k.buf("d2q"), k.buf("d2r"), k.buf("d2hl"), k.buf("d2xo")
        wks = [(wk0, k.buf("d2wk0")), (wk1, k.buf("d2wk1"))]
        acs = [(ac0, k.buf("d2ac0")), (ac1, k.buf("d2ac1"))]
        for c in range(NCH):
            k.dma(k.pool, w[:, c, :], W["in"][c * 128:(c + 1) * 128, 1024:3072], [], [w_b], w_b, partial=(c > 0))
        load_vec(cx, k.sp, gain[:], gain_b, W["norm"], NCH)
        k.dma(k.sp, cwr[:], W["conv_w"].rearrange("k (c p) -> (k c) p", p=128), [], [cwr_b], cwr_b)
        k.dma(k.sp, cbr[:], W["conv_b"].rearrange("(c p) -> c p", p=128), [], [cbr_b], cbr_b)
        pt, pt_b = cx.psum()
        k.op(k.pe, lambda: nc.tensor.transpose(pt[:, 0:64], cwr[:], cf[0:64, C_ID:C_ID + 64]),
             reads=[cwr_b, cf_b], writes=[pt_b])
        k.op(k.dve, lambda: nc.vector.tensor_copy(out=cw[:].rearrange("p k c -> p (k c)"), in_=pt[:, 0:64]),
             reads=[pt_b], writes=[cw_b])
        pt2, pt2_b = cx.psum()
        k.op(k.pe, lambda: nc.tensor.transpose(pt2[:, 0:16], cbr[:], cf[0:16, C_ID:C_ID + 16]),
             reads=[cbr_b, cf_b], writes=[pt2_b])
        k.op(k.dve, lambda: nc.vector.tensor_copy(out=cbias[:], in_=pt2[:, 0:16]), reads=[pt2_b], writes=[cbias_b])
        k.op(k.pool, lambda: nc.gpsimd.memset(halo[:], 0.0), reads=[], writes=[halo_b])

        def load(i):
            xt, xt_b = xts[i % 2]
            k.dma(k.sp, xt[:], xtile(Xi, i), [cx.dbuf(iname, i)], [xt_b], xt_b)

        load(0)
        n = 0
        for i in range(cx.NT):
            xt, xt_b = xts[i % 2]
            if i + 1 < cx.NT:
                load(i + 1)
            emit_norm(cx, xt, xt_b, (gain, gain_b), ht, ht_b, sq, sq_b, rstd, rstd_b, consts)
            for ch in range(16):
                pp, pp_b = cx.psum()
                for c in range(NCH):
                    k.op(k.pe, lambda c=c: nc.tensor.matmul(pp[:], w[:, c, ch * 128:(ch + 1) * 128], ht[:, c, :],
                                                             start=(c == 0), stop=(c == NCH - 1)),
                         reads=[w_b, ht_b], writes=[pp_b], inc=(c == NCH - 1))
                wk, wk_b = wks[n % 2]
                ac, ac_b = acs[n % 2]
                n += 1
                k.op(k.pool, lambda wk=wk: nc.gpsimd.tensor_copy(out=wk[:, 0:4], in_=halo[:, ch, :]),
                     reads=[halo_b], writes=[wk_b])
                k.op(k.act, lambda wk=wk: nc.scalar.copy(out=wk[:, 4:4 + T], in_=pp[:]), reads=[pp_b], writes=[wk_b])
                k.op(k.pool, lambda wk=wk: nc.gpsimd.tensor_copy(out=halo[:, ch, :], in_=wk[:, T:T + 4]),
                     reads=[wk_b], writes=[halo_b])
                k.op(k.dve, lambda wk=wk, ac=ac: nc.vector.tensor_scalar(
                    out=ac[:], in0=wk[:, 4:4 + T], scalar1=cw[:, 3, ch:ch + 1], scalar2=cbias[:, ch:ch + 1],
                    op0=ALU.mult, op1=ALU.add), reads=[wk_b, cw_b, cbias_b], writes=[ac_b])
                for tap in (2, 1, 0):
                    sh = 3 - tap
                    k.op(k.dve, lambda wk=wk, ac=ac, tap=tap, sh=sh: nc.vector.scalar_tensor_tensor(
                        out=ac[:], in0=wk[:, 4 - sh:4 - sh + T], scalar=cw[:, tap, ch:ch + 1], in1=ac[:],
                        op0=ALU.mult, op1=ALU.add), reads=[wk_b, cw_b, ac_b], writes=[ac_b])
                k.op(k.act, lambda ac=ac: nc.scalar.activation(out=xo[:, ch, :], in_=ac[:], func=AF.Silu),
                     reads=[ac_b], writes=[xo_b])
            k.dma(k.sp, XS.rearrange("(c p) t -> p c t", p=128)[:, :, i * T:(i + 1) * T], xo[:, 0:8, :], [xo_b],
                  [cx.dbuf("XS", i)], xo_b)
            k.dma(k.sp, BT.rearrange("(c p) t -> p c t", p=128)[:, :, i * T:(i + 1) * T], xo[:, 8:12, :], [xo_b],
                  [cx.dbuf("BT", i)], xo_b)
            k.dma(k.sp, CT.rearrange("(c p) t -> p c t", p=128)[:, :, i * T:(i + 1) * T], xo[:, 12:16, :], [xo_b],
                  [cx.dbuf("CT", i)], xo_b)
        k.release_stage()


def stage_ssd_scan(cx, li, W, consts, ZS, DT, DA, XS, BT, CT, GN):
    nc, k = cx.nc, cx.k
    S = cx.S
    L = 256
    NCK = S // L
    cb, cb_b = consts["cbf"], consts["cbf_b"]
    cf, cf_b = consts["cf"], consts["cf_b"]
    with ExitStack() as es:
        onesf = es.enter_context(nc.sbuf_tensor(f"s3of{li}", [128, 256], F32))
        selH = es.enter_context(nc.sbuf_tensor(f"s3sh{li}", [NHD, NHD, 128], F32))
        prm = es.enter_context(nc.sbuf_tensor(f"s3pr{li}", [NHD, 2], F32))
        rhsD = es.enter_context(nc.sbuf_tensor(f"s3rd{li}", [NHD, 8], F32))
        Dvec = es.enter_context(nc.sbuf_tensor(f"s3dv{li}", [128, 8], F32))
        onorm = es.enter_context(nc.sbuf_tensor(f"s3on{li}", [128, 8], F32))
        stf = es.enter_context(nc.sbuf_tensor(f"s3st{li}", [128, NHD, 64], F32))
        stb = es.enter_context(nc.sbuf_tensor(f"s3sb{li}", [128, NHD, 64], BF16))
        dtc = es.enter_context(nc.sbuf_tensor(f"s3dt{li}", [NHD, L], F32))
        dac = es.enter_context(nc.sbuf_tensor(f"s3da{li}", [NHD, L], F32))
        acum = es.enter_context(nc.sbuf_tensor(f"s3ac{li}", [NHD, L], F32))
        dg = es.enter_context(nc.sbuf_tensor(f"s3dg{li}", [NHD, NHD], F32))
        tm = es.enter_context(nc.sbuf_tensor(f"s3tm{li}", [128, 4, NHD], F32))
        tmpd = es.enter_context(nc.sbuf_tensor(f"s3td{li}", [128, 2, NHD], F32))
        wdt = es.enter_context(nc.sbuf_tensor(f"s3wd{li}", [128, 2, NHD], F32))
        xs = es.enter_context(nc.sbuf_tensor(f"s3xs{li}", [128, 8, L], BF16))
        zs = es.enter_context(nc.sbuf_tensor(f"s3zs{li}", [128, 8, L], BF16))
        Bt = es.enter_context(nc.sbuf_tensor(f"s3bt{li}", [128, NG, L], BF16))
        Ct = es.enter_context(nc.sbuf_tensor(f"s3ct{li}", [128, NG, L], BF16))
        Gm0 = es.enter_context(nc.sbuf_tensor(f"s3gm0{li}", [128, 2, L], F32))
        Gm1 = es.enter_context(nc.sbuf_tensor(f"s3gm1{li}", [128, 2, L], F32))
        xstm0 = es.enter_context(nc.sbuf_tensor(f"s3xt0{li}", [128, 2, 256], BF16))
        xstm1 = es.enter_context(nc.sbuf_tensor(f"s3xt1{li}", [128, 2, 256], BF16))
        Bw0 = es.enter_context(nc.sbuf_tensor(f"s3bw0{li}", [128, 2, 128], BF16))
        Bw1 = es.enter_context(nc.sbuf_tensor(f"s3bw1{li}", [128, 2, 128], BF16))
        xw0 = es.enter_context(nc.sbuf_tensor(f"s3xw0{li}", [128, 2, 256], BF16))
        xw1 = es.enter_context(nc.sbuf_tensor(f"s3xw1{li}", [128, 2, 256], BF16))
        Dd0 = es.enter_context(nc.sbuf_tensor(f"s3d0{li}", [128, 4, 2, L], F32))
        Dd1 = es.enter_context(nc.sbuf_tensor(f"s3d1{li}", [128, 4, 2, L], F32))
        Mh0 = es.enter_context(nc.sbuf_tensor(f"s3m0{li}", [128, 4, 2, L], BF16))
        Mh1 = es.enter_context(nc.sbuf_tensor(f"s3m1{li}", [128, 4, 2, L], BF16))
        Ea0 = es.enter_context(nc.sbuf_tensor(f"s3e0{li}", [128, 4, L], F32))
        Ea1 = es.enter_context(nc.sbuf_tensor(f"s3e1{li}", [128, 4, L], F32))
        Ch0 = es.enter_context(nc.sbuf_tensor(f"s3c0{li}", [128, 4, L], BF16))
        Ch1 = es.enter_context(nc.sbuf_tensor(f"s3c1{li}", [128, 4, L], BF16))
        yv = es.enter_context(nc.sbuf_tensor(f"s3yv{li}", [128, 2, L], F32))
        sqg = es.enter_context(nc.sbuf_tensor(f"s3sq{li}", [128, 2, L], BF16))
        rs = es.enter_context(nc.sbuf_tensor(f"s3rs{li}", [128, L], F32))
        gn = es.enter_context(nc.sbuf_tensor(f"s3gn{li}", [128, 8, L], BF16))
        B = k.buf
        onesf_b, selH_b, prm_b, rhsD_b, Dvec_b, onorm_b = B("onesf"), B("selH"), B("s3pr"), B("rhsD"), B("Dvec"), B("onorm")
        stf_b = [B(f"stf{h}") for h in range(NHD)]
        stb_b = [B(f"stb{h}") for h in range(NHD)]
        dtc_b, dac_b, acum_b, dg_b, tm_b, tmpd_b, wdt_b = (B("dtc"), B("dac"), B("acum"), B("dg"), B("tm"), B("tmpd"),
                                                           B("wdt"))
        xs_b, zs_b, Bt_b, Ct_b = B("xs"), B("zs"), B("Bt"), B("Ct")
        Gms = [(Gm0, B("Gm0")), (Gm1, B("Gm1"))]
        xstms = [(xstm0, B("xstm0")), (xstm1, B("xstm1"))]
        Bws = [(Bw0, B("Bw0")), (Bw1, B("Bw1"))]
        xws = [(xw0, B("xw0")), (xw1, B("xw1"))]
        gcount = [0]
        Dds = [(Dd0, B("Dd0")), (Dd1, B("Dd1"))]
        Mhs = [(Mh0, B("Mh0")), (Mh1, B("Mh1"))]
        Eas = [(Ea0, B("Ea0")), (Ea1, B("Ea1"))]
        Chs = [(Ch0, B("Ch0")), (Ch1, B("Ch1"))]
        yv_b, sqg_b, rs_b, gn_b = B("yv"), B("sqg"), B("rs"), B("gn")
        rot = [0]

        def ps():
            p = cx.ps[rot[0] % 4]
            rot[0] += 1
            return p

        ACB = [cx.ps[4], cx.ps[5]]

        Ybanks = [cx.ps[6], cx.ps[7]]
        k.op(k.pool, lambda: nc.gpsimd.memset(onesf[:], 1.0), reads=[], writes=[onesf_b])
        k.op(k.pool, lambda: nc.gpsimd.memset(stf[:], 0.0), reads=[], writes=stf_b)
        k.op(k.pool, lambda: nc.gpsimd.memset(stb[:], 0.0), reads=[], writes=stb_b)
        for h in range(NHD):
            k.op(k.pool, lambda h=h: nc.gpsimd.tensor_scalar(out=selH[:, h, :], in0=onesf[0:NHD, 0:128],
                                                             scalar1=cf[0:NHD, C_ID + h:C_ID + h + 1], scalar2=None,
                                                             op0=ALU.mult),
                 reads=[onesf_b, cf_b], writes=[selH_b])
        k.dma(k.sp, prm[:, 0:1], W["d"].rearrange("(p o) -> p o", o=1), [], [prm_b], prm_b,
              allow_slow_non_contiguous=True)
        load_vec(cx, k.sp, onorm[:], onorm_b, W["out_norm"], 8)
        k.op(k.dve, lambda: nc.vector.tensor_scalar(out=rhsD[:], in0=cf[0:NHD, C_SB2:C_SB2 + 8], scalar1=prm[:, 0:1],
                                                    scalar2=None, op0=ALU.mult),
             reads=[cf_b, prm_b], writes=[rhsD_b])
        pD, pD_b = ps()
        k.op(k.pe, lambda: nc.tensor.matmul(pD[:, 0:8], cf[0:NHD, C_SA:C_SA + 128], rhsD[:], start=True, stop=True),
             reads=[cf_b, rhsD_b], writes=[pD_b])
        k.op(k.dve, lambda: nc.vector.tensor_copy(out=Dvec[:], in_=pD[:, 0:8]), reads=[pD_b], writes=[Dvec_b])

        for c in range(NCK):
            ti = c // (T // L)
            cs = slice(c * L, (c + 1) * L)
            k.dma(k.sp, dtc[:], DT[:, cs], [cx.dbuf("DT", ti)], [dtc_b], dtc_b)
            k.dma(k.sp, dac[:], DA[:, cs], [cx.dbuf("DA", ti)], [dac_b], dac_b)
            k.dma(k.sp, xs[:], XS.rearrange("(c p) t -> p c t", p=128)[:, :, cs], [cx.dbuf("XS", ti)], [xs_b], xs_b)
            k.dma(k.sp, zs[:], ZS.rearrange("(c p) t -> p c t", p=128)[:, :, cs], [cx.dbuf("ZS", ti)], [zs_b], zs_b)
            k.dma(k.sp, Bt[:], BT.rearrange("(c p) t -> p c t", p=128)[:, :, cs], [cx.dbuf("BT", ti)], [Bt_b], Bt_b)
            k.dma(k.sp, Ct[:], CT.rearrange("(c p) t -> p c t", p=128)[:, :, cs], [cx.dbuf("CT", ti)], [Ct_b], Ct_b)
            k.op(k.dve, lambda: nc.vector.tensor_tensor_scan(out=acum[:], data0=onesf[0:NHD, 0:L], data1=dac[:],
                                                             initial=0.0, op0=ALU.mult, op1=ALU.add),
                 reads=[onesf_b, dac_b], writes=[acum_b])
            ptm, ptm_b = ps()
            for q in range(4):
                src, src_b = (dtc, dtc_b) if q < 2 else (acum, acum_b)
                sb_ = q % 2
                k.op(k.pe, lambda q=q, src=src, sb_=sb_: nc.tensor.transpose(
                    ptm[:, q * NHD:(q + 1) * NHD], src[:, sb_ * 128:(sb_ + 1) * 128], cf[0:NHD, C_ID:C_ID + NHD]),
                    reads=[src_b, cf_b], writes=[ptm_b])
            k.op(k.dve, lambda: nc.vector.tensor_copy(out=tm[:].rearrange("p q h -> p (q h)"), in_=ptm[:, 0:4 * NHD]),
                 reads=[ptm_b], writes=[tm_b])
            k.op(k.dve, lambda: nc.vector.tensor_scalar(out=dg[:], in0=cf[0:NHD, C_ID:C_ID + NHD],
                                                        scalar1=acum[:, L - 1:L], scalar2=None, op0=ALU.mult),
                 reads=[cf_b, acum_b], writes=[dg_b])
            pal, pal_b = ps()
            k.op(k.pe, lambda: nc.tensor.matmul(pal[:, 0:NHD], onesf[0:NHD, 0:128], dg[:], start=True, stop=True),
                 reads=[onesf_b, dg_b], writes=[pal_b])
            for sb_ in range(2):
                k.op(k.dve, lambda sb_=sb_: nc.vector.tensor_tensor(out=tmpd[:, sb_, :], in0=pal[:, 0:NHD],
                                                                    in1=tm[:, 2 + sb_, :], op=ALU.subtract),
                     reads=[pal_b, tm_b], writes=[tmpd_b])
            k.op(k.act, lambda: nc.scalar.activation(out=tmpd[:], in_=tmpd[:], func=AF.Exp),
                 reads=[tmpd_b], writes=[tmpd_b])
            k.op(k.dve, lambda: nc.vector.tensor_tensor(out=wdt[:], in0=tmpd[:], in1=tm[:, 0:2, :], op=ALU.mult),
                 reads=[tmpd_b, tm_b], writes=[wdt_b])

            def prologue(g):
                gb = gcount[0] % 2
                Gm, Gm_b = Gms[gb]
                xdt, xdt_b = xstms[gb]
                xw, xw_b = xws[gb]
                Btm, Btm_b = Bws[gb]
                pacs = []
                pacs_all[g] = pacs

                def acb(hl):
                    pacb, pacb_b = ACB[hl // 2]
                    hi = hl % 2
                    pacs.append((pacb, pacb_b, hi))
                    h = 4 * g + hl
                    k.op(k.pe, lambda pacb=pacb, h=h, hi=hi: nc.tensor.matmul(
                        pacb[:, hi * L:(hi + 1) * L], selH[:, h, :], acum[:], start=True, stop=True,
                        skip_group_check=True), reads=[selH_b, acum_b], writes=[pacb_b])

                pG, pG_b = ps()
                for sb_ in range(2):
                    k.op(k.pe, lambda sb_=sb_, pG=pG: nc.tensor.matmul(
                        pG[:, sb_ * L:(sb_ + 1) * L], Bt[:, g, sb_ * 128:(sb_ + 1) * 128], Ct[:, g, :],
                        start=True, stop=True, skip_group_check=True), reads=[Bt_b, Ct_b], writes=[pG_b])
                    acb(sb_)
                k.op(k.dve, lambda pG=pG: nc.vector.tensor_tensor(
                    out=Gm[:].rearrange("p s t -> p (s t)"), in0=pG[:, 0:2 * L], in1=cf[:, C_SM:C_SM + 512],
                    op=ALU.mult), reads=[pG_b, cf_b], writes=[Gm_b])
                px, px_b = ps()
                pxb = px[:].bitcast(BF16)
                for sb_ in range(2):
                    for ci in range(2):
                        col = sb_ * 256 + ci * 128
                        k.op(k.pe, lambda sb_=sb_, ci=ci, col=col: nc.tensor.transpose(
                            pxb[:, col:col + 128], xs[:, 2 * g + ci, sb_ * 128:(sb_ + 1) * 128], cb[:, C_ID:C_ID + 128]),
                            reads=[xs_b, cb_b], writes=[px_b])
                acb(2)
                pB, pB_b = ps()
                pBb = pB[:].bitcast(BF16)
                for sb_ in range(2):
                    k.op(k.pe, lambda sb_=sb_: nc.tensor.transpose(
                        pBb[:, sb_ * 128:(sb_ + 1) * 128], Bt[:, g, sb_ * 128:(sb_ + 1) * 128], cb[:, C_ID:C_ID + 128]),
                        reads=[Bt_b, cb_b], writes=[pB_b])
                acb(3)
                xin = pxb[:, 0:512].rearrange("p (s h d) -> p s h d", s=2, h=4)
                k.op(k.dve, lambda: nc.vector.tensor_tensor(
                    out=xdt[:].rearrange("p s (h d) -> p s h d", h=4), in0=xin,
                    in1=tm[:, 0:2, 4 * g:4 * g + 4].unsqueeze(3).to_broadcast([128, 2, 4, 64]), op=ALU.mult),
                    reads=[px_b, tm_b], writes=[xdt_b])
                k.op(k.dve, lambda: nc.vector.tensor_tensor(
                    out=xw[:].rearrange("p s (h d) -> p s h d", h=4), in0=xin,
                    in1=wdt[:, 0:2, 4 * g:4 * g + 4].unsqueeze(3).to_broadcast([128, 2, 4, 64]), op=ALU.mult),
                    reads=[px_b, wdt_b], writes=[xw_b])
                k.op(k.act, lambda: nc.scalar.copy(out=Btm[:].rearrange("p s n -> p (s n)"), in_=pBb[:, 0:256]),
                     reads=[pB_b], writes=[Btm_b])
                return gb

            def heads_front(g, gb):
                Gm, Gm_b = Gms[gb]
                Dd, Dd_b = Dds[gb]
                Mh, Mh_b = Mhs[gb]
                Ea, Ea_b = Eas[gb]
                Ch, Ch_b = Chs[gb]
                pacs = pacs_all[g]
                for hl in range(4):
                    h = 4 * g + hl
                    pacb, pacb_b, hi = pacs[hl]
                    for sb_ in range(2):
                        k.op(k.act, lambda sb_=sb_, pacb=pacb, hl=hl, h=h, hi=hi: nc.scalar.activation(
                            out=Dd[:, hl, sb_, :], in_=pacb[:, hi * L:(hi + 1) * L], func=AF.Relu,
                            bias=tm[:, 2 + sb_, h:h + 1], scale=-1.0), reads=[pacb_b, tm_b], writes=[Dd_b])
                for hp in range(2):
                    pacb, pacb_b, _ = pacs[2 * hp]
                    k.op(k.act, lambda pacb=pacb, hp=hp: nc.scalar.activation(
                        out=Ea[:, 2 * hp:2 * hp + 2, :].rearrange("p h t -> p (h t)"), in_=pacb[:, 0:2 * L],
                        func=AF.Exp), reads=[pacb_b], writes=[Ea_b])
                k.op(k.act, lambda: nc.scalar.activation(out=Dd[:].rearrange("p h s t -> p (h s t)"),
                                                         in_=Dd[:].rearrange("p h s t -> p (h s t)"), func=AF.Exp,
                                                         scale=-1.0),
                     reads=[Dd_b], writes=[Dd_b])
                k.op(k.pool, lambda: nc.gpsimd.tensor_tensor(
                    out=Ch[:], in0=Ea[:], in1=Ct[:, g, :].unsqueeze(1).to_broadcast([128, 4, L]), op=ALU.mult),
                    reads=[Ct_b, Ea_b], writes=[Ch_b])
                k.op(k.pool, lambda: nc.gpsimd.tensor_tensor(
                    out=Mh[:], in0=Dd[:], in1=Gm[:].unsqueeze(1).to_broadcast([128, 4, 2, L]), op=ALU.mult),
                    reads=[Dd_b, Gm_b], writes=[Mh_b])

            def heads_back(g, gb):
                xdt, xdt_b = xstms[gb]
                xw, xw_b = xws[gb]
                Btm, Btm_b = Bws[gb]
                Mh, Mh_b = Mhs[gb]
                Ea, Ea_b = Eas[gb]
                Ch, Ch_b = Chs[gb]
                Yt, Y_b = Ybanks[gb]
                for hl in range(4):
                    h = 4 * g + hl
                    ci, po = hl // 2, (hl % 2) * 64
                    yo = Yt[po:po + 64, ci * L:(ci + 1) * L]
                    xcol = ci * 128 + po
                    k.op(k.pe, lambda yo=yo, hl=hl, xcol=xcol: nc.tensor.matmul(
                        yo, xdt[:, 0, xcol:xcol + 64], Mh[:, hl, 0, :], start=True, stop=False, skip_group_check=True),
                        reads=[xdt_b, Mh_b], writes=[Y_b], inc=False)
                    k.op(k.pe, lambda yo=yo, hl=hl, xcol=xcol: nc.tensor.matmul(
                        yo, xdt[:, 1, xcol:xcol + 64], Mh[:, hl, 1, :], start=False, stop=False, skip_group_check=True),
                        reads=[xdt_b, Mh_b], writes=[Y_b], inc=False)
                    k.op(k.pe, lambda yo=yo, hl=hl, h=h: nc.tensor.matmul(
                        yo, stb[:, h, :], Ch[:, hl, :], start=False, stop=True, skip_group_check=True),
                        reads=[stb_b[h], Ch_b], writes=[Y_b])
                pS, pS_b = ps()
                for sb_ in range(2):
                    k.op(k.pe, lambda sb_=sb_, pS=pS: nc.tensor.matmul(
                        pS[:, 0:256], Btm[:, sb_, :], xw[:, sb_, :], start=(sb_ == 0), stop=(sb_ == 1)),
                        reads=[Btm_b, xw_b], writes=[pS_b], inc=(sb_ == 1))
                sfb = [stf_b[4 * g + x] for x in range(4)]
                sbb = [stb_b[4 * g + x] for x in range(4)]
                k.op(k.dve, lambda: nc.vector.tensor_tensor(
                    out=stf[:, 4 * g:4 * g + 4, :], in0=stf[:, 4 * g:4 * g + 4, :],
                    in1=Ea[:, :, L - 1:L].to_broadcast([128, 4, 64]), op=ALU.mult),
                    reads=sfb + [Ea_b], writes=sfb)
                k.op(k.dve, lambda pS=pS: nc.vector.tensor_tensor(
                    out=stf[:, 4 * g:4 * g + 4, :], in0=stf[:, 4 * g:4 * g + 4, :],
                    in1=pS[:, 0:256].rearrange("p (h d) -> p h d", h=4), op=ALU.add),
                    reads=sfb + [pS_b], writes=sfb)
                k.op(k.act, lambda: nc.scalar.copy(out=stb[:, 4 * g:4 * g + 4, :], in_=stf[:, 4 * g:4 * g + 4, :]),
                     reads=sfb, writes=sbb)
                for ci in range(2):
                    cc = 2 * g + ci
                    k.op(k.dve, lambda ci=ci, cc=cc: nc.vector.scalar_tensor_tensor(
                        out=yv[:, ci, :], in0=xs[:, cc, :], scalar=Dvec[:, cc:cc + 1], in1=Yt[:, ci * L:(ci + 1) * L],
                        op0=ALU.mult, op1=ALU.add), reads=[xs_b, Dvec_b, Y_b], writes=[yv_b])
                k.op(k.dve, lambda: nc.vector.tensor_tensor(out=yv[:], in0=yv[:], in1=zs[:, 2 * g:2 * g + 2, :],
                                                            op=ALU.mult), reads=[yv_b, zs_b], writes=[yv_b])
                k.op(k.act, lambda: nc.scalar.activation(out=sqg[:], in_=yv[:], func=AF.Square),
                     reads=[yv_b], writes=[sqg_b])
                pss, pss_b = ps()
                for ci in range(2):
                    k.op(k.pe, lambda ci=ci, pss=pss: nc.tensor.matmul(pss[:, 0:L], consts["ones"][:], sqg[:, ci, :],
                                                                        start=(ci == 0), stop=(ci == 1)),
                         reads=[sqg_b, consts["ones_b"]], writes=[pss_b], inc=(ci == 1))
                k.op(k.act, lambda pss=pss: nc.scalar.activation(out=rs[:], in_=pss[:, 0:L], func=AF.Sqrt,
                                                                 bias=consts["eps"][:], scale=1.0 / 256),
                     reads=[pss_b, consts["eps_b"]], writes=[rs_b])
                k.op(k.dve, lambda: nc.vector.reciprocal(out=rs[:], in_=rs[:]), reads=[rs_b], writes=[rs_b])
                for ci in range(2):
                    cc = 2 * g + ci
                    k.op(k.dve, lambda ci=ci, cc=cc: nc.vector.scalar_tensor_tensor(
                        out=gn[:, cc, :], in0=yv[:, ci, :], scalar=onorm[:, cc:cc + 1], in1=rs[:],
                        op0=ALU.mult, op1=ALU.mult), reads=[yv_b, onorm_b, rs_b], writes=[gn_b])

            gbs = {}
            pacs_all = {}
            if SSD_PIPE:
                gbs[0] = prologue(0)
                gcount[0] += 1
                heads_front(0, gbs[0])
                for g in range(NG):
                    if g + 1 < NG:
                        gbs[g + 1] = prologue(g + 1)
                        gcount[0] += 1
                        heads_front(g + 1, gbs[g + 1])
                    heads_back(g, gbs[g])
            else:
                for g in range(NG):
                    gbs[g] = prologue(g)
                    gcount[0] += 1
                    heads_front(g, gbs[g])
                    heads_back(g, gbs[g])
            k.dma(k.sp, GN.rearrange("(c p) t -> p c t", p=128)[:, :, cs], gn[:], [gn_b], [cx.dbuf("GN", ti)], gn_b,
                  partial=(c % 2 == 1))
        k.release_stage()


WEIGHT_SPECS = [
    ("mix_norm", [4, 1024]), ("ffn_norm", [4, 1024]),
    ("pool_in", [2, 1024, 512]), ("pool_group", [2, 4, 128, 256]), ("pool_scale", [2, 1024]),
    ("ssd_in", [1, 1024, 3088]), ("ssd_conv_w", [1, 4, 2048]), ("ssd_conv_b", [1, 2048]),
    ("ssd_dt_bias", [1, 16]), ("ssd_a_log", [1, 16]), ("ssd_d", [1, 16]),
    ("ssd_out_norm", [1, 1024]), ("ssd_out", [1, 1024, 1024]),
    ("sb_qkv", [1, 1024, 1536]), ("sb_q_norm", [1, 64]), ("sb_k_norm", [1, 64]), ("sb_out", [1, 512, 1024]),
    ("ffn_gate", [4, 1024, 1408]), ("ffn_up", [4, 1024, 1408]), ("ffn_down", [4, 1408, 1024]),
]


def local_weights(inp, r):
    f = lambda a: np.ascontiguousarray(np.asarray(a, dtype=np.float32))
    cat = np.concatenate
    w = {}
    for n in ("mix_norm", "ffn_norm", "pool_scale", "sb_q_norm", "sb_k_norm"):
        w[n] = f(inp[n])
    pin = np.asarray(inp["pool_in"])
    w["pool_in"] = f(cat([pin[:, :, (2 * g + r) * 128:(2 * g + r + 1) * 128] for g in range(4)], axis=2))
    w["pool_group"] = f(np.asarray(inp["pool_group"])[:, :, r * 128:(r + 1) * 128, :])
    si = np.asarray(inp["ssd_in"])
    w["ssd_in"] = f(cat([si[:, :, r * 1024:(r + 1) * 1024], si[:, :, 2048 + r * 1024:2048 + (r + 1) * 1024],
                         si[:, :, 4096 + r * 512:4096 + (r + 1) * 512], si[:, :, 5120 + r * 512:5120 + (r + 1) * 512],
                         si[:, :, 6144 + r * 16:6144 + (r + 1) * 16]], axis=2))
    cwv = np.asarray(inp["ssd_conv_w"])
    w["ssd_conv_w"] = f(cat([cwv[:, :, r * 1024:(r + 1) * 1024], cwv[:, :, 2048 + r * 512:2048 + (r + 1) * 512],
                             cwv[:, :, 3072 + r * 512:3072 + (r + 1) * 512]], axis=2))
    cbv = np.asarray(inp["ssd_conv_b"])
    w["ssd_conv_b"] = f(cat([cbv[:, r * 1024:(r + 1) * 1024], cbv[:, 2048 + r * 512:2048 + (r + 1) * 512],
                             cbv[:, 3072 + r * 512:3072 + (r + 1) * 512]], axis=1))
    for n in ("ssd_dt_bias", "ssd_a_log", "ssd_d"):
        w[n] = f(np.asarray(inp[n])[:, r * 16:(r + 1) * 16])
    w["ssd_out_norm"] = f(np.asarray(inp["ssd_out_norm"])[:, r * 1024:(r + 1) * 1024])
    w["ssd_out"] = f(np.asarray(inp["ssd_out"])[:, r * 1024:(r + 1) * 1024, :])
    q = np.asarray(inp["sb_qkv"])
    w["sb_qkv"] = f(cat([q[:, :, r * 512:(r + 1) * 512], q[:, :, 1024 + r * 512:1024 + (r + 1) * 512],
                         q[:, :, 2048 + r * 512:2048 + (r + 1) * 512]], axis=2))
    w["sb_out"] = f(np.asarray(inp["sb_out"])[:, r * 512:(r + 1) * 512, :])
    w["ffn_gate"] = f(np.asarray(inp["ffn_gate"])[:, :, r * 1408:(r + 1) * 1408])
    w["ffn_up"] = f(np.asarray(inp["ffn_up"])[:, :, r * 1408:(r + 1) * 1408])
    w["ffn_down"] = f(np.asarray(inp["ffn_down"])[:, r * 1408:(r + 1) * 1408, :])
    return w


def build_program(S, layers, do_mixer=True, do_ffn=True):
    nc = bass.Bass("TRN2", target_bir_lowering=False, num_devices=8)
    cx = Ctx(nc, S)
    k = cx.k
    NT = S // T
    xT = nc.dram_tensor("xT", [NT, D, T], F32, kind="ExternalInput").ap()
    yT = nc.dram_tensor("yT", [NT, D, T], F32, kind="ExternalOutput").ap()
    Wt = {n: nc.dram_tensor(n, shp, F32, kind="ExternalInput").ap() for n, shp in WEIGHT_SPECS}
    cx.Q = nc.dram_tensor("resQ", [NT, D, T], F32, kind="Internal").ap()
    R = [nc.dram_tensor(f"resR{i}", [NT, D, T], F32, kind="Internal").ap() for i in range(2)]
    ones = nc.alloc_sbuf_tensor("c_ones", [128, 128], BF16)
    ones_b = k.buf("ones")
    k.op(k.pool, lambda: nc.gpsimd.memset(ones[:], 1.0), reads=[], writes=[ones_b])
    epst = nc.alloc_sbuf_tensor("c_eps", [128, 1], F32)
    eps_b = k.buf("eps")
    k.op(k.pool, lambda: nc.gpsimd.memset(epst[:], EPS), reads=[], writes=[eps_b])
    onef = nc.alloc_sbuf_tensor("c_onef", [128, 1], F32)
    onef_b = k.buf("onef")
    k.op(k.pool, lambda: nc.gpsimd.memset(onef[:], 1.0), reads=[], writes=[onef_b])
    cst = nc.dram_tensor("cst", [128, C_TOT], F32, kind="ExternalInput").ap()
    cbf = nc.alloc_sbuf_tensor("c_cbf", [128, C_TOT], BF16)
    cf = nc.alloc_sbuf_tensor("c_cf", [128, C_F32W], F32)
    cbf_b, cf_b = k.buf("cbf"), k.buf("cf")
    k.dma(k.pool, cbf[:], cst[:, :], [], [cbf_b], cbf_b)
    k.dma(k.sp, cf[:], cst[:, 0:C_F32W], [], [cf_b], cf_b)
    k.stage_bufs = []
    consts = {"ones": ones, "ones_b": ones_b, "eps": epst, "eps_b": eps_b, "onef": onef, "onef_b": onef_b,
              "cbf": cbf, "cbf_b": cbf_b, "cf": cf, "cf_b": cf_b}
    scr = {}

    def scratch(name, shape, dt=BF16):
        if name not in scr:
            scr[name] = nc.dram_tensor("scr_" + name, shape, dt, kind="Internal").ap()
        return scr[name]

    cur, cur_name = xT, "xT"
    nstage = [0]

    def nxt():
        nstage[0] += 1
        return R[nstage[0] % 2], f"R{nstage[0]}"

    for idx, li in enumerate(layers):
        kind, j = li % 3, li // 3
        if do_mixer:
            mo, mo_name = nxt()
            if kind == 0:
                W = {"in": Wt["pool_in"][j], "group": Wt["pool_group"][j], "scale": Wt["pool_scale"][j],
                     "norm": Wt["mix_norm"][li]}
                stage_pool(cx, li, cur, cur_name, mo, mo_name, W, consts)
            elif kind == 2:
                QT, KT, V, OT = (scratch("QT", [512, S]), scratch("KT", [512, S]), scratch("V", [S, 512]),
                                 scratch("OT", [512, S]))
                W = {"qkv": Wt["sb_qkv"][j], "qn": Wt["sb_q_norm"][j], "kn": Wt["sb_k_norm"][j],
                     "norm": Wt["mix_norm"][li]}
                stage_sb_qkv(cx, li, cur, cur_name, W, consts, QT, KT, V)
                stage_sb_attn(cx, li, consts, QT, KT, V, OT)
                stage_outproj(cx, f"sb{li}", cur, cur_name, mo, mo_name, OT, "OT", Wt["sb_out"][j], 4)
            else:
                ZS, XS, GN = scratch("ZS", [1024, S]), scratch("XS", [1024, S]), scratch("GN", [1024, S])
                BT, CT = scratch("BT", [512, S]), scratch("CT", [512, S])
                DT, DA = scratch("DT", [NHD, S], F32), scratch("DA", [NHD, S], F32)
                W = {"in": Wt["ssd_in"][j], "conv_w": Wt["ssd_conv_w"][j], "conv_b": Wt["ssd_conv_b"][j],
                     "dt_bias": Wt["ssd_dt_bias"][j], "a_log": Wt["ssd_a_log"][j], "d": Wt["ssd_d"][j],
                     "out_norm": Wt["ssd_out_norm"][j], "norm": Wt["mix_norm"][li]}
                stage_ssd_a1(cx, li, cur, cur_name, W, consts, ZS, DT, DA)
                stage_ssd_a2(cx, li, cur, cur_name, W, consts, XS, BT, CT)
                stage_ssd_scan(cx, li, W, consts, ZS, DT, DA, XS, BT, CT, GN)
                stage_outproj(cx, f"ssd{li}", cur, cur_name, mo, mo_name, GN, "GN", Wt["ssd_out"][j], 8)
            cur, cur_name = mo, mo_name
        if do_ffn:
            Xo, oname = nxt()
            Wf = {"gate": Wt["ffn_gate"][li], "up": Wt["ffn_up"][li], "down": Wt["ffn_down"][li],
                  "norm": Wt["ffn_norm"][li]}
            stage_ffn(cx, li, cur, cur_name, Xo, oname, Wf, consts)
            cur, cur_name = Xo, oname
    cp_b = k.buf("outcopy")
    for i in range(NT):
        k.dma(k.sp, yT[i], cur[i], [cx.dbuf(cur_name, i)], [cx.dbuf("yT", i)], cp_b, is_out=True)
    k.finish()
    return nc


def make_in_maps(inputs, S):
    x = np.asarray(inputs["x"], dtype=np.float32)[:, :S]
    Bn = x.shape[0]
    NT = S // T
    cst = make_consts()
    lw = [local_weights(inputs, r) for r in range(2)]
    in_maps = []
    for c in range(8):
        b, r = (c // 2) % Bn, c % 2
        xt = np.ascontiguousarray(x[b].T.reshape(D, NT, T).transpose(1, 0, 2))
        m = {"xT": xt, "cst": cst}
        m.update(lw[r])
        in_maps.append(m)
    return in_maps


def untile(yt):
    NT = yt.shape[0]
    return np.ascontiguousarray(yt.transpose(1, 0, 2).reshape(D, NT * T).T)


_PROG_CACHE = {}


def kernel(**inputs):
    x = np.asarray(inputs["x"], dtype=np.float32)
    Bn, S, _ = x.shape
    layers = list(range(DEPTH))
    key = (S, tuple(layers))
    if key not in _PROG_CACHE:
        _PROG_CACHE[key] = build_program(S, layers)
    nc = _PROG_CACHE[key]
    in_maps = make_in_maps(inputs, S)
    res = run_bass_kernel_spmd(nc, in_maps, core_ids=list(range(8)))
    out = np.empty_like(x)
    for b in range(Bn):
        out[b] = untile(res.results[2 * b]["yT"])
    return out
```

```python
import numpy as np
from contextlib import ExitStack
import concourse.bass as bass
import concourse.mybir as mybir
from concourse.bass_utils import run_bass_kernel_spmd

F32 = mybir.dt.float32
BF16 = mybir.dt.bfloat16
AF = mybir.ActivationFunctionType
ALU = mybir.AluOpType

D = 1024
NCH = 8
T = 512
EPS = 1e-6
FFN_H = 2816
FFN_HC = 22
SSD_DI = 2048
SSD_IN = 6176
SEQ = 8192
DEPTH = 4
FH = 11
PCH = 4
NG = 4
NHD = 16
SBP = 4
PAIRS = [[0, 1], [2, 3], [4, 5], [6, 7]]
SSD_PIPE = True


class Buf:
    __slots__ = ("name", "w", "r", "dsem", "dcnt", "q")

    def __init__(self, name):
        self.name = name
        self.w = {}
        self.r = {}
        self.dsem = None
        self.dcnt = 0
        self.q = None


class EngW:
    def __init__(self, nc, eng, name):
        self.eng = eng
        self.name = name
        self.sem = nc.alloc_semaphore("tl_" + name)
        self.cnt = 0
        self.seen = {}


class K:
    def __init__(self, nc):
        self.nc = nc
        self.pe = EngW(nc, nc.tensor, "pe")
        self.act = EngW(nc, nc.scalar, "act")
        self.dve = EngW(nc, nc.vector, "dve")
        self.pool = EngW(nc, nc.gpsimd, "pool")
        self.sp = EngW(nc, nc.sync, "sp")
        self.out_events = []
        self.nbuf = 0
        self.nsem = 0
        self.cc_sem = None
        self.cc_cnt = 0
        self.free_sems = {}
        self.stage_bufs = []

    def barrier(self):
        engs = [self.pe, self.act, self.dve, self.pool, self.sp]
        evs = [(e.sem, e.cnt) for e in engs if e.cnt > 0]
        for b in self.stage_bufs:
            if b.dsem is not None and b.dcnt > 0:
                evs.append((b.dsem, b.dcnt))
        for e in engs:
            self._wait(e, evs)

    def release_stage(self):
        self.barrier()
        for b in self.stage_bufs:
            if b.dsem is not None:
                self.free_sems.setdefault(b.q.name, []).append((b.dsem, b.dcnt))
                b.dsem = None
        self.stage_bufs = []

    def buf(self, name=None):
        self.nbuf += 1
        return Buf(name or f"b{self.nbuf}")

    def _wait(self, E, events, own=False):
        for sem, cnt in events:
            if sem is E.sem and (not own or E is self.pe):
                continue
            key = id(sem)
            if E.seen.get(key, 0) < cnt:
                E.eng.wait_ge(sem, cnt)
                E.seen[key] = cnt

    def op(self, E, fn, reads=(), writes=(), inc=True):
        for b in reads:
            self._wait(E, list(b.w.values()), own=True)
        for b in writes:
            self._wait(E, list(b.w.values()), own=True)
            self._wait(E, list(b.r.values()))
        ins = fn()
        if inc:
            E.cnt += 1
            ins.then_inc(E.sem, 1)
            seq = E.cnt
        else:
            seq = E.cnt + 1
        ev = (E.sem, seq)
        k = id(E.sem)
        for b in reads:
            b.r[k] = ev
        for b in writes:
            b.w = {k: ev}
            b.r = {}
        return ins

    def dma(self, Q, out, in_, reads, writes, sb, partial=False, is_out=False, **kw):
        for b in reads:
            self._wait(Q, list(b.w.values()))
        for b in writes:
            if not partial:
                self._wait(Q, list(b.w.values()))
            self._wait(Q, list(b.r.values()))
        if sb.dsem is None:
            fl = self.free_sems.get(Q.name, [])
            if fl:
                sem, cnt = fl.pop()
                if Q.seen.get(id(sem), 0) < cnt:
                    Q.eng.wait_ge(sem, cnt)
                    Q.seen[id(sem)] = cnt
                sb.dsem, sb.dcnt = sem, cnt
            else:
                self.nsem += 1
                sb.dsem = self.nc.alloc_semaphore(f"dq{self.nsem}")
            sb.q = Q
            self.stage_bufs.append(sb)
        assert sb.q is Q
        sb.dcnt += 16
        Q.eng.dma_start(out=out, in_=in_, **kw).then_inc(sb.dsem, 16)
        ev = (sb.dsem, sb.dcnt)
        k = id(sb.dsem)
        for b in reads:
            b.r[k] = ev
        for b in writes:
            if partial:
                b.w[k] = ev
            else:
                b.w = {k: ev}
            b.r = {}
        if is_out:
            self.out_events.append(ev)

    def allreduce(self, src, dst, reads, writes):
        Q = self.pool
        for b in reads:
            self._wait(Q, list(b.w.values()))
        for b in writes:
            self._wait(Q, list(b.w.values()))
            self._wait(Q, list(b.r.values()))
        if self.cc_sem is None:
            self.cc_sem = self.nc.alloc_semaphore("cc_sem")
            self.cc_cnt = 0
        self.cc_cnt += 1
        self.nc.gpsimd.collective_compute("AllReduce", ALU.add, replica_groups=PAIRS, ins=[src],
                                          outs=[dst]).then_inc(self.cc_sem, 1)
        ev = (self.cc_sem, self.cc_cnt)
        kk = id(self.cc_sem)
        for b in reads:
            b.r[kk] = ev
        for b in writes:
            b.w = {kk: ev}
            b.r = {}

    def finish(self):
        last = {}
        for sem, cnt in self.out_events:
            k = id(sem)
            if k not in last or last[k][1] < cnt:
                last[k] = (sem, cnt)
        self._wait(self.sp, list(last.values()))


class Ctx:
    def __init__(self, nc, S):
        self.nc = nc
        self.S = S
        self.NT = S // T
        self.k = K(nc)
        self.dbufs = {}
        self.ps = []
        self.ps2 = []
        for i in range(4):
            t = nc.alloc_psum_tensor(f"psd{i}", [128, 1024], F32)
            self.ps2.append((t, self.k.buf(f"psd{i}")))
            for h in range(2):
                self.ps.append((t[:, h * 512:(h + 1) * 512], self.k.buf(f"ps{2 * i + h}")))
        self._psi = 0

    def dbuf(self, name, i):
        key = (name, i)
        if key not in self.dbufs:
            self.dbufs[key] = self.k.buf(f"{name}_{i}")
        return self.dbufs[key]

    def psum(self):
        p = self.ps[self._psi % 8]
        self._psi += 1
        return p


def xtile(X, i):
    return X[i].rearrange("(c p) t -> p c t", p=128)


def emit_norm(cx, xt, xt_b, gain, ht, ht_b, sq, sq_b, rstd, rstd_b, consts):
    nc, k = cx.nc, cx.k
    gain, gain_b = gain
    ones_bf, ones_b = consts["ones"], consts["ones_b"]
    k.op(k.act, lambda: nc.scalar.activation(out=sq[:], in_=xt[:], func=AF.Square),
         reads=[xt_b], writes=[sq_b])
    ps, ps_b = cx.psum()
    for c in range(NCH):
        k.op(k.pe, lambda c=c: nc.tensor.matmul(ps[:], ones_bf[:], sq[:, c, :],
                                                 start=(c == 0), stop=(c == NCH - 1)),
             reads=[sq_b, ones_b], writes=[ps_b], inc=(c == NCH - 1))
    k.op(k.act, lambda: nc.scalar.activation(out=rstd[:], in_=ps[:], func=AF.Sqrt, bias=consts["eps"][:],
                                             scale=1.0 / D),
         reads=[ps_b, consts["eps_b"]], writes=[rstd_b])
    k.op(k.dve, lambda: nc.vector.reciprocal(out=rstd[:], in_=rstd[:]), reads=[rstd_b], writes=[rstd_b])
    for c in range(NCH):
        k.op(k.dve, lambda c=c: nc.vector.scalar_tensor_tensor(
            out=ht[:, c, :], in0=xt[:, c, :], scalar=gain[:, c:c + 1], in1=rstd[:],
            op0=ALU.mult, op1=ALU.mult),
            reads=[xt_b, rstd_b, gain_b], writes=[ht_b])


def load_vec(cx, Q, dst, dst_b, src_1d, n, partial=False):
    cx.k.dma(Q, dst, src_1d.rearrange("(c p) -> p c", p=128), reads=[], writes=[dst_b], sb=dst_b,
             partial=partial, allow_slow_non_contiguous=True)


def store_reduce(cx, xt, xt_b, i, Xo, oname):
    k = cx.k
    k.dma(k.sp, xtile(cx.Q, i), xt[:], [xt_b], [cx.dbuf("Q", i)], xt_b)
    k.allreduce(cx.Q[i], Xo[i], [cx.dbuf("Q", i)], [cx.dbuf(oname, i)])


def stage_ffn(cx, li, Xi, iname, Xo, oname, W, consts):
    nc, k = cx.nc, cx.k
    nhc = FH
    H = nhc * 128
    with (
        nc.sbuf_tensor(f"wg{li}", [128, NCH, H], BF16) as wg,
        nc.sbuf_tensor(f"wu{li}", [128, NCH, H], BF16) as wu,
        nc.sbuf_tensor(f"wd{li}", [128, nhc, D], BF16) as wd,
        nc.sbuf_tensor(f"fg{li}", [128, NCH], F32) as gain,
        nc.sbuf_tensor(f"fx0{li}", [128, NCH, T], F32) as xt0,
        nc.sbuf_tensor(f"fx1{li}", [128, NCH, T], F32) as xt1,
        nc.sbuf_tensor(f"fh{li}", [128, NCH, T], BF16) as ht,
        nc.sbuf_tensor(f"fs{li}", [128, NCH, T], BF16) as sq,
        nc.sbuf_tensor(f"fr{li}", [128, T], F32) as rstd,
        nc.sbuf_tensor(f"fa{li}", [128, nhc, T], BF16) as act,
        nc.sbuf_tensor(f"fsg0{li}", [128, T], F32) as sg0,
        nc.sbuf_tensor(f"fsg1{li}", [128, T], F32) as sg1,
    ):
        wg_b, wu_b, wd_b, gain_b = k.buf("wg"), k.buf("wu"), k.buf("wd"), k.buf("fgain")
        xts = [(xt0, k.buf("fx0")), (xt1, k.buf("fx1"))]
        ht_b, sq_b, rstd_b, act_b = k.buf("fh"), k.buf("fs"), k.buf("fr"), k.buf("fa")
        sgs = [(sg0, k.buf("fsg0")), (sg1, k.buf("fsg1"))]
        for c in range(NCH):
            k.dma(k.pool, wg[:, c, :], W["gate"][c * 128:(c + 1) * 128, :], [], [wg_b], wg_b, partial=(c > 0))
            k.dma(k.pool, wu[:, c, :], W["up"][c * 128:(c + 1) * 128, :], [], [wu_b], wu_b, partial=(c > 0))
        for j in range(nhc):
            k.dma(k.pool, wd[:, j, :], W["down"][j * 128:(j + 1) * 128, :], [], [wd_b], wd_b, partial=(j > 0))
        load_vec(cx, k.sp, gain[:], gain_b, W["norm"], NCH)

        def load(i):
            xt, xt_b = xts[i % 2]
            k.dma(k.sp, xt[:], xtile(Xi, i), [cx.dbuf(iname, i)], [xt_b], xt_b)

        load(0)
        for i in range(cx.NT):
            xt, xt_b = xts[i % 2]
            if i + 1 < cx.NT:
                load(i + 1)
            emit_norm(cx, xt, xt_b, (gain, gain_b), ht, ht_b, sq, sq_b, rstd, rstd_b, consts)
            for j in range(nhc):
                pg, pg_b = cx.psum()
                pu, pu_b = cx.psum()
                for c in range(NCH):
                    k.op(k.pe, lambda c=c: nc.tensor.matmul(pg[:], wg[:, c, j * 128:(j + 1) * 128], ht[:, c, :],
                                                             start=(c == 0), stop=(c == NCH - 1)),
                         reads=[wg_b, ht_b], writes=[pg_b], inc=(c == NCH - 1))
                for c in range(NCH):
                    k.op(k.pe, lambda c=c: nc.tensor.matmul(pu[:], wu[:, c, j * 128:(j + 1) * 128], ht[:, c, :],
                                                             start=(c == 0), stop=(c == NCH - 1)),
                         reads=[wu_b, ht_b], writes=[pu_b], inc=(c == NCH - 1))
                sg, sg_b = sgs[j % 2]
                k.op(k.act, lambda: nc.scalar.activation(out=sg[:], in_=pg[:], func=AF.Silu),
                     reads=[pg_b], writes=[sg_b])
                k.op(k.dve, lambda: nc.vector.tensor_tensor(out=act[:, j, :], in0=sg[:], in1=pu[:], op=ALU.mult),
                     reads=[sg_b, pu_b], writes=[act_b])
            for m in range(NCH):
                po, po_b = cx.psum()
                for j in range(nhc):
                    k.op(k.pe, lambda j=j: nc.tensor.matmul(po[:], wd[:, j, m * 128:(m + 1) * 128], act[:, j, :],
                                                             start=(j == 0), stop=(j == nhc - 1)),
                         reads=[wd_b, act_b], writes=[po_b], inc=(j == nhc - 1))
                k.op(k.dve, lambda: nc.vector.scalar_tensor_tensor(out=xt[:, m, :], in0=xt[:, m, :], scalar=0.5,
                                                                   in1=po[:], op0=ALU.mult, op1=ALU.add),
                     reads=[po_b, xt_b], writes=[xt_b])
            store_reduce(cx, xt, xt_b, i, Xo, oname)
        k.release_stage()


def stage_pool(cx, li, Xi, iname, Xo, oname, W, consts):
    nc, k = cx.nc, cx.k
    E = 16
    with (
        nc.sbuf_tensor(f"pw{li}", [128, NCH, PCH * 128], BF16) as win,
        nc.sbuf_tensor(f"pg{li}", [128, 4, 256], BF16) as wgr,
        nc.sbuf_tensor(f"pgn{li}", [128, NCH], F32) as gain,
        nc.sbuf_tensor(f"psc{li}", [128, NCH], F32) as psc,
        nc.sbuf_tensor(f"pic{li}", [128, 4, E], F32) as invc,
        nc.sbuf_tensor(f"px0{li}", [128, NCH, T], F32) as xt0,
        nc.sbuf_tensor(f"px1{li}", [128, NCH, T], F32) as xt1,
        nc.sbuf_tensor(f"ph{li}", [128, NCH, T], BF16) as ht,
        nc.sbuf_tensor(f"pq{li}", [128, NCH, T], BF16) as sq,
        nc.sbuf_tensor(f"pr{li}", [128, T], F32) as rstd,
        nc.sbuf_tensor(f"pu{li}", [128, PCH, E + T], F32) as u,
        nc.sbuf_tensor(f"pa{li}", [128, PCH, E + T], F32) as A,
        nc.sbuf_tensor(f"pb{li}", [128, PCH, E + T], F32) as B,
        nc.sbuf_tensor(f"pp{li}", [128, PCH, T], BF16) as p,
        nc.sbuf_tensor(f"ptmp{li}", [128, E], F32) as tmp,
    ):
        win_b, wgr_b, gain_b, psc_b, invc_b = (k.buf("pwin"), k.buf("pwgr"), k.buf("pgain"), k.buf("psc"),
                                               k.buf("invc"))
        xts = [(xt0, k.buf("px0")), (xt1, k.buf("px1"))]
        ht_b, sq_b, rstd_b = k.buf("ph"), k.buf("pq"), k.buf("pr")
        u_b, A_b, B_b, p_b, tmp_b = k.buf("pu"), k.buf("pA"), k.buf("pB"), k.buf("pp"), k.buf("ptmp")
        for c in range(NCH):
            k.dma(k.pool, win[:, c, :], W["in"][c * 128:(c + 1) * 128, :], [], [win_b], win_b, partial=(c > 0))
        for g in range(4):
            k.dma(k.pool, wgr[:, g, :], W["group"][g, :, :], [], [wgr_b], wgr_b, partial=(g > 0))
        load_vec(cx, k.sp, gain[:], gain_b, W["norm"], NCH)
        load_vec(cx, k.sp, psc[:], psc_b, W["scale"], NCH)
        for g in range(4):
            w = 2 ** (g + 1)
            for t in range(E):
                k.op(k.pool, lambda g=g, t=t, w=w: nc.gpsimd.memset(invc[:, g, t:t + 1], 1.0 / min(t + 1, w)),
                     reads=[], writes=[invc_b])
        k.op(k.pool, lambda: nc.gpsimd.memset(u[:, :, 0:E], 0.0), reads=[], writes=[u_b])
        k.op(k.pool, lambda: nc.gpsimd.memset(A[:, :, 0:E], 0.0), reads=[], writes=[A_b])
        k.op(k.pool, lambda: nc.gpsimd.memset(B[:, :, 0:E], 0.0), reads=[], writes=[B_b])

        def load(i):
            xt, xt_b = xts[i % 2]
            k.dma(k.sp, xt[:], xtile(Xi, i), [cx.dbuf(iname, i)], [xt_b], xt_b)

        load(0)
        for i in range(cx.NT):
            xt, xt_b = xts[i % 2]
            if i + 1 < cx.NT:
                load(i + 1)
            emit_norm(cx, xt, xt_b, (gain, gain_b), ht, ht_b, sq, sq_b, rstd, rstd_b, consts)
            if i > 0:
                k.op(k.pool, lambda: nc.gpsimd.tensor_copy(out=u[:, :, 0:E], in_=u[:, :, T:T + E]),
                     reads=[u_b], writes=[u_b])
            for m in range(PCH):
                pu_, pu_b = cx.psum()
                for c in range(NCH):
                    k.op(k.pe, lambda c=c: nc.tensor.matmul(pu_[:], win[:, c, m * 128:(m + 1) * 128], ht[:, c, :],
                                                             start=(c == 0), stop=(c == NCH - 1)),
                         reads=[win_b, ht_b], writes=[pu_b], inc=(c == NCH - 1))
                k.op(k.act, lambda: nc.scalar.copy(out=u[:, m, E:E + T], in_=pu_[:]),
                     reads=[pu_b], writes=[u_b])
            k.op(k.act, lambda: nc.scalar.activation(out=xt[:], in_=xt[:], func=AF.Copy, scale=0.5), reads=[xt_b], writes=[xt_b])
            k.op(k.dve, lambda: nc.vector.tensor_tensor(out=A[:, :, 1:E + T], in0=u[:, :, 1:E + T],
                                                        in1=u[:, :, 0:E + T - 1], op=ALU.add),
                 reads=[u_b], writes=[A_b])
            k.op(k.pool, lambda: nc.gpsimd.tensor_tensor(out=B[:, 1:4, 3:E + T], in0=A[:, 1:4, 3:E + T],
                                                         in1=A[:, 1:4, 1:E + T - 2], op=ALU.add),
                 reads=[A_b], writes=[B_b])
            k.op(k.dve, lambda: nc.vector.tensor_tensor(out=A[:, 2:4, 7:E + T], in0=B[:, 2:4, 7:E + T],
                                                        in1=B[:, 2:4, 3:E + T - 4], op=ALU.add),
                 reads=[B_b], writes=[A_b])
            k.op(k.pool, lambda: nc.gpsimd.tensor_tensor(out=B[:, 3:4, 15:E + T], in0=A[:, 3:4, 15:E + T],
                                                         in1=A[:, 3:4, 7:E + T - 8], op=ALU.add),
                 reads=[A_b], writes=[B_b])
            srcs = [(A, A_b), (B, B_b), (A, A_b), (B, B_b)]
            for g in range(4):
                s_, s_b = srcs[g]
                w = 2 ** (g + 1)
                k.op(k.dve, lambda g=g, s_=s_, w=w: nc.vector.scalar_tensor_tensor(
                    out=p[:, g, :], in0=s_[:, g, E:E + T], scalar=1.0 / w,
                    in1=u[:, g, E:E + T], op0=ALU.mult, op1=ALU.subtract),
                    reads=[s_b, u_b], writes=[p_b])
                if i == 0:
                    k.op(k.dve, lambda g=g, s_=s_: nc.vector.tensor_tensor(
                        out=tmp[:], in0=s_[:, g, E:2 * E], in1=invc[:, g, :], op=ALU.mult),
                        reads=[s_b, invc_b], writes=[tmp_b])
                    k.op(k.dve, lambda g=g: nc.vector.tensor_tensor(
                        out=p[:, g, 0:E], in0=tmp[:], in1=u[:, g, E:2 * E], op=ALU.subtract),
                        reads=[tmp_b, u_b], writes=[p_b])
            for g in range(4):
                for mo in range(2):
                    cc = 2 * g + mo
                    py, py_b = cx.psum()
                    k.op(k.pe, lambda: nc.tensor.matmul(py[:], wgr[:, g, mo * 128:(mo + 1) * 128], p[:, g, :],
                                                        start=True, stop=True),
                         reads=[wgr_b, p_b], writes=[py_b])
                    k.op(k.dve, lambda cc=cc: nc.vector.scalar_tensor_tensor(
                        out=xt[:, cc, :], in0=py[:], scalar=psc[:, cc:cc + 1], in1=xt[:, cc, :],
                        op0=ALU.mult, op1=ALU.add),
                        reads=[py_b, psc_b, xt_b], writes=[xt_b])
            store_reduce(cx, xt, xt_b, i, Xo, oname)
        k.release_stage()


C_ID, C_SM, C_SA, C_SB2, C_TRI, C_OMT, C_BO, C_MB = 0, 128, 640, 768, 784, 912, 1040, 1168
C_F32W = 784
C_TOT = 1168 + 2048


def make_consts():
    c = np.zeros((128, C_TOT), np.float32)
    j = np.arange(128)
    c[:, C_ID:C_ID + 128] = np.eye(128)
    t256 = np.arange(256)
    c[:, C_SM:C_SM + 256] = (j[:, None] <= t256[None, :])
    c[:, C_SM + 256:C_SM + 512] = (128 + j[:, None] <= t256[None, :])
    k32 = np.arange(32)
    c[:32, C_SA:C_SA + 128] = ((k32[:, None] % 2) == (j[None, :] // 64))
    c[:32, C_SB2:C_SB2 + 16] = ((k32[:, None] // 2) == np.arange(16)[None, :])
    c[:, C_TRI:C_TRI + 128] = (j[:, None] >= j[None, :])
    c[:, C_OMT:C_OMT + 128] = (j[:, None] < j[None, :])
    c[:, C_BO:C_BO + 128] = ((j[:, None] // 64) == (j[None, :] // 64))
    t512 = np.arange(512)
    for o in range(4):
        valid = (128 * o + j[:, None]) < t512[None, :]
        c[:, C_MB + 512 * o:C_MB + 512 * (o + 1)] = np.where(valid, 0.0, -30000.0)
    return c


def stage_outproj(cx, tag, Xi, iname, Xo, oname, G, gname, Wd, KC):
    nc, k = cx.nc, cx.k
    with (
        nc.sbuf_tensor(f"ow{tag}", [128, KC, D], BF16) as w,
        nc.sbuf_tensor(f"ox0{tag}", [128, NCH, T], F32) as xt0,
        nc.sbuf_tensor(f"ox1{tag}", [128, NCH, T], F32) as xt1,
        nc.sbuf_tensor(f"og0{tag}", [128, KC, T], BF16) as g0,
        nc.sbuf_tensor(f"og1{tag}", [128, KC, T], BF16) as g1,
    ):
        w_b = k.buf("ow")
        xts = [(xt0, k.buf("ox0")), (xt1, k.buf("ox1"))]
        gs = [(g0, k.buf("og0")), (g1, k.buf("og1"))]
        for c in range(KC):
            k.dma(k.pool, w[:, c, :], Wd[c * 128:(c + 1) * 128, :], [], [w_b], w_b, partial=(c > 0))

        def load(i):
            xt, xt_b = xts[i % 2]
            gt, gt_b = gs[i % 2]
            k.dma(k.sp, xt[:], xtile(Xi, i), [cx.dbuf(iname, i)], [xt_b], xt_b)
            k.dma(k.sp, gt[:], G.rearrange("(c p) t -> p c t", p=128)[:, :, i * T:(i + 1) * T],
                  [cx.dbuf(gname, i)], [gt_b], gt_b)

        load(0)
        for i in range(cx.NT):
            xt, xt_b = xts[i % 2]
            gt, gt_b = gs[i % 2]
            if i + 1 < cx.NT:
                load(i + 1)
            for m in range(NCH):
                po, po_b = cx.psum()
                for c in range(KC):
                    k.op(k.pe, lambda c=c: nc.tensor.matmul(po[:], w[:, c, m * 128:(m + 1) * 128], gt[:, c, :],
                                                             start=(c == 0), stop=(c == KC - 1)),
                         reads=[w_b, gt_b], writes=[po_b], inc=(c == KC - 1))
                k.op(k.dve, lambda: nc.vector.scalar_tensor_tensor(out=xt[:, m, :], in0=xt[:, m, :], scalar=0.5,
                                                                   in1=po[:], op0=ALU.mult, op1=ALU.add),
                     reads=[po_b, xt_b], writes=[xt_b])
            store_reduce(cx, xt, xt_b, i, Xo, oname)
        k.release_stage()


def stage_sb_qkv(cx, li, Xi, iname, W, consts, QT, KT, V):
    nc, k = cx.nc, cx.k
    cb, cb_b = consts["cbf"], consts["cbf_b"]
    with (
        nc.sbuf_tensor(f"sw{li}", [128, NCH, 3 * 512], BF16) as w,
        nc.sbuf_tensor(f"sgn{li}", [128, NCH], F32) as gain,
        nc.sbuf_tensor(f"sgq{li}", [128, 2], F32) as gqk,
        nc.sbuf_tensor(f"sx0{li}", [128, NCH, T], F32) as xt0,
        nc.sbuf_tensor(f"sx1{li}", [128, NCH, T], F32) as xt1,
        nc.sbuf_tensor(f"sh{li}", [128, NCH, T], BF16) as ht,
        nc.sbuf_tensor(f"ssq{li}", [128, NCH, T], BF16) as sq,
        nc.sbuf_tensor(f"sr{li}", [128, T], F32) as rstd,
        nc.sbuf_tensor(f"sqt{li}", [128, SBP, T], BF16) as qt,
        nc.sbuf_tensor(f"skt{li}", [128, SBP, T], BF16) as kt,
        nc.sbuf_tensor(f"svt{li}", [128, 4, 512], BF16) as vt,
        nc.sbuf_tensor(f"ssc0{li}", [128, T], BF16) as sqc0,
        nc.sbuf_tensor(f"ssc1{li}", [128, T], BF16) as sqc1,
        nc.sbuf_tensor(f"srs0{li}", [128, T], F32) as rs0,
        nc.sbuf_tensor(f"srs1{li}", [128, T], F32) as rs1,
    ):
        w_b, gain_b, gqk_b = k.buf("sw"), k.buf("sgn"), k.buf("sgq")
        xts = [(xt0, k.buf("sx0")), (xt1, k.buf("sx1"))]
        ht_b, sq_b, rstd_b = k.buf("sh"), k.buf("ssq"), k.buf("sr")
        qt_b, kt_b, vt_b = k.buf("sqt"), k.buf("skt"), k.buf("svt")
        sqcs = [(sqc0, k.buf("ssc0")), (sqc1, k.buf("ssc1"))]
        rss = [(rs0, k.buf("srs0")), (rs1, k.buf("srs1"))]
        for c in range(NCH):
            k.dma(k.pool, w[:, c, :], W["qkv"][c * 128:(c + 1) * 128, :], [], [w_b], w_b, partial=(c > 0))
        load_vec(cx, k.sp, gain[:], gain_b, W["norm"], NCH)
        first = True
        for col, nm in ((0, "qn"), (1, "kn")):
            for hh in range(2):
                k.dma(k.sp, gqk[hh * 64:(hh + 1) * 64, col:col + 1], W[nm].rearrange("(p o) -> p o", o=1), [],
                      [gqk_b], gqk_b, partial=(not first), allow_slow_non_contiguous=True)
                first = False
        k.op(k.dve, lambda: nc.vector.tensor_scalar(out=gqk[:, 0:1], in0=gqk[:, 0:1], scalar1=0.125, scalar2=None,
                                                    op0=ALU.mult), reads=[gqk_b], writes=[gqk_b])

        def load(i):
            xt, xt_b = xts[i % 2]
            k.dma(k.sp, xt[:], xtile(Xi, i), [cx.dbuf(iname, i)], [xt_b], xt_b)

        load(0)
        n = 0
        for i in range(cx.NT):
            xt, xt_b = xts[i % 2]
            if i + 1 < cx.NT:
                load(i + 1)
            emit_norm(cx, xt, xt_b, (gain, gain_b), ht, ht_b, sq, sq_b, rstd, rstd_b, consts)
            for which, (dst, dst_b) in enumerate(((qt, qt_b), (kt, kt_b))):
                for oc in range(SBP):
                    c0 = which * 512 + oc * 128
                    pq, pq_b = cx.psum()
                    for c in range(NCH):
                        k.op(k.pe, lambda c=c: nc.tensor.matmul(pq[:], w[:, c, c0:c0 + 128], ht[:, c, :],
                                                                 start=(c == 0), stop=(c == NCH - 1)),
                             reads=[w_b, ht_b], writes=[pq_b], inc=(c == NCH - 1))
                    sqc, sqc_b = sqcs[n % 2]
                    rs, rs_b = rss[n % 2]
                    n += 1
                    k.op(k.act, lambda: nc.scalar.activation(out=sqc[:], in_=pq[:], func=AF.Square),
                         reads=[pq_b], writes=[sqc_b])
                    pss, pss_b = cx.psum()
                    k.op(k.pe, lambda: nc.tensor.matmul(pss[:], cb[:, C_BO:C_BO + 128], sqc[:], start=True, stop=True),
                         reads=[sqc_b, cb_b], writes=[pss_b])
                    k.op(k.act, lambda: nc.scalar.activation(out=rs[:], in_=pss[:], func=AF.Sqrt,
                                                             bias=consts["eps"][:], scale=1.0 / 64),
                         reads=[pss_b, consts["eps_b"]], writes=[rs_b])
                    k.op(k.dve, lambda: nc.vector.reciprocal(out=rs[:], in_=rs[:]), reads=[rs_b], writes=[rs_b])
                    k.op(k.dve, lambda: nc.vector.scalar_tensor_tensor(
                        out=dst[:, oc, :], in0=pq[:], scalar=gqk[:, which:which + 1], in1=rs[:],
                        op0=ALU.mult, op1=ALU.mult), reads=[pq_b, gqk_b, rs_b], writes=[dst_b])
            for blk in range(4):
                for half in range(1):
                    pv, pv_b = cx.psum()
                    c0 = 2 * 512
                    for c in range(NCH):
                        k.op(k.pe, lambda c=c: nc.tensor.matmul(pv[:], ht[:, c, blk * 128:(blk + 1) * 128],
                                                                 w[:, c, c0:c0 + 512], start=(c == 0),
                                                                 stop=(c == NCH - 1)),
                             reads=[w_b, ht_b], writes=[pv_b], inc=(c == NCH - 1))
                    k.op(k.act, lambda: nc.scalar.copy(out=vt[:, blk, half * 512:(half + 1) * 512], in_=pv[:]),
                         reads=[pv_b], writes=[vt_b])
            k.dma(k.sp, QT.rearrange("(c p) t -> p c t", p=128)[:, :, i * T:(i + 1) * T], qt[:], [qt_b],
                  [cx.dbuf("QT", i)], qt_b)
            k.dma(k.sp, KT.rearrange("(c p) t -> p c t", p=128)[:, :, i * T:(i + 1) * T], kt[:], [kt_b],
                  [cx.dbuf("KT", i)], kt_b)
            k.dma(k.sp, V[i * T:(i + 1) * T, :].rearrange("(b p) d -> p b d", p=128), vt[:], [vt_b],
                  [cx.dbuf("V", i)], vt_b)
        k.release_stage()


def stage_sb_attn(cx, li, consts, QT, KT, V, OT):
    nc, k = cx.nc, cx.k
    S = cx.S
    NB = S // 128
    cb, cb_b = consts["cbf"], consts["cbf_b"]
    NE, NSP, NG_, NA = 4, 4, 2, 3
    with (
        nc.sbuf_tensor(f"akc0{li}", [128, S], BF16) as kc0,
        nc.sbuf_tensor(f"aqc0{li}", [128, S], BF16) as qc0,
        nc.sbuf_tensor(f"avc0{li}", [128, NB, 128], BF16) as vc0,
        nc.sbuf_tensor(f"akc1{li}", [128, S], BF16) as kc1,
        nc.sbuf_tensor(f"aqc1{li}", [128, S], BF16) as qc1,
        nc.sbuf_tensor(f"avc1{li}", [128, NB, 128], BF16) as vc1,
        nc.sbuf_tensor(f"aoc{li}", [128, S], BF16) as oc,
        nc.sbuf_tensor(f"aE{li}", [128, NE, 2 * T], F32) as Et,
        nc.sbuf_tensor(f"aSP{li}", [128, NSP, 2 * T], BF16) as SPt,
        nc.sbuf_tensor(f"aG{li}", [128, NG_, 2 * T], F32) as Gt,
        nc.sbuf_tensor(f"aA{li}", [128, NA, 2 * T], BF16) as At,
    ):
        kqv = [(kc0, qc0, vc0, k.buf("akc0"), k.buf("aqc0"), k.buf("avc0")),
               (kc1, qc1, vc1, k.buf("akc1"), k.buf("aqc1"), k.buf("avc1"))]
        oc_b = k.buf("aoc")
        E_b = [k.buf(f"aE{x}") for x in range(NE)]
        SP_b = [k.buf(f"aSP{x}") for x in range(NSP)]
        G_b = [k.buf(f"aG{x}") for x in range(NG_)]
        A_b = [k.buf(f"aA{x}") for x in range(NA)]
        Zp = [cx.ps2[0], cx.ps2[1]]
        Tp, T_b = cx.ps2[2]
        Obanks = [(cx.ps[6][0], [k.buf("aO00"), k.buf("aO01")]), (cx.ps[7][0], [k.buf("aO10"), k.buf("aO11")])]
        one_ap = consts["onef"]
        cnt = {"z": 0, "e": 0, "sp": 0, "g": 0, "a": 0}

        def load(c):
            kc, qc, vc, kc_b, qc_b, vc_b = kqv[c % 2]
            rd = [cx.dbuf("KT", i) for i in range(cx.NT)]
            k.dma(k.sp, kc[:], KT[c * 128:(c + 1) * 128, :], rd, [kc_b], kc_b)
            rd = [cx.dbuf("QT", i) for i in range(cx.NT)]
            k.dma(k.sp, qc[:], QT[c * 128:(c + 1) * 128, :], rd, [qc_b], qc_b)
            rd = [cx.dbuf("V", i) for i in range(cx.NT)]
            k.dma(k.sp, vc[:], V[:, c * 128:(c + 1) * 128].rearrange("(b p) d -> p b d", p=128), rd, [vc_b], vc_b)

        load(0)
        for c in range(SBP):
            kc, qc, vc, kc_b, qc_b, vc_b = kqv[c % 2]
            if c + 1 < SBP:
                load(c + 1)
            items = [(i, J) for i in range(cx.NT) for J in range(4 * i + 3, -1, -1)]
            n = len(items)
            st = [dict() for _ in range(n)]

            def s1(b):
                i, J = items[b]
                diag = J >= 4 * i
                Z, Z_b = Zp[cnt["z"] % 2]
                cnt["z"] += 1
                st[b]["Z"] = (Z, Z_b)
                for hh in range(2):
                    pb = 64 * hh
                    Zh = Z[:, hh * T:(hh + 1) * T]
                    k.op(k.pe, lambda pb=pb, Zh=Zh: nc.tensor.matmul(
                        Zh, kc[pb:pb + 64, J * 128:(J + 1) * 128], qc[pb:pb + 64, i * T:(i + 1) * T],
                        start=True, stop=(not diag)), reads=[kc_b, qc_b], writes=[Z_b], inc=(not diag))
                    if diag:
                        o = J - 4 * i
                        k.op(k.pe, lambda Zh=Zh, o=o: nc.tensor.matmul(
                            Zh, cb[:, C_ID:C_ID + 128], cb[:, C_MB + 512 * o:C_MB + 512 * (o + 1)],
                            start=False, stop=True), reads=[cb_b], writes=[Z_b])

            def s2_e(b):
                Z, Z_b = st[b]["Z"]
                e = cnt["e"] % NE
                cnt["e"] += 1
                st[b]["e"] = e
                k.op(k.act, lambda Z=Z, e=e: nc.scalar.activation(out=Et[:, e, :], in_=Z[:, :], func=AF.Exp),
                     reads=[Z_b], writes=[E_b[e]])

            def s2_sp(b):
                e = st[b]["e"]
                sp = cnt["sp"] % NSP
                cnt["sp"] += 1
                st[b]["sp"] = sp
                k.op(k.act, lambda e=e, sp=sp: nc.scalar.activation(out=SPt[:, sp, :], in_=Et[:, e, :], func=AF.Ln,
                                                                    bias=one_ap[:], scale=1.0),
                     reads=[E_b[e], consts["onef_b"]], writes=[SP_b[sp]])

            def s3_tri(b):
                i, J = items[b]
                first = (J == 4 * i + 3)
                sp = st[b]["sp"]
                for hh in range(2):
                    k.op(k.pe, lambda hh=hh, sp=sp: nc.tensor.matmul(
                        Tp[:, hh * T:(hh + 1) * T], cb[:, C_TRI:C_TRI + 128], SPt[:, sp, hh * T:(hh + 1) * T],
                        start=first, stop=False, skip_group_check=True), reads=[SP_b[sp], cb_b], writes=[T_b])

            def s3_g(b):
                g = cnt["g"] % NG_
                cnt["g"] += 1
                st[b]["g"] = g
                k.op(k.act, lambda g=g: nc.scalar.activation(out=Gt[:, g, :], in_=Tp[:, :], func=AF.Exp, scale=-1.0),
                     reads=[T_b], writes=[G_b[g]])

            def s3_omt(b):
                i, J = items[b]
                if J == 0:
                    return
                sp = st[b]["sp"]
                for hh in range(2):
                    k.op(k.pe, lambda hh=hh, sp=sp: nc.tensor.matmul(
                        Tp[:, hh * T:(hh + 1) * T], cb[:, C_OMT:C_OMT + 128], SPt[:, sp, hh * T:(hh + 1) * T],
                        start=False, stop=False, skip_group_check=True), reads=[SP_b[sp], cb_b], writes=[T_b])

            def s4_a(b):
                e, g = st[b]["e"], st[b]["g"]
                a = cnt["a"] % NA
                cnt["a"] += 1
                st[b]["a"] = a
                k.op(k.pool, lambda e=e, g=g, a=a: nc.gpsimd.tensor_tensor(out=At[:, a, 0:T], in0=Et[:, e, 0:T],
                                                                           in1=Gt[:, g, 0:T], op=ALU.mult),
                     reads=[E_b[e], G_b[g]], writes=[A_b[a]])
                k.op(k.dve, lambda e=e, g=g, a=a: nc.vector.tensor_tensor(out=At[:, a, T:2 * T], in0=Et[:, e, T:2 * T],
                                                                          in1=Gt[:, g, T:2 * T], op=ALU.mult),
                     reads=[E_b[e], G_b[g]], writes=[A_b[a]])

            def s4_av(b):
                i, J = items[b]
                first = (J == 4 * i + 3)
                Ot, O_b = Obanks[i % 2]
                a = st[b]["a"]
                for hh in range(2):
                    pb = 64 * hh
                    k.op(k.pe, lambda pb=pb, a=a, Ot=Ot, hh=hh: nc.tensor.matmul(
                        Ot[pb:pb + 64, :], vc[:, J, pb:pb + 64], At[:, a, hh * T:(hh + 1) * T], start=first,
                        stop=(J == 0), skip_group_check=True), reads=[vc_b, A_b[a]], writes=[O_b[hh]])
                if J == 0:
                    for hh in range(2):
                        pb = 64 * hh
                        k.op(k.dve, lambda pb=pb, Ot=Ot: nc.vector.tensor_copy(
                            out=oc[pb:pb + 64, i * T:(i + 1) * T], in_=Ot[pb:pb + 64, :]),
                            reads=[O_b[hh]], writes=[oc_b])

            for t in range(n + 4):
                if 0 <= t - 3 < n:
                    s3_tri(t - 3)
                    s3_g(t - 3)
                if t < n:
                    s1(t)
                if 0 <= t - 4 < n:
                    s4_av(t - 4)
                if 0 <= t - 3 < n:
                    s3_omt(t - 3)
                    s4_a(t - 3)
                if 0 <= t - 2 < n:
                    s2_sp(t - 2)
                if 0 <= t - 1 < n:
                    s2_e(t - 1)
            wr = [cx.dbuf("OT", i) for i in range(cx.NT)]
            k.dma(k.sp, OT[c * 128:(c + 1) * 128, :], oc[:], [oc_b], wr, oc_b, partial=(c > 0))
        k.release_stage()


def stage_ssd_a1(cx, li, Xi, iname, W, consts, ZS, DT, DA):
    nc, k = cx.nc, cx.k
    with (
        nc.sbuf_tensor(f"d1w{li}", [128, NCH, 1024 + NHD], BF16) as w,
        nc.sbuf_tensor(f"d1g{li}", [128, NCH], F32) as gain,
        nc.sbuf_tensor(f"d1p{li}", [NHD, 4], F32) as prm,
        nc.sbuf_tensor(f"d1x0{li}", [128, NCH, T], F32) as xt0,
        nc.sbuf_tensor(f"d1x1{li}", [128, NCH, T], F32) as xt1,
        nc.sbuf_tensor(f"d1h{li}", [128, NCH, T], BF16) as ht,
        nc.sbuf_tensor(f"d1q{li}", [128, NCH, T], BF16) as sq,
        nc.sbuf_tensor(f"d1r{li}", [128, T], F32) as rstd,
        nc.sbuf_tensor(f"d1z{li}", [128, 8, T], BF16) as zs,
        nc.sbuf_tensor(f"d1dt{li}", [NHD, 2, T], F32) as dd,
    ):
        w_b, gain_b, prm_b = k.buf("d1w"), k.buf("d1g"), k.buf("d1p")
        xts = [(xt0, k.buf("d1x0")), (xt1, k.buf("d1x1"))]
        ht_b, sq_b, rstd_b, zs_b, dd_b = k.buf("d1h"), k.buf("d1q"), k.buf("d1r"), k.buf("d1z"), k.buf("d1dt")
        for c in range(NCH):
            k.dma(k.pool, w[:, c, 0:1024], W["in"][c * 128:(c + 1) * 128, 0:1024], [], [w_b], w_b, partial=(c > 0))
            k.dma(k.pool, w[:, c, 1024:1024 + NHD], W["in"][c * 128:(c + 1) * 128, 3072:3072 + NHD], [], [w_b], w_b,
                  partial=True)
        load_vec(cx, k.sp, gain[:], gain_b, W["norm"], NCH)
        k.dma(k.sp, prm[:, 0:1], W["dt_bias"].rearrange("(p o) -> p o", o=1), [], [prm_b], prm_b,
              allow_slow_non_contiguous=True)
        k.dma(k.sp, prm[:, 1:2], W["a_log"].rearrange("(p o) -> p o", o=1), [], [prm_b], prm_b, partial=True,
              allow_slow_non_contiguous=True)
        k.op(k.act, lambda: nc.scalar.activation(out=prm[:, 2:3], in_=prm[:, 1:2], func=AF.Exp),
             reads=[prm_b], writes=[prm_b])
        k.op(k.dve, lambda: nc.vector.tensor_scalar(out=prm[:, 2:3], in0=prm[:, 2:3], scalar1=-1.0, scalar2=None,
                                                    op0=ALU.mult), reads=[prm_b], writes=[prm_b])

        def load(i):
            xt, xt_b = xts[i % 2]
            k.dma(k.sp, xt[:], xtile(Xi, i), [cx.dbuf(iname, i)], [xt_b], xt_b)

        load(0)
        for i in range(cx.NT):
            xt, xt_b = xts[i % 2]
            if i + 1 < cx.NT:
                load(i + 1)
            emit_norm(cx, xt, xt_b, (gain, gain_b), ht, ht_b, sq, sq_b, rstd, rstd_b, consts)
            for oc in range(8):
                pz, pz_b = cx.psum()
                for c in range(NCH):
                    k.op(k.pe, lambda c=c: nc.tensor.matmul(pz[:], w[:, c, oc * 128:(oc + 1) * 128], ht[:, c, :],
                                                             start=(c == 0), stop=(c == NCH - 1)),
                         reads=[w_b, ht_b], writes=[pz_b], inc=(c == NCH - 1))
                k.op(k.act, lambda: nc.scalar.activation(out=zs[:, oc, :], in_=pz[:], func=AF.Silu),
                     reads=[pz_b], writes=[zs_b])
            pd, pd_b = cx.psum()
            for c in range(NCH):
                k.op(k.pe, lambda c=c: nc.tensor.matmul(pd[0:NHD, :], w[:, c, 1024:1024 + NHD], ht[:, c, :],
                                                         start=(c == 0), stop=(c == NCH - 1)),
                     reads=[w_b, ht_b], writes=[pd_b], inc=(c == NCH - 1))
            k.op(k.act, lambda: nc.scalar.activation(out=dd[:, 0, :], in_=pd[0:NHD, :], func=AF.Exp,
                                                     bias=prm[:, 0:1], scale=1.0),
                 reads=[pd_b, prm_b], writes=[dd_b])
            k.op(k.act, lambda: nc.scalar.activation(out=dd[:, 0, :], in_=dd[:, 0, :], func=AF.Ln,
                                                     bias=consts["onef"][0:NHD, :], scale=1.0),
                 reads=[dd_b, consts["onef_b"]], writes=[dd_b])
            k.op(k.dve, lambda: nc.vector.tensor_scalar(out=dd[:, 1, :], in0=dd[:, 0, :], scalar1=prm[:, 2:3],
                                                        scalar2=None, op0=ALU.mult),
                 reads=[dd_b, prm_b], writes=[dd_b])
            k.dma(k.sp, ZS.rearrange("(c p) t -> p c t", p=128)[:, :, i * T:(i + 1) * T], zs[:], [zs_b],
                  [cx.dbuf("ZS", i)], zs_b)
            k.dma(k.sp, DT[:, i * T:(i + 1) * T], dd[:, 0, :], [dd_b], [cx.dbuf("DT", i)], dd_b)
            k.dma(k.sp, DA[:, i * T:(i + 1) * T], dd[:, 1, :], [dd_b], [cx.dbuf("DA", i)], dd_b)
        k.release_stage()


def stage_ssd_a2(cx, li, Xi, iname, W, consts, XS, BT, CT):
    nc, k = cx.nc, cx.k
    cf, cf_b = consts["cf"], consts["cf_b"]
    with (
        nc.sbuf_tensor(f"d2w{li}", [128, NCH, 2048], BF16) as w,
        nc.sbuf_tensor(f"d2g{li}", [128, NCH], F32) as gain,
        nc.sbuf_tensor(f"d2cr{li}", [64, 128], F32) as cwr,
        nc.sbuf_tensor(f"d2br{li}", [16, 128], F32) as cbr,
        nc.sbuf_tensor(f"d2cw{li}", [128, 4, 16], F32) as cw,
        nc.sbuf_tensor(f"d2cb{li}", [128, 16], F32) as cbias,
        nc.sbuf_tensor(f"d2x0{li}", [128, NCH, T], F32) as xt0,
        nc.sbuf_tensor(f"d2x1{li}", [128, NCH, T], F32) as xt1,
        nc.sbuf_tensor(f"d2h{li}", [128, NCH, T], BF16) as ht,
        nc.sbuf_tensor(f"d2q{li}", [128, NCH, T], BF16) as sq,
        nc.sbuf_tensor(f"d2r{li}", [128, T], F32) as rstd,
        nc.sbuf_tensor(f"d2hl{li}", [128, 16, 4], F32) as halo,
        nc.sbuf_tensor(f"d2wk0{li}", [128, 4 + T], F32) as wk0,
        nc.sbuf_tensor(f"d2wk1{li}", [128, 4 + T], F32) as wk1,
        nc.sbuf_tensor(f"d2ac0{li}", [128, T], F32) as ac0,
        nc.sbuf_tensor(f"d2ac1{li}", [128, T], F32) as ac1,
        nc.sbuf_tensor(f"d2xo{li}", [128, 16, T], BF16) as xo,
    ):
        w_b, gain_b, cwr_b, cbr_b, cw_b, cbias_b = (k.buf("d2w"), k.buf("d2g"), k.buf("d2cr"), k.buf("d2br"),
                                                    k.buf("d2cw"), k.buf("d2cb"))
        xts = [(xt0, k.buf("d2x0")), (xt1, k.buf("d2x1"))]
        ht_b, sq_b, rstd_b, halo_b, xo_b = k.buf("d2h"), k.buf("d2q"), k.buf("d2r"), k.buf("d2hl"), k.buf("d2xo")
        wks = [(wk0, k.buf("d2wk0")), (wk1, k.buf("d2wk1"))]
        acs = [(ac0, k.buf("d2ac0")), (ac1, k.buf("d2ac1"))]
        for c in range(NCH):
            k.dma(k.pool, w[:, c, :], W["in"][c * 128:(c + 1) * 128, 1024:3072], [], [w_b], w_b, partial=(c > 0))
        load_vec(cx, k.sp, gain[:], gain_b, W["norm"], NCH)
        k.dma(k.sp, cwr[:], W["conv_w"].rearrange("k (c p) -> (k c) p", p=128), [], [cwr_b], cwr_b)
        k.dma(k.sp, cbr[:], W["conv_b"].rearrange("(c p) -> c p", p=128), [], [cbr_b], cbr_b)
        pt, pt_b = cx.psum()
        k.op(k.pe, lambda: nc.tensor.transpose(pt[:, 0:64], cwr[:], cf[0:64, C_ID:C_ID + 64]),
             reads=[cwr_b, cf_b], writes=[pt_b])
        k.op(k.dve, lambda: nc.vector.tensor_copy(out=cw[:].rearrange("p k c -> p (k c)"), in_=pt[:, 0:64]),
             reads=[pt_b], writes=[cw_b])
        pt2, pt2_b = cx.psum()
        k.op(k.pe, lambda: nc.tensor.transpose(pt2[:, 0:16], cbr[:], cf[0:16, C_ID:C_ID + 16]),
             reads=[cbr_b, cf_b], writes=[pt2_b])
        k.op(k.dve, lambda: nc.vector.tensor_copy(out=cbias[:], in_=pt2[:, 0:16]), reads=[pt2_b], writes=[cbias_b])
        k.op(k.pool, lambda: nc.gpsimd.memset(halo[:], 0.0), reads=[], writes=[halo_b])

        def load(i):
            xt, xt_b = xts[i % 2]
            k.dma(k.sp, xt[:], xtile(Xi, i), [cx.dbuf(iname, i)], [xt_b], xt_b)

        load(0)
        n = 0
        for i in range(cx.NT):
            xt, xt_b = xts[i % 2]
            if i + 1 < cx.NT:
                load(i + 1)
            emit_norm(cx, xt, xt_b, (gain, gain_b), ht, ht_b, sq, sq_b, rstd, rstd_b, consts)
            for ch in range(16):
                pp, pp_b = cx.psum()
                for c in range(NCH):
                    k.op(k.pe, lambda c=c: nc.tensor.matmul(pp[:], w[:, c, ch * 128:(ch + 1) * 128], ht[:, c, :],
                                                             start=(c == 0), stop=(c == NCH - 1)),
                         reads=[w_b, ht_b], writes=[pp_b], inc=(c == NCH - 1))
                wk, wk_b = wks[n % 2]
                ac, ac_b = acs[n % 2]
                n += 1
                k.op(k.pool, lambda wk=wk: nc.gpsimd.tensor_copy(out=wk[:, 0:4], in_=halo[:, ch, :]),
                     reads=[halo_b], writes=[wk_b])
                k.op(k.act, lambda wk=wk: nc.scalar.copy(out=wk[:, 4:4 + T], in_=pp[:]), reads=[pp_b], writes=[wk_b])
                k.op(k.pool, lambda wk=wk: nc.gpsimd.tensor_copy(out=halo[:, ch, :], in_=wk[:, T:T + 4]),
                     reads=[wk_b], writes=[halo_b])
                k.op(k.dve, lambda wk=wk, ac=ac: nc.vector.tensor_scalar(
                    out=ac[:], in0=wk[:, 4:4 + T], scalar1=cw[:, 3, ch:ch + 1], scalar2=cbias[:, ch:ch + 1],
                    op0=ALU.mult, op1=ALU.add), reads=[wk_b, cw_b, cbias_b], writes=[ac_b])
                for tap in (2, 1, 0):
                    sh = 3 - tap
                    k.op(k.dve, lambda wk=wk, ac=ac, tap=tap, sh=sh: nc.vector.scalar_tensor_tensor(
                        out=ac[:], in0=wk[:, 4 - sh:4 - sh + T], scalar=cw[:, tap, ch:ch + 1], in1=ac[:],
                        op0=ALU.mult, op1=ALU.add), reads=[wk_b, cw_b, ac_b], writes=[ac_b])
                k.op(k.act, lambda ac=ac: nc.scalar.activation(out=xo[:, ch, :], in_=ac[:], func=AF.Silu),
                     reads=[ac_b], writes=[xo_b])
            k.dma(k.sp, XS.rearrange("(c p) t -> p c t", p=128)[:, :, i * T:(i + 1) * T], xo[:, 0:8, :], [xo_b],
                  [cx.dbuf("XS", i)], xo_b)
            k.dma(k.sp, BT.rearrange("(c p) t -> p c t", p=128)[:, :, i * T:(i + 1) * T], xo[:, 8:12, :], [xo_b],
                  [cx.dbuf("BT", i)], xo_b)
            k.dma(k.sp, CT.rearrange("(c p) t -> p c t", p=128)[:, :, i * T:(i + 1) * T], xo[:, 12:16, :], [xo_b],
                  [cx.dbuf("CT", i)], xo_b)
        k.release_stage()


def stage_ssd_scan(cx, li, W, consts, ZS, DT, DA, XS, BT, CT, GN):
    nc, k = cx.nc, cx.k
    S = cx.S
    L = 256
    NCK = S // L
    cb, cb_b = consts["cbf"], consts["cbf_b"]
    cf, cf_b = consts["cf"], consts["cf_b"]
    with ExitStack() as es:
        onesf = es.enter_context(nc.sbuf_tensor(f"s3of{li}", [128, 256], F32))
        selH = es.enter_context(nc.sbuf_tensor(f"s3sh{li}", [NHD, NHD, 128], F32))
        prm = es.enter_context(nc.sbuf_tensor(f"s3pr{li}", [NHD, 2], F32))
        rhsD = es.enter_context(nc.sbuf_tensor(f"s3rd{li}", [NHD, 8], F32))
        Dvec = es.enter_context(nc.sbuf_tensor(f"s3dv{li}", [128, 8], F32))
        onorm = es.enter_context(nc.sbuf_tensor(f"s3on{li}", [128, 8], F32))
        stf = es.enter_context(nc.sbuf_tensor(f"s3st{li}", [128, NHD, 64], F32))
        stb = es.enter_context(nc.sbuf_tensor(f"s3sb{li}", [128, NHD, 64], BF16))
        dtc = es.enter_context(nc.sbuf_tensor(f"s3dt{li}", [NHD, L], F32))
        dac = es.enter_context(nc.sbuf_tensor(f"s3da{li}", [NHD, L], F32))
        acum = es.enter_context(nc.sbuf_tensor(f"s3ac{li}", [NHD, L], F32))
        dg = es.enter_context(nc.sbuf_tensor(f"s3dg{li}", [NHD, NHD], F32))
        tm = es.enter_context(nc.sbuf_tensor(f"s3tm{li}", [128, 4, NHD], F32))
        tmpd = es.enter_context(nc.sbuf_tensor(f"s3td{li}", [128, 2, NHD], F32))
        wdt = es.enter_context(nc.sbuf_tensor(f"s3wd{li}", [128, 2, NHD], F32))
        xs = es.enter_context(nc.sbuf_tensor(f"s3xs{li}", [128, 8, L], BF16))
        zs = es.enter_context(nc.sbuf_tensor(f"s3zs{li}", [128, 8, L], BF16))
        Bt = es.enter_context(nc.sbuf_tensor(f"s3bt{li}", [128, NG, L], BF16))
        Ct = es.enter_context(nc.sbuf_tensor(f"s3ct{li}", [128, NG, L], BF16))
        Gm0 = es.enter_context(nc.sbuf_tensor(f"s3gm0{li}", [128, 2, L], F32))
        Gm1 = es.enter_context(nc.sbuf_tensor(f"s3gm1{li}", [128, 2, L], F32))
        xstm0 = es.enter_context(nc.sbuf_tensor(f"s3xt0{li}", [128, 2, 256], BF16))
        xstm1 = es.enter_context(nc.sbuf_tensor(f"s3xt1{li}", [128, 2, 256], BF16))
        Bw0 = es.enter_context(nc.sbuf_tensor(f"s3bw0{li}", [128, 2, 128], BF16))
        Bw1 = es.enter_context(nc.sbuf_tensor(f"s3bw1{li}", [128, 2, 128], BF16))
        xw0 = es.enter_context(nc.sbuf_tensor(f"s3xw0{li}", [128, 2, 256], BF16))
        xw1 = es.enter_context(nc.sbuf_tensor(f"s3xw1{li}", [128, 2, 256], BF16))
        Dd0 = es.enter_context(nc.sbuf_tensor(f"s3d0{li}", [128, 4, 2, L], F32))
        Dd1 = es.enter_context(nc.sbuf_tensor(f"s3d1{li}", [128, 4, 2, L], F32))
        Mh0 = es.enter_context(nc.sbuf_tensor(f"s3m0{li}", [128, 4, 2, L], BF16))
        Mh1 = es.enter_context(nc.sbuf_tensor(f"s3m1{li}", [128, 4, 2, L], BF16))
        Ea0 = es.enter_context(nc.sbuf_tensor(f"s3e0{li}", [128, 4, L], F32))
        Ea1 = es.enter_context(nc.sbuf_tensor(f"s3e1{li}", [128, 4, L], F32))
        Ch0 = es.enter_context(nc.sbuf_tensor(f"s3c0{li}", [128, 4, L], BF16))
        Ch1 = es.enter_context(nc.sbuf_tensor(f"s3c1{li}", [128, 4, L], BF16))
        yv = es.enter_context(nc.sbuf_tensor(f"s3yv{li}", [128, 2, L], F32))
        sqg = es.enter_context(nc.sbuf_tensor(f"s3sq{li}", [128, 2, L], BF16))
        rs = es.enter_context(nc.sbuf_tensor(f"s3rs{li}", [128, L], F32))
        gn = es.enter_context(nc.sbuf_tensor(f"s3gn{li}", [128, 8, L], BF16))
        B = k.buf
        onesf_b, selH_b, prm_b, rhsD_b, Dvec_b, onorm_b = B("onesf"), B("selH"), B("s3pr"), B("rhsD"), B("Dvec"), B("onorm")
        stf_b = [B(f"stf{h}") for h in range(NHD)]
        stb_b = [B(f"stb{h}") for h in range(NHD)]
        dtc_b, dac_b, acum_b, dg_b, tm_b, tmpd_b, wdt_b = (B("dtc"), B("dac"), B("acum"), B("dg"), B("tm"), B("tmpd"),
                                                           B("wdt"))
        xs_b, zs_b, Bt_b, Ct_b = B("xs"), B("zs"), B("Bt"), B("Ct")
        Gms = [(Gm0, B("Gm0")), (Gm1, B("Gm1"))]
        xstms = [(xstm0, B("xstm0")), (xstm1, B("xstm1"))]
        Bws = [(Bw0, B("Bw0")), (Bw1, B("Bw1"))]
        xws = [(xw0, B("xw0")), (xw1, B("xw1"))]
        gcount = [0]
        Dds = [(Dd0, B("Dd0")), (Dd1, B("Dd1"))]
        Mhs = [(Mh0, B("Mh0")), (Mh1, B("Mh1"))]
        Eas = [(Ea0, B("Ea0")), (Ea1, B("Ea1"))]
        Chs = [(Ch0, B("Ch0")), (Ch1, B("Ch1"))]
        yv_b, sqg_b, rs_b, gn_b = B("yv"), B("sqg"), B("rs"), B("gn")
        rot = [0]

        def ps():
            p = cx.ps[rot[0] % 4]
            rot[0] += 1
            return p

        ACB = [cx.ps[4], cx.ps[5]]

        Ybanks = [cx.ps[6], cx.ps[7]]
        k.op(k.pool, lambda: nc.gpsimd.memset(onesf[:], 1.0), reads=[], writes=[onesf_b])
        k.op(k.pool, lambda: nc.gpsimd.memset(stf[:], 0.0), reads=[], writes=stf_b)
        k.op(k.pool, lambda: nc.gpsimd.memset(stb[:], 0.0), reads=[], writes=stb_b)
        for h in range(NHD):
            k.op(k.pool, lambda h=h: nc.gpsimd.tensor_scalar(out=selH[:, h, :], in0=onesf[0:NHD, 0:128],
                                                             scalar1=cf[0:NHD, C_ID + h:C_ID + h + 1], scalar2=None,
                                                             op0=ALU.mult),
                 reads=[onesf_b, cf_b], writes=[selH_b])
        k.dma(k.sp, prm[:, 0:1], W["d"].rearrange("(p o) -> p o", o=1), [], [prm_b], prm_b,
              allow_slow_non_contiguous=True)
        load_vec(cx, k.sp, onorm[:], onorm_b, W["out_norm"], 8)
        k.op(k.dve, lambda: nc.vector.tensor_scalar(out=rhsD[:], in0=cf[0:NHD, C_SB2:C_SB2 + 8], scalar1=prm[:, 0:1],
                                                    scalar2=None, op0=ALU.mult),
             reads=[cf_b, prm_b], writes=[rhsD_b])
        pD, pD_b = ps()
        k.op(k.pe, lambda: nc.tensor.matmul(pD[:, 0:8], cf[0:NHD, C_SA:C_SA + 128], rhsD[:], start=True, stop=True),
             reads=[cf_b, rhsD_b], writes=[pD_b])
        k.op(k.dve, lambda: nc.vector.tensor_copy(out=Dvec[:], in_=pD[:, 0:8]), reads=[pD_b], writes=[Dvec_b])

        for c in range(NCK):
            ti = c // (T // L)
            cs = slice(c * L, (c + 1) * L)
            k.dma(k.sp, dtc[:], DT[:, cs], [cx.dbuf("DT", ti)], [dtc_b], dtc_b)
            k.dma(k.sp, dac[:], DA[:, cs], [cx.dbuf("DA", ti)], [dac_b], dac_b)
            k.dma(k.sp, xs[:], XS.rearrange("(c p) t -> p c t", p=128)[:, :, cs], [cx.dbuf("XS", ti)], [xs_b], xs_b)
            k.dma(k.sp, zs[:], ZS.rearrange("(c p) t -> p c t", p=128)[:, :, cs], [cx.dbuf("ZS", ti)], [zs_b], zs_b)
            k.dma(k.sp, Bt[:], BT.rearrange("(c p) t -> p c t", p=128)[:, :, cs], [cx.dbuf("BT", ti)], [Bt_b], Bt_b)
            k.dma(k.sp, Ct[:], CT.rearrange("(c p) t -> p c t", p=128)[:, :, cs], [cx.dbuf("CT", ti)], [Ct_b], Ct_b)
            k.op(k.dve, lambda: nc.vector.tensor_tensor_scan(out=acum[:], data0=onesf[0:NHD, 0:L], data1=dac[:],
                                                             initial=0.0, op0=ALU.mult, op1=ALU.add),
                 reads=[onesf_b, dac_b], writes=[acum_b])
            ptm, ptm_b = ps()
            for q in range(4):
                src, src_b = (dtc, dtc_b) if q < 2 else (acum, acum_b)
                sb_ = q % 2
                k.op(k.pe, lambda q=q, src=src, sb_=sb_: nc.tensor.transpose(
                    ptm[:, q * NHD:(q + 1) * NHD], src[:, sb_ * 128:(sb_ + 1) * 128], cf[0:NHD, C_ID:C_ID + NHD]),
                    reads=[src_b, cf_b], writes=[ptm_b])
            k.op(k.dve, lambda: nc.vector.tensor_copy(out=tm[:].rearrange("p q h -> p (q h)"), in_=ptm[:, 0:4 * NHD]),
                 reads=[ptm_b], writes=[tm_b])
            k.op(k.dve, lambda: nc.vector.tensor_scalar(out=dg[:], in0=cf[0:NHD, C_ID:C_ID + NHD],
                                                        scalar1=acum[:, L - 1:L], scalar2=None, op0=ALU.mult),
                 reads=[cf_b, acum_b], writes=[dg_b])
            pal, pal_b = ps()
            k.op(k.pe, lambda: nc.tensor.matmul(pal[:, 0:NHD], onesf[0:NHD, 0:128], dg[:], start=True, stop=True),
                 reads=[onesf_b, dg_b], writes=[pal_b])
            for sb_ in range(2):
                k.op(k.dve, lambda sb_=sb_: nc.vector.tensor_tensor(out=tmpd[:, sb_, :], in0=pal[:, 0:NHD],
                                                                    in1=tm[:, 2 + sb_, :], op=ALU.subtract),
                     reads=[pal_b, tm_b], writes=[tmpd_b])
            k.op(k.act, lambda: nc.scalar.activation(out=tmpd[:], in_=tmpd[:], func=AF.Exp),
                 reads=[tmpd_b], writes=[tmpd_b])
            k.op(k.dve, lambda: nc.vector.tensor_tensor(out=wdt[:], in0=tmpd[:], in1=tm[:, 0:2, :], op=ALU.mult),
                 reads=[tmpd_b, tm_b], writes=[wdt_b])

            def prologue(g):
                gb = gcount[0] % 2
                Gm, Gm_b = Gms[gb]
                xdt, xdt_b = xstms[gb]
                xw, xw_b = xws[gb]
                Btm, Btm_b = Bws[gb]
                pacs = []
                pacs_all[g] = pacs

                def acb(hl):
                    pacb, pacb_b = ACB[hl // 2]
                    hi = hl % 2
                    pacs.append((pacb, pacb_b, hi))
                    h = 4 * g + hl
                    k.op(k.pe, lambda pacb=pacb, h=h, hi=hi: nc.tensor.matmul(
                        pacb[:, hi * L:(hi + 1) * L], selH[:, h, :], acum[:], start=True, stop=True,
                        skip_group_check=True), reads=[selH_b, acum_b], writes=[pacb_b])

                pG, pG_b = ps()
                for sb_ in range(2):
                    k.op(k.pe, lambda sb_=sb_, pG=pG: nc.tensor.matmul(
                        pG[:, sb_ * L:(sb_ + 1) * L], Bt[:, g, sb_ * 128:(sb_ + 1) * 128], Ct[:, g, :],
                        start=True, stop=True, skip_group_check=True), reads=[Bt_b, Ct_b], writes=[pG_b])
                    acb(sb_)
                k.op(k.dve, lambda pG=pG: nc.vector.tensor_tensor(
                    out=Gm[:].rearrange("p s t -> p (s t)"), in0=pG[:, 0:2 * L], in1=cf[:, C_SM:C_SM + 512],
                    op=ALU.mult), reads=[pG_b, cf_b], writes=[Gm_b])
                px, px_b = ps()
                pxb = px[:].bitcast(BF16)
                for sb_ in range(2):
                    for ci in range(2):
                        col = sb_ * 256 + ci * 128
                        k.op(k.pe, lambda sb_=sb_, ci=ci, col=col: nc.tensor.transpose(
                            pxb[:, col:col + 128], xs[:, 2 * g + ci, sb_ * 128:(sb_ + 1) * 128], cb[:, C_ID:C_ID + 128]),
                            reads=[xs_b, cb_b], writes=[px_b])
                acb(2)
                pB, pB_b = ps()
                pBb = pB[:].bitcast(BF16)
                for sb_ in range(2):
                    k.op(k.pe, lambda sb_=sb_: nc.tensor.transpose(
                        pBb[:, sb_ * 128:(sb_ + 1) * 128], Bt[:, g, sb_ * 128:(sb_ + 1) * 128], cb[:, C_ID:C_ID + 128]),
                        reads=[Bt_b, cb_b], writes=[pB_b])
                acb(3)
                xin = pxb[:, 0:512].rearrange("p (s h d) -> p s h d", s=2, h=4)
                k.op(k.dve, lambda: nc.vector.tensor_tensor(
                    out=xdt[:].rearrange("p s (h d) -> p s h d", h=4), in0=xin,
                    in1=tm[:, 0:2, 4 * g:4 * g + 4].unsqueeze(3).to_broadcast([128, 2, 4, 64]), op=ALU.mult),
                    reads=[px_b, tm_b], writes=[xdt_b])
                k.op(k.dve, lambda: nc.vector.tensor_tensor(
                    out=xw[:].rearrange("p s (h d) -> p s h d", h=4), in0=xin,
                    in1=wdt[:, 0:2, 4 * g:4 * g + 4].unsqueeze(3).to_broadcast([128, 2, 4, 64]), op=ALU.mult),
                    reads=[px_b, wdt_b], writes=[xw_b])
                k.op(k.act, lambda: nc.scalar.copy(out=Btm[:].rearrange("p s n -> p (s n)"), in_=pBb[:, 0:256]),
                     reads=[pB_b], writes=[Btm_b])
                return gb

            def heads_front(g, gb):
                Gm, Gm_b = Gms[gb]
                Dd, Dd_b = Dds[gb]
                Mh, Mh_b = Mhs[gb]
                Ea, Ea_b = Eas[gb]
                Ch, Ch_b = Chs[gb]
                pacs = pacs_all[g]
                for hl in range(4):
                    h = 4 * g + hl
                    pacb, pacb_b, hi = pacs[hl]
                    for sb_ in range(2):
                        k.op(k.act, lambda sb_=sb_, pacb=pacb, hl=hl, h=h, hi=hi: nc.scalar.activation(
                            out=Dd[:, hl, sb_, :], in_=pacb[:, hi * L:(hi + 1) * L], func=AF.Relu,
                            bias=tm[:, 2 + sb_, h:h + 1], scale=-1.0), reads=[pacb_b, tm_b], writes=[Dd_b])
                for hp in range(2):
                    pacb, pacb_b, _ = pacs[2 * hp]
                    k.op(k.act, lambda pacb=pacb, hp=hp: nc.scalar.activation(
                        out=Ea[:, 2 * hp:2 * hp + 2, :].rearrange("p h t -> p (h t)"), in_=pacb[:, 0:2 * L],
                        func=AF.Exp), reads=[pacb_b], writes=[Ea_b])
                k.op(k.act, lambda: nc.scalar.activation(out=Dd[:].rearrange("p h s t -> p (h s t)"),
                                                         in_=Dd[:].rearrange("p h s t -> p (h s t)"), func=AF.Exp,
                                                         scale=-1.0),
                     reads=[Dd_b], writes=[Dd_b])
                k.op(k.pool, lambda: nc.gpsimd.tensor_tensor(
                    out=Ch[:], in0=Ea[:], in1=Ct[:, g, :].unsqueeze(1).to_broadcast([128, 4, L]), op=ALU.mult),
                    reads=[Ct_b, Ea_b], writes=[Ch_b])
                k.op(k.pool, lambda: nc.gpsimd.tensor_tensor(
                    out=Mh[:], in0=Dd[:], in1=Gm[:].unsqueeze(1).to_broadcast([128, 4, 2, L]), op=ALU.mult),
                    reads=[Dd_b, Gm_b], writes=[Mh_b])

            def heads_back(g, gb):
                xdt, xdt_b = xstms[gb]
                xw, xw_b = xws[gb]
                Btm, Btm_b = Bws[gb]
                Mh, Mh_b = Mhs[gb]
                Ea, Ea_b = Eas[gb]
                Ch, Ch_b = Chs[gb]
                Yt, Y_b = Ybanks[gb]
                for hl in range(4):
                    h = 4 * g + hl
                    ci, po = hl // 2, (hl % 2) * 64
                    yo = Yt[po:po + 64, ci * L:(ci + 1) * L]
                    xcol = ci * 128 + po
                    k.op(k.pe, lambda yo=yo, hl=hl, xcol=xcol: nc.tensor.matmul(
                        yo, xdt[:, 0, xcol:xcol + 64], Mh[:, hl, 0, :], start=True, stop=False, skip_group_check=True),
                        reads=[xdt_b, Mh_b], writes=[Y_b], inc=False)
                    k.op(k.pe, lambda yo=yo, hl=hl, xcol=xcol: nc.tensor.matmul(
                        yo, xdt[:, 1, xcol:xcol + 64], Mh[:, hl, 1, :], start=False, stop=False, skip_group_check=True),
                        reads=[xdt_b, Mh_b], writes=[Y_b], inc=False)
                    k.op(k.pe, lambda yo=yo, hl=hl, h=h: nc.tensor.matmul(
                        yo, stb[:, h, :], Ch[:, hl, :], start=False, stop=True, skip_group_check=True),
                        reads=[stb_b[h], Ch_b], writes=[Y_b])
                pS, pS_b = ps()
                for sb_ in range(2):
                    k.op(k.pe, lambda sb_=sb_, pS=pS: nc.tensor.matmul(
                        pS[:, 0:256], Btm[:, sb_, :], xw[:, sb_, :], start=(sb_ == 0), stop=(sb_ == 1)),
                        reads=[Btm_b, xw_b], writes=[pS_b], inc=(sb_ == 1))
                sfb = [stf_b[4 * g + x] for x in range(4)]
                sbb = [stb_b[4 * g + x] for x in range(4)]
                k.op(k.dve, lambda: nc.vector.tensor_tensor(
                    out=stf[:, 4 * g:4 * g + 4, :], in0=stf[:, 4 * g:4 * g + 4, :],
                    in1=Ea[:, :, L - 1:L].to_broadcast([128, 4, 64]), op=ALU.mult),
                    reads=sfb + [Ea_b], writes=sfb)
                k.op(k.dve, lambda pS=pS: nc.vector.tensor_tensor(
                    out=stf[:, 4 * g:4 * g + 4, :], in0=stf[:, 4 * g:4 * g + 4, :],
                    in1=pS[:, 0:256].rearrange("p (h d) -> p h d", h=4), op=ALU.add),
                    reads=sfb + [pS_b], writes=sfb)
                k.op(k.act, lambda: nc.scalar.copy(out=stb[:, 4 * g:4 * g + 4, :], in_=stf[:, 4 * g:4 * g + 4, :]),
                     reads=sfb, writes=sbb)
                for ci in range(2):
                    cc = 2 * g + ci
                    k.op(k.dve, lambda ci=ci, cc=cc: nc.vector.scalar_tensor_tensor(
                        out=yv[:, ci, :], in0=xs[:, cc, :], scalar=Dvec[:, cc:cc + 1], in1=Yt[:, ci * L:(ci + 1) * L],
                        op0=ALU.mult, op1=ALU.add), reads=[xs_b, Dvec_b, Y_b], writes=[yv_b])
                k.op(k.dve, lambda: nc.vector.tensor_tensor(out=yv[:], in0=yv[:], in1=zs[:, 2 * g:2 * g + 2, :],
                                                            op=ALU.mult), reads=[yv_b, zs_b], writes=[yv_b])
                k.op(k.act, lambda: nc.scalar.activation(out=sqg[:], in_=yv[:], func=AF.Square),
                     reads=[yv_b], writes=[sqg_b])
                pss, pss_b = ps()
                for ci in range(2):
                    k.op(k.pe, lambda ci=ci, pss=pss: nc.tensor.matmul(pss[:, 0:L], consts["ones"][:], sqg[:, ci, :],
                                                                        start=(ci == 0), stop=(ci == 1)),
                         reads=[sqg_b, consts["ones_b"]], writes=[pss_b], inc=(ci == 1))
                k.op(k.act, lambda pss=pss: nc.scalar.activation(out=rs[:], in_=pss[:, 0:L], func=AF.Sqrt,
                                                                 bias=consts["eps"][:], scale=1.0 / 256),
                     reads=[pss_b, consts["eps_b"]], writes=[rs_b])
                k.op(k.dve, lambda: nc.vector.reciprocal(out=rs[:], in_=rs[:]), reads=[rs_b], writes=[rs_b])
                for ci in range(2):
                    cc = 2 * g + ci
                    k.op(k.dve, lambda ci=ci, cc=cc: nc.vector.scalar_tensor_tensor(
                        out=gn[:, cc, :], in0=yv[:, ci, :], scalar=onorm[:, cc:cc + 1], in1=rs[:],
                        op0=ALU.mult, op1=ALU.mult), reads=[yv_b, onorm_b, rs_b], writes=[gn_b])

            gbs = {}
            pacs_all = {}
            if SSD_PIPE:
                gbs[0] = prologue(0)
                gcount[0] += 1
                heads_front(0, gbs[0])
                for g in range(NG):
                    if g + 1 < NG:
                        gbs[g + 1] = prologue(g + 1)
                        gcount[0] += 1
                        heads_front(g + 1, gbs[g + 1])
                    heads_back(g, gbs[g])
            else:
                for g in range(NG):
                    gbs[g] = prologue(g)
                    gcount[0] += 1
                    heads_front(g, gbs[g])
                    heads_back(g, gbs[g])
            k.dma(k.sp, GN.rearrange("(c p) t -> p c t", p=128)[:, :, cs], gn[:], [gn_b], [cx.dbuf("GN", ti)], gn_b,
                  partial=(c % 2 == 1))
        k.release_stage()


WEIGHT_SPECS = [
    ("mix_norm", [4, 1024]), ("ffn_norm", [4, 1024]),
    ("pool_in", [2, 1024, 512]), ("pool_group", [2, 4, 128, 256]), ("pool_scale", [2, 1024]),
    ("ssd_in", [1, 1024, 3088]), ("ssd_conv_w", [1, 4, 2048]), ("ssd_conv_b", [1, 2048]),
    ("ssd_dt_bias", [1, 16]), ("ssd_a_log", [1, 16]), ("ssd_d", [1, 16]),
    ("ssd_out_norm", [1, 1024]), ("ssd_out", [1, 1024, 1024]),
    ("sb_qkv", [1, 1024, 1536]), ("sb_q_norm", [1, 64]), ("sb_k_norm", [1, 64]), ("sb_out", [1, 512, 1024]),
    ("ffn_gate", [4, 1024, 1408]), ("ffn_up", [4, 1024, 1408]), ("ffn_down", [4, 1408, 1024]),
]


def local_weights(inp, r):
    f = lambda a: np.ascontiguousarray(np.asarray(a, dtype=np.float32))
    cat = np.concatenate
    w = {}
    for n in ("mix_norm", "ffn_norm", "pool_scale", "sb_q_norm", "sb_k_norm"):
        w[n] = f(inp[n])
    pin = np.asarray(inp["pool_in"])
    w["pool_in"] = f(cat([pin[:, :, (2 * g + r) * 128:(2 * g + r + 1) * 128] for g in range(4)], axis=2))
    w["pool_group"] = f(np.asarray(inp["pool_group"])[:, :, r * 128:(r + 1) * 128, :])
    si = np.asarray(inp["ssd_in"])
    w["ssd_in"] = f(cat([si[:, :, r * 1024:(r + 1) * 1024], si[:, :, 2048 + r * 1024:2048 + (r + 1) * 1024],
                         si[:, :, 4096 + r * 512:4096 + (r + 1) * 512], si[:, :, 5120 + r * 512:5120 + (r + 1) * 512],
                         si[:, :, 6144 + r * 16:6144 + (r + 1) * 16]], axis=2))
    cwv = np.asarray(inp["ssd_conv_w"])
    w["ssd_conv_w"] = f(cat([cwv[:, :, r * 1024:(r + 1) * 1024], cwv[:, :, 2048 + r * 512:2048 + (r + 1) * 512],
                             cwv[:, :, 3072 + r * 512:3072 + (r + 1) * 512]], axis=2))
    cbv = np.asarray(inp["ssd_conv_b"])
    w["ssd_conv_b"] = f(cat([cbv[:, r * 1024:(r + 1) * 1024], cbv[:, 2048 + r * 512:2048 + (r + 1) * 512],
                             cbv[:, 3072 + r * 512:3072 + (r + 1) * 512]], axis=1))
    for n in ("ssd_dt_bias", "ssd_a_log", "ssd_d"):
        w[n] = f(np.asarray(inp[n])[:, r * 16:(r + 1) * 16])
    w["ssd_out_norm"] = f(np.asarray(inp["ssd_out_norm"])[:, r * 1024:(r + 1) * 1024])
    w["ssd_out"] = f(np.asarray(inp["ssd_out"])[:, r * 1024:(r + 1) * 1024, :])
    q = np.asarray(inp["sb_qkv"])
    w["sb_qkv"] = f(cat([q[:, :, r * 512:(r + 1) * 512], q[:, :, 1024 + r * 512:1024 + (r + 1) * 512],
                         q[:, :, 2048 + r * 512:2048 + (r + 1) * 512]], axis=2))
    w["sb_out"] = f(np.asarray(inp["sb_out"])[:, r * 512:(r + 1) * 512, :])
    w["ffn_gate"] = f(np.asarray(inp["ffn_gate"])[:, :, r * 1408:(r + 1) * 1408])
    w["ffn_up"] = f(np.asarray(inp["ffn_up"])[:, :, r * 1408:(r + 1) * 1408])
    w["ffn_down"] = f(np.asarray(inp["ffn_down"])[:, r * 1408:(r + 1) * 1408, :])
    return w


def build_program(S, layers, do_mixer=True, do_ffn=True):
    nc = bass.Bass("TRN2", target_bir_lowering=False, num_devices=8)
    cx = Ctx(nc, S)
    k = cx.k
    NT = S // T
    xT = nc.dram_tensor("xT", [NT, D, T], F32, kind="ExternalInput").ap()
    yT = nc.dram_tensor("yT", [NT, D, T], F32, kind="ExternalOutput").ap()
    Wt = {n: nc.dram_tensor(n, shp, F32, kind="ExternalInput").ap() for n, shp in WEIGHT_SPECS}
    cx.Q = nc.dram_tensor("resQ", [NT, D, T], F32, kind="Internal").ap()
    R = [nc.dram_tensor(f"resR{i}", [NT, D, T], F32, kind="Internal").ap() for i in range(2)]
    ones = nc.alloc_sbuf_tensor("c_ones", [128, 128], BF16)
    ones_b = k.buf("ones")
    k.op(k.pool, lambda: nc.gpsimd.memset(ones[:], 1.0), reads=[], writes=[ones_b])
    epst = nc.alloc_sbuf_tensor("c_eps", [128, 1], F32)
    eps_b = k.buf("eps")
    k.op(k.pool, lambda: nc.gpsimd.memset(epst[:], EPS), reads=[], writes=[eps_b])
    onef = nc.alloc_sbuf_tensor("c_onef", [128, 1], F32)
    onef_b = k.buf("onef")
    k.op(k.pool, lambda: nc.gpsimd.memset(onef[:], 1.0), reads=[], writes=[onef_b])
    cst = nc.dram_tensor("cst", [128, C_TOT], F32, kind="ExternalInput").ap()
    cbf = nc.alloc_sbuf_tensor("c_cbf", [128, C_TOT], BF16)
    cf = nc.alloc_sbuf_tensor("c_cf", [128, C_F32W], F32)
    cbf_b, cf_b = k.buf("cbf"), k.buf("cf")
    k.dma(k.pool, cbf[:], cst[:, :], [], [cbf_b], cbf_b)
    k.dma(k.sp, cf[:], cst[:, 0:C_F32W], [], [cf_b], cf_b)
    k.stage_bufs = []
    consts = {"ones": ones, "ones_b": ones_b, "eps": epst, "eps_b": eps_b, "onef": onef, "onef_b": onef_b,
              "cbf": cbf, "cbf_b": cbf_b, "cf": cf, "cf_b": cf_b}
    scr = {}

    def scratch(name, shape, dt=BF16):
        if name not in scr:
            scr[name] = nc.dram_tensor("scr_" + name, shape, dt, kind="Internal").ap()
        return scr[name]

    cur, cur_name = xT, "xT"
    nstage = [0]

    def nxt():
        nstage[0] += 1
        return R[nstage[0] % 2], f"R{nstage[0]}"

    for idx, li in enumerate(layers):
        kind, j = li % 3, li // 3
        if do_mixer:
            mo, mo_name = nxt()
            if kind == 0:
                W = {"in": Wt["pool_in"][j], "group": Wt["pool_group"][j], "scale": Wt["pool_scale"][j],
                     "norm": Wt["mix_norm"][li]}
                stage_pool(cx, li, cur, cur_name, mo, mo_name, W, consts)
            elif kind == 2:
                QT, KT, V, OT = (scratch("QT", [512, S]), scratch("KT", [512, S]), scratch("V", [S, 512]),
                                 scratch("OT", [512, S]))
                W = {"qkv": Wt["sb_qkv"][j], "qn": Wt["sb_q_norm"][j], "kn": Wt["sb_k_norm"][j],
                     "norm": Wt["mix_norm"][li]}
                stage_sb_qkv(cx, li, cur, cur_name, W, consts, QT, KT, V)
                stage_sb_attn(cx, li, consts, QT, KT, V, OT)
                stage_outproj(cx, f"sb{li}", cur, cur_name, mo, mo_name, OT, "OT", Wt["sb_out"][j], 4)
            else:
                ZS, XS, GN = scratch("ZS", [1024, S]), scratch("XS", [1024, S]), scratch("GN", [1024, S])
                BT, CT = scratch("BT", [512, S]), scratch("CT", [512, S])
                DT, DA = scratch("DT", [NHD, S], F32), scratch("DA", [NHD, S], F32)
                W = {"in": Wt["ssd_in"][j], "conv_w": Wt["ssd_conv_w"][j], "conv_b": Wt["ssd_conv_b"][j],
                     "dt_bias": Wt["ssd_dt_bias"][j], "a_log": Wt["ssd_a_log"][j], "d": Wt["ssd_d"][j],
                     "out_norm": Wt["ssd_out_norm"][j], "norm": Wt["mix_norm"][li]}
                stage_ssd_a1(cx, li, cur, cur_name, W, consts, ZS, DT, DA)
                stage_ssd_a2(cx, li, cur, cur_name, W, consts, XS, BT, CT)
                stage_ssd_scan(cx, li, W, consts, ZS, DT, DA, XS, BT, CT, GN)
                stage_outproj(cx, f"ssd{li}", cur, cur_name, mo, mo_name, GN, "GN", Wt["ssd_out"][j], 8)
            cur, cur_name = mo, mo_name
        if do_ffn:
            Xo, oname = nxt()
            Wf = {"gate": Wt["ffn_gate"][li], "up": Wt["ffn_up"][li], "down": Wt["ffn_down"][li],
                  "norm": Wt["ffn_norm"][li]}
            stage_ffn(cx, li, cur, cur_name, Xo, oname, Wf, consts)
            cur, cur_name = Xo, oname
    cp_b = k.buf("outcopy")
    for i in range(NT):
        k.dma(k.sp, yT[i], cur[i], [cx.dbuf(cur_name, i)], [cx.dbuf("yT", i)], cp_b, is_out=True)
    k.finish()
    return nc


def make_in_maps(inputs, S):
    x = np.asarray(inputs["x"], dtype=np.float32)[:, :S]
    Bn = x.shape[0]
    NT = S // T
    cst = make_consts()
    lw = [local_weights(inputs, r) for r in range(2)]
    in_maps = []
    for c in range(8):
        b, r = (c // 2) % Bn, c % 2
        xt = np.ascontiguousarray(x[b].T.reshape(D, NT, T).transpose(1, 0, 2))
        m = {"xT": xt, "cst": cst}
        m.update(lw[r])
        in_maps.append(m)
    return in_maps


def untile(yt):
    NT = yt.shape[0]
    return np.ascontiguousarray(yt.transpose(1, 0, 2).reshape(D, NT * T).T)


_PROG_CACHE = {}


def kernel(**inputs):
    x = np.asarray(inputs["x"], dtype=np.float32)
    Bn, S, _ = x.shape
    layers = list(range(DEPTH))
    key = (S, tuple(layers))
    if key not in _PROG_CACHE:
        _PROG_CACHE[key] = build_program(S, layers)
    nc = _PROG_CACHE[key]
    in_maps = make_in_maps(inputs, S)
    res = run_bass_kernel_spmd(nc, in_maps, core_ids=list(range(8)))
    out = np.empty_like(x)
    for b in range(Bn):
        out[b] = untile(res.results[2 * b]["yT"])
    return out
```

```python
import numpy as np
from contextlib import ExitStack
import concourse.bass as bass
import concourse.mybir as mybir
from concourse.bass_utils import run_bass_kernel_spmd

F32 = mybir.dt.float32
BF16 = mybir.dt.bfloat16
AF = mybir.ActivationFunctionType
ALU = mybir.AluOpType

D = 1024
NCH = 8
T = 512
EPS = 1e-6
FFN_H = 2816
FFN_HC = 22
SSD_DI = 2048
SSD_IN = 6176
SEQ = 8192
DEPTH = 4
FH = 11
PCH = 4
NG = 4
NHD = 16
SBP = 4
PAIRS = [[0, 1], [2, 3], [4, 5], [6, 7]]
SSD_PIPE = True


class Buf:
    __slots__ = ("name", "w", "r", "dsem", "dcnt", "q")

    def __init__(self, name):
        self.name = name
        self.w = {}
        self.r = {}
        self.dsem = None
        self.dcnt = 0
        self.q = None


class EngW:
    def __init__(self, nc, eng, name):
        self.eng = eng
        self.name = name
        self.sem = nc.alloc_semaphore("tl_" + name)
        self.cnt = 0
        self.seen = {}


class K:
    def __init__(self, nc):
        self.nc = nc
        self.pe = EngW(nc, nc.tensor, "pe")
        self.act = EngW(nc, nc.scalar, "act")
        self.dve = EngW(nc, nc.vector, "dve")
        self.pool = EngW(nc, nc.gpsimd, "pool")
        self.sp = EngW(nc, nc.sync, "sp")
        self.out_events = []
        self.nbuf = 0
        self.nsem = 0
        self.cc_sem = None
        self.cc_cnt = 0
        self.free_sems = {}
        self.stage_bufs = []

    def barrier(self):
        engs = [self.pe, self.act, self.dve, self.pool, self.sp]
        evs = [(e.sem, e.cnt) for e in engs if e.cnt > 0]
        for b in self.stage_bufs:
            if b.dsem is not None and b.dcnt > 0:
                evs.append((b.dsem, b.dcnt))
        for e in engs:
            self._wait(e, evs)

    def release_stage(self):
        self.barrier()
        for b in self.stage_bufs:
            if b.dsem is not None:
                self.free_sems.setdefault(b.q.name, []).append((b.dsem, b.dcnt))
                b.dsem = None
        self.stage_bufs = []

    def buf(self, name=None):
        self.nbuf += 1
        return Buf(name or f"b{self.nbuf}")

    def _wait(self, E, events, own=False):
        for sem, cnt in events:
            if sem is E.sem and (not own or E is self.pe):
                continue
            key = id(sem)
            if E.seen.get(key, 0) < cnt:
                E.eng.wait_ge(sem, cnt)
                E.seen[key] = cnt

    def op(self, E, fn, reads=(), writes=(), inc=True):
        for b in reads:
            self._wait(E, list(b.w.values()), own=True)
        for b in writes:
            self._wait(E, list(b.w.values()), own=True)
            self._wait(E, list(b.r.values()))
        ins = fn()
        if inc:
            E.cnt += 1
            ins.then_inc(E.sem, 1)
            seq = E.cnt
        else:
            seq = E.cnt + 1
        ev = (E.sem, seq)
        k = id(E.sem)
        for b in reads:
            b.r[k] = ev
        for b in writes:
            b.w = {k: ev}
            b.r = {}
        return ins

    def dma(self, Q, out, in_, reads, writes, sb, partial=False, is_out=False, **kw):
        for b in reads:
            self._wait(Q, list(b.w.values()))
        for b in writes:
            if not partial:
                self._wait(Q, list(b.w.values()))
            self._wait(Q, list(b.r.values()))
        if sb.dsem is None:
            fl = self.free_sems.get(Q.name, [])
            if fl:
                sem, cnt = fl.pop()
                if Q.seen.get(id(sem), 0) < cnt:
                    Q.eng.wait_ge(sem, cnt)
                    Q.seen[id(sem)] = cnt
                sb.dsem, sb.dcnt = sem, cnt
            else:
                self.nsem += 1
                sb.dsem = self.nc.alloc_semaphore(f"dq{self.nsem}")
            sb.q = Q
            self.stage_bufs.append(sb)
        assert sb.q is Q
        sb.dcnt += 16
        Q.eng.dma_start(out=out, in_=in_, **kw).then_inc(sb.dsem, 16)
        ev = (sb.dsem, sb.dcnt)
        k = id(sb.dsem)
        for b in reads:
            b.r[k] = ev
        for b in writes:
            if partial:
                b.w[k] = ev
            else:
                b.w = {k: ev}
            b.r = {}
        if is_out:
            self.out_events.append(ev)

    def allreduce(self, src, dst, reads, writes):
        Q = self.pool
        for b in reads:
            self._wait(Q, list(b.w.values()))
        for b in writes:
            self._wait(Q, list(b.w.values()))
            self._wait(Q, list(b.r.values()))
        if self.cc_sem is None:
            self.cc_sem = self.nc.alloc_semaphore("cc_sem")
            self.cc_cnt = 0
        self.cc_cnt += 1
        self.nc.gpsimd.collective_compute("AllReduce", ALU.add, replica_groups=PAIRS, ins=[src],
                                          outs=[dst]).then_inc(self.cc_sem, 1)
        ev = (self.cc_sem, self.cc_cnt)
        kk = id(self.cc_sem)
        for b in reads:
            b.r[kk] = ev
        for b in writes:
            b.w = {kk: ev}
            b.r = {}

    def finish(self):
        last = {}
        for sem, cnt in self.out_events:
            k = id(sem)
            if k not in last or last[k][1] < cnt:
                last[k] = (sem, cnt)
        self._wait(self.sp, list(last.values()))


class Ctx:
    def __init__(self, nc, S):
        self.nc = nc
        self.S = S
        self.NT = S // T
        self.k = K(nc)
        self.dbufs = {}
        self.ps = []
        self.ps2 = []
        for i in range(4):
            t = nc.alloc_psum_tensor(f"psd{i}", [128, 1024], F32)
            self.ps2.append((t, self.k.buf(f"psd{i}")))
            for h in range(2):
                self.ps.append((t[:, h * 512:(h + 1) * 512], self.k.buf(f"ps{2 * i + h}")))
        self._psi = 0

    def dbuf(self, name, i):
        key = (name, i)
        if key not in self.dbufs:
            self.dbufs[key] = self.k.buf(f"{name}_{i}")
        return self.dbufs[key]

    def psum(self):
        p = self.ps[self._psi % 8]
        self._psi += 1
        return p


def xtile(X, i):
    return X[i].rearrange("(c p) t -> p c t", p=128)


def emit_norm(cx, xt, xt_b, gain, ht, ht_b, sq, sq_b, rstd, rstd_b, consts):
    nc, k = cx.nc, cx.k
    gain, gain_b = gain
    ones_bf, ones_b = consts["ones"], consts["ones_b"]
    k.op(k.act, lambda: nc.scalar.activation(out=sq[:], in_=xt[:], func=AF.Square),
         reads=[xt_b], writes=[sq_b])
    ps, ps_b = cx.psum()
    for c in range(NCH):
        k.op(k.pe, lambda c=c: nc.tensor.matmul(ps[:], ones_bf[:], sq[:, c, :],
                                                 start=(c == 0), stop=(c == NCH - 1)),
             reads=[sq_b, ones_b], writes=[ps_b], inc=(c == NCH - 1))
    k.op(k.act, lambda: nc.scalar.activation(out=rstd[:], in_=ps[:], func=AF.Ln, bias=consts["eps"][:],
                                             scale=1.0 / D),
         reads=[ps_b, consts["eps_b"]], writes=[rstd_b])
    k.op(k.act, lambda: nc.scalar.activation(out=rstd[:], in_=rstd[:], func=AF.Exp, scale=-0.5),
         reads=[rstd_b], writes=[rstd_b])
    for c in range(NCH):
        k.op(k.dve, lambda c=c: nc.vector.scalar_tensor_tensor(
            out=ht[:, c, :], in0=xt[:, c, :], scalar=gain[:, c:c + 1], in1=rstd[:],
            op0=ALU.mult, op1=ALU.mult),
            reads=[xt_b, rstd_b, gain_b], writes=[ht_b])


def load_vec(cx, Q, dst, dst_b, src_1d, n, partial=False):
    cx.k.dma(Q, dst, src_1d.rearrange("(c p) -> p c", p=128), reads=[], writes=[dst_b], sb=dst_b,
             partial=partial, allow_slow_non_contiguous=True)


def store_reduce(cx, xt, xt_b, i, Xo, oname):
    k = cx.k
    k.dma(k.sp, xtile(cx.Q, i), xt[:], [xt_b], [cx.dbuf("Q", i)], xt_b)
    k.allreduce(cx.Q[i], Xo[i], [cx.dbuf("Q", i)], [cx.dbuf(oname, i)])


def stage_ffn(cx, li, Xi, iname, Xo, oname, W, consts):
    nc, k = cx.nc, cx.k
    nhc = FH
    H = nhc * 128
    with (
        nc.sbuf_tensor(f"wg{li}", [128, NCH, H], BF16) as wg,
        nc.sbuf_tensor(f"wu{li}", [128, NCH, H], BF16) as wu,
        nc.sbuf_tensor(f"wd{li}", [128, nhc, D], BF16) as wd,
        nc.sbuf_tensor(f"fg{li}", [128, NCH], F32) as gain,
        nc.sbuf_tensor(f"fx0{li}", [128, NCH, T], F32) as xt0,
        nc.sbuf_tensor(f"fx1{li}", [128, NCH, T], F32) as xt1,
        nc.sbuf_tensor(f"fh{li}", [128, NCH, T], BF16) as ht,
        nc.sbuf_tensor(f"fs{li}", [128, NCH, T], BF16) as sq,
        nc.sbuf_tensor(f"fr{li}", [128, T], F32) as rstd,
        nc.sbuf_tensor(f"fa{li}", [128, nhc, T], BF16) as act,
        nc.sbuf_tensor(f"fsg0{li}", [128, T], F32) as sg0,
        nc.sbuf_tensor(f"fsg1{li}", [128, T], F32) as sg1,
    ):
        wg_b, wu_b, wd_b, gain_b = k.buf("wg"), k.buf("wu"), k.buf("wd"), k.buf("fgain")
        xts = [(xt0, k.buf("fx0")), (xt1, k.buf("fx1"))]
        ht_b, sq_b, rstd_b, act_b = k.buf("fh"), k.buf("fs"), k.buf("fr"), k.buf("fa")
        sgs = [(sg0, k.buf("fsg0")), (sg1, k.buf("fsg1"))]
        for c in range(NCH):
            k.dma(k.pool, wg[:, c, :], W["gate"][c * 128:(c + 1) * 128, :], [], [wg_b], wg_b, partial=(c > 0))
            k.dma(k.pool, wu[:, c, :], W["up"][c * 128:(c + 1) * 128, :], [], [wu_b], wu_b, partial=(c > 0))
        for j in range(nhc):
            k.dma(k.pool, wd[:, j, :], W["down"][j * 128:(j + 1) * 128, :], [], [wd_b], wd_b, partial=(j > 0))
        load_vec(cx, k.sp, gain[:], gain_b, W["norm"], NCH)

        def load(i):
            xt, xt_b = xts[i % 2]
            k.dma(k.sp, xt[:], xtile(Xi, i), [cx.dbuf(iname, i)], [xt_b], xt_b)

        load(0)
        for i in range(cx.NT):
            xt, xt_b = xts[i % 2]
            if i + 1 < cx.NT:
                load(i + 1)
            emit_norm(cx, xt, xt_b, (gain, gain_b), ht, ht_b, sq, sq_b, rstd, rstd_b, consts)
            for j in range(nhc):
                pg, pg_b = cx.psum()
                pu, pu_b = cx.psum()
                for c in range(NCH):
                    k.op(k.pe, lambda c=c: nc.tensor.matmul(pg[:], wg[:, c, j * 128:(j + 1) * 128], ht[:, c, :],
                                                             start=(c == 0), stop=(c == NCH - 1)),
                         reads=[wg_b, ht_b], writes=[pg_b], inc=(c == NCH - 1))
                for c in range(NCH):
                    k.op(k.pe, lambda c=c: nc.tensor.matmul(pu[:], wu[:, c, j * 128:(j + 1) * 128], ht[:, c, :],
                                                             start=(c == 0), stop=(c == NCH - 1)),
                         reads=[wu_b, ht_b], writes=[pu_b], inc=(c == NCH - 1))
                sg, sg_b = sgs[j % 2]
                k.op(k.act, lambda: nc.scalar.activation(out=sg[:], in_=pg[:], func=AF.Silu),
                     reads=[pg_b], writes=[sg_b])
                k.op(k.dve, lambda: nc.vector.tensor_tensor(out=act[:, j, :], in0=sg[:], in1=pu[:], op=ALU.mult),
                     reads=[sg_b, pu_b], writes=[act_b])
            for m in range(NCH):
                po, po_b = cx.psum()
                for j in range(nhc):
                    k.op(k.pe, lambda j=j: nc.tensor.matmul(po[:], wd[:, j, m * 128:(m + 1) * 128], act[:, j, :],
                                                             start=(j == 0), stop=(j == nhc - 1)),
                         reads=[wd_b, act_b], writes=[po_b], inc=(j == nhc - 1))
                k.op(k.dve, lambda: nc.vector.scalar_tensor_tensor(out=xt[:, m, :], in0=xt[:, m, :], scalar=0.5,
                                                                   in1=po[:], op0=ALU.mult, op1=ALU.add),
                     reads=[po_b, xt_b], writes=[xt_b])
            store_reduce(cx, xt, xt_b, i, Xo, oname)
        k.release_stage()


def stage_pool(cx, li, Xi, iname, Xo, oname, W, consts):
    nc, k = cx.nc, cx.k
    E = 16
    with (
        nc.sbuf_tensor(f"pw{li}", [128, NCH, PCH * 128], BF16) as win,
        nc.sbuf_tensor(f"pg{li}", [128, 4, 256], BF16) as wgr,
        nc.sbuf_tensor(f"pgn{li}", [128, NCH], F32) as gain,
        nc.sbuf_tensor(f"psc{li}", [128, NCH], F32) as psc,
        nc.sbuf_tensor(f"pic{li}", [128, 4, E], F32) as invc,
        nc.sbuf_tensor(f"px0{li}", [128, NCH, T], F32) as xt0,
        nc.sbuf_tensor(f"px1{li}", [128, NCH, T], F32) as xt1,
        nc.sbuf_tensor(f"ph{li}", [128, NCH, T], BF16) as ht,
        nc.sbuf_tensor(f"pq{li}", [128, NCH, T], BF16) as sq,
        nc.sbuf_tensor(f"pr{li}", [128, T], F32) as rstd,
        nc.sbuf_tensor(f"pu{li}", [128, PCH, E + T], F32) as u,
        nc.sbuf_tensor(f"pa{li}", [128, PCH, E + T], F32) as A,
        nc.sbuf_tensor(f"pb{li}", [128, PCH, E + T], F32) as B,
        nc.sbuf_tensor(f"pp{li}", [128, PCH, T], BF16) as p,
        nc.sbuf_tensor(f"ptmp{li}", [128, E], F32) as tmp,
    ):
        win_b, wgr_b, gain_b, psc_b, invc_b = (k.buf("pwin"), k.buf("pwgr"), k.buf("pgain"), k.buf("psc"),
                                               k.buf("invc"))
        xts = [(xt0, k.buf("px0")), (xt1, k.buf("px1"))]
        ht_b, sq_b, rstd_b = k.buf("ph"), k.buf("pq"), k.buf("pr")
        u_b, A_b, B_b, p_b, tmp_b = k.buf("pu"), k.buf("pA"), k.buf("pB"), k.buf("pp"), k.buf("ptmp")
        for c in range(NCH):
            k.dma(k.pool, win[:, c, :], W["in"][c * 128:(c + 1) * 128, :], [], [win_b], win_b, partial=(c > 0))
        for g in range(4):
            k.dma(k.pool, wgr[:, g, :], W["group"][g, :, :], [], [wgr_b], wgr_b, partial=(g > 0))
        load_vec(cx, k.sp, gain[:], gain_b, W["norm"], NCH)
        load_vec(cx, k.sp, psc[:], psc_b, W["scale"], NCH)
        for g in range(4):
            w = 2 ** (g + 1)
            for t in range(E):
                k.op(k.pool, lambda g=g, t=t, w=w: nc.gpsimd.memset(invc[:, g, t:t + 1], 1.0 / min(t + 1, w)),
                     reads=[], writes=[invc_b])
        k.op(k.pool, lambda: nc.gpsimd.memset(u[:, :, 0:E], 0.0), reads=[], writes=[u_b])
        k.op(k.pool, lambda: nc.gpsimd.memset(A[:, :, 0:E], 0.0), reads=[], writes=[A_b])
        k.op(k.pool, lambda: nc.gpsimd.memset(B[:, :, 0:E], 0.0), reads=[], writes=[B_b])

        def load(i):
            xt, xt_b = xts[i % 2]
            k.dma(k.sp, xt[:], xtile(Xi, i), [cx.dbuf(iname, i)], [xt_b], xt_b)

        load(0)
        for i in range(cx.NT):
            xt, xt_b = xts[i % 2]
            if i + 1 < cx.NT:
                load(i + 1)
            emit_norm(cx, xt, xt_b, (gain, gain_b), ht, ht_b, sq, sq_b, rstd, rstd_b, consts)
            if i > 0:
                k.op(k.pool, lambda: nc.gpsimd.tensor_copy(out=u[:, :, 0:E], in_=u[:, :, T:T + E]),
                     reads=[u_b], writes=[u_b])
            for m in range(PCH):
                pu_, pu_b = cx.psum()
                for c in range(NCH):
                    k.op(k.pe, lambda c=c: nc.tensor.matmul(pu_[:], win[:, c, m * 128:(m + 1) * 128], ht[:, c, :],
                                                             start=(c == 0), stop=(c == NCH - 1)),
                         reads=[win_b, ht_b], writes=[pu_b], inc=(c == NCH - 1))
                k.op(k.act, lambda: nc.scalar.copy(out=u[:, m, E:E + T], in_=pu_[:]),
                     reads=[pu_b], writes=[u_b])
            k.op(k.act, lambda: nc.scalar.activation(out=xt[:], in_=xt[:], func=AF.Copy, scale=0.5), reads=[xt_b], writes=[xt_b])
            k.op(k.dve, lambda: nc.vector.tensor_tensor(out=A[:, :, 1:E + T], in0=u[:, :, 1:E + T],
                                                        in1=u[:, :, 0:E + T - 1], op=ALU.add),
                 reads=[u_b], writes=[A_b])
            k.op(k.pool, lambda: nc.gpsimd.tensor_tensor(out=B[:, 1:4, 3:E + T], in0=A[:, 1:4, 3:E + T],
                                                         in1=A[:, 1:4, 1:E + T - 2], op=ALU.add),
                 reads=[A_b], writes=[B_b])
            k.op(k.dve, lambda: nc.vector.tensor_tensor(out=A[:, 2:4, 7:E + T], in0=B[:, 2:4, 7:E + T],
                                                        in1=B[:, 2:4, 3:E + T - 4], op=ALU.add),
                 reads=[B_b], writes=[A_b])
            k.op(k.pool, lambda: nc.gpsimd.tensor_tensor(out=B[:, 3:4, 15:E + T], in0=A[:, 3:4, 15:E + T],
                                                         in1=A[:, 3:4, 7:E + T - 8], op=ALU.add),
                 reads=[A_b], writes=[B_b])
            srcs = [(A, A_b), (B, B_b), (A, A_b), (B, B_b)]
            for g in range(4):
                s_, s_b = srcs[g]
                w = 2 ** (g + 1)
                k.op(k.dve, lambda g=g, s_=s_, w=w: nc.vector.scalar_tensor_tensor(
                    out=p[:, g, :], in0=s_[:, g, E:E + T], scalar=1.0 / w,
                    in1=u[:, g, E:E + T], op0=ALU.mult, op1=ALU.subtract),
                    reads=[s_b, u_b], writes=[p_b])
                if i == 0:
                    k.op(k.dve, lambda g=g, s_=s_: nc.vector.tensor_tensor(
                        out=tmp[:], in0=s_[:, g, E:2 * E], in1=invc[:, g, :], op=ALU.mult),
                        reads=[s_b, invc_b], writes=[tmp_b])
                    k.op(k.dve, lambda g=g: nc.vector.tensor_tensor(
                        out=p[:, g, 0:E], in0=tmp[:], in1=u[:, g, E:2 * E], op=ALU.subtract),
                        reads=[tmp_b, u_b], writes=[p_b])
            for g in range(4):
                for mo in range(2):
                    cc = 2 * g + mo
                    py, py_b = cx.psum()
                    k.op(k.pe, lambda: nc.tensor.matmul(py[:], wgr[:, g, mo * 128:(mo + 1) * 128], p[:, g, :],
                                                        start=True, stop=True),
                         reads=[wgr_b, p_b], writes=[py_b])
                    k.op(k.dve, lambda cc=cc: nc.vector.scalar_tensor_tensor(
                        out=xt[:, cc, :], in0=py[:], scalar=psc[:, cc:cc + 1], in1=xt[:, cc, :],
                        op0=ALU.mult, op1=ALU.add),
                        reads=[py_b, psc_b, xt_b], writes=[xt_b])
            store_reduce(cx, xt, xt_b, i, Xo, oname)
        k.release_stage()


C_ID, C_SM, C_SA, C_SB2, C_TRI, C_OMT, C_BO, C_MB = 0, 128, 640, 768, 784, 912, 1040, 1168
C_F32W = 784
C_TOT = 1168 + 2048


def make_consts():
    c = np.zeros((128, C_TOT), np.float32)
    j = np.arange(128)
    c[:, C_ID:C_ID + 128] = np.eye(128)
    t256 = np.arange(256)
    c[:, C_SM:C_SM + 256] = (j[:, None] <= t256[None, :])
    c[:, C_SM + 256:C_SM + 512] = (128 + j[:, None] <= t256[None, :])
    k32 = np.arange(32)
    c[:32, C_SA:C_SA + 128] = ((k32[:, None] % 2) == (j[None, :] // 64))
    c[:32, C_SB2:C_SB2 + 16] = ((k32[:, None] // 2) == np.arange(16)[None, :])
    c[:, C_TRI:C_TRI + 128] = (j[:, None] >= j[None, :])
    c[:, C_OMT:C_OMT + 128] = (j[:, None] < j[None, :])
    c[:, C_BO:C_BO + 128] = ((j[:, None] // 64) == (j[None, :] // 64))
    t512 = np.arange(512)
    for o in range(4):
        valid = (128 * o + j[:, None]) < t512[None, :]
        c[:, C_MB + 512 * o:C_MB + 512 * (o + 1)] = np.where(valid, 0.0, -30000.0)
    return c


def stage_outproj(cx, tag, Xi, iname, Xo, oname, G, gname, Wd, KC):
    nc, k = cx.nc, cx.k
    with (
        nc.sbuf_tensor(f"ow{tag}", [128, KC, D], BF16) as w,
        nc.sbuf_tensor(f"ox0{tag}", [128, NCH, T], F32) as xt0,
        nc.sbuf_tensor(f"ox1{tag}", [128, NCH, T], F32) as xt1,
        nc.sbuf_tensor(f"og0{tag}", [128, KC, T], BF16) as g0,
        nc.sbuf_tensor(f"og1{tag}", [128, KC, T], BF16) as g1,
    ):
        w_b = k.buf("ow")
        xts = [(xt0, k.buf("ox0")), (xt1, k.buf("ox1"))]
        gs = [(g0, k.buf("og0")), (g1, k.buf("og1"))]
        for c in range(KC):
            k.dma(k.pool, w[:, c, :], Wd[c * 128:(c + 1) * 128, :], [], [w_b], w_b, partial=(c > 0))

        def load(i):
            xt, xt_b = xts[i % 2]
            gt, gt_b = gs[i % 2]
            k.dma(k.sp, xt[:], xtile(Xi, i), [cx.dbuf(iname, i)], [xt_b], xt_b)
            k.dma(k.sp, gt[:], G.rearrange("(c p) t -> p c t", p=128)[:, :, i * T:(i + 1) * T],
                  [cx.dbuf(gname, i)], [gt_b], gt_b)

        load(0)
        for i in range(cx.NT):
            xt, xt_b = xts[i % 2]
            gt, gt_b = gs[i % 2]
            if i + 1 < cx.NT:
                load(i + 1)
            for m in range(NCH):
                po, po_b = cx.psum()
                for c in range(KC):
                    k.op(k.pe, lambda c=c: nc.tensor.matmul(po[:], w[:, c, m * 128:(m + 1) * 128], gt[:, c, :],
                                                             start=(c == 0), stop=(c == KC - 1)),
                         reads=[w_b, gt_b], writes=[po_b], inc=(c == KC - 1))
                k.op(k.dve, lambda: nc.vector.scalar_tensor_tensor(out=xt[:, m, :], in0=xt[:, m, :], scalar=0.5,
                                                                   in1=po[:], op0=ALU.mult, op1=ALU.add),
                     reads=[po_b, xt_b], writes=[xt_b])
            store_reduce(cx, xt, xt_b, i, Xo, oname)
        k.release_stage()


def stage_sb_qkv(cx, li, Xi, iname, W, consts, QT, KT, V):
    nc, k = cx.nc, cx.k
    cb, cb_b = consts["cbf"], consts["cbf_b"]
    with (
        nc.sbuf_tensor(f"sw{li}", [128, NCH, 3 * 512], BF16) as w,
        nc.sbuf_tensor(f"sgn{li}", [128, NCH], F32) as gain,
        nc.sbuf_tensor(f"sgq{li}", [128, 2], F32) as gqk,
        nc.sbuf_tensor(f"sx0{li}", [128, NCH, T], F32) as xt0,
        nc.sbuf_tensor(f"sx1{li}", [128, NCH, T], F32) as xt1,
        nc.sbuf_tensor(f"sh{li}", [128, NCH, T], BF16) as ht,
        nc.sbuf_tensor(f"ssq{li}", [128, NCH, T], BF16) as sq,
        nc.sbuf_tensor(f"sr{li}", [128, T], F32) as rstd,
        nc.sbuf_tensor(f"sqt{li}", [128, SBP, T], BF16) as qt,
        nc.sbuf_tensor(f"skt{li}", [128, SBP, T], BF16) as kt,
        nc.sbuf_tensor(f"svt{li}", [128, 4, 512], BF16) as vt,
        nc.sbuf_tensor(f"ssc0{li}", [128, T], BF16) as sqc0,
        nc.sbuf_tensor(f"ssc1{li}", [128, T], BF16) as sqc1,
        nc.sbuf_tensor(f"srs0{li}", [128, T], F32) as rs0,
        nc.sbuf_tensor(f"srs1{li}", [128, T], F32) as rs1,
    ):
        w_b, gain_b, gqk_b = k.buf("sw"), k.buf("sgn"), k.buf("sgq")
        xts = [(xt0, k.buf("sx0")), (xt1, k.buf("sx1"))]
        ht_b, sq_b, rstd_b = k.buf("sh"), k.buf("ssq"), k.buf("sr")
        qt_b, kt_b, vt_b = k.buf("sqt"), k.buf("skt"), k.buf("svt")
        sqcs = [(sqc0, k.buf("ssc0")), (sqc1, k.buf("ssc1"))]
        rss = [(rs0, k.buf("srs0")), (rs1, k.buf("srs1"))]
        for c in range(NCH):
            k.dma(k.pool, w[:, c, :], W["qkv"][c * 128:(c + 1) * 128, :], [], [w_b], w_b, partial=(c > 0))
        load_vec(cx, k.sp, gain[:], gain_b, W["norm"], NCH)
        first = True
        for col, nm in ((0, "qn"), (1, "kn")):
            for hh in range(2):
                k.dma(k.sp, gqk[hh * 64:(hh + 1) * 64, col:col + 1], W[nm].rearrange("(p o) -> p o", o=1), [],
                      [gqk_b], gqk_b, partial=(not first), allow_slow_non_contiguous=True)
                first = False
        k.op(k.dve, lambda: nc.vector.tensor_scalar(out=gqk[:, 0:1], in0=gqk[:, 0:1], scalar1=0.125, scalar2=None,
                                                    op0=ALU.mult), reads=[gqk_b], writes=[gqk_b])

        def load(i):
            xt, xt_b = xts[i % 2]
            k.dma(k.sp, xt[:], xtile(Xi, i), [cx.dbuf(iname, i)], [xt_b], xt_b)

        load(0)
        n = 0
        for i in range(cx.NT):
            xt, xt_b = xts[i % 2]
            if i + 1 < cx.NT:
                load(i + 1)
            emit_norm(cx, xt, xt_b, (gain, gain_b), ht, ht_b, sq, sq_b, rstd, rstd_b, consts)
            for which, (dst, dst_b) in enumerate(((qt, qt_b), (kt, kt_b))):
                for oc in range(SBP):
                    c0 = which * 512 + oc * 128
                    pq, pq_b = cx.psum()
                    for c in range(NCH):
                        k.op(k.pe, lambda c=c: nc.tensor.matmul(pq[:], w[:, c, c0:c0 + 128], ht[:, c, :],
                                                                 start=(c == 0), stop=(c == NCH - 1)),
                             reads=[w_b, ht_b], writes=[pq_b], inc=(c == NCH - 1))
                    sqc, sqc_b = sqcs[n % 2]
                    rs, rs_b = rss[n % 2]
                    n += 1
                    k.op(k.act, lambda: nc.scalar.activation(out=sqc[:], in_=pq[:], func=AF.Square),
                         reads=[pq_b], writes=[sqc_b])
                    pss, pss_b = cx.psum()
                    k.op(k.pe, lambda: nc.tensor.matmul(pss[:], cb[:, C_BO:C_BO + 128], sqc[:], start=True, stop=True),
                         reads=[sqc_b, cb_b], writes=[pss_b])
                    k.op(k.act, lambda: nc.scalar.activation(out=rs[:], in_=pss[:], func=AF.Ln,
                                                             bias=consts["eps"][:], scale=1.0 / 64),
                         reads=[pss_b, consts["eps_b"]], writes=[rs_b])
                    k.op(k.act, lambda: nc.scalar.activation(out=rs[:], in_=rs[:], func=AF.Exp, scale=-0.5),
                         reads=[rs_b], writes=[rs_b])
                    k.op(k.dve, lambda: nc.vector.scalar_tensor_tensor(
                        out=dst[:, oc, :], in0=pq[:], scalar=gqk[:, which:which + 1], in1=rs[:],
                        op0=ALU.mult, op1=ALU.mult), reads=[pq_b, gqk_b, rs_b], writes=[dst_b])
            for blk in range(4):
                for half in range(1):
                    pv, pv_b = cx.psum()
                    c0 = 2 * 512
                    for c in range(NCH):
                        k.op(k.pe, lambda c=c: nc.tensor.matmul(pv[:], ht[:, c, blk * 128:(blk + 1) * 128],
                                                                 w[:, c, c0:c0 + 512], start=(c == 0),
                                                                 stop=(c == NCH - 1)),
                             reads=[w_b, ht_b], writes=[pv_b], inc=(c == NCH - 1))
                    k.op(k.act, lambda: nc.scalar.copy(out=vt[:, blk, half * 512:(half + 1) * 512], in_=pv[:]),
                         reads=[pv_b], writes=[vt_b])
            k.dma(k.sp, QT.rearrange("(c p) t -> p c t", p=128)[:, :, i * T:(i + 1) * T], qt[:], [qt_b],
                  [cx.dbuf("QT", i)], qt_b)
            k.dma(k.sp, KT.rearrange("(c p) t -> p c t", p=128)[:, :, i * T:(i + 1) * T], kt[:], [kt_b],
                  [cx.dbuf("KT", i)], kt_b)
            k.dma(k.sp, V[i * T:(i + 1) * T, :].rearrange("(b p) d -> p b d", p=128), vt[:], [vt_b],
                  [cx.dbuf("V", i)], vt_b)
        k.release_stage()


def stage_sb_attn(cx, li, consts, QT, KT, V, OT):
    nc, k = cx.nc, cx.k
    S = cx.S
    NB = S // 128
    cb, cb_b = consts["cbf"], consts["cbf_b"]
    NE, NSP, NG_, NA = 4, 4, 2, 3
    with (
        nc.sbuf_tensor(f"akc0{li}", [128, S], BF16) as kc0,
        nc.sbuf_tensor(f"aqc0{li}", [128, S], BF16) as qc0,
        nc.sbuf_tensor(f"avc0{li}", [128, NB, 128], BF16) as vc0,
        nc.sbuf_tensor(f"akc1{li}", [128, S], BF16) as kc1,
        nc.sbuf_tensor(f"aqc1{li}", [128, S], BF16) as qc1,
        nc.sbuf_tensor(f"avc1{li}", [128, NB, 128], BF16) as vc1,
        nc.sbuf_tensor(f"aoc{li}", [128, S], BF16) as oc,
        nc.sbuf_tensor(f"aE{li}", [128, NE, 2 * T], F32) as Et,
        nc.sbuf_tensor(f"aSP{li}", [128, NSP, 2 * T], BF16) as SPt,
        nc.sbuf_tensor(f"aG{li}", [128, NG_, 2 * T], F32) as Gt,
        nc.sbuf_tensor(f"aA{li}", [128, NA, 2 * T], BF16) as At,
    ):
        kqv = [(kc0, qc0, vc0, k.buf("akc0"), k.buf("aqc0"), k.buf("avc0")),
               (kc1, qc1, vc1, k.buf("akc1"), k.buf("aqc1"), k.buf("avc1"))]
        oc_b = k.buf("aoc")
        E_b = [k.buf(f"aE{x}") for x in range(NE)]
        SP_b = [k.buf(f"aSP{x}") for x in range(NSP)]
        G_b = [k.buf(f"aG{x}") for x in range(NG_)]
        A_b = [k.buf(f"aA{x}") for x in range(NA)]
        Zp = [cx.ps2[0], cx.ps2[1]]
        Tp, T_b = cx.ps2[2]
        Obanks = [(cx.ps[6][0], [k.buf("aO00"), k.buf("aO01")]), (cx.ps[7][0], [k.buf("aO10"), k.buf("aO11")])]
        one_ap = consts["onef"]
        cnt = {"z": 0, "e": 0, "sp": 0, "g": 0, "a": 0}

        def load(c):
            kc, qc, vc, kc_b, qc_b, vc_b = kqv[c % 2]
            rd = [cx.dbuf("KT", i) for i in range(cx.NT)]
            k.dma(k.sp, kc[:], KT[c * 128:(c + 1) * 128, :], rd, [kc_b], kc_b)
            rd = [cx.dbuf("QT", i) for i in range(cx.NT)]
            k.dma(k.sp, qc[:], QT[c * 128:(c + 1) * 128, :], rd, [qc_b], qc_b)
            rd = [cx.dbuf("V", i) for i in range(cx.NT)]
            k.dma(k.sp, vc[:], V[:, c * 128:(c + 1) * 128].rearrange("(b p) d -> p b d", p=128), rd, [vc_b], vc_b)

        load(0)
        for c in range(SBP):
            kc, qc, vc, kc_b, qc_b, vc_b = kqv[c % 2]
            if c + 1 < SBP:
                load(c + 1)
            items = [(i, J) for i in range(cx.NT) for J in range(4 * i + 3, -1, -1)]
            n = len(items)
            st = [dict() for _ in range(n)]

            def s1(b):
                i, J = items[b]
                diag = J >= 4 * i
                Z, Z_b = Zp[cnt["z"] % 2]
                cnt["z"] += 1
                st[b]["Z"] = (Z, Z_b)
                for hh in range(2):
                    pb = 64 * hh
                    Zh = Z[:, hh * T:(hh + 1) * T]
                    k.op(k.pe, lambda pb=pb, Zh=Zh: nc.tensor.matmul(
                        Zh, kc[pb:pb + 64, J * 128:(J + 1) * 128], qc[pb:pb + 64, i * T:(i + 1) * T],
                        start=True, stop=(not diag)), reads=[kc_b, qc_b], writes=[Z_b], inc=(not diag))
                    if diag:
                        o = J - 4 * i
                        k.op(k.pe, lambda Zh=Zh, o=o: nc.tensor.matmul(
                            Zh, cb[:, C_ID:C_ID + 128], cb[:, C_MB + 512 * o:C_MB + 512 * (o + 1)],
                            start=False, stop=True), reads=[cb_b], writes=[Z_b])

            def s2_e(b):
                Z, Z_b = st[b]["Z"]
                e = cnt["e"] % NE
                cnt["e"] += 1
                st[b]["e"] = e
                k.op(k.act, lambda Z=Z, e=e: nc.scalar.activation(out=Et[:, e, :], in_=Z[:, :], func=AF.Exp),
                     reads=[Z_b], writes=[E_b[e]])

            def s2_sp(b):
                e = st[b]["e"]
                sp = cnt["sp"] % NSP
                cnt["sp"] += 1
                st[b]["sp"] = sp
                k.op(k.act, lambda e=e, sp=sp: nc.scalar.activation(out=SPt[:, sp, :], in_=Et[:, e, :], func=AF.Ln,
                                                                    bias=one_ap[:], scale=1.0),
                     reads=[E_b[e], consts["onef_b"]], writes=[SP_b[sp]])

            def s3_tri(b):
                i, J = items[b]
                first = (J == 4 * i + 3)
                sp = st[b]["sp"]
                for hh in range(2):
                    k.op(k.pe, lambda hh=hh, sp=sp: nc.tensor.matmul(
                        Tp[:, hh * T:(hh + 1) * T], cb[:, C_TRI:C_TRI + 128], SPt[:, sp, hh * T:(hh + 1) * T],
                        start=first, stop=False, skip_group_check=True), reads=[SP_b[sp], cb_b], writes=[T_b])

            def s3_g(b):
                g = cnt["g"] % NG_
                cnt["g"] += 1
                st[b]["g"] = g
                k.op(k.act, lambda g=g: nc.scalar.activation(out=Gt[:, g, :], in_=Tp[:, :], func=AF.Exp, scale=-1.0),
                     reads=[T_b], writes=[G_b[g]])

            def s3_omt(b):
                i, J = items[b]
                if J == 0:
                    return
                sp = st[b]["sp"]
                for hh in range(2):
                    k.op(k.pe, lambda hh=hh, sp=sp: nc.tensor.matmul(
                        Tp[:, hh * T:(hh + 1) * T], cb[:, C_OMT:C_OMT + 128], SPt[:, sp, hh * T:(hh + 1) * T],
                        start=False, stop=False, skip_group_check=True), reads=[SP_b[sp], cb_b], writes=[T_b])

            def s4_a(b):
                e, g = st[b]["e"], st[b]["g"]
                a = cnt["a"] % NA
                cnt["a"] += 1
                st[b]["a"] = a
                k.op(k.pool, lambda e=e, g=g, a=a: nc.gpsimd.tensor_tensor(out=At[:, a, 0:T], in0=Et[:, e, 0:T],
                                                                           in1=Gt[:, g, 0:T], op=ALU.mult),
                     reads=[E_b[e], G_b[g]], writes=[A_b[a]])
                k.op(k.dve, lambda e=e, g=g, a=a: nc.vector.tensor_tensor(out=At[:, a, T:2 * T], in0=Et[:, e, T:2 * T],
                                                                          in1=Gt[:, g, T:2 * T], op=ALU.mult),
                     reads=[E_b[e], G_b[g]], writes=[A_b[a]])

            def s4_av(b):
                i, J = items[b]
                first = (J == 4 * i + 3)
                Ot, O_b = Obanks[i % 2]
                a = st[b]["a"]
                for hh in range(2):
                    pb = 64 * hh
                    k.op(k.pe, lambda pb=pb, a=a, Ot=Ot, hh=hh: nc.tensor.matmul(
                        Ot[pb:pb + 64, :], vc[:, J, pb:pb + 64], At[:, a, hh * T:(hh + 1) * T], start=first,
                        stop=(J == 0), skip_group_check=True), reads=[vc_b, A_b[a]], writes=[O_b[hh]])
                if J == 0:
                    for hh in range(2):
                        pb = 64 * hh
                        k.op(k.dve, lambda pb=pb, Ot=Ot: nc.vector.tensor_copy(
                            out=oc[pb:pb + 64, i * T:(i + 1) * T], in_=Ot[pb:pb + 64, :]),
                            reads=[O_b[hh]], writes=[oc_b])

            for t in range(n + 4):
                if 0 <= t - 3 < n:
                    s3_tri(t - 3)
                    s3_g(t - 3)
                if t < n:
                    s1(t)
                if 0 <= t - 4 < n:
                    s4_av(t - 4)
                if 0 <= t - 3 < n:
                    s3_omt(t - 3)
                    s4_a(t - 3)
                if 0 <= t - 2 < n:
                    s2_sp(t - 2)
                if 0 <= t - 1 < n:
                    s2_e(t - 1)
            wr = [cx.dbuf("OT", i) for i in range(cx.NT)]
            k.dma(k.sp, OT[c * 128:(c + 1) * 128, :], oc[:], [oc_b], wr, oc_b, partial=(c > 0))
        k.release_stage()


def stage_ssd_a1(cx, li, Xi, iname, W, consts, ZS, DT, DA):
    nc, k = cx.nc, cx.k
    with (
        nc.sbuf_tensor(f"d1w{li}", [128, NCH, 1024 + NHD], BF16) as w,
        nc.sbuf_tensor(f"d1g{li}", [128, NCH], F32) as gain,
        nc.sbuf_tensor(f"d1p{li}", [NHD, 4], F32) as prm,
        nc.sbuf_tensor(f"d1x0{li}", [128, NCH, T], F32) as xt0,
        nc.sbuf_tensor(f"d1x1{li}", [128, NCH, T], F32) as xt1,
        nc.sbuf_tensor(f"d1h{li}", [128, NCH, T], BF16) as ht,
        nc.sbuf_tensor(f"d1q{li}", [128, NCH, T], BF16) as sq,
        nc.sbuf_tensor(f"d1r{li}", [128, T], F32) as rstd,
        nc.sbuf_tensor(f"d1z{li}", [128, 8, T], BF16) as zs,
        nc.sbuf_tensor(f"d1dt{li}", [NHD, 2, T], F32) as dd,
    ):
        w_b, gain_b, prm_b = k.buf("d1w"), k.buf("d1g"), k.buf("d1p")
        xts = [(xt0, k.buf("d1x0")), (xt1, k.buf("d1x1"))]
        ht_b, sq_b, rstd_b, zs_b, dd_b = k.buf("d1h"), k.buf("d1q"), k.buf("d1r"), k.buf("d1z"), k.buf("d1dt")
        for c in range(NCH):
            k.dma(k.pool, w[:, c, 0:1024], W["in"][c * 128:(c + 1) * 128, 0:1024], [], [w_b], w_b, partial=(c > 0))
            k.dma(k.pool, w[:, c, 1024:1024 + NHD], W["in"][c * 128:(c + 1) * 128, 3072:3072 + NHD], [], [w_b], w_b,
                  partial=True)
        load_vec(cx, k.sp, gain[:], gain_b, W["norm"], NCH)
        k.dma(k.sp, prm[:, 0:1], W["dt_bias"].rearrange("(p o) -> p o", o=1), [], [prm_b], prm_b,
              allow_slow_non_contiguous=True)
        k.dma(k.sp, prm[:, 1:2], W["a_log"].rearrange("(p o) -> p o", o=1), [], [prm_b], prm_b, partial=True,
              allow_slow_non_contiguous=True)
        k.op(k.act, lambda: nc.scalar.activation(out=prm[:, 2:3], in_=prm[:, 1:2], func=AF.Exp),
             reads=[prm_b], writes=[prm_b])
        k.op(k.dve, lambda: nc.vector.tensor_scalar(out=prm[:, 2:3], in0=prm[:, 2:3], scalar1=-1.0, scalar2=None,
                                                    op0=ALU.mult), reads=[prm_b], writes=[prm_b])

        def load(i):
            xt, xt_b = xts[i % 2]
            k.dma(k.sp, xt[:], xtile(Xi, i), [cx.dbuf(iname, i)], [xt_b], xt_b)

        load(0)
        for i in range(cx.NT):
            xt, xt_b = xts[i % 2]
            if i + 1 < cx.NT:
                load(i + 1)
            emit_norm(cx, xt, xt_b, (gain, gain_b), ht, ht_b, sq, sq_b, rstd, rstd_b, consts)
            for oc in range(8):
                pz, pz_b = cx.psum()
                for c in range(NCH):
                    k.op(k.pe, lambda c=c: nc.tensor.matmul(pz[:], w[:, c, oc * 128:(oc + 1) * 128], ht[:, c, :],
                                                             start=(c == 0), stop=(c == NCH - 1)),
                         reads=[w_b, ht_b], writes=[pz_b], inc=(c == NCH - 1))
                k.op(k.act, lambda: nc.scalar.activation(out=zs[:, oc, :], in_=pz[:], func=AF.Silu),
                     reads=[pz_b], writes=[zs_b])
            pd, pd_b = cx.psum()
            for c in range(NCH):
                k.op(k.pe, lambda c=c: nc.tensor.matmul(pd[0:NHD, :], w[:, c, 1024:1024 + NHD], ht[:, c, :],
                                                         start=(c == 0), stop=(c == NCH - 1)),
                     reads=[w_b, ht_b], writes=[pd_b], inc=(c == NCH - 1))
            k.op(k.act, lambda: nc.scalar.activation(out=dd[:, 0, :], in_=pd[0:NHD, :], func=AF.Exp,
                                                     bias=prm[:, 0:1], scale=1.0),
                 reads=[pd_b, prm_b], writes=[dd_b])
            k.op(k.act, lambda: nc.scalar.activation(out=dd[:, 0, :], in_=dd[:, 0, :], func=AF.Ln,
                                                     bias=consts["onef"][0:NHD, :], scale=1.0),
                 reads=[dd_b, consts["onef_b"]], writes=[dd_b])
            k.op(k.dve, lambda: nc.vector.tensor_scalar(out=dd[:, 1, :], in0=dd[:, 0, :], scalar1=prm[:, 2:3],
                                                        scalar2=None, op0=ALU.mult),
                 reads=[dd_b, prm_b], writes=[dd_b])
            k.dma(k.sp, ZS.rearrange("(c p) t -> p c t", p=128)[:, :, i * T:(i + 1) * T], zs[:], [zs_b],
                  [cx.dbuf("ZS", i)], zs_b)
            k.dma(k.sp, DT[:, i * T:(i + 1) * T], dd[:, 0, :], [dd_b], [cx.dbuf("DT", i)], dd_b)
            k.dma(k.sp, DA[:, i * T:(i + 1) * T], dd[:, 1, :], [dd_b], [cx.dbuf("DA", i)], dd_b)
        k.release_stage()


def stage_ssd_a2(cx, li, Xi, iname, W, consts, XS, BT, CT):
    nc, k = cx.nc, cx.k
    cf, cf_b = consts["cf"], consts["cf_b"]
    with (
        nc.sbuf_tensor(f"d2w{li}", [128, NCH, 2048], BF16) as w,
        nc.sbuf_tensor(f"d2g{li}", [128, NCH], F32) as gain,
        nc.sbuf_tensor(f"d2cr{li}", [64, 128], F32) as cwr,
        nc.sbuf_tensor(f"d2br{li}", [16, 128], F32) as cbr,
        nc.sbuf_tensor(f"d2cw{li}", [128, 4, 16], F32) as cw,
        nc.sbuf_tensor(f"d2cb{li}", [128, 16], F32) as cbias,
        nc.sbuf_tensor(f"d2x0{li}", [128, NCH, T], F32) as xt0,
        nc.sbuf_tensor(f"d2x1{li}", [128, NCH, T], F32) as xt1,
        nc.sbuf_tensor(f"d2h{li}", [128, NCH, T], BF16) as ht,
        nc.sbuf_tensor(f"d2q{li}", [128, NCH, T], BF16) as sq,
        nc.sbuf_tensor(f"d2r{li}", [128, T], F32) as rstd,
        nc.sbuf_tensor(f"d2hl{li}", [128, 16, 4], F32) as halo,
        nc.sbuf_tensor(f"d2wk0{li}", [128, 4 + T], F32) as wk0,
        nc.sbuf_tensor(f"d2wk1{li}", [128, 4 + T], F32) as wk1,
        nc.sbuf_tensor(f"d2ac0{li}", [128, T], F32) as ac0,
        nc.sbuf_tensor(f"d2ac1{li}", [128, T], F32) as ac1,
        nc.sbuf_tensor(f"d2xo{li}", [128, 16, T], BF16) as xo,
    ):
        w_b, gain_b, cwr_b, cbr_b, cw_b, cbias_b = (k.buf("d2w"), k.buf("d2g"), k.buf("d2cr"), k.buf("d2br"),
                                                    k.buf("d2cw"), k.buf("d2cb"))
        xts = [(xt0, k.buf("d2x0")), (xt1, k.buf("d2x1"))]
        ht_b, sq_b, rstd_b, halo_b, xo_b = k.buf("d2h"), k.buf("d2q"), k.buf("d2r"), k.buf("d2hl"), k.buf("d2xo")
        wks = [(wk0, k.buf("d2wk0")), (wk1, k.buf("d2wk1"))]
        acs = [(ac0, k.buf("d2ac0")), (ac1, k.buf("d2ac1"))]
        for c in range(NCH):
            k.dma(k.pool, w[:, c, :], W["in"][c * 128:(c + 1) * 128, 1024:3072], [], [w_b], w_b, partial=(c > 0))
        load_vec(cx, k.sp, gain[:], gain_b, W["norm"], NCH)
        k.dma(k.sp, cwr[:], W["conv_w"].rearrange("k (c p) -> (k c) p", p=128), [], [cwr_b], cwr_b)
        k.dma(k.sp, cbr[:], W["conv_b"].rearrange("(c p) -> c p", p=128), [], [cbr_b], cbr_b)
        pt, pt_b = cx.psum()
        k.op(k.pe, lambda: nc.tensor.transpose(pt[:, 0:64], cwr[:], cf[0:64, C_ID:C_ID + 64]),
             reads=[cwr_b, cf_b], writes=[pt_b])
        k.op(k.dve, lambda: nc.vector.tensor_copy(out=cw[:].rearrange("p k c -> p (k c)"), in_=pt[:, 0:64]),
             reads=[pt_b], writes=[cw_b])
        pt2, pt2_b = cx.psum()
        k.op(k.pe, lambda: nc.tensor.transpose(pt2[:, 0:16], cbr[:], cf[0:16, C_ID:C_ID + 16]),
             reads=[cbr_b, cf_b], writes=[pt2_b])
        k.op(k.dve, lambda: nc.vector.tensor_copy(out=cbias[:], in_=pt2[:, 0:16]), reads=[pt2_b], writes=[cbias_b])
        k.op(k.pool, lambda: nc.gpsimd.memset(halo[:], 0.0), reads=[], writes=[halo_b])

        def load(i):
            xt, xt_b = xts[i % 2]
            k.dma(k.sp, xt[:], xtile(Xi, i), [cx.dbuf(iname, i)], [xt_b], xt_b)

        load(0)
        n = 0
        for i in range(cx.NT):
            xt, xt_b = xts[i % 2]
            if i + 1 < cx.NT:
                load(i + 1)
            emit_norm(cx, xt, xt_b, (gain, gain_b), ht, ht_b, sq, sq_b, rstd, rstd_b, consts)
            for ch in range(16):
                pp, pp_b = cx.psum()
                for c in range(NCH):
                    k.op(k.pe, lambda c=c: nc.tensor.matmul(pp[:], w[:, c, ch * 128:(ch + 1) * 128], ht[:, c, :],
                                                             start=(c == 0), stop=(c == NCH - 1)),
                         reads=[w_b, ht_b], writes=[pp_b], inc=(c == NCH - 1))
                wk, wk_b = wks[n % 2]
                ac, ac_b = acs[n % 2]
                n += 1
                k.op(k.pool, lambda wk=wk: nc.gpsimd.tensor_copy(out=wk[:, 0:4], in_=halo[:, ch, :]),
                     reads=[halo_b], writes=[wk_b])
                k.op(k.act, lambda wk=wk: nc.scalar.copy(out=wk[:, 4:4 + T], in_=pp[:]), reads=[pp_b], writes=[wk_b])
                k.op(k.pool, lambda wk=wk: nc.gpsimd.tensor_copy(out=halo[:, ch, :], in_=wk[:, T:T + 4]),
                     reads=[wk_b], writes=[halo_b])
                k.op(k.dve, lambda wk=wk, ac=ac: nc.vector.tensor_scalar(
                    out=ac[:], in0=wk[:, 4:4 + T], scalar1=cw[:, 3, ch:ch + 1], scalar2=cbias[:, ch:ch + 1],
                    op0=ALU.mult, op1=ALU.add), reads=[wk_b, cw_b, cbias_b], writes=[ac_b])
                for tap in (2, 1, 0):
                    sh = 3 - tap
                    k.op(k.dve, lambda wk=wk, ac=ac, tap=tap, sh=sh: nc.vector.scalar_tensor_tensor(
                        out=ac[:], in0=wk[:, 4 - sh:4 - sh + T], scalar=cw[:, tap, ch:ch + 1], in1=ac[:],
                        op0=ALU.mult, op1=ALU.add), reads=[wk_b, cw_b, ac_b], writes=[ac_b])
                k.op(k.act, lambda ac=ac: nc.scalar.activation(out=xo[:, ch, :], in_=ac[:], func=AF.Silu),
                     reads=[ac_b], writes=[xo_b])
            k.dma(k.sp, XS.rearrange("(c p) t -> p c t", p=128)[:, :, i * T:(i + 1) * T], xo[:, 0:8, :], [xo_b],
                  [cx.dbuf("XS", i)], xo_b)
            k.dma(k.sp, BT.rearrange("(c p) t -> p c t", p=128)[:, :, i * T:(i + 1) * T], xo[:, 8:12, :], [xo_b],
                  [cx.dbuf("BT", i)], xo_b)
            k.dma(k.sp, CT.rearrange("(c p) t -> p c t", p=128)[:, :, i * T:(i + 1) * T], xo[:, 12:16, :], [xo_b],
                  [cx.dbuf("CT", i)], xo_b)
        k.release_stage()


def stage_ssd_scan(cx, li, W, consts, ZS, DT, DA, XS, BT, CT, GN):
    nc, k = cx.nc, cx.k
    S = cx.S
    L = 256
    NCK = S // L
    cb, cb_b = consts["cbf"], consts["cbf_b"]
    cf, cf_b = consts["cf"], consts["cf_b"]
    with ExitStack() as es:
        onesf = es.enter_context(nc.sbuf_tensor(f"s3of{li}", [128, 256], F32))
        selH = es.enter_context(nc.sbuf_tensor(f"s3sh{li}", [NHD, NHD, 128], F32))
        prm = es.enter_context(nc.sbuf_tensor(f"s3pr{li}", [NHD, 2], F32))
        rhsD = es.enter_context(nc.sbuf_tensor(f"s3rd{li}", [NHD, 8], F32))
        Dvec = es.enter_context(nc.sbuf_tensor(f"s3dv{li}", [128, 8], F32))
        onorm = es.enter_context(nc.sbuf_tensor(f"s3on{li}", [128, 8], F32))
        stf = es.enter_context(nc.sbuf_tensor(f"s3st{li}", [128, NHD, 64], F32))
        stb = es.enter_context(nc.sbuf_tensor(f"s3sb{li}", [128, NHD, 64], BF16))
        dtc = es.enter_context(nc.sbuf_tensor(f"s3dt{li}", [NHD, L], F32))
        dac = es.enter_context(nc.sbuf_tensor(f"s3da{li}", [NHD, L], F32))
        acum = es.enter_context(nc.sbuf_tensor(f"s3ac{li}", [NHD, L], F32))
        dg = es.enter_context(nc.sbuf_tensor(f"s3dg{li}", [NHD, NHD], F32))
        tm = es.enter_context(nc.sbuf_tensor(f"s3tm{li}", [128, 4, NHD], F32))
        tmpd = es.enter_context(nc.sbuf_tensor(f"s3td{li}", [128, 2, NHD], F32))
        wdt = es.enter_context(nc.sbuf_tensor(f"s3wd{li}", [128, 2, NHD], F32))
        xs = es.enter_context(nc.sbuf_tensor(f"s3xs{li}", [128, 8, L], BF16))
        zs = es.enter_context(nc.sbuf_tensor(f"s3zs{li}", [128, 8, L], BF16))
        Bt = es.enter_context(nc.sbuf_tensor(f"s3bt{li}", [128, NG, L], BF16))
        Ct = es.enter_context(nc.sbuf_tensor(f"s3ct{li}", [128, NG, L], BF16))
        Gm0 = es.enter_context(nc.sbuf_tensor(f"s3gm0{li}", [128, 2, L], F32))
        Gm1 = es.enter_context(nc.sbuf_tensor(f"s3gm1{li}", [128, 2, L], F32))
        xstm0 = es.enter_context(nc.sbuf_tensor(f"s3xt0{li}", [128, 2, 256], BF16))
        xstm1 = es.enter_context(nc.sbuf_tensor(f"s3xt1{li}", [128, 2, 256], BF16))
        Bw0 = es.enter_context(nc.sbuf_tensor(f"s3bw0{li}", [128, 2, 128], BF16))
        Bw1 = es.enter_context(nc.sbuf_tensor(f"s3bw1{li}", [128, 2, 128], BF16))
        xw0 = es.enter_context(nc.sbuf_tensor(f"s3xw0{li}", [128, 2, 256], BF16))
        xw1 = es.enter_context(nc.sbuf_tensor(f"s3xw1{li}", [128, 2, 256], BF16))
        Dd0 = es.enter_context(nc.sbuf_tensor(f"s3d0{li}", [128, 4, 2, L], F32))
        Dd1 = es.enter_context(nc.sbuf_tensor(f"s3d1{li}", [128, 4, 2, L], F32))
        Mh0 = es.enter_context(nc.sbuf_tensor(f"s3m0{li}", [128, 4, 2, L], BF16))
        Mh1 = es.enter_context(nc.sbuf_tensor(f"s3m1{li}", [128, 4, 2, L], BF16))
        Ea0 = es.enter_context(nc.sbuf_tensor(f"s3e0{li}", [128, 4, L], F32))
        Ea1 = es.enter_context(nc.sbuf_tensor(f"s3e1{li}", [128, 4, L], F32))
        Ch0 = es.enter_context(nc.sbuf_tensor(f"s3c0{li}", [128, 4, L], BF16))
        Ch1 = es.enter_context(nc.sbuf_tensor(f"s3c1{li}", [128, 4, L], BF16))
        yv = es.enter_context(nc.sbuf_tensor(f"s3yv{li}", [128, 2, L], F32))
        yv2 = es.enter_context(nc.sbuf_tensor(f"s3yw{li}", [128, 2, L], F32))
        sqg = es.enter_context(nc.sbuf_tensor(f"s3sq{li}", [128, 2, L], BF16))
        rs = es.enter_context(nc.sbuf_tensor(f"s3rs{li}", [128, L], F32))
        gn = es.enter_context(nc.sbuf_tensor(f"s3gn{li}", [128, 8, L], BF16))
        B = k.buf
        onesf_b, selH_b, prm_b, rhsD_b, Dvec_b, onorm_b = B("onesf"), B("selH"), B("s3pr"), B("rhsD"), B("Dvec"), B("onorm")
        stf_b = [B(f"stf{h}") for h in range(NHD)]
        stb_b = [B(f"stb{h}") for h in range(NHD)]
        dtc_b, dac_b, acum_b, dg_b, tm_b, tmpd_b, wdt_b = (B("dtc"), B("dac"), B("acum"), B("dg"), B("tm"), B("tmpd"),
                                                           B("wdt"))
        xs_b, zs_b, Bt_b, Ct_b = B("xs"), B("zs"), B("Bt"), B("Ct")
        Gms = [(Gm0, B("Gm0")), (Gm1, B("Gm1"))]
        xstms = [(xstm0, B("xstm0")), (xstm1, B("xstm1"))]
        Bws = [(Bw0, B("Bw0")), (Bw1, B("Bw1"))]
        xws = [(xw0, B("xw0")), (xw1, B("xw1"))]
        gcount = [0]
        Mh3_b = [B("Mh3_0"), B("Mh3_1")]
        Dds = [(Dd0, B("Dd0")), (Dd1, B("Dd1"))]
        Mhs = [(Mh0, B("Mh0")), (Mh1, B("Mh1"))]
        Eas = [(Ea0, B("Ea0")), (Ea1, B("Ea1"))]
        Chs = [(Ch0, B("Ch0")), (Ch1, B("Ch1"))]
        yv_b, sqg_b, rs_b, gn_b = B("yv"), B("sqg"), B("rs"), B("gn")
        yvs = [(yv, yv_b), (yv2, B("yv2"))]
        CB = [{"dtc": (dtc, dtc_b), "dac": (dac, dac_b), "acum": (acum, acum_b), "dg": (dg, dg_b), "tm": (tm, tm_b),
               "tmpd": (tmpd, tmpd_b), "wdt": (wdt, wdt_b), "xs": (xs, xs_b), "zs": (zs, zs_b), "Bt": (Bt, Bt_b),
               "Ct": (Ct, Ct_b)}]
        d2 = {}
        for nm, shp, dt_ in (("dtc", [NHD, L], F32), ("dac", [NHD, L], F32), ("acum", [NHD, L], F32),
                             ("dg", [NHD, NHD], F32), ("tm", [128, 4, NHD], F32), ("tmpd", [128, 2, NHD], F32),
                             ("wdt", [128, 2, NHD], F32), ("xs", [128, 8, L], BF16), ("zs", [128, 8, L], BF16),
                             ("Bt", [128, NG, L], BF16), ("Ct", [128, NG, L], BF16)):
            d2[nm] = (es.enter_context(nc.sbuf_tensor(f"s3{nm}2_{li}", shp, dt_)), B(nm + "2"))
        CB.append(d2)
        gn2 = es.enter_context(nc.sbuf_tensor(f"s3gn2_{li}", [128, 8, L], BF16))
        GNs = [(gn, gn_b), (gn2, B("gn2"))]
        rot = [0]

        def ps():
            p = cx.ps[rot[0] % 4]
            rot[0] += 1
            return p

        ACB = [cx.ps[4], cx.ps[5]]

        Ybanks = [cx.ps[6], cx.ps[7]]
        k.op(k.pool, lambda: nc.gpsimd.memset(onesf[:], 1.0), reads=[], writes=[onesf_b])
        k.op(k.pool, lambda: nc.gpsimd.memset(stf[:], 0.0), reads=[], writes=stf_b)
        k.op(k.pool, lambda: nc.gpsimd.memset(stb[:], 0.0), reads=[], writes=stb_b)
        for h in range(NHD):
            k.op(k.pool, lambda h=h: nc.gpsimd.tensor_scalar(out=selH[:, h, :], in0=onesf[0:NHD, 0:128],
                                                             scalar1=cf[0:NHD, C_ID + h:C_ID + h + 1], scalar2=None,
                                                             op0=ALU.mult),
                 reads=[onesf_b, cf_b], writes=[selH_b])
        k.dma(k.sp, prm[:, 0:1], W["d"].rearrange("(p o) -> p o", o=1), [], [prm_b], prm_b,
              allow_slow_non_contiguous=True)
        load_vec(cx, k.sp, onorm[:], onorm_b, W["out_norm"], 8)
        k.op(k.dve, lambda: nc.vector.tensor_scalar(out=rhsD[:], in0=cf[0:NHD, C_SB2:C_SB2 + 8], scalar1=prm[:, 0:1],
                                                    scalar2=None, op0=ALU.mult),
             reads=[cf_b, prm_b], writes=[rhsD_b])
        pD, pD_b = ps()
        k.op(k.pe, lambda: nc.tensor.matmul(pD[:, 0:8], cf[0:NHD, C_SA:C_SA + 128], rhsD[:], start=True, stop=True),
             reads=[cf_b, rhsD_b], writes=[pD_b])
        k.op(k.dve, lambda: nc.vector.tensor_copy(out=Dvec[:], in_=pD[:, 0:8]), reads=[pD_b], writes=[Dvec_b])

        def chunk_pro(c):
            CBd = CB[c % 2]
            dtc, dtc_b = CBd["dtc"]
            dac, dac_b = CBd["dac"]
            acum, acum_b = CBd["acum"]
            dg, dg_b = CBd["dg"]
            tm, tm_b = CBd["tm"]
            tmpd, tmpd_b = CBd["tmpd"]
            wdt, wdt_b = CBd["wdt"]
            xs, xs_b = CBd["xs"]
            zs, zs_b = CBd["zs"]
            Bt, Bt_b = CBd["Bt"]
            Ct, Ct_b = CBd["Ct"]
            ti = c // (T // L)
            cs = slice(c * L, (c + 1) * L)
            k.dma(k.sp, dtc[:], DT[:, cs], [cx.dbuf("DT", ti)], [dtc_b], dtc_b)
            k.dma(k.sp, dac[:], DA[:, cs], [cx.dbuf("DA", ti)], [dac_b], dac_b)
            k.dma(k.sp, xs[:], XS.rearrange("(c p) t -> p c t", p=128)[:, :, cs], [cx.dbuf("XS", ti)], [xs_b], xs_b)
            k.dma(k.sp, zs[:], ZS.rearrange("(c p) t -> p c t", p=128)[:, :, cs], [cx.dbuf("ZS", ti)], [zs_b], zs_b)
            k.dma(k.sp, Bt[:], BT.rearrange("(c p) t -> p c t", p=128)[:, :, cs], [cx.dbuf("BT", ti)], [Bt_b], Bt_b)
            k.dma(k.sp, Ct[:], CT.rearrange("(c p) t -> p c t", p=128)[:, :, cs], [cx.dbuf("CT", ti)], [Ct_b], Ct_b)
            k.op(k.dve, lambda: nc.vector.tensor_tensor_scan(out=acum[:], data0=onesf[0:NHD, 0:L], data1=dac[:],
                                                             initial=0.0, op0=ALU.mult, op1=ALU.add),
                 reads=[onesf_b, dac_b], writes=[acum_b])
            ptm, ptm_b = ps()
            for q in range(4):
                src, src_b = (dtc, dtc_b) if q < 2 else (acum, acum_b)
                sb_ = q % 2
                k.op(k.pe, lambda q=q, src=src, sb_=sb_: nc.tensor.transpose(
                    ptm[:, q * NHD:(q + 1) * NHD], src[:, sb_ * 128:(sb_ + 1) * 128], cf[0:NHD, C_ID:C_ID + NHD]),
                    reads=[src_b, cf_b], writes=[ptm_b])
            k.op(k.dve, lambda: nc.vector.tensor_copy(out=tm[:].rearrange("p q h -> p (q h)"), in_=ptm[:, 0:4 * NHD]),
                 reads=[ptm_b], writes=[tm_b])
            k.op(k.dve, lambda: nc.vector.tensor_scalar(out=dg[:], in0=cf[0:NHD, C_ID:C_ID + NHD],
                                                        scalar1=acum[:, L - 1:L], scalar2=None, op0=ALU.mult),
                 reads=[cf_b, acum_b], writes=[dg_b])
            pal, pal_b = ps()
            k.op(k.pe, lambda: nc.tensor.matmul(pal[:, 0:NHD], onesf[0:NHD, 0:128], dg[:], start=True, stop=True),
                 reads=[onesf_b, dg_b], writes=[pal_b])
            for sb_ in range(2):
                k.op(k.dve, lambda sb_=sb_: nc.vector.tensor_tensor(out=tmpd[:, sb_, :], in0=pal[:, 0:NHD],
                                                                    in1=tm[:, 2 + sb_, :], op=ALU.subtract),
                     reads=[pal_b, tm_b], writes=[tmpd_b])
            k.op(k.act, lambda: nc.scalar.activation(out=tmpd[:], in_=tmpd[:], func=AF.Exp),
                 reads=[tmpd_b], writes=[tmpd_b])
            k.op(k.dve, lambda: nc.vector.tensor_tensor(out=wdt[:], in0=tmpd[:], in1=tm[:, 0:2, :], op=ALU.mult),
                 reads=[tmpd_b, tm_b], writes=[wdt_b])


        chunk_pro(0)
        for c in range(NCK):
            ti = c // (T // L)
            cs = slice(c * L, (c + 1) * L)
            CBd = CB[c % 2]
            dtc, dtc_b = CBd["dtc"]
            dac, dac_b = CBd["dac"]
            acum, acum_b = CBd["acum"]
            tm, tm_b = CBd["tm"]
            wdt, wdt_b = CBd["wdt"]
            xs, xs_b = CBd["xs"]
            zs, zs_b = CBd["zs"]
            Bt, Bt_b = CBd["Bt"]
            Ct, Ct_b = CBd["Ct"]
            gn, gn_b = GNs[c % 2]

            def prologue(g):
                gb = gcount[0] % 2
                Gm, Gm_b = Gms[gb]
                xdt, xdt_b = xstms[gb]
                xw, xw_b = xws[gb]
                Btm, Btm_b = Bws[gb]
                pacs = []
                pacs_all[g] = pacs

                def acb(hl):
                    pacb, pacb_b = ACB[hl // 2]
                    hi = hl % 2
                    pacs.append((pacb, pacb_b, hi))
                    h = 4 * g + hl
                    k.op(k.pe, lambda pacb=pacb, h=h, hi=hi: nc.tensor.matmul(
                        pacb[:, hi * L:(hi + 1) * L], selH[:, h, :], acum[:], start=True, stop=True,
                        skip_group_check=True), reads=[selH_b, acum_b], writes=[pacb_b])

                pG, pG_b = ps()
                for sb_ in range(2):
                    k.op(k.pe, lambda sb_=sb_, pG=pG: nc.tensor.matmul(
                        pG[:, sb_ * L:(sb_ + 1) * L], Bt[:, g, sb_ * 128:(sb_ + 1) * 128], Ct[:, g, :],
                        start=True, stop=True, skip_group_check=True), reads=[Bt_b, Ct_b], writes=[pG_b])
                    acb(sb_)
                k.op(k.dve, lambda pG=pG: nc.vector.tensor_tensor(
                    out=Gm[:].rearrange("p s t -> p (s t)"), in0=pG[:, 0:2 * L], in1=cf[:, C_SM:C_SM + 512],
                    op=ALU.mult), reads=[pG_b, cf_b], writes=[Gm_b])
                px, px_b = ps()
                pxb = px[:].bitcast(BF16)
                for sb_ in range(2):
                    for ci in range(2):
                        col = sb_ * 256 + ci * 128
                        k.op(k.pe, lambda sb_=sb_, ci=ci, col=col: nc.tensor.transpose(
                            pxb[:, col:col + 128], xs[:, 2 * g + ci, sb_ * 128:(sb_ + 1) * 128], cb[:, C_ID:C_ID + 128]),
                            reads=[xs_b, cb_b], writes=[px_b])
                acb(2)
                pB, pB_b = ps()
                pBb = pB[:].bitcast(BF16)
                for sb_ in range(2):
                    k.op(k.pe, lambda sb_=sb_: nc.tensor.transpose(
                        pBb[:, sb_ * 128:(sb_ + 1) * 128], Bt[:, g, sb_ * 128:(sb_ + 1) * 128], cb[:, C_ID:C_ID + 128]),
                        reads=[Bt_b, cb_b], writes=[pB_b])
                acb(3)
                xin = pxb[:, 0:512].rearrange("p (s h d) -> p s h d", s=2, h=4)
                k.op(k.dve, lambda: nc.vector.tensor_tensor(
                    out=xdt[:].rearrange("p s (h d) -> p s h d", h=4), in0=xin,
                    in1=tm[:, 0:2, 4 * g:4 * g + 4].unsqueeze(3).to_broadcast([128, 2, 4, 64]), op=ALU.mult),
                    reads=[px_b, tm_b], writes=[xdt_b])
                k.op(k.dve, lambda: nc.vector.tensor_tensor(
                    out=xw[:].rearrange("p s (h d) -> p s h d", h=4), in0=xin,
                    in1=wdt[:, 0:2, 4 * g:4 * g + 4].unsqueeze(3).to_broadcast([128, 2, 4, 64]), op=ALU.mult),
                    reads=[px_b, wdt_b], writes=[xw_b])
                k.op(k.act, lambda: nc.scalar.copy(out=Btm[:].rearrange("p s n -> p (s n)"), in_=pBb[:, 0:256]),
                     reads=[pB_b], writes=[Btm_b])
                return gb

            def heads_front(g, gb):
                Gm, Gm_b = Gms[gb]
                Dd, Dd_b = Dds[gb]
                Mh, Mh_b = Mhs[gb]
                Ea, Ea_b = Eas[gb]
                Ch, Ch_b = Chs[gb]
                pacs = pacs_all[g]
                for hl in range(4):
                    h = 4 * g + hl
                    pacb, pacb_b, hi = pacs[hl]
                    for sb_ in range(2):
                        k.op(k.act, lambda sb_=sb_, pacb=pacb, hl=hl, h=h, hi=hi: nc.scalar.activation(
                            out=Dd[:, hl, sb_, :], in_=pacb[:, hi * L:(hi + 1) * L], func=AF.Relu,
                            bias=tm[:, 2 + sb_, h:h + 1], scale=-1.0), reads=[pacb_b, tm_b], writes=[Dd_b])
                for hp in range(2):
                    pacb, pacb_b, _ = pacs[2 * hp]
                    k.op(k.act, lambda pacb=pacb, hp=hp: nc.scalar.activation(
                        out=Ea[:, 2 * hp:2 * hp + 2, :].rearrange("p h t -> p (h t)"), in_=pacb[:, 0:2 * L],
                        func=AF.Exp), reads=[pacb_b], writes=[Ea_b])
                k.op(k.act, lambda: nc.scalar.activation(out=Dd[:].rearrange("p h s t -> p (h s t)"),
                                                         in_=Dd[:].rearrange("p h s t -> p (h s t)"), func=AF.Exp,
                                                         scale=-1.0),
                     reads=[Dd_b], writes=[Dd_b])
                k.op(k.pool, lambda: nc.gpsimd.tensor_tensor(
                    out=Ch[:], in0=Ea[:], in1=Ct[:, g, :].unsqueeze(1).to_broadcast([128, 4, L]), op=ALU.mult),
                    reads=[Ct_b, Ea_b], writes=[Ch_b])
                k.op(k.dve, lambda: nc.vector.tensor_tensor(
                    out=Mh[:, 0:3], in0=Dd[:, 0:3], in1=Gm[:].unsqueeze(1).to_broadcast([128, 3, 2, L]), op=ALU.mult),
                    reads=[Dd_b, Gm_b], writes=[Mh_b])
                k.op(k.pool, lambda: nc.gpsimd.tensor_tensor(
                    out=Mh[:, 3:4], in0=Dd[:, 3:4], in1=Gm[:].unsqueeze(1).to_broadcast([128, 1, 2, L]), op=ALU.mult),
                    reads=[Dd_b, Gm_b], writes=[Mh3_b[gb]])

            def heads_back(g, gb):
                xdt, xdt_b = xstms[gb]
                xw, xw_b = xws[gb]
                Btm, Btm_b = Bws[gb]
                Mh, Mh_b = Mhs[gb]
                Ea, Ea_b = Eas[gb]
                Ch, Ch_b = Chs[gb]
                Yt, Y_b = Ybanks[gb]
                for hl in range(4):
                    h = 4 * g + hl
                    ci, po = hl // 2, (hl % 2) * 64
                    yo = Yt[po:po + 64, ci * L:(ci + 1) * L]
                    xcol = ci * 128 + po
                    mhb = Mh_b if hl < 3 else Mh3_b[gb]
                    k.op(k.pe, lambda yo=yo, hl=hl, xcol=xcol: nc.tensor.matmul(
                        yo, xdt[:, 0, xcol:xcol + 64], Mh[:, hl, 0, :], start=True, stop=False, skip_group_check=True),
                        reads=[xdt_b, mhb], writes=[Y_b], inc=False)
                    k.op(k.pe, lambda yo=yo, hl=hl, xcol=xcol: nc.tensor.matmul(
                        yo, xdt[:, 1, xcol:xcol + 64], Mh[:, hl, 1, :], start=False, stop=False, skip_group_check=True),
                        reads=[xdt_b, mhb], writes=[Y_b], inc=False)
                    k.op(k.pe, lambda yo=yo, hl=hl, h=h: nc.tensor.matmul(
                        yo, stb[:, h, :], Ch[:, hl, :], start=False, stop=True, skip_group_check=True),
                        reads=[stb_b[h], Ch_b], writes=[Y_b])
                pS, pS_b = ps()
                for sb_ in range(2):
                    k.op(k.pe, lambda sb_=sb_, pS=pS: nc.tensor.matmul(
                        pS[:, 0:256], Btm[:, sb_, :], xw[:, sb_, :], start=(sb_ == 0), stop=(sb_ == 1)),
                        reads=[Btm_b, xw_b], writes=[pS_b], inc=(sb_ == 1))
                sfb = [stf_b[4 * g + x] for x in range(4)]
                sbb = [stb_b[4 * g + x] for x in range(4)]
                k.op(k.dve, lambda: nc.vector.tensor_tensor(
                    out=stf[:, 4 * g:4 * g + 4, :], in0=stf[:, 4 * g:4 * g + 4, :],
                    in1=Ea[:, :, L - 1:L].to_broadcast([128, 4, 64]), op=ALU.mult),
                    reads=sfb + [Ea_b], writes=sfb)
                k.op(k.dve, lambda pS=pS: nc.vector.tensor_tensor(
                    out=stf[:, 4 * g:4 * g + 4, :], in0=stf[:, 4 * g:4 * g + 4, :],
                    in1=pS[:, 0:256].rearrange("p (h d) -> p h d", h=4), op=ALU.add),
                    reads=sfb + [pS_b], writes=sfb)
                k.op(k.act, lambda: nc.scalar.copy(out=stb[:, 4 * g:4 * g + 4, :], in_=stf[:, 4 * g:4 * g + 4, :]),
                     reads=sfb, writes=sbb)
                yv, yv_b = yvs[gb]
                for ci in range(2):
                    cc = 2 * g + ci
                    k.op(k.dve, lambda ci=ci, cc=cc: nc.vector.scalar_tensor_tensor(
                        out=yv[:, ci, :], in0=xs[:, cc, :], scalar=Dvec[:, cc:cc + 1], in1=Yt[:, ci * L:(ci + 1) * L],
                        op0=ALU.mult, op1=ALU.add), reads=[xs_b, Dvec_b, Y_b], writes=[yv_b])
                k.op(k.dve, lambda: nc.vector.tensor_tensor(out=yv[:], in0=yv[:], in1=zs[:, 2 * g:2 * g + 2, :],
                                                            op=ALU.mult), reads=[yv_b, zs_b], writes=[yv_b])

            def epi(g, gb):
                yv, yv_b = yvs[gb]
                k.op(k.act, lambda: nc.scalar.activation(out=sqg[:], in_=yv[:], func=AF.Square),
                     reads=[yv_b], writes=[sqg_b])
                pss, pss_b = ps()
                for ci in range(2):
                    k.op(k.pe, lambda ci=ci, pss=pss: nc.tensor.matmul(pss[:, 0:L], consts["ones"][:], sqg[:, ci, :],
                                                                        start=(ci == 0), stop=(ci == 1)),
                         reads=[sqg_b, consts["ones_b"]], writes=[pss_b], inc=(ci == 1))
                k.op(k.act, lambda pss=pss: nc.scalar.activation(out=rs[:], in_=pss[:, 0:L], func=AF.Ln,
                                                                 bias=consts["eps"][:], scale=1.0 / 256),
                     reads=[pss_b, consts["eps_b"]], writes=[rs_b])
                k.op(k.act, lambda: nc.scalar.activation(out=rs[:], in_=rs[:], func=AF.Exp, scale=-0.5),
                     reads=[rs_b], writes=[rs_b])
                for ci in range(2):
                    cc = 2 * g + ci
                    k.op(k.dve, lambda ci=ci, cc=cc: nc.vector.scalar_tensor_tensor(
                        out=gn[:, cc, :], in0=yv[:, ci, :], scalar=onorm[:, cc:cc + 1], in1=rs[:],
                        op0=ALU.mult, op1=ALU.mult), reads=[yv_b, onorm_b, rs_b], writes=[gn_b])

            gbs = {}
            pacs_all = {}
            if SSD_PIPE:
                gbs[0] = prologue(0)
                gcount[0] += 1
                heads_front(0, gbs[0])
                for g in range(NG):
                    if g + 1 < NG:
                        gbs[g + 1] = prologue(g + 1)
                        gcount[0] += 1
                        heads_front(g + 1, gbs[g + 1])
                    heads_back(g, gbs[g])
                    if g >= 1:
                        epi(g - 1, gbs[g - 1])
                    if g == 0 and c + 1 < NCK:
                        chunk_pro(c + 1)
                epi(NG - 1, gbs[NG - 1])
            else:
                for g in range(NG):
                    gbs[g] = prologue(g)
                    gcount[0] += 1
                    heads_front(g, gbs[g])
                    heads_back(g, gbs[g])
                    epi(g, gbs[g])
            k.dma(k.sp, GN.rearrange("(c p) t -> p c t", p=128)[:, :, cs], gn[:], [gn_b], [cx.dbuf("GN", ti)], gn_b,
                  partial=(c % 2 == 1))
        k.release_stage()


WEIGHT_SPECS = [
    ("mix_norm", [4, 1024]), ("ffn_norm", [4, 1024]),
    ("pool_in", [2, 1024, 512]), ("pool_group", [2, 4, 128, 256]), ("pool_scale", [2, 1024]),
    ("ssd_in", [1, 1024, 3088]), ("ssd_conv_w", [1, 4, 2048]), ("ssd_conv_b", [1, 2048]),
    ("ssd_dt_bias", [1, 16]), ("ssd_a_log", [1, 16]), ("ssd_d", [1, 16]),
    ("ssd_out_norm", [1, 1024]), ("ssd_out", [1, 1024, 1024]),
    ("sb_qkv", [1, 1024, 1536]), ("sb_q_norm", [1, 64]), ("sb_k_norm", [1, 64]), ("sb_out", [1, 512, 1024]),
    ("ffn_gate", [4, 1024, 1408]), ("ffn_up", [4, 1024, 1408]), ("ffn_down", [4, 1408, 1024]),
]


def local_weights(inp, r):
    f = lambda a: np.ascontiguousarray(np.asarray(a, dtype=np.float32))
    cat = np.concatenate
    w = {}
    for n in ("mix_norm", "ffn_norm", "pool_scale", "sb_q_norm", "sb_k_norm"):
        w[n] = f(inp[n])
    pin = np.asarray(inp["pool_in"])
    w["pool_in"] = f(cat([pin[:, :, (2 * g + r) * 128:(2 * g + r + 1) * 128] for g in range(4)], axis=2))
    w["pool_group"] = f(np.asarray(inp["pool_group"])[:, :, r * 128:(r + 1) * 128, :])
    si = np.asarray(inp["ssd_in"])
    w["ssd_in"] = f(cat([si[:, :, r * 1024:(r + 1) * 1024], si[:, :, 2048 + r * 1024:2048 + (r + 1) * 1024],
                         si[:, :, 4096 + r * 512:4096 + (r + 1) * 512], si[:, :, 5120 + r * 512:5120 + (r + 1) * 512],
                         si[:, :, 6144 + r * 16:6144 + (r + 1) * 16]], axis=2))
    cwv = np.asarray(inp["ssd_conv_w"])
    w["ssd_conv_w"] = f(cat([cwv[:, :, r * 1024:(r + 1) * 1024], cwv[:, :, 2048 + r * 512:2048 + (r + 1) * 512],
                             cwv[:, :, 3072 + r * 512:3072 + (r + 1) * 512]], axis=2))
    cbv = np.asarray(inp["ssd_conv_b"])
    w["ssd_conv_b"] = f(cat([cbv[:, r * 1024:(r + 1) * 1024], cbv[:, 2048 + r * 512:2048 + (r + 1) * 512],
                             cbv[:, 3072 + r * 512:3072 + (r + 1) * 512]], axis=1))
    for n in ("ssd_dt_bias", "ssd_a_log", "ssd_d"):
        w[n] = f(np.asarray(inp[n])[:, r * 16:(r + 1) * 16])
    w["ssd_out_norm"] = f(np.asarray(inp["ssd_out_norm"])[:, r * 1024:(r + 1) * 1024])
    w["ssd_out"] = f(np.asarray(inp["ssd_out"])[:, r * 1024:(r + 1) * 1024, :])
    q = np.asarray(inp["sb_qkv"])
    w["sb_qkv"] = f(cat([q[:, :, r * 512:(r + 1) * 512], q[:, :, 1024 + r * 512:1024 + (r + 1) * 512],
                         q[:, :, 2048 + r * 512:2048 + (r + 1) * 512]], axis=2))
    w["sb_out"] = f(np.asarray(inp["sb_out"])[:, r * 512:(r + 1) * 512, :])
    w["ffn_gate"] = f(np.asarray(inp["ffn_gate"])[:, :, r * 1408:(r + 1) * 1408])
    w["ffn_up"] = f(np.asarray(inp["ffn_up"])[:, :, r * 1408:(r + 1) * 1408])
    w["ffn_down"] = f(np.asarray(inp["ffn_down"])[:, r * 1408:(r + 1) * 1408, :])
    return w


def build_program(S, layers, do_mixer=True, do_ffn=True):
    nc = bass.Bass("TRN2", target_bir_lowering=False, num_devices=8)
    cx = Ctx(nc, S)
    k = cx.k
    NT = S // T
    xT = nc.dram_tensor("xT", [NT, D, T], F32, kind="ExternalInput").ap()
    yT = nc.dram_tensor("yT", [NT, D, T], F32, kind="ExternalOutput").ap()
    Wt = {n: nc.dram_tensor(n, shp, F32, kind="ExternalInput").ap() for n, shp in WEIGHT_SPECS}
    cx.Q = nc.dram_tensor("resQ", [NT, D, T], F32, kind="Internal").ap()
    R = [nc.dram_tensor(f"resR{i}", [NT, D, T], F32, kind="Internal").ap() for i in range(2)]
    ones = nc.alloc_sbuf_tensor("c_ones", [128, 128], BF16)
    ones_b = k.buf("ones")
    k.op(k.pool, lambda: nc.gpsimd.memset(ones[:], 1.0), reads=[], writes=[ones_b])
    epst = nc.alloc_sbuf_tensor("c_eps", [128, 1], F32)
    eps_b = k.buf("eps")
    k.op(k.pool, lambda: nc.gpsimd.memset(epst[:], EPS), reads=[], writes=[eps_b])
    onef = nc.alloc_sbuf_tensor("c_onef", [128, 1], F32)
    onef_b = k.buf("onef")
    k.op(k.pool, lambda: nc.gpsimd.memset(onef[:], 1.0), reads=[], writes=[onef_b])
    cst = nc.dram_tensor("cst", [128, C_TOT], F32, kind="ExternalInput").ap()
    cbf = nc.alloc_sbuf_tensor("c_cbf", [128, C_TOT], BF16)
    cf = nc.alloc_sbuf_tensor("c_cf", [128, C_F32W], F32)
    cbf_b, cf_b = k.buf("cbf"), k.buf("cf")
    k.dma(k.pool, cbf[:], cst[:, :], [], [cbf_b], cbf_b)
    k.dma(k.sp, cf[:], cst[:, 0:C_F32W], [], [cf_b], cf_b)
    k.stage_bufs = []
    consts = {"ones": ones, "ones_b": ones_b, "eps": epst, "eps_b": eps_b, "onef": onef, "onef_b": onef_b,
              "cbf": cbf, "cbf_b": cbf_b, "cf": cf, "cf_b": cf_b}
    scr = {}

    def scratch(name, shape, dt=BF16):
        if name not in scr:
            scr[name] = nc.dram_tensor("scr_" + name, shape, dt, kind="Internal").ap()
        return scr[name]

    cur, cur_name = xT, "xT"
    nstage = [0]

    def nxt():
        nstage[0] += 1
        return R[nstage[0] % 2], f"R{nstage[0]}"

    for idx, li in enumerate(layers):
        kind, j = li % 3, li // 3
        if do_mixer:
            mo, mo_name = nxt()
            if kind == 0:
                W = {"in": Wt["pool_in"][j], "group": Wt["pool_group"][j], "scale": Wt["pool_scale"][j],
                     "norm": Wt["mix_norm"][li]}
                stage_pool(cx, li, cur, cur_name, mo, mo_name, W, consts)
            elif kind == 2:
                QT, KT, V, OT = (scratch("QT", [512, S]), scratch("KT", [512, S]), scratch("V", [S, 512]),
                                 scratch("OT", [512, S]))
                W = {"qkv": Wt["sb_qkv"][j], "qn": Wt["sb_q_norm"][j], "kn": Wt["sb_k_norm"][j],
                     "norm": Wt["mix_norm"][li]}
                stage_sb_qkv(cx, li, cur, cur_name, W, consts, QT, KT, V)
                stage_sb_attn(cx, li, consts, QT, KT, V, OT)
                stage_outproj(cx, f"sb{li}", cur, cur_name, mo, mo_name, OT, "OT", Wt["sb_out"][j], 4)
            else:
                ZS, XS, GN = scratch("ZS", [1024, S]), scratch("XS", [1024, S]), scratch("GN", [1024, S])
                BT, CT = scratch("BT", [512, S]), scratch("CT", [512, S])
                DT, DA = scratch("DT", [NHD, S], F32), scratch("DA", [NHD, S], F32)
                W = {"in": Wt["ssd_in"][j], "conv_w": Wt["ssd_conv_w"][j], "conv_b": Wt["ssd_conv_b"][j],
                     "dt_bias": Wt["ssd_dt_bias"][j], "a_log": Wt["ssd_a_log"][j], "d": Wt["ssd_d"][j],
                     "out_norm": Wt["ssd_out_norm"][j], "norm": Wt["mix_norm"][li]}
                stage_ssd_a1(cx, li, cur, cur_name, W, consts, ZS, DT, DA)
                stage_ssd_a2(cx, li, cur, cur_name, W, consts, XS, BT, CT)
                stage_ssd_scan(cx, li, W, consts, ZS, DT, DA, XS, BT, CT, GN)
                stage_outproj(cx, f"ssd{li}", cur, cur_name, mo, mo_name, GN, "GN", Wt["ssd_out"][j], 8)
            cur, cur_name = mo, mo_name
        if do_ffn:
            Xo, oname = nxt()
            Wf = {"gate": Wt["ffn_gate"][li], "up": Wt["ffn_up"][li], "down": Wt["ffn_down"][li],
                  "norm": Wt["ffn_norm"][li]}
            stage_ffn(cx, li, cur, cur_name, Xo, oname, Wf, consts)
            cur, cur_name = Xo, oname
    cp_b = k.buf("outcopy")
    for i in range(NT):
        k.dma(k.sp, yT[i], cur[i], [cx.dbuf(cur_name, i)], [cx.dbuf("yT", i)], cp_b, is_out=True)
    k.finish()
    return nc


def make_in_maps(inputs, S):
    x = np.asarray(inputs["x"], dtype=np.float32)[:, :S]
    Bn = x.shape[0]
    NT = S // T
    cst = make_consts()
    lw = [local_weights(inputs, r) for r in range(2)]
    in_maps = []
    for c in range(8):
        b, r = (c // 2) % Bn, c % 2
        xt = np.ascontiguousarray(x[b].T.reshape(D, NT, T).transpose(1, 0, 2))
        m = {"xT": xt, "cst": cst}
        m.update(lw[r])
        in_maps.append(m)
    return in_maps


def untile(yt):
    NT = yt.shape[0]
    return np.ascontiguousarray(yt.transpose(1, 0, 2).reshape(D, NT * T).T)


_PROG_CACHE = {}


def kernel(**inputs):
    x = np.asarray(inputs["x"], dtype=np.float32)
    Bn, S, _ = x.shape
    layers = list(range(DEPTH))
    key = (S, tuple(layers))
    if key not in _PROG_CACHE:
        _PROG_CACHE[key] = build_program(S, layers)
    nc = _PROG_CACHE[key]
    in_maps = make_in_maps(inputs, S)
    res = run_bass_kernel_spmd(nc, in_maps, core_ids=list(range(8)))
    out = np.empty_like(x)
    for b in range(Bn):
        out[b] = untile(res.results[2 * b]["yT"])
    return out
```

```python
import numpy as np
from contextlib import ExitStack
import concourse.bass as bass
import concourse.mybir as mybir
from concourse.bass_utils import run_bass_kernel_spmd

F32 = mybir.dt.float32
BF16 = mybir.dt.bfloat16
AF = mybir.ActivationFunctionType
ALU = mybir.AluOpType

D = 1024
NCH = 8
T = 512
EPS = 1e-6
FFN_H = 2816
FFN_HC = 22
SSD_DI = 2048
SSD_IN = 6176
SEQ = 8192
DEPTH = 4
FH = 11
PCH = 4
NG = 4
NHD = 16
SBP = 4
PAIRS = [[0, 1], [2, 3], [4, 5], [6, 7]]
SSD_PIPE = True


class Buf:
    __slots__ = ("name", "w", "r", "dsem", "dcnt", "q")

    def __init__(self, name):
        self.name = name
        self.w = {}
        self.r = {}
        self.dsem = None
        self.dcnt = 0
        self.q = None


class EngW:
    def __init__(self, nc, eng, name):
        self.eng = eng
        self.name = name
        self.sem = nc.alloc_semaphore("tl_" + name)
        self.cnt = 0
        self.seen = {}


class K:
    def __init__(self, nc):
        self.nc = nc
        self.pe = EngW(nc, nc.tensor, "pe")
        self.act = EngW(nc, nc.scalar, "act")
        self.dve = EngW(nc, nc.vector, "dve")
        self.pool = EngW(nc, nc.gpsimd, "pool")
        self.sp = EngW(nc, nc.sync, "sp")
        self.out_events = []
        self.nbuf = 0
        self.nsem = 0
        self.cc_sem = None
        self.cc_cnt = 0
        self.free_sems = {}
        self.stage_bufs = []

    def barrier(self):
        engs = [self.pe, self.act, self.dve, self.pool, self.sp]
        evs = [(e.sem, e.cnt) for e in engs if e.cnt > 0]
        for b in self.stage_bufs:
            if b.dsem is not None and b.dcnt > 0:
                evs.append((b.dsem, b.dcnt))
        for e in engs:
            self._wait(e, evs)

    def release_stage(self):
        self.barrier()
        for b in self.stage_bufs:
            if b.dsem is not None:
                self.free_sems.setdefault(b.q.name, []).append((b.dsem, b.dcnt))
                b.dsem = None
        self.stage_bufs = []

    def buf(self, name=None):
        self.nbuf += 1
        return Buf(name or f"b{self.nbuf}")

    def _wait(self, E, events, own=False):
        for sem, cnt in events:
            if sem is E.sem and (not own or E is self.pe):
                continue
            key = id(sem)
            if E.seen.get(key, 0) < cnt:
                E.eng.wait_ge(sem, cnt)
                E.seen[key] = cnt

    def op(self, E, fn, reads=(), writes=(), inc=True):
        for b in reads:
            self._wait(E, list(b.w.values()), own=True)
        for b in writes:
            self._wait(E, list(b.w.values()), own=True)
            self._wait(E, list(b.r.values()))
        ins = fn()
        if inc:
            E.cnt += 1
            ins.then_inc(E.sem, 1)
            seq = E.cnt
        else:
            seq = E.cnt + 1
        ev = (E.sem, seq)
        k = id(E.sem)
        for b in reads:
            b.r[k] = ev
        for b in writes:
            b.w = {k: ev}
            b.r = {}
        return ins

    def dma(self, Q, out, in_, reads, writes, sb, partial=False, is_out=False, **kw):
        for b in reads:
            self._wait(Q, list(b.w.values()))
        for b in writes:
            if not partial:
                self._wait(Q, list(b.w.values()))
            self._wait(Q, list(b.r.values()))
        if sb.dsem is None:
            fl = self.free_sems.get(Q.name, [])
            if fl:
                sem, cnt = fl.pop()
                if Q.seen.get(id(sem), 0) < cnt:
                    Q.eng.wait_ge(sem, cnt)
                    Q.seen[id(sem)] = cnt
                sb.dsem, sb.dcnt = sem, cnt
            else:
                self.nsem += 1
                sb.dsem = self.nc.alloc_semaphore(f"dq{self.nsem}")
            sb.q = Q
            self.stage_bufs.append(sb)
        assert sb.q is Q
        sb.dcnt += 16
        Q.eng.dma_start(out=out, in_=in_, **kw).then_inc(sb.dsem, 16)
        ev = (sb.dsem, sb.dcnt)
        k = id(sb.dsem)
        for b in reads:
            b.r[k] = ev
        for b in writes:
            if partial:
                b.w[k] = ev
            else:
                b.w = {k: ev}
            b.r = {}
        if is_out:
            self.out_events.append(ev)

    def allreduce(self, src, dst, reads, writes):
        Q = self.pool
        for b in reads:
            self._wait(Q, list(b.w.values()))
        for b in writes:
            self._wait(Q, list(b.w.values()))
            self._wait(Q, list(b.r.values()))
        if self.cc_sem is None:
            self.cc_sem = self.nc.alloc_semaphore("cc_sem")
            self.cc_cnt = 0
        self.cc_cnt += 1
        self.nc.gpsimd.collective_compute("AllReduce", ALU.add, replica_groups=PAIRS, ins=[src],
                                          outs=[dst]).then_inc(self.cc_sem, 1)
        ev = (self.cc_sem, self.cc_cnt)
        kk = id(self.cc_sem)
        for b in reads:
            b.r[kk] = ev
        for b in writes:
            b.w = {kk: ev}
            b.r = {}

    def finish(self):
        last = {}
        for sem, cnt in self.out_events:
            k = id(sem)
            if k not in last or last[k][1] < cnt:
                last[k] = (sem, cnt)
        self._wait(self.sp, list(last.values()))


class Ctx:
    def __init__(self, nc, S):
        self.nc = nc
        self.S = S
        self.NT = S // T
        self.k = K(nc)
        self.dbufs = {}
        self.ps = []
        self.ps2 = []
        for i in range(4):
            t = nc.alloc_psum_tensor(f"psd{i}", [128, 1024], F32)
            self.ps2.append((t, self.k.buf(f"psd{i}")))
            for h in range(2):
                self.ps.append((t[:, h * 512:(h + 1) * 512], self.k.buf(f"ps{2 * i + h}")))
        self._psi = 0

    def dbuf(self, name, i):
        key = (name, i)
        if key not in self.dbufs:
            self.dbufs[key] = self.k.buf(f"{name}_{i}")
        return self.dbufs[key]

    def psum(self):
        p = self.ps[self._psi % 8]
        self._psi += 1
        return p


def xtile(X, i):
    return X[i].rearrange("(c p) t -> p c t", p=128)


def emit_norm(cx, xt, xt_b, gain, ht, ht_b, sq, sq_b, rstd, rstd_b, consts):
    nc, k = cx.nc, cx.k
    gain, gain_b = gain
    ones_bf, ones_b = consts["ones"], consts["ones_b"]
    k.op(k.act, lambda: nc.scalar.activation(out=sq[:], in_=xt[:], func=AF.Square),
         reads=[xt_b], writes=[sq_b])
    ps, ps_b = cx.psum()
    for c in range(NCH):
        k.op(k.pe, lambda c=c: nc.tensor.matmul(ps[:], ones_bf[:], sq[:, c, :],
                                                 start=(c == 0), stop=(c == NCH - 1)),
             reads=[sq_b, ones_b], writes=[ps_b], inc=(c == NCH - 1))
    k.op(k.act, lambda: nc.scalar.activation(out=rstd[:], in_=ps[:], func=AF.Ln, bias=consts["eps"][:],
                                             scale=1.0 / D),
         reads=[ps_b, consts["eps_b"]], writes=[rstd_b])
    k.op(k.act, lambda: nc.scalar.activation(out=rstd[:], in_=rstd[:], func=AF.Exp, scale=-0.5),
         reads=[rstd_b], writes=[rstd_b])
    for c in range(NCH):
        k.op(k.dve, lambda c=c: nc.vector.scalar_tensor_tensor(
            out=ht[:, c, :], in0=xt[:, c, :], scalar=gain[:, c:c + 1], in1=rstd[:],
            op0=ALU.mult, op1=ALU.mult),
            reads=[xt_b, rstd_b, gain_b], writes=[ht_b])


def load_vec(cx, Q, dst, dst_b, src_1d, n, partial=False):
    cx.k.dma(Q, dst, src_1d.rearrange("(c p) -> p c", p=128), reads=[], writes=[dst_b], sb=dst_b,
             partial=partial, allow_slow_non_contiguous=True)


def store_reduce(cx, xt, xt_b, i, Xo, oname):
    k = cx.k
    k.dma(k.sp, xtile(cx.Q, i), xt[:], [xt_b], [cx.dbuf("Q", i)], xt_b)
    k.allreduce(cx.Q[i], Xo[i], [cx.dbuf("Q", i)], [cx.dbuf(oname, i)])


def stage_ffn(cx, li, Xi, iname, Xo, oname, W, consts):
    nc, k = cx.nc, cx.k
    nhc = FH
    H = nhc * 128
    with (
        nc.sbuf_tensor(f"wg{li}", [128, NCH, H], BF16) as wg,
        nc.sbuf_tensor(f"wu{li}", [128, NCH, H], BF16) as wu,
        nc.sbuf_tensor(f"wd{li}", [128, nhc, D], BF16) as wd,
        nc.sbuf_tensor(f"fg{li}", [128, NCH], F32) as gain,
        nc.sbuf_tensor(f"fx0{li}", [128, NCH, T], F32) as xt0,
        nc.sbuf_tensor(f"fx1{li}", [128, NCH, T], F32) as xt1,
        nc.sbuf_tensor(f"fh{li}", [128, NCH, T], BF16) as ht0,
        nc.sbuf_tensor(f"fhb{li}", [128, NCH, T], BF16) as ht1,
        nc.sbuf_tensor(f"fs{li}", [128, NCH, T], BF16) as sq,
        nc.sbuf_tensor(f"fr{li}", [128, T], F32) as rstd,
        nc.sbuf_tensor(f"fa{li}", [128, nhc, T], BF16) as act,
        nc.sbuf_tensor(f"fsg0{li}", [128, T], F32) as sg0,
        nc.sbuf_tensor(f"fsg1{li}", [128, T], F32) as sg1,
    ):
        wg_b, wu_b, wd_b, gain_b = k.buf("wg"), k.buf("wu"), k.buf("wd"), k.buf("fgain")
        xts = [(xt0, k.buf("fx0")), (xt1, k.buf("fx1"))]
        sq_b, rstd_b, act_b = k.buf("fs"), k.buf("fr"), k.buf("fa")
        hts = [(ht0, k.buf("fh0")), (ht1, k.buf("fh1"))]
        sgs = [(sg0, k.buf("fsg0")), (sg1, k.buf("fsg1"))]
        for c in range(NCH):
            k.dma(k.pool, wg[:, c, :], W["gate"][c * 128:(c + 1) * 128, :], [], [wg_b], wg_b, partial=(c > 0))
            k.dma(k.pool, wu[:, c, :], W["up"][c * 128:(c + 1) * 128, :], [], [wu_b], wu_b, partial=(c > 0))
        for j in range(nhc):
            k.dma(k.pool, wd[:, j, :], W["down"][j * 128:(j + 1) * 128, :], [], [wd_b], wd_b, partial=(j > 0))
        load_vec(cx, k.sp, gain[:], gain_b, W["norm"], NCH)

        def load(i):
            xt, xt_b = xts[i % 2]
            k.dma(k.sp, xt[:], xtile(Xi, i), [cx.dbuf(iname, i)], [xt_b], xt_b)

        def norm(i):
            xt, xt_b = xts[i % 2]
            ht, ht_b = hts[i % 2]
            emit_norm(cx, xt, xt_b, (gain, gain_b), ht, ht_b, sq, sq_b, rstd, rstd_b, consts)

        load(0)
        norm(0)
        for i in range(cx.NT):
            xt, xt_b = xts[i % 2]
            ht, ht_b = hts[i % 2]
            if i + 1 < cx.NT:
                load(i + 1)
            for j in range(nhc):
                pg, pg_b = cx.psum()
                pu, pu_b = cx.psum()
                for c in range(NCH):
                    k.op(k.pe, lambda c=c: nc.tensor.matmul(pg[:], wg[:, c, j * 128:(j + 1) * 128], ht[:, c, :],
                                                             start=(c == 0), stop=(c == NCH - 1)),
                         reads=[wg_b, ht_b], writes=[pg_b], inc=(c == NCH - 1))
                for c in range(NCH):
                    k.op(k.pe, lambda c=c: nc.tensor.matmul(pu[:], wu[:, c, j * 128:(j + 1) * 128], ht[:, c, :],
                                                             start=(c == 0), stop=(c == NCH - 1)),
                         reads=[wu_b, ht_b], writes=[pu_b], inc=(c == NCH - 1))
                sg, sg_b = sgs[j % 2]
                k.op(k.act, lambda: nc.scalar.activation(out=sg[:], in_=pg[:], func=AF.Silu),
                     reads=[pg_b], writes=[sg_b])
                k.op(k.dve, lambda: nc.vector.tensor_tensor(out=act[:, j, :], in0=sg[:], in1=pu[:], op=ALU.mult),
                     reads=[sg_b, pu_b], writes=[act_b])
            if i + 1 < cx.NT:
                norm(i + 1)
            for m in range(NCH):
                po, po_b = cx.psum()
                for j in range(nhc):
                    k.op(k.pe, lambda j=j: nc.tensor.matmul(po[:], wd[:, j, m * 128:(m + 1) * 128], act[:, j, :],
                                                             start=(j == 0), stop=(j == nhc - 1)),
                         reads=[wd_b, act_b], writes=[po_b], inc=(j == nhc - 1))
                k.op(k.dve, lambda: nc.vector.scalar_tensor_tensor(out=xt[:, m, :], in0=xt[:, m, :], scalar=0.5,
                                                                   in1=po[:], op0=ALU.mult, op1=ALU.add),
                     reads=[po_b, xt_b], writes=[xt_b])
            store_reduce(cx, xt, xt_b, i, Xo, oname)
        k.release_stage()


def stage_pool(cx, li, Xi, iname, Xo, oname, W, consts):
    nc, k = cx.nc, cx.k
    E = 16
    with (
        nc.sbuf_tensor(f"pw{li}", [128, NCH, PCH * 128], BF16) as win,
        nc.sbuf_tensor(f"pg{li}", [128, 4, 256], BF16) as wgr,
        nc.sbuf_tensor(f"pgn{li}", [128, NCH], F32) as gain,
        nc.sbuf_tensor(f"psc{li}", [128, NCH], F32) as psc,
        nc.sbuf_tensor(f"pic{li}", [128, 4, E], F32) as invc,
        nc.sbuf_tensor(f"px0{li}", [128, NCH, T], F32) as xt0,
        nc.sbuf_tensor(f"px1{li}", [128, NCH, T], F32) as xt1,
        nc.sbuf_tensor(f"ph{li}", [128, NCH, T], BF16) as ht,
        nc.sbuf_tensor(f"pq{li}", [128, NCH, T], BF16) as sq,
        nc.sbuf_tensor(f"pr{li}", [128, T], F32) as rstd,
        nc.sbuf_tensor(f"pu{li}", [128, PCH, E + T], F32) as u,
        nc.sbuf_tensor(f"pa{li}", [128, PCH, E + T], F32) as A,
        nc.sbuf_tensor(f"pb{li}", [128, PCH, E + T], F32) as B,
        nc.sbuf_tensor(f"pp{li}", [128, PCH, T], BF16) as p,
        nc.sbuf_tensor(f"ptmp{li}", [128, E], F32) as tmp,
    ):
        win_b, wgr_b, gain_b, psc_b, invc_b = (k.buf("pwin"), k.buf("pwgr"), k.buf("pgain"), k.buf("psc"),
                                               k.buf("invc"))
        xts = [(xt0, k.buf("px0")), (xt1, k.buf("px1"))]
        ht_b, sq_b, rstd_b = k.buf("ph"), k.buf("pq"), k.buf("pr")
        u_b, A_b, B_b, p_b, tmp_b = k.buf("pu"), k.buf("pA"), k.buf("pB"), k.buf("pp"), k.buf("ptmp")
        for c in range(NCH):
            k.dma(k.pool, win[:, c, :], W["in"][c * 128:(c + 1) * 128, :], [], [win_b], win_b, partial=(c > 0))
        for g in range(4):
            k.dma(k.pool, wgr[:, g, :], W["group"][g, :, :], [], [wgr_b], wgr_b, partial=(g > 0))
        load_vec(cx, k.sp, gain[:], gain_b, W["norm"], NCH)
        load_vec(cx, k.sp, psc[:], psc_b, W["scale"], NCH)
        for g in range(4):
            w = 2 ** (g + 1)
            for t in range(E):
                k.op(k.pool, lambda g=g, t=t, w=w: nc.gpsimd.memset(invc[:, g, t:t + 1], 1.0 / min(t + 1, w)),
                     reads=[], writes=[invc_b])
        k.op(k.pool, lambda: nc.gpsimd.memset(u[:, :, 0:E], 0.0), reads=[], writes=[u_b])
        k.op(k.pool, lambda: nc.gpsimd.memset(A[:, :, 0:E], 0.0), reads=[], writes=[A_b])
        k.op(k.pool, lambda: nc.gpsimd.memset(B[:, :, 0:E], 0.0), reads=[], writes=[B_b])

        def load(i):
            xt, xt_b = xts[i % 2]
            k.dma(k.sp, xt[:], xtile(Xi, i), [cx.dbuf(iname, i)], [xt_b], xt_b)

        load(0)
        for i in range(cx.NT):
            xt, xt_b = xts[i % 2]
            if i + 1 < cx.NT:
                load(i + 1)
            emit_norm(cx, xt, xt_b, (gain, gain_b), ht, ht_b, sq, sq_b, rstd, rstd_b, consts)
            if i > 0:
                k.op(k.pool, lambda: nc.gpsimd.tensor_copy(out=u[:, :, 0:E], in_=u[:, :, T:T + E]),
                     reads=[u_b], writes=[u_b])
            for m in range(PCH):
                pu_, pu_b = cx.psum()
                for c in range(NCH):
                    k.op(k.pe, lambda c=c: nc.tensor.matmul(pu_[:], win[:, c, m * 128:(m + 1) * 128], ht[:, c, :],
                                                             start=(c == 0), stop=(c == NCH - 1)),
                         reads=[win_b, ht_b], writes=[pu_b], inc=(c == NCH - 1))
                k.op(k.act, lambda: nc.scalar.copy(out=u[:, m, E:E + T], in_=pu_[:]),
                     reads=[pu_b], writes=[u_b])
            k.op(k.act, lambda: nc.scalar.activation(out=xt[:], in_=xt[:], func=AF.Copy, scale=0.5), reads=[xt_b], writes=[xt_b])
            k.op(k.dve, lambda: nc.vector.tensor_tensor(out=A[:, :, 1:E + T], in0=u[:, :, 1:E + T],
                                                        in1=u[:, :, 0:E + T - 1], op=ALU.add),
                 reads=[u_b], writes=[A_b])
            k.op(k.pool, lambda: nc.gpsimd.tensor_tensor(out=B[:, 1:4, 3:E + T], in0=A[:, 1:4, 3:E + T],
                                                         in1=A[:, 1:4, 1:E + T - 2], op=ALU.add),
                 reads=[A_b], writes=[B_b])
            k.op(k.dve, lambda: nc.vector.tensor_tensor(out=A[:, 2:4, 7:E + T], in0=B[:, 2:4, 7:E + T],
                                                        in1=B[:, 2:4, 3:E + T - 4], op=ALU.add),
                 reads=[B_b], writes=[A_b])
            k.op(k.pool, lambda: nc.gpsimd.tensor_tensor(out=B[:, 3:4, 15:E + T], in0=A[:, 3:4, 15:E + T],
                                                         in1=A[:, 3:4, 7:E + T - 8], op=ALU.add),
                 reads=[A_b], writes=[B_b])
            srcs = [(A, A_b), (B, B_b), (A, A_b), (B, B_b)]
            for g in range(4):
                s_, s_b = srcs[g]
                w = 2 ** (g + 1)
                k.op(k.dve, lambda g=g, s_=s_, w=w: nc.vector.scalar_tensor_tensor(
                    out=p[:, g, :], in0=s_[:, g, E:E + T], scalar=1.0 / w,
                    in1=u[:, g, E:E + T], op0=ALU.mult, op1=ALU.subtract),
                    reads=[s_b, u_b], writes=[p_b])
                if i == 0:
                    k.op(k.dve, lambda g=g, s_=s_: nc.vector.tensor_tensor(
                        out=tmp[:], in0=s_[:, g, E:2 * E], in1=invc[:, g, :], op=ALU.mult),
                        reads=[s_b, invc_b], writes=[tmp_b])
                    k.op(k.dve, lambda g=g: nc.vector.tensor_tensor(
                        out=p[:, g, 0:E], in0=tmp[:], in1=u[:, g, E:2 * E], op=ALU.subtract),
                        reads=[tmp_b, u_b], writes=[p_b])
            for g in range(4):
                for mo in range(2):
                    cc = 2 * g + mo
                    py, py_b = cx.psum()
                    k.op(k.pe, lambda: nc.tensor.matmul(py[:], wgr[:, g, mo * 128:(mo + 1) * 128], p[:, g, :],
                                                        start=True, stop=True),
                         reads=[wgr_b, p_b], writes=[py_b])
                    k.op(k.dve, lambda cc=cc: nc.vector.scalar_tensor_tensor(
                        out=xt[:, cc, :], in0=py[:], scalar=psc[:, cc:cc + 1], in1=xt[:, cc, :],
                        op0=ALU.mult, op1=ALU.add),
                        reads=[py_b, psc_b, xt_b], writes=[xt_b])
            store_reduce(cx, xt, xt_b, i, Xo, oname)
        k.release_stage()


C_ID, C_SM, C_SA, C_SB2, C_TRI, C_OMT, C_BO, C_MB = 0, 128, 640, 768, 784, 912, 1040, 1168
C_F32W = 784
C_TOT = 1168 + 2048


def make_consts():
    c = np.zeros((128, C_TOT), np.float32)
    j = np.arange(128)
    c[:, C_ID:C_ID + 128] = np.eye(128)
    t256 = np.arange(256)
    c[:, C_SM:C_SM + 256] = (j[:, None] <= t256[None, :])
    c[:, C_SM + 256:C_SM + 512] = (128 + j[:, None] <= t256[None, :])
    k32 = np.arange(32)
    c[:32, C_SA:C_SA + 128] = ((k32[:, None] % 2) == (j[None, :] // 64))
    c[:32, C_SB2:C_SB2 + 16] = ((k32[:, None] // 2) == np.arange(16)[None, :])
    c[:, C_TRI:C_TRI + 128] = (j[:, None] >= j[None, :])
    c[:, C_OMT:C_OMT + 128] = (j[:, None] < j[None, :])
    c[:, C_BO:C_BO + 128] = ((j[:, None] // 64) == (j[None, :] // 64))
    t512 = np.arange(512)
    for o in range(4):
        valid = (128 * o + j[:, None]) < t512[None, :]
        c[:, C_MB + 512 * o:C_MB + 512 * (o + 1)] = np.where(valid, 0.0, -30000.0)
    return c


def stage_outproj(cx, tag, Xi, iname, Xo, oname, G, gname, Wd, KC):
    nc, k = cx.nc, cx.k
    with (
        nc.sbuf_tensor(f"ow{tag}", [128, KC, D], BF16) as w,
        nc.sbuf_tensor(f"ox0{tag}", [128, NCH, T], F32) as xt0,
        nc.sbuf_tensor(f"ox1{tag}", [128, NCH, T], F32) as xt1,
        nc.sbuf_tensor(f"og0{tag}", [128, KC, T], BF16) as g0,
        nc.sbuf_tensor(f"og1{tag}", [128, KC, T], BF16) as g1,
    ):
        w_b = k.buf("ow")
        xts = [(xt0, k.buf("ox0")), (xt1, k.buf("ox1"))]
        gs = [(g0, k.buf("og0")), (g1, k.buf("og1"))]
        for c in range(KC):
            k.dma(k.pool, w[:, c, :], Wd[c * 128:(c + 1) * 128, :], [], [w_b], w_b, partial=(c > 0))

        def load(i):
            xt, xt_b = xts[i % 2]
            gt, gt_b = gs[i % 2]
            k.dma(k.sp, xt[:], xtile(Xi, i), [cx.dbuf(iname, i)], [xt_b], xt_b)
            k.dma(k.sp, gt[:], G.rearrange("(c p) t -> p c t", p=128)[:, :, i * T:(i + 1) * T],
                  [cx.dbuf(gname, i)], [gt_b], gt_b)

        load(0)
        for i in range(cx.NT):
            xt, xt_b = xts[i % 2]
            gt, gt_b = gs[i % 2]
            if i + 1 < cx.NT:
                load(i + 1)
            for m in range(NCH):
                po, po_b = cx.psum()
                for c in range(KC):
                    k.op(k.pe, lambda c=c: nc.tensor.matmul(po[:], w[:, c, m * 128:(m + 1) * 128], gt[:, c, :],
                                                             start=(c == 0), stop=(c == KC - 1)),
                         reads=[w_b, gt_b], writes=[po_b], inc=(c == KC - 1))
                k.op(k.dve, lambda: nc.vector.scalar_tensor_tensor(out=xt[:, m, :], in0=xt[:, m, :], scalar=0.5,
                                                                   in1=po[:], op0=ALU.mult, op1=ALU.add),
                     reads=[po_b, xt_b], writes=[xt_b])
            store_reduce(cx, xt, xt_b, i, Xo, oname)
        k.release_stage()


def stage_sb_qkv(cx, li, Xi, iname, W, consts, QT, KT, V):
    nc, k = cx.nc, cx.k
    cb, cb_b = consts["cbf"], consts["cbf_b"]
    with (
        nc.sbuf_tensor(f"sw{li}", [128, NCH, 3 * 512], BF16) as w,
        nc.sbuf_tensor(f"sgn{li}", [128, NCH], F32) as gain,
        nc.sbuf_tensor(f"sgq{li}", [128, 2], F32) as gqk,
        nc.sbuf_tensor(f"sx0{li}", [128, NCH, T], F32) as xt0,
        nc.sbuf_tensor(f"sx1{li}", [128, NCH, T], F32) as xt1,
        nc.sbuf_tensor(f"sh{li}", [128, NCH, T], BF16) as ht,
        nc.sbuf_tensor(f"ssq{li}", [128, NCH, T], BF16) as sq,
        nc.sbuf_tensor(f"sr{li}", [128, T], F32) as rstd,
        nc.sbuf_tensor(f"sqt{li}", [128, SBP, T], BF16) as qt,
        nc.sbuf_tensor(f"skt{li}", [128, SBP, T], BF16) as kt,
        nc.sbuf_tensor(f"svt{li}", [128, 4, 512], BF16) as vt,
        nc.sbuf_tensor(f"ssc0{li}", [128, T], BF16) as sqc0,
        nc.sbuf_tensor(f"ssc1{li}", [128, T], BF16) as sqc1,
        nc.sbuf_tensor(f"srs0{li}", [128, T], F32) as rs0,
        nc.sbuf_tensor(f"srs1{li}", [128, T], F32) as rs1,
    ):
        w_b, gain_b, gqk_b = k.buf("sw"), k.buf("sgn"), k.buf("sgq")
        xts = [(xt0, k.buf("sx0")), (xt1, k.buf("sx1"))]
        ht_b, sq_b, rstd_b = k.buf("sh"), k.buf("ssq"), k.buf("sr")
        qt_b, kt_b, vt_b = k.buf("sqt"), k.buf("skt"), k.buf("svt")
        sqcs = [(sqc0, k.buf("ssc0")), (sqc1, k.buf("ssc1"))]
        rss = [(rs0, k.buf("srs0")), (rs1, k.buf("srs1"))]
        for c in range(NCH):
            k.dma(k.pool, w[:, c, :], W["qkv"][c * 128:(c + 1) * 128, :], [], [w_b], w_b, partial=(c > 0))
        load_vec(cx, k.sp, gain[:], gain_b, W["norm"], NCH)
        first = True
        for col, nm in ((0, "qn"), (1, "kn")):
            for hh in range(2):
                k.dma(k.sp, gqk[hh * 64:(hh + 1) * 64, col:col + 1], W[nm].rearrange("(p o) -> p o", o=1), [],
                      [gqk_b], gqk_b, partial=(not first), allow_slow_non_contiguous=True)
                first = False
        k.op(k.dve, lambda: nc.vector.tensor_scalar(out=gqk[:, 0:1], in0=gqk[:, 0:1], scalar1=0.125, scalar2=None,
                                                    op0=ALU.mult), reads=[gqk_b], writes=[gqk_b])

        def load(i):
            xt, xt_b = xts[i % 2]
            k.dma(k.sp, xt[:], xtile(Xi, i), [cx.dbuf(iname, i)], [xt_b], xt_b)

        load(0)
        n = 0
        for i in range(cx.NT):
            xt, xt_b = xts[i % 2]
            if i + 1 < cx.NT:
                load(i + 1)
            emit_norm(cx, xt, xt_b, (gain, gain_b), ht, ht_b, sq, sq_b, rstd, rstd_b, consts)
            for which, (dst, dst_b) in enumerate(((qt, qt_b), (kt, kt_b))):
                for oc in range(SBP):
                    c0 = which * 512 + oc * 128
                    pq, pq_b = cx.psum()
                    for c in range(NCH):
                        k.op(k.pe, lambda c=c: nc.tensor.matmul(pq[:], w[:, c, c0:c0 + 128], ht[:, c, :],
                                                                 start=(c == 0), stop=(c == NCH - 1)),
                             reads=[w_b, ht_b], writes=[pq_b], inc=(c == NCH - 1))
                    sqc, sqc_b = sqcs[n % 2]
                    rs, rs_b = rss[n % 2]
                    n += 1
                    k.op(k.act, lambda: nc.scalar.activation(out=sqc[:], in_=pq[:], func=AF.Square),
                         reads=[pq_b], writes=[sqc_b])
                    pss, pss_b = cx.psum()
                    k.op(k.pe, lambda: nc.tensor.matmul(pss[:], cb[:, C_BO:C_BO + 128], sqc[:], start=True, stop=True),
                         reads=[sqc_b, cb_b], writes=[pss_b])
                    k.op(k.act, lambda: nc.scalar.activation(out=rs[:], in_=pss[:], func=AF.Ln,
                                                             bias=consts["eps"][:], scale=1.0 / 64),
                         reads=[pss_b, consts["eps_b"]], writes=[rs_b])
                    k.op(k.act, lambda: nc.scalar.activation(out=rs[:], in_=rs[:], func=AF.Exp, scale=-0.5),
                         reads=[rs_b], writes=[rs_b])
                    k.op(k.dve, lambda: nc.vector.scalar_tensor_tensor(
                        out=dst[:, oc, :], in0=pq[:], scalar=gqk[:, which:which + 1], in1=rs[:],
                        op0=ALU.mult, op1=ALU.mult), reads=[pq_b, gqk_b, rs_b], writes=[dst_b])
            for blk in range(4):
                for half in range(1):
                    pv, pv_b = cx.psum()
                    c0 = 2 * 512
                    for c in range(NCH):
                        k.op(k.pe, lambda c=c: nc.tensor.matmul(pv[:], ht[:, c, blk * 128:(blk + 1) * 128],
                                                                 w[:, c, c0:c0 + 512], start=(c == 0),
                                                                 stop=(c == NCH - 1)),
                             reads=[w_b, ht_b], writes=[pv_b], inc=(c == NCH - 1))
                    k.op(k.act, lambda: nc.scalar.copy(out=vt[:, blk, half * 512:(half + 1) * 512], in_=pv[:]),
                         reads=[pv_b], writes=[vt_b])
            k.dma(k.sp, QT.rearrange("(c p) t -> p c t", p=128)[:, :, i * T:(i + 1) * T], qt[:], [qt_b],
                  [cx.dbuf("QT", i)], qt_b)
            k.dma(k.sp, KT.rearrange("(c p) t -> p c t", p=128)[:, :, i * T:(i + 1) * T], kt[:], [kt_b],
                  [cx.dbuf("KT", i)], kt_b)
            k.dma(k.sp, V[i * T:(i + 1) * T, :].rearrange("(b p) d -> p b d", p=128), vt[:], [vt_b],
                  [cx.dbuf("V", i)], vt_b)
        k.release_stage()


def stage_sb_attn(cx, li, consts, QT, KT, V, OT):
    nc, k = cx.nc, cx.k
    S = cx.S
    NB = S // 128
    cb, cb_b = consts["cbf"], consts["cbf_b"]
    NE, NSP, NG_, NA = 4, 4, 2, 3
    with (
        nc.sbuf_tensor(f"akc0{li}", [128, S], BF16) as kc0,
        nc.sbuf_tensor(f"aqc0{li}", [128, S], BF16) as qc0,
        nc.sbuf_tensor(f"avc0{li}", [128, NB, 128], BF16) as vc0,
        nc.sbuf_tensor(f"akc1{li}", [128, S], BF16) as kc1,
        nc.sbuf_tensor(f"aqc1{li}", [128, S], BF16) as qc1,
        nc.sbuf_tensor(f"avc1{li}", [128, NB, 128], BF16) as vc1,
        nc.sbuf_tensor(f"aoc{li}", [128, S], BF16) as oc,
        nc.sbuf_tensor(f"aE{li}", [128, NE, 2 * T], F32) as Et,
        nc.sbuf_tensor(f"aSP{li}", [128, NSP, 2 * T], BF16) as SPt,
        nc.sbuf_tensor(f"aG{li}", [128, NG_, 2 * T], F32) as Gt,
        nc.sbuf_tensor(f"aA{li}", [128, NA, 2 * T], BF16) as At,
    ):
        kqv = [(kc0, qc0, vc0, k.buf("akc0"), k.buf("aqc0"), k.buf("avc0")),
               (kc1, qc1, vc1, k.buf("akc1"), k.buf("aqc1"), k.buf("avc1"))]
        oc_b = k.buf("aoc")
        E_b = [k.buf(f"aE{x}") for x in range(NE)]
        SP_b = [k.buf(f"aSP{x}") for x in range(NSP)]
        G_b = [k.buf(f"aG{x}") for x in range(NG_)]
        A_b = [k.buf(f"aA{x}") for x in range(NA)]
        Zp = [cx.ps2[0], cx.ps2[1]]
        Tp, T_b = cx.ps2[2]
        Obanks = [(cx.ps[6][0], [k.buf("aO00"), k.buf("aO01")]), (cx.ps[7][0], [k.buf("aO10"), k.buf("aO11")])]
        one_ap = consts["onef"]
        cnt = {"z": 0, "e": 0, "sp": 0, "g": 0, "a": 0}

        def load(c):
            kc, qc, vc, kc_b, qc_b, vc_b = kqv[c % 2]
            rd = [cx.dbuf("KT", i) for i in range(cx.NT)]
            k.dma(k.sp, kc[:], KT[c * 128:(c + 1) * 128, :], rd, [kc_b], kc_b)
            rd = [cx.dbuf("QT", i) for i in range(cx.NT)]
            k.dma(k.sp, qc[:], QT[c * 128:(c + 1) * 128, :], rd, [qc_b], qc_b)
            rd = [cx.dbuf("V", i) for i in range(cx.NT)]
            k.dma(k.sp, vc[:], V[:, c * 128:(c + 1) * 128].rearrange("(b p) d -> p b d", p=128), rd, [vc_b], vc_b)

        load(0)
        for c in range(SBP):
            kc, qc, vc, kc_b, qc_b, vc_b = kqv[c % 2]
            if c + 1 < SBP:
                load(c + 1)
            items = [(i, J) for i in range(cx.NT) for J in range(4 * i + 3, -1, -1)]
            n = len(items)
            st = [dict() for _ in range(n)]

            def s1(b):
                i, J = items[b]
                diag = J >= 4 * i
                Z, Z_b = Zp[cnt["z"] % 2]
                cnt["z"] += 1
                st[b]["Z"] = (Z, Z_b)
                for hh in range(2):
                    pb = 64 * hh
                    Zh = Z[:, hh * T:(hh + 1) * T]
                    k.op(k.pe, lambda pb=pb, Zh=Zh: nc.tensor.matmul(
                        Zh, kc[pb:pb + 64, J * 128:(J + 1) * 128], qc[pb:pb + 64, i * T:(i + 1) * T],
                        start=True, stop=(not diag)), reads=[kc_b, qc_b], writes=[Z_b], inc=(not diag))
                    if diag:
                        o = J - 4 * i
                        k.op(k.pe, lambda Zh=Zh, o=o: nc.tensor.matmul(
                            Zh, cb[:, C_ID:C_ID + 128], cb[:, C_MB + 512 * o:C_MB + 512 * (o + 1)],
                            start=False, stop=True), reads=[cb_b], writes=[Z_b])

            def s2_e(b):
                Z, Z_b = st[b]["Z"]
                e = cnt["e"] % NE
                cnt["e"] += 1
                st[b]["e"] = e
                k.op(k.act, lambda Z=Z, e=e: nc.scalar.activation(out=Et[:, e, :], in_=Z[:, :], func=AF.Exp),
                     reads=[Z_b], writes=[E_b[e]])

            def s2_sp(b):
                e = st[b]["e"]
                sp = cnt["sp"] % NSP
                cnt["sp"] += 1
                st[b]["sp"] = sp
                k.op(k.act, lambda e=e, sp=sp: nc.scalar.activation(out=SPt[:, sp, :], in_=Et[:, e, :], func=AF.Ln,
                                                                    bias=one_ap[:], scale=1.0),
                     reads=[E_b[e], consts["onef_b"]], writes=[SP_b[sp]])

            def s3_tri(b):
                i, J = items[b]
                first = (J == 4 * i + 3)
                sp = st[b]["sp"]
                for hh in range(2):
                    k.op(k.pe, lambda hh=hh, sp=sp: nc.tensor.matmul(
                        Tp[:, hh * T:(hh + 1) * T], cb[:, C_TRI:C_TRI + 128], SPt[:, sp, hh * T:(hh + 1) * T],
                        start=first, stop=False, skip_group_check=True), reads=[SP_b[sp], cb_b], writes=[T_b])

            def s3_g(b):
                g = cnt["g"] % NG_
                cnt["g"] += 1
                st[b]["g"] = g
                k.op(k.act, lambda g=g: nc.scalar.activation(out=Gt[:, g, :], in_=Tp[:, :], func=AF.Exp, scale=-1.0),
                     reads=[T_b], writes=[G_b[g]])

            def s3_omt(b):
                i, J = items[b]
                if J == 0:
                    return
                sp = st[b]["sp"]
                for hh in range(2):
                    k.op(k.pe, lambda hh=hh, sp=sp: nc.tensor.matmul(
                        Tp[:, hh * T:(hh + 1) * T], cb[:, C_OMT:C_OMT + 128], SPt[:, sp, hh * T:(hh + 1) * T],
                        start=False, stop=False, skip_group_check=True), reads=[SP_b[sp], cb_b], writes=[T_b])

            def s4_a(b):
                e, g = st[b]["e"], st[b]["g"]
                a = cnt["a"] % NA
                cnt["a"] += 1
                st[b]["a"] = a
                k.op(k.pool, lambda e=e, g=g, a=a: nc.gpsimd.tensor_tensor(out=At[:, a, 0:T], in0=Et[:, e, 0:T],
                                                                           in1=Gt[:, g, 0:T], op=ALU.mult),
                     reads=[E_b[e], G_b[g]], writes=[A_b[a]])
                k.op(k.dve, lambda e=e, g=g, a=a: nc.vector.tensor_tensor(out=At[:, a, T:2 * T], in0=Et[:, e, T:2 * T],
                                                                          in1=Gt[:, g, T:2 * T], op=ALU.mult),
                     reads=[E_b[e], G_b[g]], writes=[A_b[a]])

            def s4_av(b):
                i, J = items[b]
                first = (J == 4 * i + 3)
                Ot, O_b = Obanks[i % 2]
                a = st[b]["a"]
                for hh in range(2):
                    pb = 64 * hh
                    k.op(k.pe, lambda pb=pb, a=a, Ot=Ot, hh=hh: nc.tensor.matmul(
                        Ot[pb:pb + 64, :], vc[:, J, pb:pb + 64], At[:, a, hh * T:(hh + 1) * T], start=first,
                        stop=(J == 0), skip_group_check=True), reads=[vc_b, A_b[a]], writes=[O_b[hh]])
                if J == 0:
                    for hh in range(2):
                        pb = 64 * hh
                        k.op(k.dve, lambda pb=pb, Ot=Ot: nc.vector.tensor_copy(
                            out=oc[pb:pb + 64, i * T:(i + 1) * T], in_=Ot[pb:pb + 64, :]),
                            reads=[O_b[hh]], writes=[oc_b])

            for t in range(n + 4):
                if 0 <= t - 3 < n:
                    s3_tri(t - 3)
                    s3_g(t - 3)
                if t < n:
                    s1(t)
                if 0 <= t - 4 < n:
                    s4_av(t - 4)
                if 0 <= t - 3 < n:
                    s3_omt(t - 3)
                    s4_a(t - 3)
                if 0 <= t - 2 < n:
                    s2_sp(t - 2)
                if 0 <= t - 1 < n:
                    s2_e(t - 1)
            wr = [cx.dbuf("OT", i) for i in range(cx.NT)]
            k.dma(k.sp, OT[c * 128:(c + 1) * 128, :], oc[:], [oc_b], wr, oc_b, partial=(c > 0))
        k.release_stage()


def stage_ssd_a1(cx, li, Xi, iname, W, consts, ZS, DT, DA):
    nc, k = cx.nc, cx.k
    with (
        nc.sbuf_tensor(f"d1w{li}", [128, NCH, 1024 + NHD], BF16) as w,
        nc.sbuf_tensor(f"d1g{li}", [128, NCH], F32) as gain,
        nc.sbuf_tensor(f"d1p{li}", [NHD, 4], F32) as prm,
        nc.sbuf_tensor(f"d1x0{li}", [128, NCH, T], F32) as xt0,
        nc.sbuf_tensor(f"d1x1{li}", [128, NCH, T], F32) as xt1,
        nc.sbuf_tensor(f"d1h{li}", [128, NCH, T], BF16) as ht,
        nc.sbuf_tensor(f"d1q{li}", [128, NCH, T], BF16) as sq,
        nc.sbuf_tensor(f"d1r{li}", [128, T], F32) as rstd,
        nc.sbuf_tensor(f"d1z{li}", [128, 8, T], BF16) as zs,
        nc.sbuf_tensor(f"d1dt{li}", [NHD, 2, T], F32) as dd,
    ):
        w_b, gain_b, prm_b = k.buf("d1w"), k.buf("d1g"), k.buf("d1p")
        xts = [(xt0, k.buf("d1x0")), (xt1, k.buf("d1x1"))]
        ht_b, sq_b, rstd_b, zs_b, dd_b = k.buf("d1h"), k.buf("d1q"), k.buf("d1r"), k.buf("d1z"), k.buf("d1dt")
        for c in range(NCH):
            k.dma(k.pool, w[:, c, 0:1024], W["in"][c * 128:(c + 1) * 128, 0:1024], [], [w_b], w_b, partial=(c > 0))
            k.dma(k.pool, w[:, c, 1024:1024 + NHD], W["in"][c * 128:(c + 1) * 128, 3072:3072 + NHD], [], [w_b], w_b,
                  partial=True)
        load_vec(cx, k.sp, gain[:], gain_b, W["norm"], NCH)
        k.dma(k.sp, prm[:, 0:1], W["dt_bias"].rearrange("(p o) -> p o", o=1), [], [prm_b], prm_b,
              allow_slow_non_contiguous=True)
        k.dma(k.sp, prm[:, 1:2], W["a_log"].rearrange("(p o) -> p o", o=1), [], [prm_b], prm_b, partial=True,
              allow_slow_non_contiguous=True)
        k.op(k.act, lambda: nc.scalar.activation(out=prm[:, 2:3], in_=prm[:, 1:2], func=AF.Exp),
             reads=[prm_b], writes=[prm_b])
        k.op(k.dve, lambda: nc.vector.tensor_scalar(out=prm[:, 2:3], in0=prm[:, 2:3], scalar1=-1.0, scalar2=None,
                                                    op0=ALU.mult), reads=[prm_b], writes=[prm_b])

        def load(i):
            xt, xt_b = xts[i % 2]
            k.dma(k.sp, xt[:], xtile(Xi, i), [cx.dbuf(iname, i)], [xt_b], xt_b)

        load(0)
        for i in range(cx.NT):
            xt, xt_b = xts[i % 2]
            if i + 1 < cx.NT:
                load(i + 1)
            emit_norm(cx, xt, xt_b, (gain, gain_b), ht, ht_b, sq, sq_b, rstd, rstd_b, consts)
            for oc in range(8):
                pz, pz_b = cx.psum()
                for c in range(NCH):
                    k.op(k.pe, lambda c=c: nc.tensor.matmul(pz[:], w[:, c, oc * 128:(oc + 1) * 128], ht[:, c, :],
                                                             start=(c == 0), stop=(c == NCH - 1)),
                         reads=[w_b, ht_b], writes=[pz_b], inc=(c == NCH - 1))
                k.op(k.act, lambda: nc.scalar.activation(out=zs[:, oc, :], in_=pz[:], func=AF.Silu),
                     reads=[pz_b], writes=[zs_b])
            pd, pd_b = cx.psum()
            for c in range(NCH):
                k.op(k.pe, lambda c=c: nc.tensor.matmul(pd[0:NHD, :], w[:, c, 1024:1024 + NHD], ht[:, c, :],
                                                         start=(c == 0), stop=(c == NCH - 1)),
                     reads=[w_b, ht_b], writes=[pd_b], inc=(c == NCH - 1))
            k.op(k.act, lambda: nc.scalar.activation(out=dd[:, 0, :], in_=pd[0:NHD, :], func=AF.Exp,
                                                     bias=prm[:, 0:1], scale=1.0),
                 reads=[pd_b, prm_b], writes=[dd_b])
            k.op(k.act, lambda: nc.scalar.activation(out=dd[:, 0, :], in_=dd[:, 0, :], func=AF.Ln,
                                                     bias=consts["onef"][0:NHD, :], scale=1.0),
                 reads=[dd_b, consts["onef_b"]], writes=[dd_b])
            k.op(k.dve, lambda: nc.vector.tensor_scalar(out=dd[:, 1, :], in0=dd[:, 0, :], scalar1=prm[:, 2:3],
                                                        scalar2=None, op0=ALU.mult),
                 reads=[dd_b, prm_b], writes=[dd_b])
            k.dma(k.sp, ZS.rearrange("(c p) t -> p c t", p=128)[:, :, i * T:(i + 1) * T], zs[:], [zs_b],
                  [cx.dbuf("ZS", i)], zs_b)
            k.dma(k.sp, DT[:, i * T:(i + 1) * T], dd[:, 0, :], [dd_b], [cx.dbuf("DT", i)], dd_b)
            k.dma(k.sp, DA[:, i * T:(i + 1) * T], dd[:, 1, :], [dd_b], [cx.dbuf("DA", i)], dd_b)
        k.release_stage()


def stage_ssd_a2(cx, li, Xi, iname, W, consts, XS, BT, CT):
    nc, k = cx.nc, cx.k
    cf, cf_b = consts["cf"], consts["cf_b"]
    with (
        nc.sbuf_tensor(f"d2w{li}", [128, NCH, 2048], BF16) as w,
        nc.sbuf_tensor(f"d2g{li}", [128, NCH], F32) as gain,
        nc.sbuf_tensor(f"d2cr{li}", [64, 128], F32) as cwr,
        nc.sbuf_tensor(f"d2br{li}", [16, 128], F32) as cbr,
        nc.sbuf_tensor(f"d2cw{li}", [128, 4, 16], F32) as cw,
        nc.sbuf_tensor(f"d2cb{li}", [128, 16], F32) as cbias,
        nc.sbuf_tensor(f"d2x0{li}", [128, NCH, T], F32) as xt0,
        nc.sbuf_tensor(f"d2x1{li}", [128, NCH, T], F32) as xt1,
        nc.sbuf_tensor(f"d2h{li}", [128, NCH, T], BF16) as ht,
        nc.sbuf_tensor(f"d2q{li}", [128, NCH, T], BF16) as sq,
        nc.sbuf_tensor(f"d2r{li}", [128, T], F32) as rstd,
        nc.sbuf_tensor(f"d2hl{li}", [128, 16, 4], F32) as halo,
        nc.sbuf_tensor(f"d2wk0{li}", [128, 4 + T], F32) as wk0,
        nc.sbuf_tensor(f"d2wk1{li}", [128, 4 + T], F32) as wk1,
        nc.sbuf_tensor(f"d2ac0{li}", [128, T], F32) as ac0,
        nc.sbuf_tensor(f"d2ac1{li}", [128, T], F32) as ac1,
        nc.sbuf_tensor(f"d2xo{li}", [128, 16, T], BF16) as xo,
    ):
        w_b, gain_b, cwr_b, cbr_b, cw_b, cbias_b = (k.buf("d2w"), k.buf("d2g"), k.buf("d2cr"), k.buf("d2br"),
                                                    k.buf("d2cw"), k.buf("d2cb"))
        xts = [(xt0, k.buf("d2x0")), (xt1, k.buf("d2x1"))]
        ht_b, sq_b, rstd_b, halo_b, xo_b = k.buf("d2h"), k.buf("d2q"), k.buf("d2r"), k.buf("d2hl"), k.buf("d2xo")
        wks = [(wk0, k.buf("d2wk0")), (wk1, k.buf("d2wk1"))]
        acs = [(ac0, k.buf("d2ac0")), (ac1, k.buf("d2ac1"))]
        for c in range(NCH):
            k.dma(k.pool, w[:, c, :], W["in"][c * 128:(c + 1) * 128, 1024:3072], [], [w_b], w_b, partial=(c > 0))
        load_vec(cx, k.sp, gain[:], gain_b, W["norm"], NCH)
        k.dma(k.sp, cwr[:], W["conv_w"].rearrange("k (c p) -> (k c) p", p=128), [], [cwr_b], cwr_b)
        k.dma(k.sp, cbr[:], W["conv_b"].rearrange("(c p) -> c p", p=128), [], [cbr_b], cbr_b)
        pt, pt_b = cx.psum()
        k.op(k.pe, lambda: nc.tensor.transpose(pt[:, 0:64], cwr[:], cf[0:64, C_ID:C_ID + 64]),
             reads=[cwr_b, cf_b], writes=[pt_b])
        k.op(k.dve, lambda: nc.vector.tensor_copy(out=cw[:].rearrange("p k c -> p (k c)"), in_=pt[:, 0:64]),
             reads=[pt_b], writes=[cw_b])
        pt2, pt2_b = cx.psum()
        k.op(k.pe, lambda: nc.tensor.transpose(pt2[:, 0:16], cbr[:], cf[0:16, C_ID:C_ID + 16]),
             reads=[cbr_b, cf_b], writes=[pt2_b])
        k.op(k.dve, lambda: nc.vector.tensor_copy(out=cbias[:], in_=pt2[:, 0:16]), reads=[pt2_b], writes=[cbias_b])
        k.op(k.pool, lambda: nc.gpsimd.memset(halo[:], 0.0), reads=[], writes=[halo_b])

        def load(i):
            xt, xt_b = xts[i % 2]
            k.dma(k.sp, xt[:], xtile(Xi, i), [cx.dbuf(iname, i)], [xt_b], xt_b)

        load(0)
        n = 0
        for i in range(cx.NT):
            xt, xt_b = xts[i % 2]
            if i + 1 < cx.NT:
                load(i + 1)
            emit_norm(cx, xt, xt_b, (gain, gain_b), ht, ht_b, sq, sq_b, rstd, rstd_b, consts)
            for ch in range(16):
                pp, pp_b = cx.psum()
                for c in range(NCH):
                    k.op(k.pe, lambda c=c: nc.tensor.matmul(pp[:], w[:, c, ch * 128:(ch + 1) * 128], ht[:, c, :],
                                                             start=(c == 0), stop=(c == NCH - 1)),
                         reads=[w_b, ht_b], writes=[pp_b], inc=(c == NCH - 1))
                wk, wk_b = wks[n % 2]
                ac, ac_b = acs[n % 2]
                n += 1
                k.op(k.pool, lambda wk=wk: nc.gpsimd.tensor_copy(out=wk[:, 0:4], in_=halo[:, ch, :]),
                     reads=[halo_b], writes=[wk_b])
                k.op(k.act, lambda wk=wk: nc.scalar.copy(out=wk[:, 4:4 + T], in_=pp[:]), reads=[pp_b], writes=[wk_b])
                k.op(k.pool, lambda wk=wk: nc.gpsimd.tensor_copy(out=halo[:, ch, :], in_=wk[:, T:T + 4]),
                     reads=[wk_b], writes=[halo_b])
                k.op(k.dve, lambda wk=wk, ac=ac: nc.vector.tensor_scalar(
                    out=ac[:], in0=wk[:, 4:4 + T], scalar1=cw[:, 3, ch:ch + 1], scalar2=cbias[:, ch:ch + 1],
                    op0=ALU.mult, op1=ALU.add), reads=[wk_b, cw_b, cbias_b], writes=[ac_b])
                for tap in (2, 1, 0):
                    sh = 3 - tap
                    k.op(k.dve, lambda wk=wk, ac=ac, tap=tap, sh=sh: nc.vector.scalar_tensor_tensor(
                        out=ac[:], in0=wk[:, 4 - sh:4 - sh + T], scalar=cw[:, tap, ch:ch + 1], in1=ac[:],
                        op0=ALU.mult, op1=ALU.add), reads=[wk_b, cw_b, ac_b], writes=[ac_b])
                k.op(k.act, lambda ac=ac: nc.scalar.activation(out=xo[:, ch, :], in_=ac[:], func=AF.Silu),
                     reads=[ac_b], writes=[xo_b])
            k.dma(k.sp, XS.rearrange("(c p) t -> p c t", p=128)[:, :, i * T:(i + 1) * T], xo[:, 0:8, :], [xo_b],
                  [cx.dbuf("XS", i)], xo_b)
            k.dma(k.sp, BT.rearrange("(c p) t -> p c t", p=128)[:, :, i * T:(i + 1) * T], xo[:, 8:12, :], [xo_b],
                  [cx.dbuf("BT", i)], xo_b)
            k.dma(k.sp, CT.rearrange("(c p) t -> p c t", p=128)[:, :, i * T:(i + 1) * T], xo[:, 12:16, :], [xo_b],
                  [cx.dbuf("CT", i)], xo_b)
        k.release_stage()


def stage_ssd_scan(cx, li, W, consts, ZS, DT, DA, XS, BT, CT, GN):
    nc, k = cx.nc, cx.k
    S = cx.S
    L = 256
    NCK = S // L
    cb, cb_b = consts["cbf"], consts["cbf_b"]
    cf, cf_b = consts["cf"], consts["cf_b"]
    with ExitStack() as es:
        onesf = es.enter_context(nc.sbuf_tensor(f"s3of{li}", [128, 256], F32))
        selH = es.enter_context(nc.sbuf_tensor(f"s3sh{li}", [NHD, NHD, 128], F32))
        prm = es.enter_context(nc.sbuf_tensor(f"s3pr{li}", [NHD, 2], F32))
        rhsD = es.enter_context(nc.sbuf_tensor(f"s3rd{li}", [NHD, 8], F32))
        Dvec = es.enter_context(nc.sbuf_tensor(f"s3dv{li}", [128, 8], F32))
        onorm = es.enter_context(nc.sbuf_tensor(f"s3on{li}", [128, 8], F32))
        stf = es.enter_context(nc.sbuf_tensor(f"s3st{li}", [128, NHD, 64], F32))
        stb = es.enter_context(nc.sbuf_tensor(f"s3sb{li}", [128, NHD, 64], BF16))
        dtc = es.enter_context(nc.sbuf_tensor(f"s3dt{li}", [NHD, L], F32))
        dac = es.enter_context(nc.sbuf_tensor(f"s3da{li}", [NHD, L], F32))
        acum = es.enter_context(nc.sbuf_tensor(f"s3ac{li}", [NHD, L], F32))
        dg = es.enter_context(nc.sbuf_tensor(f"s3dg{li}", [NHD, NHD], F32))
        tm = es.enter_context(nc.sbuf_tensor(f"s3tm{li}", [128, 4, NHD], F32))
        tmpd = es.enter_context(nc.sbuf_tensor(f"s3td{li}", [128, 2, NHD], F32))
        wdt = es.enter_context(nc.sbuf_tensor(f"s3wd{li}", [128, 2, NHD], F32))
        xs = es.enter_context(nc.sbuf_tensor(f"s3xs{li}", [128, 8, L], BF16))
        zs = es.enter_context(nc.sbuf_tensor(f"s3zs{li}", [128, 8, L], BF16))
        Bt = es.enter_context(nc.sbuf_tensor(f"s3bt{li}", [128, NG, L], BF16))
        Ct = es.enter_context(nc.sbuf_tensor(f"s3ct{li}", [128, NG, L], BF16))
        Gm0 = es.enter_context(nc.sbuf_tensor(f"s3gm0{li}", [128, 2, L], F32))
        Gm1 = es.enter_context(nc.sbuf_tensor(f"s3gm1{li}", [128, 2, L], F32))
        xstm0 = es.enter_context(nc.sbuf_tensor(f"s3xt0{li}", [128, 2, 256], BF16))
        xstm1 = es.enter_context(nc.sbuf_tensor(f"s3xt1{li}", [128, 2, 256], BF16))
        Bw0 = es.enter_context(nc.sbuf_tensor(f"s3bw0{li}", [128, 2, 128], BF16))
        Bw1 = es.enter_context(nc.sbuf_tensor(f"s3bw1{li}", [128, 2, 128], BF16))
        xw0 = es.enter_context(nc.sbuf_tensor(f"s3xw0{li}", [128, 2, 256], BF16))
        xw1 = es.enter_context(nc.sbuf_tensor(f"s3xw1{li}", [128, 2, 256], BF16))
        Dd0 = es.enter_context(nc.sbuf_tensor(f"s3d0{li}", [128, 4, 2, L], F32))
        Dd1 = es.enter_context(nc.sbuf_tensor(f"s3d1{li}", [128, 4, 2, L], F32))
        Mh0 = es.enter_context(nc.sbuf_tensor(f"s3m0{li}", [128, 4, 2, L], BF16))
        Mh1 = es.enter_context(nc.sbuf_tensor(f"s3m1{li}", [128, 4, 2, L], BF16))
        Ea0 = es.enter_context(nc.sbuf_tensor(f"s3e0{li}", [128, 4, L], F32))
        Ea1 = es.enter_context(nc.sbuf_tensor(f"s3e1{li}", [128, 4, L], F32))
        Ch0 = es.enter_context(nc.sbuf_tensor(f"s3c0{li}", [128, 4, L], BF16))
        Ch1 = es.enter_context(nc.sbuf_tensor(f"s3c1{li}", [128, 4, L], BF16))
        yv = es.enter_context(nc.sbuf_tensor(f"s3yv{li}", [128, 2, L], F32))
        yv2 = es.enter_context(nc.sbuf_tensor(f"s3yw{li}", [128, 2, L], F32))
        sqg = es.enter_context(nc.sbuf_tensor(f"s3sq{li}", [128, 2, L], BF16))
        rs = es.enter_context(nc.sbuf_tensor(f"s3rs{li}", [128, L], F32))
        gn = es.enter_context(nc.sbuf_tensor(f"s3gn{li}", [128, 8, L], BF16))
        B = k.buf
        onesf_b, selH_b, prm_b, rhsD_b, Dvec_b, onorm_b = B("onesf"), B("selH"), B("s3pr"), B("rhsD"), B("Dvec"), B("onorm")
        stf_b = [B(f"stf{h}") for h in range(NHD)]
        stb_b = [B(f"stb{h}") for h in range(NHD)]
        dtc_b, dac_b, acum_b, dg_b, tm_b, tmpd_b, wdt_b = (B("dtc"), B("dac"), B("acum"), B("dg"), B("tm"), B("tmpd"),
                                                           B("wdt"))
        xs_b, zs_b, Bt_b, Ct_b = B("xs"), B("zs"), B("Bt"), B("Ct")
        Gms = [(Gm0, B("Gm0")), (Gm1, B("Gm1"))]
        xstms = [(xstm0, B("xstm0")), (xstm1, B("xstm1"))]
        Bws = [(Bw0, B("Bw0")), (Bw1, B("Bw1"))]
        xws = [(xw0, B("xw0")), (xw1, B("xw1"))]
        gcount = [0]
        Mh3_b = [B("Mh3_0"), B("Mh3_1")]
        Dds = [(Dd0, B("Dd0")), (Dd1, B("Dd1"))]
        Mhs = [(Mh0, B("Mh0")), (Mh1, B("Mh1"))]
        Eas = [(Ea0, B("Ea0")), (Ea1, B("Ea1"))]
        Chs = [(Ch0, B("Ch0")), (Ch1, B("Ch1"))]
        yv_b, sqg_b, rs_b, gn_b = B("yv"), B("sqg"), B("rs"), B("gn")
        yvs = [(yv, yv_b), (yv2, B("yv2"))]
        CB = [{"dtc": (dtc, dtc_b), "dac": (dac, dac_b), "acum": (acum, acum_b), "dg": (dg, dg_b), "tm": (tm, tm_b),
               "tmpd": (tmpd, tmpd_b), "wdt": (wdt, wdt_b), "xs": (xs, xs_b), "zs": (zs, zs_b), "Bt": (Bt, Bt_b),
               "Ct": (Ct, Ct_b)}]
        d2 = {}
        for nm, shp, dt_ in (("dtc", [NHD, L], F32), ("dac", [NHD, L], F32), ("acum", [NHD, L], F32),
                             ("dg", [NHD, NHD], F32), ("tm", [128, 4, NHD], F32), ("tmpd", [128, 2, NHD], F32),
                             ("wdt", [128, 2, NHD], F32), ("xs", [128, 8, L], BF16), ("zs", [128, 8, L], BF16),
                             ("Bt", [128, NG, L], BF16), ("Ct", [128, NG, L], BF16)):
            d2[nm] = (es.enter_context(nc.sbuf_tensor(f"s3{nm}2_{li}", shp, dt_)), B(nm + "2"))
        CB.append(d2)
        gn2 = es.enter_context(nc.sbuf_tensor(f"s3gn2_{li}", [128, 8, L], BF16))
        GNs = [(gn, gn_b), (gn2, B("gn2"))]
        rot = [0]

        def ps():
            p = cx.ps[rot[0] % 4]
            rot[0] += 1
            return p

        ACB = [cx.ps[4], cx.ps[5]]

        Ybanks = [cx.ps[6], cx.ps[7]]
        k.op(k.pool, lambda: nc.gpsimd.memset(onesf[:], 1.0), reads=[], writes=[onesf_b])
        k.op(k.pool, lambda: nc.gpsimd.memset(stf[:], 0.0), reads=[], writes=stf_b)
        k.op(k.pool, lambda: nc.gpsimd.memset(stb[:], 0.0), reads=[], writes=stb_b)
        for h in range(NHD):
            k.op(k.pool, lambda h=h: nc.gpsimd.tensor_scalar(out=selH[:, h, :], in0=onesf[0:NHD, 0:128],
                                                             scalar1=cf[0:NHD, C_ID + h:C_ID + h + 1], scalar2=None,
                                                             op0=ALU.mult),
                 reads=[onesf_b, cf_b], writes=[selH_b])
        k.dma(k.sp, prm[:, 0:1], W["d"].rearrange("(p o) -> p o", o=1), [], [prm_b], prm_b,
              allow_slow_non_contiguous=True)
        load_vec(cx, k.sp, onorm[:], onorm_b, W["out_norm"], 8)
        k.op(k.dve, lambda: nc.vector.tensor_scalar(out=rhsD[:], in0=cf[0:NHD, C_SB2:C_SB2 + 8], scalar1=prm[:, 0:1],
                                                    scalar2=None, op0=ALU.mult),
             reads=[cf_b, prm_b], writes=[rhsD_b])
        pD, pD_b = ps()
        k.op(k.pe, lambda: nc.tensor.matmul(pD[:, 0:8], cf[0:NHD, C_SA:C_SA + 128], rhsD[:], start=True, stop=True),
             reads=[cf_b, rhsD_b], writes=[pD_b])
        k.op(k.dve, lambda: nc.vector.tensor_copy(out=Dvec[:], in_=pD[:, 0:8]), reads=[pD_b], writes=[Dvec_b])

        def chunk_pro(c):
            CBd = CB[c % 2]
            dtc, dtc_b = CBd["dtc"]
            dac, dac_b = CBd["dac"]
            acum, acum_b = CBd["acum"]
            dg, dg_b = CBd["dg"]
            tm, tm_b = CBd["tm"]
            tmpd, tmpd_b = CBd["tmpd"]
            wdt, wdt_b = CBd["wdt"]
            xs, xs_b = CBd["xs"]
            zs, zs_b = CBd["zs"]
            Bt, Bt_b = CBd["Bt"]
            Ct, Ct_b = CBd["Ct"]
            ti = c // (T // L)
            cs = slice(c * L, (c + 1) * L)
            k.dma(k.sp, dtc[:], DT[:, cs], [cx.dbuf("DT", ti)], [dtc_b], dtc_b)
            k.dma(k.sp, dac[:], DA[:, cs], [cx.dbuf("DA", ti)], [dac_b], dac_b)
            k.dma(k.sp, xs[:], XS.rearrange("(c p) t -> p c t", p=128)[:, :, cs], [cx.dbuf("XS", ti)], [xs_b], xs_b)
            k.dma(k.sp, zs[:], ZS.rearrange("(c p) t -> p c t", p=128)[:, :, cs], [cx.dbuf("ZS", ti)], [zs_b], zs_b)
            k.dma(k.sp, Bt[:], BT.rearrange("(c p) t -> p c t", p=128)[:, :, cs], [cx.dbuf("BT", ti)], [Bt_b], Bt_b)
            k.dma(k.sp, Ct[:], CT.rearrange("(c p) t -> p c t", p=128)[:, :, cs], [cx.dbuf("CT", ti)], [Ct_b], Ct_b)
            k.op(k.dve, lambda: nc.vector.tensor_tensor_scan(out=acum[:], data0=onesf[0:NHD, 0:L], data1=dac[:],
                                                             initial=0.0, op0=ALU.mult, op1=ALU.add),
                 reads=[onesf_b, dac_b], writes=[acum_b])
            ptm, ptm_b = ps()
            for q in range(4):
                src, src_b = (dtc, dtc_b) if q < 2 else (acum, acum_b)
                sb_ = q % 2
                k.op(k.pe, lambda q=q, src=src, sb_=sb_: nc.tensor.transpose(
                    ptm[:, q * NHD:(q + 1) * NHD], src[:, sb_ * 128:(sb_ + 1) * 128], cf[0:NHD, C_ID:C_ID + NHD]),
                    reads=[src_b, cf_b], writes=[ptm_b])
            k.op(k.dve, lambda: nc.vector.tensor_copy(out=tm[:].rearrange("p q h -> p (q h)"), in_=ptm[:, 0:4 * NHD]),
                 reads=[ptm_b], writes=[tm_b])
            k.op(k.dve, lambda: nc.vector.tensor_scalar(out=dg[:], in0=cf[0:NHD, C_ID:C_ID + NHD],
                                                        scalar1=acum[:, L - 1:L], scalar2=None, op0=ALU.mult),
                 reads=[cf_b, acum_b], writes=[dg_b])
            pal, pal_b = ps()
            k.op(k.pe, lambda: nc.tensor.matmul(pal[:, 0:NHD], onesf[0:NHD, 0:128], dg[:], start=True, stop=True),
                 reads=[onesf_b, dg_b], writes=[pal_b])
            for sb_ in range(2):
                k.op(k.dve, lambda sb_=sb_: nc.vector.tensor_tensor(out=tmpd[:, sb_, :], in0=pal[:, 0:NHD],
                                                                    in1=tm[:, 2 + sb_, :], op=ALU.subtract),
                     reads=[pal_b, tm_b], writes=[tmpd_b])
            k.op(k.act, lambda: nc.scalar.activation(out=tmpd[:], in_=tmpd[:], func=AF.Exp),
                 reads=[tmpd_b], writes=[tmpd_b])
            k.op(k.dve, lambda: nc.vector.tensor_tensor(out=wdt[:], in0=tmpd[:], in1=tm[:, 0:2, :], op=ALU.mult),
                 reads=[tmpd_b, tm_b], writes=[wdt_b])


        chunk_pro(0)
        for c in range(NCK):
            ti = c // (T // L)
            cs = slice(c * L, (c + 1) * L)
            CBd = CB[c % 2]
            dtc, dtc_b = CBd["dtc"]
            dac, dac_b = CBd["dac"]
            acum, acum_b = CBd["acum"]
            tm, tm_b = CBd["tm"]
            wdt, wdt_b = CBd["wdt"]
            xs, xs_b = CBd["xs"]
            zs, zs_b = CBd["zs"]
            Bt, Bt_b = CBd["Bt"]
            Ct, Ct_b = CBd["Ct"]
            gn, gn_b = GNs[c % 2]

            def prologue(g):
                gb = gcount[0] % 2
                Gm, Gm_b = Gms[gb]
                xdt, xdt_b = xstms[gb]
                xw, xw_b = xws[gb]
                Btm, Btm_b = Bws[gb]
                pacs = []
                pacs_all[g] = pacs

                def acb(hl):
                    pacb, pacb_b = ACB[hl // 2]
                    hi = hl % 2
                    pacs.append((pacb, pacb_b, hi))
                    h = 4 * g + hl
                    k.op(k.pe, lambda pacb=pacb, h=h, hi=hi: nc.tensor.matmul(
                        pacb[:, hi * L:(hi + 1) * L], selH[:, h, :], acum[:], start=True, stop=True,
                        skip_group_check=True), reads=[selH_b, acum_b], writes=[pacb_b])

                pG, pG_b = ps()
                for sb_ in range(2):
                    k.op(k.pe, lambda sb_=sb_, pG=pG: nc.tensor.matmul(
                        pG[:, sb_ * L:(sb_ + 1) * L], Bt[:, g, sb_ * 128:(sb_ + 1) * 128], Ct[:, g, :],
                        start=True, stop=True, skip_group_check=True), reads=[Bt_b, Ct_b], writes=[pG_b])
                    acb(sb_)
                k.op(k.dve, lambda pG=pG: nc.vector.tensor_tensor(
                    out=Gm[:].rearrange("p s t -> p (s t)"), in0=pG[:, 0:2 * L], in1=cf[:, C_SM:C_SM + 512],
                    op=ALU.mult), reads=[pG_b, cf_b], writes=[Gm_b])
                px, px_b = ps()
                pxb = px[:].bitcast(BF16)
                for sb_ in range(2):
                    for ci in range(2):
                        col = sb_ * 256 + ci * 128
                        k.op(k.pe, lambda sb_=sb_, ci=ci, col=col: nc.tensor.transpose(
                            pxb[:, col:col + 128], xs[:, 2 * g + ci, sb_ * 128:(sb_ + 1) * 128], cb[:, C_ID:C_ID + 128]),
                            reads=[xs_b, cb_b], writes=[px_b])
                acb(2)
                pB, pB_b = ps()
                pBb = pB[:].bitcast(BF16)
                for sb_ in range(2):
                    k.op(k.pe, lambda sb_=sb_: nc.tensor.transpose(
                        pBb[:, sb_ * 128:(sb_ + 1) * 128], Bt[:, g, sb_ * 128:(sb_ + 1) * 128], cb[:, C_ID:C_ID + 128]),
                        reads=[Bt_b, cb_b], writes=[pB_b])
                acb(3)
                xin = pxb[:, 0:512].rearrange("p (s h d) -> p s h d", s=2, h=4)
                k.op(k.dve, lambda: nc.vector.tensor_tensor(
                    out=xdt[:].rearrange("p s (h d) -> p s h d", h=4), in0=xin,
                    in1=tm[:, 0:2, 4 * g:4 * g + 4].unsqueeze(3).to_broadcast([128, 2, 4, 64]), op=ALU.mult),
                    reads=[px_b, tm_b], writes=[xdt_b])
                k.op(k.dve, lambda: nc.vector.tensor_tensor(
                    out=xw[:].rearrange("p s (h d) -> p s h d", h=4), in0=xin,
                    in1=wdt[:, 0:2, 4 * g:4 * g + 4].unsqueeze(3).to_broadcast([128, 2, 4, 64]), op=ALU.mult),
                    reads=[px_b, wdt_b], writes=[xw_b])
                k.op(k.act, lambda: nc.scalar.copy(out=Btm[:].rearrange("p s n -> p (s n)"), in_=pBb[:, 0:256]),
                     reads=[pB_b], writes=[Btm_b])
                return gb

            def heads_front(g, gb):
                Gm, Gm_b = Gms[gb]
                Dd, Dd_b = Dds[gb]
                Mh, Mh_b = Mhs[gb]
                Ea, Ea_b = Eas[gb]
                Ch, Ch_b = Chs[gb]
                pacs = pacs_all[g]
                for hl in range(4):
                    h = 4 * g + hl
                    pacb, pacb_b, hi = pacs[hl]
                    for sb_ in range(2):
                        k.op(k.act, lambda sb_=sb_, pacb=pacb, hl=hl, h=h, hi=hi: nc.scalar.activation(
                            out=Dd[:, hl, sb_, :], in_=pacb[:, hi * L:(hi + 1) * L], func=AF.Relu,
                            bias=tm[:, 2 + sb_, h:h + 1], scale=-1.0), reads=[pacb_b, tm_b], writes=[Dd_b])
                for hp in range(2):
                    pacb, pacb_b, _ = pacs[2 * hp]
                    k.op(k.act, lambda pacb=pacb, hp=hp: nc.scalar.activation(
                        out=Ea[:, 2 * hp:2 * hp + 2, :].rearrange("p h t -> p (h t)"), in_=pacb[:, 0:2 * L],
                        func=AF.Exp), reads=[pacb_b], writes=[Ea_b])
                k.op(k.act, lambda: nc.scalar.activation(out=Dd[:].rearrange("p h s t -> p (h s t)"),
                                                         in_=Dd[:].rearrange("p h s t -> p (h s t)"), func=AF.Exp,
                                                         scale=-1.0),
                     reads=[Dd_b], writes=[Dd_b])
                k.op(k.pool, lambda: nc.gpsimd.tensor_tensor(
                    out=Ch[:], in0=Ea[:], in1=Ct[:, g, :].unsqueeze(1).to_broadcast([128, 4, L]), op=ALU.mult),
                    reads=[Ct_b, Ea_b], writes=[Ch_b])
                k.op(k.dve, lambda: nc.vector.tensor_tensor(
                    out=Mh[:, 0:3], in0=Dd[:, 0:3], in1=Gm[:].unsqueeze(1).to_broadcast([128, 3, 2, L]), op=ALU.mult),
                    reads=[Dd_b, Gm_b], writes=[Mh_b])
                k.op(k.pool, lambda: nc.gpsimd.tensor_tensor(
                    out=Mh[:, 3:4], in0=Dd[:, 3:4], in1=Gm[:].unsqueeze(1).to_broadcast([128, 1, 2, L]), op=ALU.mult),
                    reads=[Dd_b, Gm_b], writes=[Mh3_b[gb]])

            def heads_back(g, gb):
                xdt, xdt_b = xstms[gb]
                xw, xw_b = xws[gb]
                Btm, Btm_b = Bws[gb]
                Mh, Mh_b = Mhs[gb]
                Ea, Ea_b = Eas[gb]
                Ch, Ch_b = Chs[gb]
                Yt, Y_b = Ybanks[gb]
                for hl in range(4):
                    h = 4 * g + hl
                    ci, po = hl // 2, (hl % 2) * 64
                    yo = Yt[po:po + 64, ci * L:(ci + 1) * L]
                    xcol = ci * 128 + po
                    mhb = Mh_b if hl < 3 else Mh3_b[gb]
                    k.op(k.pe, lambda yo=yo, hl=hl, xcol=xcol: nc.tensor.matmul(
                        yo, xdt[:, 0, xcol:xcol + 64], Mh[:, hl, 0, :], start=True, stop=False, skip_group_check=True),
                        reads=[xdt_b, mhb], writes=[Y_b], inc=False)
                    k.op(k.pe, lambda yo=yo, hl=hl, xcol=xcol: nc.tensor.matmul(
                        yo, xdt[:, 1, xcol:xcol + 64], Mh[:, hl, 1, :], start=False, stop=False, skip_group_check=True),
                        reads=[xdt_b, mhb], writes=[Y_b], inc=False)
                    k.op(k.pe, lambda yo=yo, hl=hl, h=h: nc.tensor.matmul(
                        yo, stb[:, h, :], Ch[:, hl, :], start=False, stop=True, skip_group_check=True),
                        reads=[stb_b[h], Ch_b], writes=[Y_b])
                pS, pS_b = ps()
                for sb_ in range(2):
                    k.op(k.pe, lambda sb_=sb_, pS=pS: nc.tensor.matmul(
                        pS[:, 0:256], Btm[:, sb_, :], xw[:, sb_, :], start=(sb_ == 0), stop=(sb_ == 1)),
                        reads=[Btm_b, xw_b], writes=[pS_b], inc=(sb_ == 1))
                sfb = [stf_b[4 * g + x] for x in range(4)]
                sbb = [stb_b[4 * g + x] for x in range(4)]
                k.op(k.dve, lambda: nc.vector.tensor_tensor(
                    out=stf[:, 4 * g:4 * g + 4, :], in0=stf[:, 4 * g:4 * g + 4, :],
                    in1=Ea[:, :, L - 1:L].to_broadcast([128, 4, 64]), op=ALU.mult),
                    reads=sfb + [Ea_b], writes=sfb)
                k.op(k.dve, lambda pS=pS: nc.vector.tensor_tensor(
                    out=stf[:, 4 * g:4 * g + 4, :], in0=stf[:, 4 * g:4 * g + 4, :],
                    in1=pS[:, 0:256].rearrange("p (h d) -> p h d", h=4), op=ALU.add),
                    reads=sfb + [pS_b], writes=sfb)
                k.op(k.act, lambda: nc.scalar.copy(out=stb[:, 4 * g:4 * g + 4, :], in_=stf[:, 4 * g:4 * g + 4, :]),
                     reads=sfb, writes=sbb)
                yv, yv_b = yvs[gb]
                for ci in range(2):
                    cc = 2 * g + ci
                    k.op(k.dve, lambda ci=ci, cc=cc: nc.vector.scalar_tensor_tensor(
                        out=yv[:, ci, :], in0=xs[:, cc, :], scalar=Dvec[:, cc:cc + 1], in1=Yt[:, ci * L:(ci + 1) * L],
                        op0=ALU.mult, op1=ALU.add), reads=[xs_b, Dvec_b, Y_b], writes=[yv_b])
                k.op(k.dve, lambda: nc.vector.tensor_tensor(out=yv[:], in0=yv[:], in1=zs[:, 2 * g:2 * g + 2, :],
                                                            op=ALU.mult), reads=[yv_b, zs_b], writes=[yv_b])

            def epi(g, gb):
                yv, yv_b = yvs[gb]
                k.op(k.act, lambda: nc.scalar.activation(out=sqg[:], in_=yv[:], func=AF.Square),
                     reads=[yv_b], writes=[sqg_b])
                pss, pss_b = ps()
                for ci in range(2):
                    k.op(k.pe, lambda ci=ci, pss=pss: nc.tensor.matmul(pss[:, 0:L], consts["ones"][:], sqg[:, ci, :],
                                                                        start=(ci == 0), stop=(ci == 1)),
                         reads=[sqg_b, consts["ones_b"]], writes=[pss_b], inc=(ci == 1))
                k.op(k.act, lambda pss=pss: nc.scalar.activation(out=rs[:], in_=pss[:, 0:L], func=AF.Ln,
                                                                 bias=consts["eps"][:], scale=1.0 / 256),
                     reads=[pss_b, consts["eps_b"]], writes=[rs_b])
                k.op(k.act, lambda: nc.scalar.activation(out=rs[:], in_=rs[:], func=AF.Exp, scale=-0.5),
                     reads=[rs_b], writes=[rs_b])
                for ci in range(2):
                    cc = 2 * g + ci
                    k.op(k.dve, lambda ci=ci, cc=cc: nc.vector.scalar_tensor_tensor(
                        out=gn[:, cc, :], in0=yv[:, ci, :], scalar=onorm[:, cc:cc + 1], in1=rs[:],
                        op0=ALU.mult, op1=ALU.mult), reads=[yv_b, onorm_b, rs_b], writes=[gn_b])

            gbs = {}
            pacs_all = {}
            if SSD_PIPE:
                gbs[0] = prologue(0)
                gcount[0] += 1
                heads_front(0, gbs[0])
                for g in range(NG):
                    if g + 1 < NG:
                        gbs[g + 1] = prologue(g + 1)
                        gcount[0] += 1
                        heads_front(g + 1, gbs[g + 1])
                    heads_back(g, gbs[g])
                    if g >= 1:
                        epi(g - 1, gbs[g - 1])
                    if g == 0 and c + 1 < NCK:
                        chunk_pro(c + 1)
                epi(NG - 1, gbs[NG - 1])
            else:
                for g in range(NG):
                    gbs[g] = prologue(g)
                    gcount[0] += 1
                    heads_front(g, gbs[g])
                    heads_back(g, gbs[g])
                    epi(g, gbs[g])
            k.dma(k.sp, GN.rearrange("(c p) t -> p c t", p=128)[:, :, cs], gn[:], [gn_b], [cx.dbuf("GN", ti)], gn_b,
                  partial=(c % 2 == 1))
        k.release_stage()


WEIGHT_SPECS = [
    ("mix_norm", [4, 1024]), ("ffn_norm", [4, 1024]),
    ("pool_in", [2, 1024, 512]), ("pool_group", [2, 4, 128, 256]), ("pool_scale", [2, 1024]),
    ("ssd_in", [1, 1024, 3088]), ("ssd_conv_w", [1, 4, 2048]), ("ssd_conv_b", [1, 2048]),
    ("ssd_dt_bias", [1, 16]), ("ssd_a_log", [1, 16]), ("ssd_d", [1, 16]),
    ("ssd_out_norm", [1, 1024]), ("ssd_out", [1, 1024, 1024]),
    ("sb_qkv", [1, 1024, 1536]), ("sb_q_norm", [1, 64]), ("sb_k_norm", [1, 64]), ("sb_out", [1, 512, 1024]),
    ("ffn_gate", [4, 1024, 1408]), ("ffn_up", [4, 1024, 1408]), ("ffn_down", [4, 1408, 1024]),
]


def local_weights(inp, r):
    f = lambda a: np.ascontiguousarray(np.asarray(a, dtype=np.float32))
    cat = np.concatenate
    w = {}
    for n in ("mix_norm", "ffn_norm", "pool_scale", "sb_q_norm", "sb_k_norm"):
        w[n] = f(inp[n])
    pin = np.asarray(inp["pool_in"])
    w["pool_in"] = f(cat([pin[:, :, (2 * g + r) * 128:(2 * g + r + 1) * 128] for g in range(4)], axis=2))
    w["pool_group"] = f(np.asarray(inp["pool_group"])[:, :, r * 128:(r + 1) * 128, :])
    si = np.asarray(inp["ssd_in"])
    w["ssd_in"] = f(cat([si[:, :, r * 1024:(r + 1) * 1024], si[:, :, 2048 + r * 1024:2048 + (r + 1) * 1024],
                         si[:, :, 4096 + r * 512:4096 + (r + 1) * 512], si[:, :, 5120 + r * 512:5120 + (r + 1) * 512],
                         si[:, :, 6144 + r * 16:6144 + (r + 1) * 16]], axis=2))
    cwv = np.asarray(inp["ssd_conv_w"])
    w["ssd_conv_w"] = f(cat([cwv[:, :, r * 1024:(r + 1) * 1024], cwv[:, :, 2048 + r * 512:2048 + (r + 1) * 512],
                             cwv[:, :, 3072 + r * 512:3072 + (r + 1) * 512]], axis=2))
    cbv = np.asarray(inp["ssd_conv_b"])
    w["ssd_conv_b"] = f(cat([cbv[:, r * 1024:(r + 1) * 1024], cbv[:, 2048 + r * 512:2048 + (r + 1) * 512],
                             cbv[:, 3072 + r * 512:3072 + (r + 1) * 512]], axis=1))
    for n in ("ssd_dt_bias", "ssd_a_log", "ssd_d"):
        w[n] = f(np.asarray(inp[n])[:, r * 16:(r + 1) * 16])
    w["ssd_out_norm"] = f(np.asarray(inp["ssd_out_norm"])[:, r * 1024:(r + 1) * 1024])
    w["ssd_out"] = f(np.asarray(inp["ssd_out"])[:, r * 1024:(r + 1) * 1024, :])
    q = np.asarray(inp["sb_qkv"])
    w["sb_qkv"] = f(cat([q[:, :, r * 512:(r + 1) * 512], q[:, :, 1024 + r * 512:1024 + (r + 1) * 512],
                         q[:, :, 2048 + r * 512:2048 + (r + 1) * 512]], axis=2))
    w["sb_out"] = f(np.asarray(inp["sb_out"])[:, r * 512:(r + 1) * 512, :])
    w["ffn_gate"] = f(np.asarray(inp["ffn_gate"])[:, :, r * 1408:(r + 1) * 1408])
    w["ffn_up"] = f(np.asarray(inp["ffn_up"])[:, :, r * 1408:(r + 1) * 1408])
    w["ffn_down"] = f(np.asarray(inp["ffn_down"])[:, r * 1408:(r + 1) * 1408, :])
    return w


def build_program(S, layers, do_mixer=True, do_ffn=True):
    nc = bass.Bass("TRN2", target_bir_lowering=False, num_devices=8)
    cx = Ctx(nc, S)
    k = cx.k
    NT = S // T
    xT = nc.dram_tensor("xT", [NT, D, T], F32, kind="ExternalInput").ap()
    yT = nc.dram_tensor("yT", [NT, D, T], F32, kind="ExternalOutput").ap()
    Wt = {n: nc.dram_tensor(n, shp, F32, kind="ExternalInput").ap() for n, shp in WEIGHT_SPECS}
    cx.Q = nc.dram_tensor("resQ", [NT, D, T], F32, kind="Internal").ap()
    R = [nc.dram_tensor(f"resR{i}", [NT, D, T], F32, kind="Internal").ap() for i in range(2)]
    ones = nc.alloc_sbuf_tensor("c_ones", [128, 128], BF16)
    ones_b = k.buf("ones")
    k.op(k.pool, lambda: nc.gpsimd.memset(ones[:], 1.0), reads=[], writes=[ones_b])
    epst = nc.alloc_sbuf_tensor("c_eps", [128, 1], F32)
    eps_b = k.buf("eps")
    k.op(k.pool, lambda: nc.gpsimd.memset(epst[:], EPS), reads=[], writes=[eps_b])
    onef = nc.alloc_sbuf_tensor("c_onef", [128, 1], F32)
    onef_b = k.buf("onef")
    k.op(k.pool, lambda: nc.gpsimd.memset(onef[:], 1.0), reads=[], writes=[onef_b])
    cst = nc.dram_tensor("cst", [128, C_TOT], F32, kind="ExternalInput").ap()
    cbf = nc.alloc_sbuf_tensor("c_cbf", [128, C_TOT], BF16)
    cf = nc.alloc_sbuf_tensor("c_cf", [128, C_F32W], F32)
    cbf_b, cf_b = k.buf("cbf"), k.buf("cf")
    k.dma(k.pool, cbf[:], cst[:, :], [], [cbf_b], cbf_b)
    k.dma(k.sp, cf[:], cst[:, 0:C_F32W], [], [cf_b], cf_b)
    k.stage_bufs = []
    consts = {"ones": ones, "ones_b": ones_b, "eps": epst, "eps_b": eps_b, "onef": onef, "onef_b": onef_b,
              "cbf": cbf, "cbf_b": cbf_b, "cf": cf, "cf_b": cf_b}
    scr = {}

    def scratch(name, shape, dt=BF16):
        if name not in scr:
            scr[name] = nc.dram_tensor("scr_" + name, shape, dt, kind="Internal").ap()
        return scr[name]

    cur, cur_name = xT, "xT"
    nstage = [0]

    def nxt():
        nstage[0] += 1
        return R[nstage[0] % 2], f"R{nstage[0]}"

    for idx, li in enumerate(layers):
        kind, j = li % 3, li // 3
        if do_mixer:
            mo, mo_name = nxt()
            if kind == 0:
                W = {"in": Wt["pool_in"][j], "group": Wt["pool_group"][j], "scale": Wt["pool_scale"][j],
                     "norm": Wt["mix_norm"][li]}
                stage_pool(cx, li, cur, cur_name, mo, mo_name, W, consts)
            elif kind == 2:
                QT, KT, V, OT = (scratch("QT", [512, S]), scratch("KT", [512, S]), scratch("V", [S, 512]),
                                 scratch("OT", [512, S]))
                W = {"qkv": Wt["sb_qkv"][j], "qn": Wt["sb_q_norm"][j], "kn": Wt["sb_k_norm"][j],
                     "norm": Wt["mix_norm"][li]}
                stage_sb_qkv(cx, li, cur, cur_name, W, consts, QT, KT, V)
                stage_sb_attn(cx, li, consts, QT, KT, V, OT)
                stage_outproj(cx, f"sb{li}", cur, cur_name, mo, mo_name, OT, "OT", Wt["sb_out"][j], 4)
            else:
                ZS, XS, GN = scratch("ZS", [1024, S]), scratch("XS", [1024, S]), scratch("GN", [1024, S])
                BT, CT = scratch("BT", [512, S]), scratch("CT", [512, S])
                DT, DA = scratch("DT", [NHD, S], F32), scratch("DA", [NHD, S], F32)
                W = {"in": Wt["ssd_in"][j], "conv_w": Wt["ssd_conv_w"][j], "conv_b": Wt["ssd_conv_b"][j],
                     "dt_bias": Wt["ssd_dt_bias"][j], "a_log": Wt["ssd_a_log"][j], "d": Wt["ssd_d"][j],
                     "out_norm": Wt["ssd_out_norm"][j], "norm": Wt["mix_norm"][li]}
                stage_ssd_a1(cx, li, cur, cur_name, W, consts, ZS, DT, DA)
                stage_ssd_a2(cx, li, cur, cur_name, W, consts, XS, BT, CT)
                stage_ssd_scan(cx, li, W, consts, ZS, DT, DA, XS, BT, CT, GN)
                stage_outproj(cx, f"ssd{li}", cur, cur_name, mo, mo_name, GN, "GN", Wt["ssd_out"][j], 8)
            cur, cur_name = mo, mo_name
        if do_ffn:
            Xo, oname = nxt()
            Wf = {"gate": Wt["ffn_gate"][li], "up": Wt["ffn_up"][li], "down": Wt["ffn_down"][li],
                  "norm": Wt["ffn_norm"][li]}
            stage_ffn(cx, li, cur, cur_name, Xo, oname, Wf, consts)
            cur, cur_name = Xo, oname
    cp_b = k.buf("outcopy")
    for i in range(NT):
        k.dma(k.sp, yT[i], cur[i], [cx.dbuf(cur_name, i)], [cx.dbuf("yT", i)], cp_b, is_out=True)
    k.finish()
    return nc


def make_in_maps(inputs, S):
    x = np.asarray(inputs["x"], dtype=np.float32)[:, :S]
    Bn = x.shape[0]
    NT = S // T
    cst = make_consts()
    lw = [local_weights(inputs, r) for r in range(2)]
    in_maps = []
    for c in range(8):
        b, r = (c // 2) % Bn, c % 2
        xt = np.ascontiguousarray(x[b].T.reshape(D, NT, T).transpose(1, 0, 2))
        m = {"xT": xt, "cst": cst}
        m.update(lw[r])
        in_maps.append(m)
    return in_maps


def untile(yt):
    NT = yt.shape[0]
    return np.ascontiguousarray(yt.transpose(1, 0, 2).reshape(D, NT * T).T)


_PROG_CACHE = {}


def kernel(**inputs):
    x = np.asarray(inputs["x"], dtype=np.float32)
    Bn, S, _ = x.shape
    layers = list(range(DEPTH))
    key = (S, tuple(layers))
    if key not in _PROG_CACHE:
        _PROG_CACHE[key] = build_program(S, layers)
    nc = _PROG_CACHE[key]
    in_maps = make_in_maps(inputs, S)
    res = run_bass_kernel_spmd(nc, in_maps, core_ids=list(range(8)))
    out = np.empty_like(x)
    for b in range(Bn):
        out[b] = untile(res.results[2 * b]["yT"])
    return out
```
